# Optimizing a Trainium2 kernel written in Bass

```python
import math
import jax, jax.numpy as jnp
from jax import lax
import numpy as np

D_MODEL = 1024
BATCH = 16
SEQ = 4096
DEPTH = 4

SB_HEADS = 8
SB_HEAD_DIM = D_MODEL // SB_HEADS
SB_WIDTH = SB_HEADS * SB_HEAD_DIM
SB_BLOCK = 128
S5_WIDTH = D_MODEL // 2
S5_GROUP = 16
S5_GROUPS = S5_WIDTH // S5_GROUP
S5_STATE = 64
S5_DT_MIN = 1e-3
S5_DT_MAX = 1e-1
EVEN_IN = 4 * SB_WIDTH + 2 * S5_WIDTH
EVEN_MIX = SB_WIDTH + S5_WIDTH
GLA_HEADS = 4
GLA_KEY = D_MODEL // 2
GLA_VAL = D_MODEL
GLA_DK = GLA_KEY // GLA_HEADS
GLA_DV = GLA_VAL // GLA_HEADS
GLA_RANK = 16
GLA_TAU = 16.0
GLA_CHUNK = 64
ODD_IN = 2 * GLA_KEY + 2 * GLA_VAL + GLA_RANK
N_EVEN = (DEPTH + 1) // 2
N_ODD = DEPTH // 2
EPS = 1e-6

kernel_name = "hybrid_stickbreak_s5_gla_trunk"


def rms_norm(x, g):
    xf = x.astype(jnp.float32)
    y = xf * lax.rsqrt(jnp.mean(xf * xf, axis=-1, keepdims=True) + EPS)
    return (y * g.astype(jnp.float32)).astype(x.dtype)


def stick_breaking_attention(q, k, v):
    bsz, L, h, dh = q.shape
    qf = q.astype(jnp.float32) * (dh ** -0.5)
    kf = k.astype(jnp.float32)
    vf = v.astype(jnp.float32)
    outs = []
    for blk in range(L // SB_BLOCK):
        q0 = blk * SB_BLOCK
        kend = q0 + SB_BLOCK
        z = jnp.einsum('bthd,bshd->bhts', qf[:, q0:kend], kf[:, :kend])
        t_idx = q0 + jnp.arange(SB_BLOCK)[:, None]
        s_idx = jnp.arange(kend)[None, :]
        causal = s_idx < t_idx
        log_beta = jax.nn.log_sigmoid(z)
        log_one_minus = jnp.where(causal, log_beta - z, 0.0)
        rest = lax.cumsum(log_one_minus, axis=3, reverse=True) - log_one_minus
        w = jnp.where(causal, jnp.exp(log_beta + rest), 0.0)
        outs.append(jnp.einsum('bhts,bshd->bthd', w, vf[:, :kend]))
    return jnp.concatenate(outs, axis=1).astype(q.dtype)


def s5_mixer(u, lam_re, lam_im, log_dt, b_re, b_im, c_re, c_im, d_skip, w_glu, b_glu):
    f32 = jnp.float32
    bsz, L, _ = u.shape
    uf = u.astype(f32).reshape(bsz, L, S5_GROUPS, S5_GROUP)
    lam = lax.complex(lam_re.astype(f32), lam_im.astype(f32))
    dt = jnp.exp(log_dt.astype(f32))[:, None]
    lam_bar = jnp.exp(lam * dt)
    bmat = lax.complex(b_re.astype(f32), b_im.astype(f32))
    b_bar = ((lam_bar - 1.0) / lam)[..., None] * bmat
    bu = lax.complex(jnp.einsum('gnp,blgp->blgn', b_bar.real, uf),
                     jnp.einsum('gnp,blgp->blgn', b_bar.imag, uf))
    a = jnp.broadcast_to(lam_bar, bu.shape)

    def combine(left, right):
        a_l, x_l = left
        a_r, x_r = right
        return a_r * a_l, a_r * x_l + x_r

    _, hstate = lax.associative_scan(combine, (a, bu), axis=1)
    y = (jnp.einsum('gpn,blgn->blgp', c_re.astype(f32), hstate.real)
         - jnp.einsum('gpn,blgn->blgp', c_im.astype(f32), hstate.imag)
         + d_skip.astype(f32) * uf)
    y = jax.nn.gelu(y.reshape(bsz, L, S5_WIDTH))
    y = y * jax.nn.sigmoid(y @ w_glu.astype(f32) + b_glu.astype(f32))
    return y.astype(u.dtype)


def gla_chunked(q, k, v, log_a):
    f32 = jnp.float32
    bsz, L, h, dk = q.shape
    dv = v.shape[-1]
    c = GLA_CHUNK
    n = L // c
    qc = (q.astype(f32) * (dk ** -0.5)).reshape(bsz, n, c, h, dk)
    kc = k.astype(f32).reshape(bsz, n, c, h, dk)
    vc = v.astype(f32).reshape(bsz, n, c, h, dv)
    g = jnp.cumsum(log_a.astype(f32).reshape(bsz, n, c, h, dk), axis=2)
    g_last = g[:, :, -1]
    q_dec = qc * jnp.exp(g)
    k_inv = kc * jnp.exp(-g)
    k_dec = kc * jnp.exp(g_last[:, :, None] - g)
    scores = jnp.einsum('bnthk,bnshk->bnhts', q_dec, k_inv)
    mask = jnp.tril(jnp.ones((c, c), dtype=bool))
    scores = jnp.where(mask, scores, 0.0)
    o_intra = jnp.einsum('bnhts,bnshv->bnthv', scores, vc)

    def step(state, inp):
        qd, kd, vv, gl = inp
        o = jnp.einsum('bthk,bhkv->bthv', qd, state)
        state = jnp.exp(gl)[..., None] * state + jnp.einsum('bthk,bthv->bhkv', kd, vv)
        return state, o

    s0 = jnp.zeros((bsz, h, dk, dv), f32)
    _, o_inter = lax.scan(step, s0, (jnp.moveaxis(q_dec, 1, 0), jnp.moveaxis(k_dec, 1, 0),
                                     jnp.moveaxis(vc, 1, 0), jnp.moveaxis(g_last, 1, 0)))
    o = o_intra + jnp.moveaxis(o_inter, 0, 1)
    return o.reshape(bsz, L, h, dv)


def setup_inputs(seed: int = 0) -> dict:
    key = jax.random.key(seed)
    ks = jax.random.split(key, 24)
    f32 = jnp.float32
    nrm = lambda k, shape, s: jax.random.normal(k, shape, f32) * s
    x = jax.random.normal(ks[0], (BATCH, SEQ, D_MODEL), f32)
    even_norm_g = 1.0 + nrm(ks[1], (N_EVEN, D_MODEL), 0.02)
    even_w_in = nrm(ks[2], (N_EVEN, D_MODEL, EVEN_IN), D_MODEL ** -0.5)
    sb_q_norm_g = 1.0 + nrm(ks[3], (N_EVEN, SB_HEAD_DIM), 0.02)
    sb_k_norm_g = 1.0 + nrm(ks[4], (N_EVEN, SB_HEAD_DIM), 0.02)
    s5_lambda_re = -0.5 + nrm(ks[5], (N_EVEN, S5_GROUPS, S5_STATE), 0.01)
    n_idx = jnp.arange(S5_STATE, dtype=f32)
    s5_lambda_im = math.pi * n_idx + nrm(ks[6], (N_EVEN, S5_GROUPS, S5_STATE), 0.01)
    s5_log_dt = jax.random.uniform(ks[7], (N_EVEN, S5_GROUPS), f32,
                                   math.log(S5_DT_MIN), math.log(S5_DT_MAX))
    s5_b_re = nrm(ks[8], (N_EVEN, S5_GROUPS, S5_STATE, S5_GROUP), (2 * S5_GROUP) ** -0.5)
    s5_b_im = nrm(ks[9], (N_EVEN, S5_GROUPS, S5_STATE, S5_GROUP), (2 * S5_GROUP) ** -0.5)
    s5_c_re = nrm(ks[10], (N_EVEN, S5_GROUPS, S5_GROUP, S5_STATE), S5_STATE ** -0.5)
    s5_c_im = nrm(ks[11], (N_EVEN, S5_GROUPS, S5_GROUP, S5_STATE), S5_STATE ** -0.5)
    s5_d = nrm(ks[12], (N_EVEN, S5_GROUPS, S5_GROUP), 1.0)
    s5_w_glu = nrm(ks[13], (N_EVEN, S5_WIDTH, S5_WIDTH), S5_WIDTH ** -0.5)
    s5_b_glu = nrm(ks[14], (N_EVEN, S5_WIDTH), 0.02)
    even_w_out = nrm(ks[15], (N_EVEN, EVEN_MIX, D_MODEL), EVEN_MIX ** -0.5)
    odd_norm_g = 1.0 + nrm(ks[16], (N_ODD, D_MODEL), 0.02)
    odd_w_in = nrm(ks[17], (N_ODD, D_MODEL, ODD_IN), D_MODEL ** -0.5)
    gla_w_gate = nrm(ks[18], (N_ODD, GLA_RANK, GLA_KEY), GLA_RANK ** -0.5)
    gla_b_gate = nrm(ks[19], (N_ODD, GLA_KEY), 0.1)
    gla_o_norm_g = 1.0 + nrm(ks[20], (N_ODD, GLA_DV), 0.02)
    odd_w_out = nrm(ks[21], (N_ODD, GLA_VAL, D_MODEL), GLA_VAL ** -0.5)
    return {"x": x, "even_norm_g": even_norm_g, "even_w_in": even_w_in,
            "sb_q_norm_g": sb_q_norm_g, "sb_k_norm_g": sb_k_norm_g,
            "s5_lambda_re": s5_lambda_re, "s5_lambda_im": s5_lambda_im, "s5_log_dt": s5_log_dt,
            "s5_b_re": s5_b_re, "s5_b_im": s5_b_im, "s5_c_re": s5_c_re, "s5_c_im": s5_c_im,
            "s5_d": s5_d, "s5_w_glu": s5_w_glu, "s5_b_glu": s5_b_glu, "even_w_out": even_w_out,
            "odd_norm_g": odd_norm_g, "odd_w_in": odd_w_in, "gla_w_gate": gla_w_gate,
            "gla_b_gate": gla_b_gate, "gla_o_norm_g": gla_o_norm_g, "odd_w_out": odd_w_out}


def reference(x, even_norm_g, even_w_in, sb_q_norm_g, sb_k_norm_g, s5_lambda_re, s5_lambda_im,
              s5_log_dt, s5_b_re, s5_b_im, s5_c_re, s5_c_im, s5_d, s5_w_glu, s5_b_glu, even_w_out,
              odd_norm_g, odd_w_in, gla_w_gate, gla_b_gate, gla_o_norm_g, odd_w_out):
    bsz, L, _ = x.shape
    for layer in range(DEPTH):
        i = layer // 2
        if layer % 2 == 0:
            h = rms_norm(x, even_norm_g[i])
            proj = h @ even_w_in[i]
            q, k, v, z_a, u, z_b = jnp.split(
                proj, [SB_WIDTH, 2 * SB_WIDTH, 3 * SB_WIDTH, 4 * SB_WIDTH,
                       4 * SB_WIDTH + S5_WIDTH], axis=-1)
            q = rms_norm(q.reshape(bsz, L, SB_HEADS, SB_HEAD_DIM), sb_q_norm_g[i])
            k = rms_norm(k.reshape(bsz, L, SB_HEADS, SB_HEAD_DIM), sb_k_norm_g[i])
            v = v.reshape(bsz, L, SB_HEADS, SB_HEAD_DIM)
            o_a = stick_breaking_attention(q, k, v).reshape(bsz, L, SB_WIDTH) * jax.nn.silu(z_a)
            o_b = s5_mixer(u, s5_lambda_re[i], s5_lambda_im[i], s5_log_dt[i], s5_b_re[i],
                           s5_b_im[i], s5_c_re[i], s5_c_im[i], s5_d[i], s5_w_glu[i],
                           s5_b_glu[i]) * jax.nn.silu(z_b)
            x = x + jnp.concatenate([o_a, o_b], axis=-1) @ even_w_out[i]
        else:
            h = rms_norm(x, odd_norm_g[i])
            proj = h @ odd_w_in[i]
            q, k, v, z, r = jnp.split(
                proj, [GLA_KEY, 2 * GLA_KEY, 2 * GLA_KEY + GLA_VAL,
                       2 * GLA_KEY + 2 * GLA_VAL], axis=-1)
            log_a = jax.nn.log_sigmoid((r @ gla_w_gate[i] + gla_b_gate[i]).astype(jnp.float32)) / GLA_TAU
            o = gla_chunked(q.reshape(bsz, L, GLA_HEADS, GLA_DK),
                            k.reshape(bsz, L, GLA_HEADS, GLA_DK),
                            v.reshape(bsz, L, GLA_HEADS, GLA_DV),
                            log_a.reshape(bsz, L, GLA_HEADS, GLA_DK))
            o = rms_norm(o, gla_o_norm_g[i]).astype(x.dtype).reshape(bsz, L, GLA_VAL)
            x = x + (o * jax.nn.silu(z)) @ odd_w_out[i]
    return x
```

```python
import math
from contextlib import ExitStack

import numpy as np
import concourse.bass as bass
import concourse.mybir as mybir
from concourse.bass_utils import run_bass_kernel_spmd

F32 = mybir.dt.float32
BF16 = mybir.dt.bfloat16
AF = mybir.ActivationFunctionType
ALU = mybir.AluOpType
AX = mybir.AxisListType

D_MODEL = 1024
EPS = 1e-6
N_CORES = 8
import os
OVERLAP = False
N_FILL = int(os.environ.get('N_FILL', '0'))
SB_W = int(os.environ.get('SB_W', '2'))
S5_W = int(os.environ.get('S5_W', '3'))
S5_STOP = int(os.environ.get('S5_STOP', '0'))


class _Op:
    __slots__ = ("eng", "fn", "deps", "val", "needed", "dkey", "dval", "same_ok")


class Prog:
    ENGS = ("pe", "act", "dve", "pool", "sp")

    def __init__(self, nc, es):
        self.nc = nc
        self.es = es
        self.q = {e: [] for e in self.ENGS}
        self.W = {}
        self.R = {}
        self.dcount = {}
        self.n_ops = 0
        self.last_op = {}
        self.last_dma = {}
        self.pending = {}

    @staticmethod
    def _is_psum(b):
        return (isinstance(b, tuple) and b[0] == "psM") or b == "psT"

    @staticmethod
    def _chan(op):
        return ("d", op.dkey) if op.dkey is not None else op.eng

    def op(self, eng, fn, reads=(), writes=(), dkey=None, waw=True, same_ok=False):
        if eng == "pe":
            same_ok = True
        o = _Op()
        o.eng, o.fn, o.val, o.needed, o.dkey, o.same_ok = eng, fn, 0, False, dkey, same_ok
        deps = {}

        def add(d):
            deps[id(d)] = d

        for b in reads:
            for d in self.W.get(b, {}).values():
                add(d)
            if self._is_psum(b):
                for chn, d in self.R.get(b, {}).items():
                    if chn != eng:
                        add(d)
        for b in writes:
            for d in self.R.get(b, {}).values():
                add(d)
            if waw:
                for d in self.W.get(b, {}).values():
                    add(d)
        if eng in self.pending:
            for d in self.pending.pop(eng):
                add(d)
        o.deps = list(deps.values())
        if dkey is None:
            self.last_op[eng] = o
        else:
            self.last_dma[dkey] = o
        if dkey is not None:
            self.dcount[dkey] = self.dcount.get(dkey, 0) + 16
            o.dval = self.dcount[dkey]
        ch = self._chan(o)
        for b in reads:
            self.R.setdefault(b, {})[ch] = o
        for b in writes:
            if waw:
                self.W[b] = {ch: o}
                self.R[b] = {}
            else:
                self.W.setdefault(b, {})[ch] = o
        self.q[eng].append(o)
        self.n_ops += 1
        return o

    def barrier(self):
        deps = list(self.last_op.values()) + list(self.last_dma.values())
        for e in self.ENGS:
            self.pending[e] = list(deps)

    def dma(self, fn, reads=(), writes=(), dkey=None, eng="sp", waw=False):
        assert dkey is not None
        return self.op(eng, fn, reads, writes, dkey=dkey, waw=waw)

    def finalize(self, final_keys=()):
        nc, es = self.nc, self.es
        for e in self.ENGS:
            for o in self.q[e]:
                for d in o.deps:
                    if d.dkey is None:
                        if d.eng == o.eng and o.same_ok:
                            continue
                        d.needed = True
        for e in self.ENGS:
            c = 0
            for o in self.q[e]:
                if o.dkey is None and o.needed:
                    c += 1
                    o.val = c
        esem = {e: es.enter_context(nc.semaphore("S_" + e)) for e in self.ENGS}
        dsem = {}
        for k in self.dcount:
            dsem[k] = es.enter_context(nc.semaphore("D%d" % len(dsem)))
        self.n_sems = len(esem) + len(dsem)
        handles = {"pe": "tensor", "act": "scalar", "dve": "vector", "pool": "gpsimd", "sp": "sync"}
        block = es.enter_context(nc.Block())
        final_waits = [(dsem[k], self.dcount[k]) for k in final_keys]
        for e in self.ENGS:
            ops = self.q[e]

            def body(eng, ops=ops, e=e):
                seen = {}
                for o in ops:
                    need = {}
                    for d in o.deps:
                        if d.dkey is not None:
                            key, v = ("d", d.dkey), d.dval
                        else:
                            if d.eng == e and o.same_ok:
                                continue
                            key, v = d.eng, d.val
                        if v > seen.get(key, 0) and v > need.get(key, 0):
                            need[key] = v
                    for key, v in need.items():
                        seen[key] = v
                        sem = dsem[key[1]] if isinstance(key, tuple) else esem[key]
                        eng.wait_ge(sem, v)
                    ins = o.fn(eng)
                    if o.dkey is not None:
                        ins.then_inc(dsem[o.dkey], 16)
                    elif o.needed:
                        ins.then_inc(esem[e], 1)
                if e == "sp":
                    for sem, v in final_waits:
                        eng.wait_ge(sem, v)

            getattr(block, handles[e])(body)


class Ring:
    def __init__(self, name, n):
        self.name, self.n, self.i = name, n, -1

    def next(self):
        self.i += 1
        return self.i % self.n

    def key(self, slot):
        return (self.name, slot)


def run_interleaved(gens):
    gens = list(gens)
    while gens:
        for g in list(gens):
            try:
                next(g)
            except StopIteration:
                gens.remove(g)


def interleave(gens):
    gens = list(gens)
    while gens:
        for g in list(gens):
            try:
                next(g)
            except StopIteration:
                gens.remove(g)
        yield


def interleave1(gens):
    gens = list(gens)
    while gens:
        for g in list(gens):
            try:
                next(g)
            except StopIteration:
                gens.remove(g)
                continue
            yield


def run_weighted(pairs):
    pairs = list(pairs)
    while pairs:
        for p in list(pairs):
            g, w = p
            for _ in range(w):
                try:
                    next(g)
                except StopIteration:
                    pairs.remove(p)
                    break


class Builder:
    def __init__(self, L=4096, NSEQ=2, depth=4, dump=()):
        self.L, self.NSEQ, self.depth, self.dump = L, NSEQ, depth, tuple(dump)
        self.NG = L // 512
        self.nc = bass.Bass("TRN2", target_bir_lowering=False)
        self.es = ExitStack()
        self.rings = {}

    def sb(self, name, shape, dt):
        return self.es.enter_context(self.nc.sbuf_tensor(name, list(shape), dt))

    def ps(self, name, shape, dt):
        return self.es.enter_context(self.nc.psum_tensor(name, list(shape), dt))

    def dram(self, name, shape, dt, kind=None):
        if name in self.dump:
            kind = "ExternalOutput"
        if kind is None:
            return self.nc.dram_tensor(name, list(shape), dt).ap()
        return self.nc.dram_tensor(name, list(shape), dt, kind=kind).ap()

    def ring(self, name, n, shape, dt):
        tiles = [self.sb("%s%d" % (name, i), shape, dt) for i in range(n)]
        r = Ring(name, n)
        r.tiles = tiles
        self.rings[name] = r
        return r

    def nxt(self, r):
        s = r.next()
        return r.tiles[s], r.key(s)

    def build(self):
        nc, es = self.nc, self.es
        L, NSEQ = self.L, self.NSEQ
        with es:
            self.P = P = Prog(nc, es)
            self.declare_io()
            self.setup_consts()
            x_in = self.x_in
            bufs = [self.xbufA, self.xbufB]
            for layer in range(self.depth):
                last = layer == self.depth - 1
                x_out = self.y_out if last else bufs[layer % 2]
                kin = "x_in" if layer == 0 else "xbuf%d" % ((layer - 1) % 2)
                kout = "y_out" if last else "xbuf%d" % (layer % 2)
                if layer % 2 == 0 and os.environ.get('ONLY_ODD') != '1':
                    self.even_layer(layer // 2, x_in, kin, x_out, kout)
                else:
                    self.odd_layer(layer // 2, x_in, kin, x_out, kout)
                x_in = x_out
            P.finalize(final_keys=self.final_keys)
        return nc

    def declare_io(self):
        L, NSEQ = self.L, self.NSEQ
        d = self.dram
        self.x_in = d("x", [NSEQ, L, 1024], F32, "ExternalInput")
        self.y_out = d("y", [NSEQ, L, 1024], F32, "ExternalOutput")
        self.xbufA = d("xbufA", [NSEQ, L, 1024], F32)
        self.xbufB = d("xbufB", [NSEQ, L, 1024], F32)
        I = lambda n, s: d(n, s, F32, "ExternalInput")
        self.even_norm_g = I("even_norm_g", [2, 128, 8])
        self.even_w_in = I("even_w_in", [2, 1024, 5120])
        self.sb_q_g = I("sb_q_norm_g", [2, 128, 1])
        self.sb_k_g = I("sb_k_norm_g", [2, 128, 1])
        self.even_w_out = I("even_w_out", [2, 1536, 1024])
        self.odd_norm_g = I("odd_norm_g", [2, 128, 8])
        self.odd_w_in = I("odd_w_in", [2, 1024, 3088])
        self.gla_w_gate = I("gla_w_gate", [2, 16, 512])
        self.gla_b_gate = I("gla_b_gate", [2, 128, 4])
        self.gla_o_g = I("gla_o_norm_g", [2, 128, 2])
        self.odd_w_out = I("odd_w_out", [2, 1024, 1024])
        self.s5_lam = I("s5_lam", [2, 128, 2, 32])
        self.s5_dt = I("s5_log_dt", [2, 128, 32])
        self.s5_b0 = I("s5_b0", [2, 128, 32, 16])
        self.s5_c0 = I("s5_c0", [2, 128, 32, 16])
        self.s5_d = I("s5_d", [2, 128, 4])
        self.s5_w_glu = I("s5_w_glu", [2, 512, 512])
        self.s5_b_glu = I("s5_b_glu", [2, 128, 4])
        B = lambda n, s: d(n, s, BF16)
        Fd = lambda n, s: d(n, s, F32)
        self.qT = B("qT", [NSEQ, 8, 128, L])
        self.kT = B("kT", [NSEQ, 8, 128, L])
        self.vtm = B("vtm", [NSEQ, L, 1024])
        self.gzT = Fd("gzT", [NSEQ, 1536, L])
        self.uT = B("uT", [NSEQ, 512, L])
        self.mixT = B("mixT", [NSEQ, 1536, L])
        self.q2T = Fd("q2T", [NSEQ, 512, L])
        self.k2T = Fd("k2T", [NSEQ, 512, L])
        self.lgT = Fd("lgT", [NSEQ, 512, L])
        self.final_keys = []

    def setup_consts(self):
        P, nc = self.P, self.nc
        sb = self.sb
        self.identf = sb("identf", [128, 128], F32)
        self.ident = sb("ident", [128, 128], BF16)
        self.ones = sb("ones", [128, 128], BF16)
        self.negones = sb("negones", [128, 128], BF16)
        self.maskLT = sb("maskLT", [128, 128], BF16)
        self.maskLE = sb("maskLE", [128, 128], F32)
        self.negtri = sb("negtri", [128, 128], BF16)
        tmpf = sb("ctmpf", [128, 128], F32)
        identf, ident = self.identf, self.ident
        P.op("pool", lambda e: e.memset(identf[:], 0.0), writes=["identf"])
        P.op("pool", lambda e: e.affine_select(out=identf[:], in_=identf[:], pattern=[[-1, 128]],
                                               compare_op=ALU.not_equal, fill=1.0, base=0,
                                               channel_multiplier=1), reads=["identf"], writes=["identf"])
        P.op("dve", lambda e: e.tensor_copy(out=ident[:], in_=identf[:]), reads=["identf"], writes=["ident"])
        P.op("pool", lambda e: e.memset(self.ones[:], 1.0), writes=["ones"])
        P.op("pool", lambda e: e.memset(self.negones[:], -1.0), writes=["negones"])
        P.op("pool", lambda e: e.memset(tmpf[:], 1.0), writes=["ctmpf"])
        P.op("pool", lambda e: e.affine_select(out=tmpf[:], in_=tmpf[:], pattern=[[1, 128]],
                                               compare_op=ALU.is_gt, fill=0.0, base=0,
                                               channel_multiplier=-1), reads=["ctmpf"], writes=["ctmpf"])
        P.op("dve", lambda e: e.tensor_copy(out=self.maskLT[:], in_=tmpf[:]), reads=["ctmpf"], writes=["maskLT"])
        P.op("pool", lambda e: e.memset(self.maskLE[:], 1.0), writes=["maskLE"])
        P.op("pool", lambda e: e.affine_select(out=self.maskLE[:], in_=self.maskLE[:], pattern=[[1, 128]],
                                               compare_op=ALU.is_ge, fill=0.0, base=0,
                                               channel_multiplier=-1), reads=["maskLE"], writes=["maskLE"])
        P.op("pool", lambda e: e.memset(tmpf[:], -1.0), reads=["maskLT"], writes=["ctmpf"])
        P.op("pool", lambda e: e.affine_select(out=tmpf[:], in_=tmpf[:], pattern=[[-1, 128]],
                                               compare_op=ALU.is_ge, fill=0.0, base=0,
                                               channel_multiplier=1), reads=["ctmpf"], writes=["ctmpf"])
        P.op("dve", lambda e: e.tensor_copy(out=self.negtri[:], in_=tmpf[:]), reads=["ctmpf"], writes=["negtri"])
        self.negbig = sb("negbig", [128, 128], BF16)
        P.op("pool", lambda e: e.memset(tmpf[:], -30000.0), reads=["negtri"], writes=["ctmpf"])
        P.op("pool", lambda e: e.affine_select(out=tmpf[:], in_=tmpf[:], pattern=[[-1, 128]],
                                               compare_op=ALU.is_ge, fill=0.0, base=0,
                                               channel_multiplier=1), reads=["ctmpf"], writes=["ctmpf"])
        P.op("dve", lambda e: e.tensor_copy(out=self.negbig[:], in_=tmpf[:]), reads=["ctmpf"], writes=["negbig"])
        self.CONST = ["ident", "identf", "ones", "negones", "maskLT", "maskLE", "negtri"]

        self.arena = sb("arena", [128, 40960], BF16)
        self.Wb = self.arena[:, :].rearrange("p (k f) -> p k f", k=8)
        self.Wo = sb("Wo", [128, 12, 1024], BF16)
        self.wstage = self.ring("wst", 2, [128, 1024], F32)
        self.gcol = sb("gcol", [128, 8], F32)
        self.xt = self.ring("xt", 2, [128, 4, 1024], F32)
        self.junk = sb("junk", [128, 1024], BF16)
        self.ss4 = sb("ss4", [128, 4], F32)
        self.rstd4 = sb("rstd4", [128, 4], F32)
        self.hb = self.ring("hb", 2, [128, 1024], BF16)
        self.hT = self.ring("hT", 2, [128, 8, 512], BF16)
        self.psT = [self.ps("psT%d" % i, [128, 1024], BF16) for i in range(1)]
        self.psM = [self.ps("psM%d" % i, [128, 512], F32) for i in range(7)]
        self.psM_i = 0
        self.ev32 = self.ring("ev32", 3, [128, 512], F32)
        self.evbf = self.ring("evbf", 4, [128, 512], BF16)
        self.wk32 = self.ring("wk32", 3, [128, 512], F32)
        self.wkbf = self.ring("wkbf", 3, [128, 512], BF16)

    def ov(self, off, shape, dt):
        n = int(np.prod(shape[1:]))
        esz = 4 if dt == F32 else 2
        a = self.arena[:, off // 2: off // 2 + n * esz // 2]
        if dt == F32:
            a = a.bitcast(F32)
        if len(shape) == 3:
            a = a.rearrange("p (a b) -> p a b", a=shape[1])
        return a

    def psum(self):
        if getattr(self, "ps_restrict", False) and os.environ.get("PSR", "1") == "1":
            self.psR_i = getattr(self, "psR_i", 0) + 1
            if self.psR_i % 2:
                return self.psM[6], ("psM", 6)
            return self.psT[0][:, :].bitcast(F32), "psT"
        i = self.psM_i % len(self.psM)
        self.psM_i += 1
        return self.psM[i], ("psM", i)

    def load_weights(self, w_dram, n_k, n_f, dst, dst_key, gain_dram=None):
        P = self.P
        gcol = self.gcol
        if gain_dram is not None:
            P.dma(lambda e: e.dma_start(out=gcol[:], in_=gain_dram), writes=["gcol"], dkey="gcol")
        CH = 1024
        first = True
        for kc in range(n_k):
            for f0 in range(0, n_f, CH):
                fw = min(CH, n_f - f0)
                st, sk = self.nxt(self.wstage)
                P.dma(lambda e, st=st, kc=kc, f0=f0, fw=fw: e.dma_start(
                    out=st[:, :fw], in_=w_dram[kc * 128:(kc + 1) * 128, f0:f0 + fw]), writes=[sk], dkey=sk)
                eng = "pool" if (kc % 2 == 0) else "dve"
                if gain_dram is not None:
                    P.op(eng, lambda e, st=st, kc=kc, f0=f0, fw=fw: e.tensor_scalar(
                        out=dst[:, kc, f0:f0 + fw], in0=st[:, :fw], scalar1=gcol[:, kc:kc + 1], scalar2=None,
                        op0=ALU.mult), reads=[sk, "gcol"], writes=[dst_key], waw=first)
                else:
                    P.op(eng, lambda e, st=st, kc=kc, f0=f0, fw=fw: e.tensor_copy(
                        out=dst[:, kc, f0:f0 + fw], in_=st[:, :fw]), reads=[sk], writes=[dst_key], waw=first)
                first = False

    def prefetch_x(self, x_dram, xkey, s, tg):
        P = self.P
        xt, xk = self.nxt(self.xt)
        P.dma(lambda e: e.dma_start(out=xt[:], in_=x_dram[s, tg * 512:(tg + 1) * 512, :].rearrange(
            "(j p) d -> p j d", p=128)), reads=[xkey], writes=[xk], dkey=xk)
        if not hasattr(self, "xpref"):
            self.xpref = {}
        self.xpref[(xkey, s, tg)] = (xt, xk)

    def load_norm_group(self, x_dram, xkey, s, tg):
        P = self.P
        if (xkey, s, tg) not in getattr(self, "xpref", {}):
            self.prefetch_x(x_dram, xkey, s, tg)
        xt, xk = self.xpref.pop((xkey, s, tg))
        ss4, rstd4, junk = self.ss4, self.rstd4, self.junk
        for j in range(4):
            P.op("act", lambda e, j=j: e.activation(out=junk[:], in_=xt[:, j, :], func=AF.Square,
                                                    accum_out=ss4[:, j:j + 1]),
                 reads=[xk], writes=["junk", "ss4"])
        P.op("act", lambda e: e.activation(out=rstd4[:], in_=ss4[:], func=AF.Ln, scale=1.0 / 1024, bias=EPS),
             reads=["ss4"], writes=["rstd4"])
        P.op("act", lambda e: e.activation(out=rstd4[:], in_=rstd4[:], func=AF.Exp, scale=-0.5), reads=["rstd4"], writes=["rstd4"])
        hT, hk = self.nxt(self.hT)
        psT = self.psT[0]
        for j in range(4):
            hb, hbk = self.nxt(self.hb)
            P.op("dve", lambda e, j=j, hb=hb: e.tensor_scalar(out=hb[:], in0=xt[:, j, :], scalar1=rstd4[:, j:j + 1],
                                                              scalar2=None, op0=ALU.mult),
                 reads=[xk, "rstd4"], writes=[hbk])
            for kc in range(8):
                P.op("pe", lambda e, kc=kc, hb=hb: e.transpose(out=psT[:, kc * 128:(kc + 1) * 128],
                                                               in_=hb[:, kc * 128:(kc + 1) * 128],
                                                               identity=self.ident[:]),
                     reads=[hbk, "ident"], writes=["psT"], same_ok=True, waw=(kc == 0))
            P.op("act", lambda e, j=j: e.copy(out=hT[:, :, j * 128:(j + 1) * 128],
                                              in_=psT[:].rearrange("p (k t) -> p k t", k=8)),
                 reads=["psT"], writes=[hk], waw=(j == 0))
        return hT, hk, xt, xk

    def proj_fm(self, hT, hk, f0, M=128):
        P = self.P
        Wb = self.Wb
        ps, pk = self.psum()
        for kc in range(8):
            P.op("pe", lambda e, kc=kc: e.matmul(ps[:M, :], lhsT=Wb[:, kc, f0:f0 + M], rhs=hT[:, kc, :],
                                                 start=(kc == 0), stop=(kc == 7)),
                 reads=[hk, "Wb"], writes=[pk], same_ok=True, waw=(kc == 0))
        return ps, pk

    def proj_tm(self, hT, hk, j, f0):
        P = self.P
        Wb = self.Wb
        ps, pk = self.psum()
        for kc in range(8):
            P.op("pe", lambda e, kc=kc: e.matmul(ps[:, :], lhsT=hT[:, kc, j * 128:(j + 1) * 128],
                                                 rhs=Wb[:, kc, f0:f0 + 512], start=(kc == 0), stop=(kc == 7)),
                 reads=[hk, "Wb"], writes=[pk], same_ok=True, waw=(kc == 0))
        return ps, pk

    def store(self, tile_ap, tkey, dram_ap, dkey_dram):
        self.P.dma(lambda e: e.dma_start(out=dram_ap, in_=tile_ap), reads=[tkey], writes=[dkey_dram], dkey=tkey)

    def even_layer(self, i, x_in, kin, x_out, kout):
        P = self.P
        L, NSEQ = self.L, self.NSEQ
        self.load_weights(self.even_w_in[i], 8, 5120, self.Wb, "Wb", gain_dram=self.even_norm_g[i])
        self.load_weights(self.even_w_out[i], 12, 1024, self.Wo, "Wo")
        if not hasattr(self, "qkg"):
            self.qkg = self.sb("qkg", [128, 2], F32)
        qkg = self.qkg
        P.dma(lambda e: e.dma_start(out=qkg[:, 0:1], in_=self.sb_q_g[i]), writes=["qkg"], dkey="qkg")
        P.dma(lambda e: e.dma_start(out=qkg[:, 1:2], in_=self.sb_k_g[i]), writes=["qkg"], dkey="qkg")
        P.op("dve", lambda e: e.tensor_scalar(out=qkg[:, 0:1], in0=qkg[:, 0:1], scalar1=128 ** -0.5, scalar2=None,
                                              op0=ALU.mult), reads=["qkg"], writes=["qkg"])
        self.s5_prep(i)
        groups = [(s, tg) for s in range(NSEQ) for tg in range(self.NG)]
        for k, (s, tg) in enumerate(groups):
            if k == 0:
                self.prefetch_x(x_in, kin, s, tg)
            if k + 1 < len(groups):
                self.prefetch_x(x_in, kin, *groups[k + 1])
            self.even_inproj_group(i, x_in, kin, s, tg)
        for s in range(NSEQ):
            for _ in self.sb_attention(s):
                pass
        for _ in self.s5_main(i):
            pass
        for s in range(NSEQ):
            self.out_proj(s, x_in, kin, x_out, kout, 12, self.mixT, "mixT")

    def even_inproj_group(self, i, x_in, kin, s, tg):
        P = self.P
        hT, hk, xt, xk = self.load_norm_group(x_in, kin, s, tg)
        cols = slice(tg * 512, (tg + 1) * 512)
        for which in range(2):
            dst = self.qT if which == 0 else self.kT
            dkey = "qT" if which == 0 else "kT"
            for h in range(8):
                ps, pk = self.proj_fm(hT, hk, which * 1024 + h * 128)
                sq, sqk = self.nxt(self.wkbf)
                P.op("act", lambda e, ps=ps, sq=sq: e.activation(out=sq[:], in_=ps[:], func=AF.Square),
                     reads=[pk], writes=[sqk])
                ps2, pk2 = self.psum()
                P.op("pe", lambda e, ps2=ps2, sq=sq: e.matmul(ps2[:], lhsT=self.ones[:], rhs=sq[:], start=True, stop=True),
                     reads=[sqk, "ones"], writes=[pk2])
                rs, rsk = self.nxt(self.wk32)
                P.op("act", lambda e, ps2=ps2, rs=rs: e.activation(out=rs[:], in_=ps2[:], func=AF.Ln,
                                                                   scale=1.0 / 128, bias=EPS),
                     reads=[pk2], writes=[rsk])
                P.op("act", lambda e, rs=rs: e.activation(out=rs[:], in_=rs[:], func=AF.Exp, scale=-0.5), reads=[rsk], writes=[rsk])
                ob, obk = self.nxt(self.evbf)
                P.op("dve", lambda e, ps=ps, rs=rs, ob=ob, which=which: e.scalar_tensor_tensor(
                    out=ob[:], in0=ps[:], scalar=self.qkg[:, which:which + 1], in1=rs[:], op0=ALU.mult, op1=ALU.mult),
                    reads=[pk, rsk, "qkg"], writes=[obk])
                self.store(ob[:], obk, dst[s, h, :, cols], (dkey, s))
        for j in range(4):
            for half in range(2):
                ps, pk = self.proj_tm(hT, hk, j, 2048 + half * 512)
                ob, obk = self.nxt(self.evbf)
                P.op("act" if half else "dve",
                     (lambda e, ps=ps, ob=ob: e.copy(out=ob[:], in_=ps[:])) if half else
                     (lambda e, ps=ps, ob=ob: e.tensor_copy(out=ob[:], in_=ps[:])),
                     reads=[pk], writes=[obk])
                r0 = tg * 512 + j * 128
                self.store(ob[:], obk, self.vtm[s, r0:r0 + 128, half * 512:(half + 1) * 512], ("vtm", s))
        for c in range(12):
            f0 = 3072 + c * 128 if c < 8 else 4608 + (c - 8) * 128
            ps, pk = self.proj_fm(hT, hk, f0)
            ob, obk = self.nxt(self.ev32)
            P.op("act", lambda e, ps=ps, ob=ob: e.activation(out=ob[:], in_=ps[:], func=AF.Silu),
                 reads=[pk], writes=[obk])
            self.store(ob[:], obk, self.gzT[s, c * 128:(c + 1) * 128, cols], ("gzT", s))
        for c in range(4):
            ps, pk = self.proj_fm(hT, hk, 4096 + c * 128)
            ob, obk = self.nxt(self.evbf)
            P.op("dve", lambda e, ps=ps, ob=ob: e.tensor_copy(out=ob[:], in_=ps[:]), reads=[pk], writes=[obk])
            self.store(ob[:], obk, self.uT[s, c * 128:(c + 1) * 128, cols], ("uT", s))

    def out_proj(self, s, x_in, kin, x_out, kout, n_k, mix_dram, mixkey):
        P = self.P
        if not hasattr(self, "mxin"):
            self.mxin = Ring("mxin", 2)
            self.mxin.tiles = [self.ov(k * 12288, [128, 12, 512], BF16) for k in range(2)]
            self.xres = self.xt
        def issue(tg):
            mx, mk = self.nxt(self.mxin)
            P.dma(lambda e, mx=mx, tg=tg: e.dma_start(
                out=mx[:, :n_k, :], in_=mix_dram[s, :n_k * 128, tg * 512:(tg + 1) * 512].rearrange(
                    "(kc p) t -> p kc t", p=128)), reads=[(mixkey, s)], writes=[mk], dkey=mk)
            xr, xrk = self.nxt(self.xres)
            P.dma(lambda e, xr=xr, tg=tg: e.dma_start(
                out=xr[:], in_=x_in[s, tg * 512:(tg + 1) * 512, :].rearrange("(j p) d -> p j d", p=128)),
                reads=[kin], writes=[xrk], dkey=xrk)
            return mx, mk, xr, xrk

        pend = issue(0)
        for tg in range(self.NG):
            mx, mk, xr, xrk = pend
            if tg + 1 < self.NG:
                pend = issue(tg + 1)
            for j in range(4):
                for half in range(2):
                    ps, pk = self.psum()
                    for kc in range(n_k):
                        P.op("pe", lambda e, ps=ps, kc=kc, j=j, half=half, mx=mx: e.matmul(
                            ps[:], lhsT=mx[:, kc, j * 128:(j + 1) * 128],
                            rhs=self.Wo[:, kc, half * 512:(half + 1) * 512], start=(kc == 0), stop=(kc == n_k - 1)),
                            reads=[mk, "Wo"], writes=[pk], same_ok=True, waw=(kc == 0))
                    P.op("dve", lambda e, ps=ps, j=j, half=half, xr=xr: e.tensor_tensor(
                        out=xr[:, j, half * 512:(half + 1) * 512], in0=ps[:], in1=xr[:, j, half * 512:(half + 1) * 512],
                        op=ALU.add), reads=[pk, xrk], writes=[xrk])
            P.dma(lambda e, xr=xr, tg=tg: e.dma_start(
                out=x_out[s, tg * 512:(tg + 1) * 512, :].rearrange("(j p) d -> p j d", p=128), in_=xr[:]),
                reads=[xrk], writes=[kout], dkey=(xrk, "st"))
            if kout == "y_out":
                if (xrk, "st") not in self.final_keys:
                    self.final_keys.append((xrk, "st"))

    def sb_attention(self, s):
        P = self.P
        L = self.L
        NB = L // 128
        if not OVERLAP:
            P.barrier()
        KB = 1024
        qkv = []
        NQ = 1 if OVERLAP else 2
        for r in range(NQ):
            base = r * 3 * (L * 2)
            qkv.append((self.ov(base, [128, L], BF16), self.ov(base + 2 * L, [128, L], BF16),
                        self.ov(base + 4 * L, [128, NB, 128], BF16)))
        cbase = NQ * 3 * L * 2
        NCH = 3
        chains = []
        for c in range(NCH):
            b = cbase + c * 7 * KB
            chains.append(dict(e32=self.ov(b, [128, 512], F32), Lb=self.ov(b + 2 * KB, [128, 512], BF16),
                               wb=self.ov(b + 3 * KB, [128, 512], BF16), R32=self.ov(b + 4 * KB, [128, 512], F32),
                               Rbf=self.ov(b + 6 * KB, [128, 512], BF16), id=c,
                               psZ=self.psM[2 * c], psZk=("psM", 2 * c), psO=self.psM[2 * c + 1], psOk=("psM", 2 * c + 1)))
        gbase = cbase + NCH * 7 * KB
        gz = [self.ov(gbase + k * 2 * KB, [128, 512], F32) for k in range(NCH)]
        assert gbase + NCH * 2 * KB <= (51 * KB if OVERLAP else 81920)
        for h in range(8):
            qh, kh, vh = qkv[h % NQ]
            kq, kk, kv = ("sbq", h % NQ), ("sbk", h % NQ), ("sbv", h % NQ)
            P.dma(lambda e, qh=qh, h=h: e.dma_start(out=qh, in_=self.qT[s, h]), reads=[("qT", s)], writes=[kq], dkey=kq)
            P.dma(lambda e, kh=kh, h=h: e.dma_start(out=kh, in_=self.kT[s, h]), reads=[("kT", s)], writes=[kk], dkey=kk)
            P.dma(lambda e, vh=vh, h=h: e.dma_start(
                out=vh, in_=self.vtm[s, :, h * 128:(h + 1) * 128].rearrange("(b p) d -> p b d", p=128)),
                reads=[("vtm", s)], writes=[kv], dkey=kv)

            def chain(qg, ch, h=h, qh=qh, kh=kh, vh=vh, kq=kq, kk=kk, kv=kv):
                c = ch["id"]
                tag = lambda n: ("sbc", n, c)
                e32, Lb, wb, R32, Rbf = ch["e32"], ch["Lb"], ch["wb"], ch["R32"], ch["Rbf"]
                psZ, psZk, psO, psOk = ch["psZ"], ch["psZk"], ch["psO"], ch["psOk"]
                g = gz[c]
                P.dma(lambda e: e.dma_start(out=g, in_=self.gzT[s, h * 128:(h + 1) * 128, qg * 512:(qg + 1) * 512]),
                      reads=[("gzT", s)], writes=[tag("gz")], dkey=tag("gz"))
                P.op("pool", lambda e: e.memset(R32, 0.0), writes=[tag("R32")])
                kbs = list(reversed(range(4 * qg + 4)))

                def zmm(kb):
                    c0 = max(0, kb - 4 * qg) * 128
                    q0 = qg * 512 + c0
                    P.op("pe", lambda e: e.matmul(
                        psZ[:, c0:512], lhsT=kh[:, kb * 128:(kb + 1) * 128], rhs=qh[:, q0:qg * 512 + 512],
                        start=True, stop=False, skip_group_check=True), reads=[kq, kk], writes=[psZk])
                    if kb >= 4 * qg:
                        P.op("pe", lambda e: e.matmul(
                            psZ[:, c0:c0 + 128], lhsT=self.ident[:], rhs=self.negbig[:], start=False, stop=False,
                            skip_group_check=True), reads=["ident", "negbig"], writes=[psZk], waw=False)

                zmm(kbs[0])
                for idx, kb in enumerate(kbs):
                    c0 = max(0, kb - 4 * qg) * 128
                    N = 512 - c0
                    diag = kb >= 4 * qg
                    P.op("act", lambda e, c0=c0: e.activation(out=e32[:, c0:512], in_=psZ[:, c0:512], func=AF.Exp),
                         reads=[psZk], writes=[tag("e32")])
                    yield
                    P.op("act", lambda e, c0=c0: e.activation(out=Lb[:, c0:512], in_=e32[:, c0:512], func=AF.Ln, bias=1.0),
                         reads=[tag("e32")], writes=[tag("Lb")])
                    yield
                    P.op("pe", lambda e, c0=c0, idx=idx: e.matmul(
                        psZ[:, c0:512], lhsT=self.negtri[:], rhs=Lb[:, c0:512], start=False, stop=(idx == 0),
                        skip_group_check=True), reads=[tag("Lb"), "negtri"], writes=[psZk], same_ok=True, waw=False)
                    if idx > 0:
                        P.op("pe", lambda e, c0=c0: e.matmul(
                            psZ[:, c0:512], lhsT=self.negones[:], rhs=Rbf[:, c0:512], start=False, stop=True,
                            skip_group_check=True), reads=[tag("Rbf"), "negones"], writes=[psZk], same_ok=True, waw=False)
                    P.op("act", lambda e, c0=c0: e.activation(out=wb[:, c0:512], in_=psZ[:, c0:512], func=AF.Exp),
                         reads=[psZk], writes=[tag("wb")])
                    yield
                    if idx + 1 < len(kbs):
                        zmm(kbs[idx + 1])
                    P.op("pe", lambda e, kb=kb, c0=c0, idx=idx: e.matmul(
                        psO[:, c0:512], lhsT=vh[:, kb, :], rhs=wb[:, c0:512], start=(idx == 0), stop=(idx == len(kbs) - 1),
                        skip_group_check=True), reads=[tag("wb"), kv], writes=[psOk], same_ok=True, waw=(idx == 0))
                    for _f in range(N_FILL):
                        P.op("pe", lambda e: e.matmul(self.psM[6][:, :], lhsT=self.ones[:], rhs=qh[:, 0:512], start=True,
                                                      stop=True, skip_group_check=True), reads=[], writes=[("psM", 6)])
                    if idx < len(kbs) - 1:
                        c1 = max(0, kbs[idx + 1] - 4 * qg) * 128
                        P.op("pool", lambda e, c0=c0: e.tensor_tensor(out=R32[:, c0:512], in0=R32[:, c0:512],
                                                                      in1=Lb[:, c0:512], op=ALU.add),
                             reads=[tag("Lb"), tag("R32")], writes=[tag("R32")])
                        P.op("dve", lambda e, c1=c1: e.tensor_copy(out=Rbf[:, c1:512], in_=R32[:, c1:512]),
                             reads=[tag("R32")], writes=[tag("Rbf")])
                    yield
                ob, obk = self.nxt(self.evbf)
                P.op("dve", lambda e, ob=ob: e.tensor_tensor(out=ob[:], in0=psO[:], in1=g, op=ALU.mult),
                     reads=[psOk, tag("gz")], writes=[obk])
                self.store(ob[:], obk, self.mixT[s, h * 128:(h + 1) * 128, qg * 512:(qg + 1) * 512], ("mixT", s))
                yield

            def head_gen(c):
                for qg in range(self.NG - 1 - c, -1, -NCH):
                    yield from chain(qg, chains[c])
            yield from interleave([head_gen(c) for c in range(NCH)])

    def s5_prep(self, i):
        pass

    def s5_consts(self):
        if hasattr(self, "Jm"):
            return
        P, sb = self.P, self.sb
        self.Jm = sb("Jm", [128, 128], F32)
        self.nJm = sb("nJm", [128, 128], F32)
        self.bd = sb("bdmask", [128, 128], F32)
        self.Em = sb("Emat", [8, 128], F32)
        self.rowm = sb("rowm", [128, 2], F32)
        self.sgn = sb("sgn", [128, 1], F32)
        self.s5sm = sb("s5sm", [128, 12], F32)
        Jm, nJm, bd, Em, rowm, sgn = self.Jm, self.nJm, self.bd, self.Em, self.rowm, self.sgn
        P.op("pool", lambda e: e.memset(Jm[:], 0.0), writes=["Jm"])
        P.op("pool", lambda e: e.affine_select(out=Jm[:], in_=Jm[:], pattern=[[-1, 128]], compare_op=ALU.not_equal,
                                               fill=-1.0, base=64, channel_multiplier=1), reads=["Jm"], writes=["Jm"])
        P.op("pool", lambda e: e.affine_select(out=Jm[:], in_=Jm[:], pattern=[[-1, 128]], compare_op=ALU.not_equal,
                                               fill=1.0, base=-64, channel_multiplier=1), reads=["Jm"], writes=["Jm"])
        P.op("dve", lambda e: e.tensor_scalar(out=nJm[:], in0=Jm[:], scalar1=-1.0, scalar2=None, op0=ALU.mult),
             reads=["Jm"], writes=["nJm"])
        P.op("pool", lambda e: e.memset(Em[:], 1.0), writes=["Em"])
        P.op("pool", lambda e: e.affine_select(out=Em[:], in_=Em[:], pattern=[[1, 128]], compare_op=ALU.is_ge,
                                               fill=0.0, base=0, channel_multiplier=-16), reads=["Em"], writes=["Em"])
        P.op("pool", lambda e: e.affine_select(out=Em[:], in_=Em[:], pattern=[[-1, 128]], compare_op=ALU.is_ge,
                                               fill=0.0, base=15, channel_multiplier=16), reads=["Em"], writes=["Em"])
        ps, pk = self.psum()
        P.op("pe", lambda e: e.matmul(ps[:, 0:128], lhsT=Em[:], rhs=Em[:], start=True, stop=True), reads=["Em"], writes=[pk])
        P.op("dve", lambda e: e.tensor_copy(out=bd[:], in_=ps[:, 0:128]), reads=[pk], writes=["bd"])
        P.op("dve", lambda e: e.tensor_reduce(out=rowm[:], in_=bd[:].rearrange("p (q m w) -> p m q w", q=4, m=2, w=16),
                                              axis=AX.XY, op=ALU.add), reads=["bd"], writes=["rowm"])
        P.op("dve", lambda e: e.tensor_scalar(out=rowm[:], in0=rowm[:], scalar1=1.0 / 16, scalar2=None, op0=ALU.mult),
             reads=["rowm"], writes=["rowm"])
        self.Ecol = sb("Ecol", [128, 8], F32)
        ps2, pk2 = self.psum()
        P.op("pe", lambda e: e.transpose(out=ps2[:, 0:8], in_=Em[:], identity=self.identf[0:8, 0:8]), reads=["Em", "identf"], writes=[pk2])
        P.op("dve", lambda e: e.tensor_copy(out=self.Ecol[:], in_=ps2[:, 0:8]), reads=[pk2], writes=["Ecol"])
        self.gml = Ring("gml", 2)
        self.gml.tiles = [t[:, :].rearrange("p (m k) -> p m k", m=8) for t in self.hb.tiles]
        P.op("pool", lambda e: e.memset(sgn[0:64, :], 1.0), writes=["sgn"])
        P.op("pool", lambda e: e.memset(sgn[64:128, :], -1.0), writes=["sgn"], waw=False)

    def s5_main(self, i):
        P = self.P
        L, NSEQ = self.L, self.NSEQ
        NCH = L // 8
        self.s5_consts()
        P.barrier()
        KB = 1024
        ov = self.ov
        SM = ov(0, [128, 16, 32], F32)
        LAM = ov(2 * KB, [128, 2, 32], F32)
        MTr = [ov(2 * KB + 512 + k * 512, [128, 128], F32) for k in range(3)]
        AT = ov(4 * KB, [128, 32, 128], F32)
        Am = ov(20 * KB, [128, 32, 128], F32)
        Wsb = ov(36 * KB, [128, 8, 512], F32)
        Vsb = ov(52 * KB, [128, 9, 512], F32)
        PWre = ov(70 * KB, [128, 17, 32], F32)
        PWim = ov(70 * KB + 2176, [128, 17, 32], F32)
        B0 = self.ev32.tiles[0]
        C0 = self.ev32.tiles[1]
        Vpad = self.xt.tiles[0][:, :, :].rearrange("p a b -> p (a b)").bitcast(BF16).rearrange(
            "p (m g c) -> p m g c", m=8, g=32)
        WTf = self.xt.tiles[1][:, :, :].rearrange("p a b -> p (a b)").bitcast(BF16)[:, 0:4096].rearrange(
            "p (t j k) -> p t j k", t=4, j=8)
        Kblk = self.hT.tiles[0][:, :, :].rearrange("p a b -> p (a b)").rearrange("p (t j k) -> p t j k", t=4, j=8)
        WG = self.hT.tiles[1][:, 0:4, :]
        s5sm = self.s5sm
        identf, Jm, nJm = self.identf, self.Jm, self.nJm
        sm = lambda k: SM[:, k, :]
        K = lambda n: ("s5", n)

        P.dma(lambda e: e.dma_start(out=LAM, in_=self.s5_lam[i]), writes=[K("LAM")], dkey=K("LAM"))
        P.dma(lambda e: e.dma_start(out=sm(0), in_=self.s5_dt[i]), writes=[K("sm0")], dkey=K("sm0"))
        P.dma(lambda e: e.dma_start(out=B0[:].rearrange("p (g c) -> p g c", g=32), in_=self.s5_b0[i]),
              writes=[("ev32", 0)], dkey=("ev32", 0))
        P.dma(lambda e: e.dma_start(out=C0[:].rearrange("p (g c) -> p g c", g=32), in_=self.s5_c0[i]),
              writes=[("ev32", 1)], dkey=("ev32", 1))
        P.dma(lambda e: e.dma_start(out=s5sm[:, 0:4], in_=self.s5_d[i]), writes=["s5sm"], dkey="s5sm")
        P.dma(lambda e: e.dma_start(out=s5sm[:, 4:8], in_=self.s5_b_glu[i]), writes=["s5sm"], dkey="s5sm")
        for kc in range(4):
            st, sk = self.nxt(self.wstage)
            P.dma(lambda e, st=st, kc=kc: e.dma_start(out=st[:, :512], in_=self.s5_w_glu[i, kc * 128:(kc + 1) * 128, :]),
                  writes=[sk], dkey=sk)
            P.op("dve", lambda e, st=st, kc=kc: e.tensor_copy(out=WG[:, kc, :], in_=st[:, :512]), reads=[sk],
                 writes=[K("WG")], waw=(kc == 0))
        lre, lim = LAM[:, 0, :], LAM[:, 1, :]
        kl = K("LAM")

        def tt(eng, out, a, b, op, rk, wk):
            P.op(eng, lambda e: e.tensor_tensor(out=out, in0=a, in1=b, op=op), reads=rk, writes=wk)

        def ts(eng, out, a, s1, s2, op0, op1, rk, wk):
            if op1 is None:
                P.op(eng, lambda e: e.tensor_scalar(out=out, in0=a, scalar1=s1, scalar2=None, op0=op0), reads=rk, writes=wk)
            else:
                P.op(eng, lambda e: e.tensor_scalar(out=out, in0=a, scalar1=s1, scalar2=s2, op0=op0, op1=op1),
                     reads=rk, writes=wk)

        S = K("SM")
        P.op("act", lambda e: e.activation(out=sm(0), in_=sm(0), func=AF.Exp), reads=[K("sm0")], writes=[S])
        tt("dve", sm(1), lre, sm(0), ALU.mult, [kl, S], [S])
        P.op("act", lambda e: e.activation(out=sm(1), in_=sm(1), func=AF.Exp), reads=[S], writes=[S])
        tt("dve", sm(2), lim, sm(0), ALU.mult, [kl, S], [S])
        ts("dve", sm(3), sm(2), math.pi / 2, None, ALU.add, None, [S], [S])
        for src in (2, 3):
            ts("dve", sm(6), sm(src), 0.0, None, ALU.mult, None, [S], [S])
            for j in range(6):
                ts("dve", sm(5), sm(src), (2 * j + 1) * math.pi, -2 * math.pi, ALU.is_ge, ALU.mult, [S], [S])
                tt("dve", sm(6), sm(6), sm(5), ALU.add, [S], [S])
            tt("dve", sm(src), sm(src), sm(6), ALU.add, [S], [S])
        P.op("act", lambda e: e.activation(out=sm(7), in_=sm(2), func=AF.Sin), reads=[S], writes=[S])
        P.op("act", lambda e: e.activation(out=sm(8), in_=sm(3), func=AF.Sin), reads=[S], writes=[S])
        PK = K("PW")
        tt("dve", PWre[:, 1, :], sm(1), sm(8), ALU.mult, [S], [PK])
        tt("dve", PWim[:, 1, :], sm(1), sm(7), ALU.mult, [S], [PK])
        are, aim = PWre[:, 1, :], PWim[:, 1, :]
        tt("dve", sm(9), lre, lre, ALU.mult, [kl], [S])
        tt("dve", sm(10), lim, lim, ALU.mult, [kl], [S])
        tt("dve", sm(9), sm(9), sm(10), ALU.add, [S], [S])
        P.op("dve", lambda e: e.reciprocal(out=sm(10), in_=sm(9)), reads=[S], writes=[S])
        ts("dve", sm(11), are, -1.0, None, ALU.add, None, [PK], [S])
        tt("dve", sm(12), sm(11), lre, ALU.mult, [S, kl], [S])
        tt("dve", sm(13), aim, lim, ALU.mult, [PK, kl], [S])
        tt("dve", sm(12), sm(12), sm(13), ALU.add, [S], [S])
        tt("dve", PWre[:, 0, :], sm(12), sm(10), ALU.mult, [S], [PK])
        tt("dve", sm(12), aim, lre, ALU.mult, [PK, kl], [S])
        tt("dve", sm(13), sm(11), lim, ALU.mult, [S, kl], [S])
        tt("dve", sm(12), sm(12), sm(13), ALU.subtract, [S], [S])
        tt("dve", PWim[:, 0, :], sm(12), sm(10), ALU.mult, [S], [PK])

        def cmul(dst, a, b):
            tt("dve", sm(12), PWre[:, a, :], PWre[:, b, :], ALU.mult, [PK], [S])
            tt("dve", sm(13), PWim[:, a, :], PWim[:, b, :], ALU.mult, [PK], [S])
            tt("dve", sm(14), PWre[:, a, :], PWim[:, b, :], ALU.mult, [PK], [S])
            tt("dve", sm(15), PWim[:, a, :], PWre[:, b, :], ALU.mult, [PK], [S])
            tt("dve", PWre[:, dst, :], sm(12), sm(13), ALU.subtract, [S], [PK])
            tt("dve", PWim[:, dst, :], sm(14), sm(15), ALU.add, [S], [PK])

        for k in range(2, 9):
            cmul(k, k - 1, 1)
        for k in range(9, 17):
            cmul(k, k - 1, k - 1)

        if S5_STOP == 1:
            return
        def build_M(out, pidx, g, transpose, eng2="dve", rk=(), wk=()):
            P.op("act", lambda e: e.activation(out=out, in_=identf[:], func=AF.Copy, scale=PWre[:, pidx, g:g + 1]),
                 reads=[PK, "identf"] + list(rk), writes=list(wk))
            Jx = nJm if transpose else Jm
            P.op("dve", lambda e: e.scalar_tensor_tensor(out=out, in0=Jx[:], scalar=PWim[:, pidx, g:g + 1], in1=out,
                                                         op0=ALU.mult, op1=ALU.add),
                 reads=[PK, "Jm", "nJm"] + list(wk), writes=list(wk))

        for g in range(32):
            build_M(AT[:, g, :], 1, g, True, wk=[K("AT")])
            build_M(Am[:, g, :], 1, g, False, wk=[K("A")])
        if S5_STOP == 2:
            return
        psW, pkW = self.psum()
        mi = 0
        for g in range(32):
            mt = MTr[mi % 3]
            mk = K(("MT", mi % 3))
            mi += 1
            build_M(mt, 0, g, True, wk=[mk])
            P.op("pe", lambda e, g=g, mt=mt: e.matmul(psW[:, g * 16:(g + 1) * 16], lhsT=mt, rhs=B0[:, g * 16:(g + 1) * 16],
                                                      start=True, stop=True, skip_group_check=True),
                 reads=[mk, ("ev32", 0)], writes=[pkW], waw=(g == 0))
        P.op("act", lambda e: e.copy(out=Wsb[:, 0, :], in_=psW[:]), reads=[pkW], writes=[K("W0")])
        for j in range(1, 8):
            psW, pkW = self.psum()
            for g in range(32):
                P.op("pe", lambda e, g=g, j=j, psW=psW: e.matmul(
                    psW[:, g * 16:(g + 1) * 16], lhsT=AT[:, g, :], rhs=Wsb[:, j - 1, g * 16:(g + 1) * 16],
                    start=True, stop=True, skip_group_check=True), reads=[K("AT"), K("W%d" % (j - 1))], writes=[pkW],
                    waw=(g == 0))
            P.op("act", lambda e, j=j, psW=psW: e.copy(out=Wsb[:, j, :], in_=psW[:]), reads=[pkW], writes=[K("W%d" % j)])
        if S5_STOP == 3:
            return
        P.op("dve", lambda e: e.tensor_scalar(out=Vsb[:, 0, :], in0=C0[:], scalar1=self.sgn[:, 0:1], scalar2=None,
                                              op0=ALU.mult), reads=[("ev32", 1), "sgn"], writes=[K("V0")])
        for j in range(1, 9):
            psV, pkV = self.psum()
            for g in range(32):
                P.op("pe", lambda e, g=g, j=j, psV=psV: e.matmul(
                    psV[:, g * 16:(g + 1) * 16], lhsT=Am[:, g, :], rhs=Vsb[:, j - 1, g * 16:(g + 1) * 16],
                    start=True, stop=True, skip_group_check=True), reads=[K("A"), K("V%d" % (j - 1))], writes=[pkV],
                    waw=(g == 0))
            P.op("act", lambda e, j=j, psV=psV: e.copy(out=Vsb[:, j, :], in_=psV[:]), reads=[pkV], writes=[K("V%d" % j)])
        if S5_STOP == 4:
            return
        for T in range(4):
            for tau in range(8):
                ps, pk = self.psum()
                P.op("pe", lambda e, T=T, tau=tau, ps=ps: e.matmul(
                    ps[:, 0:128], lhsT=Wsb[:, tau, T * 128:(T + 1) * 128], rhs=Vsb[:, 0, T * 128:(T + 1) * 128],
                    start=True, stop=True), reads=[K("W%d" % tau), K("V0")], writes=[pk])
                if tau == 0:
                    tmp, tk = self.nxt(self.wk32)
                    P.op("dve", lambda e, ps=ps, tmp=tmp: e.tensor_tensor(out=tmp[:, 0:128], in0=ps[:, 0:128], in1=self.bd[:],
                                                                          op=ALU.mult), reads=[pk, "bd"], writes=[tk])
                    P.op("dve", lambda e, T=T, tmp=tmp: e.scalar_tensor_tensor(
                        out=Kblk[:, T, 0, :], in0=identf[:], scalar=s5sm[:, T:T + 1], in1=tmp[:, 0:128], op0=ALU.mult,
                        op1=ALU.add), reads=[tk, "s5sm", "identf"], writes=[K("Kblk")], waw=False)
                else:
                    P.op("dve", lambda e, T=T, tau=tau, ps=ps: e.tensor_tensor(
                        out=Kblk[:, T, tau, :], in0=ps[:, 0:128], in1=self.bd[:], op=ALU.mult), reads=[pk, "bd"],
                        writes=[K("Kblk")], waw=False)
        if S5_STOP == 5:
            return
        for T in range(4):
            for j in range(8):
                ps, pk = self.psum()
                P.op("pe", lambda e, T=T, j=j, ps=ps: e.transpose(out=ps[:, 0:128], in_=Wsb[:, j, T * 128:(T + 1) * 128],
                                                                  identity=identf[:]),
                     reads=[K("W%d" % j), "identf"], writes=[pk])
                P.op("act", lambda e, T=T, j=j, ps=ps: e.copy(out=WTf[:, T, j, :], in_=ps[:, 0:128]),
                     reads=[pk], writes=[K("WTm")], waw=False)
        if S5_STOP == 6:
            return
        P.op("pool", lambda e: e.memset(Vpad.rearrange("p m g c -> p (m g c)"), 0.0), writes=[K("Vpad")])
        for m in range(8):
            for mem in range(2):
                P.op("dve" if mem else "pool", lambda e, m=m, mem=mem: e.tensor_copy(
                    out=Vpad[:, m, mem::2, 16 * mem:16 * mem + 16],
                    in_=Vsb[:, m + 1, :].rearrange("p (g c) -> p g c", g=32)[:, mem::2, :]),
                    reads=[K("V%d" % (m + 1)), K("Vpad")], writes=[K("Vpad")], waw=False)

        if S5_STOP == 7:
            return
        if os.environ.get("DBG_S5") == "1":
            dbg = {"dPWre": (PWre, [128, 17, 32]), "dPWim": (PWim, [128, 17, 32]), "dWsb": (Wsb, [128, 8, 512]),
                   "dVsb": (Vsb, [128, 9, 512])}
            P.barrier()
            for nm, (ap_, shp) in dbg.items():
                dt_ = self.nc.dram_tensor(nm, shp, F32, kind="ExternalOutput").ap()
                P.dma(lambda e, ap_=ap_, dt_=dt_: e.dma_start(out=dt_, in_=ap_), writes=[("dbg", nm)], dkey=("dbg", nm))
            dk = self.nc.dram_tensor("dKblk", [128, 4, 8, 128], BF16, kind="ExternalOutput").ap()
            P.dma(lambda e: e.dma_start(out=dk, in_=Kblk), writes=[("dbg", "k")], dkey=("dbg", "k"))
            P.barrier()
        yield
        Wof = self.Wo[:, :, :].rearrange("p a b -> p (a b)")

        def wov(off, shape, dt):
            n = int(np.prod(shape[1:]))
            esz = 4 if dt == F32 else 2
            a = Wof[:, off // 2: off // 2 + n * esz // 2]
            if dt == F32:
                a = a.bitcast(F32)
            if len(shape) == 3:
                a = a.rearrange("p (a b) -> p a b", a=shape[1])
            return a

        P.barrier()
        U = ov(0, [128, L], BF16)
        Hprev = ov(2 * L, [128, 8, NCH], BF16)
        b1 = 2 * L + 16 * NCH
        ytile = ov(b1, [128, L], F32)
        cH32 = [ov(b1 + 4 * L + k * 2 * KB, [128, 512], F32) for k in range(8)]
        cHb = [ov(b1 + 4 * L + 16 * KB + k * KB, [128, 512], BF16) for k in range(8)]
        assert b1 + 4 * L + 24 * KB <= 70 * KB
        y2g = ov(b1, [128, 4, 512], F32)
        y2bf = ov(b1 + 8 * KB, [128, 4, 512], BF16)
        if not hasattr(self, "y2T"):
            self.y2T = self.dram("y2T", [NSEQ, 512, L], F32)
        nsteps = int(math.log2(NCH))
        xt1b = self.xt.tiles[1][:, :, :].rearrange("p a b -> p (a b)").bitcast(BF16)[:, 4096:8192].rearrange(
            "p (c k) -> p c k", c=32)
        cgm = [[xt1b[:, ci * 4 + r, :] for r in range(4)] for ci in range(8)]
        hT1b = self.hT.tiles[1][:, 4:8, :].rearrange("p a b -> p (a b)").rearrange("p (c k) -> p c k", c=16)
        cMT = [[hT1b[:, ci * 2 + r, :] for r in range(2)] for ci in range(8)]
        YT = K("ytile")
        for s in range(NSEQ):
            for T in range(4):
                P.dma(lambda e, s=s, T=T: e.dma_start(out=U, in_=self.uT[s, T * 128:(T + 1) * 128, :]),
                      reads=[("uT", s)], writes=[K("U")], dkey=K("U"))

                def gchain(g8, ci, T=T, s=s):
                    g = 8 * T + g8
                    H32, hk32 = cH32[ci], K(("cH32", ci))
                    Hb, hkb = cHb[ci], K(("cHb", ci))
                    psG, pkG = self.psum()
                    for m in range(8):
                        gmt = cgm[ci][m % 4]
                        gmk = K(("cgm", ci, m % 4))
                        if m % 2:
                            P.op("act", lambda e, m=m, gmt=gmt: e.activation(
                                out=gmt, in_=WTf[:, T, 7 - m, :], func=AF.Copy, scale=self.Ecol[:, g8:g8 + 1]),
                                reads=[K("WTm"), "Ecol"], writes=[gmk])
                        else:
                            P.op("dve", lambda e, m=m, gmt=gmt: e.tensor_scalar(
                                out=gmt, in0=WTf[:, T, 7 - m, :], scalar1=self.Ecol[:, g8:g8 + 1], scalar2=None,
                                op0=ALU.mult), reads=[K("WTm"), "Ecol"], writes=[gmk])
                        P.op("pe", lambda e, m=m, gmt=gmt: e.matmul(
                            psG[:, :NCH], lhsT=gmt, rhs=U[:, m::8], start=(m == 0), stop=(m == 7)),
                            reads=[gmk, K("U")], writes=[pkG], waw=(m == 0))
                    P.op("act", lambda e: e.copy(out=H32[:, :NCH], in_=psG[:, :NCH]), reads=[pkG], writes=[hk32])
                    P.op("dve", lambda e: e.tensor_copy(out=Hb[:, :NCH], in_=psG[:, :NCH]), reads=[pkG], writes=[hkb])
                    yield
                    for j in range(nsteps):
                        sft = 1 << j
                        mt = cMT[ci][j % 2]
                        mk = K(("cMT", ci, j % 2))
                        build_M(mt, 8 + j, g, True, wk=[mk])
                        yield
                        psS, pkS = self.psum()
                        P.op("pe", lambda e, mt=mt, sft=sft, psS=psS: e.matmul(
                            psS[:, :NCH - sft], lhsT=mt, rhs=Hb[:, 0:NCH - sft], start=True, stop=True),
                            reads=[mk, hkb], writes=[pkS])
                        P.op("dve", lambda e, psS=psS, sft=sft: e.tensor_tensor(
                            out=H32[:, sft:NCH], in0=H32[:, sft:NCH], in1=psS[:, :NCH - sft], op=ALU.add),
                            reads=[hk32, pkS], writes=[hk32])
                        yield
                        if j < nsteps - 1:
                            P.op("act", lambda e, sft=sft: e.copy(out=Hb[:, sft:NCH], in_=H32[:, sft:NCH]),
                                 reads=[hk32], writes=[hkb])
                            yield
                    P.op("act", lambda e: e.copy(out=Hprev[:, g8, :], in_=H32[:, 0:NCH]),
                         reads=[hk32], writes=[K(("Hp", g8))])
                    yield

                yield from interleave([gchain(ci, ci) for ci in range(8)])
                for m in range(8):
                    psY, pkY = self.psum()
                    for tau in range(m + 1):
                        P.op("pe", lambda e, T=T, tau=tau, m=m, psY=psY: e.matmul(
                            psY[:, :NCH], lhsT=Kblk[:, T, tau, :], rhs=U[:, (m - tau)::8], start=(tau == 0), stop=False,
                            skip_group_check=True), reads=[K("Kblk"), K("U")], writes=[pkY], waw=(tau == 0))
                    for g8 in range(8):
                        q = g8 // 2
                        P.op("pe", lambda e, T=T, g8=g8, q=q, m=m, psY=psY: e.matmul(
                            psY[32 * q:32 * q + 32, 1:NCH], lhsT=Vpad[:, m, 8 * T + g8, :], rhs=Hprev[:, g8, 0:NCH - 1],
                            start=False, stop=(g8 == 7), tile_position=(0, 32 * q), skip_group_check=True),
                            reads=[K("Vpad"), K(("Hp", g8))], writes=[pkY], waw=False)
                    P.op("act", lambda e, m=m, psY=psY: e.copy(out=ytile[:, m::8], in_=psY[:, :NCH]), reads=[pkY],
                         writes=[YT], waw=(m == 0))
                    yield
                for c in range(L // 512):
                    cs = slice(c * 512, (c + 1) * 512)
                    t1, t1k = self.nxt(self.wk32)
                    P.op("dve", lambda e, t1=t1, cs=cs: e.tensor_tensor(out=t1[:], in0=ytile[:, cs], in1=ytile[:, cs], op=ALU.mult),
                         reads=[YT], writes=[t1k])
                    P.op("pool", lambda e, t1=t1: e.tensor_scalar(out=t1[:], in0=t1[:], scalar1=0.044715, scalar2=1.0,
                                                                  op0=ALU.mult, op1=ALU.add), reads=[t1k], writes=[t1k])
                    yield
                    P.op("dve", lambda e, t1=t1, cs=cs: e.tensor_tensor(out=t1[:], in0=t1[:], in1=ytile[:, cs], op=ALU.mult),
                         reads=[t1k, YT], writes=[t1k])
                    P.op("act", lambda e, t1=t1: e.activation(out=t1[:], in_=t1[:], func=AF.Sigmoid, scale=1.5957691216),
                         reads=[t1k], writes=[t1k])
                    yield
                    ob, obk = self.nxt(self.ev32)
                    P.op("dve", lambda e, t1=t1, ob=ob, cs=cs: e.tensor_tensor(out=ob[:], in0=t1[:], in1=ytile[:, cs], op=ALU.mult),
                         reads=[t1k, YT], writes=[obk])
                    self.store(ob[:], obk, self.y2T[s, T * 128:(T + 1) * 128, cs], ("y2T", s))
                    yield
            for tg in range(self.NG):
                cs = slice(tg * 512, (tg + 1) * 512)
                yg = y2g
                ygk = K(("y2g", 0))
                P.dma(lambda e, yg=yg, cs=cs, s=s: e.dma_start(out=yg, in_=self.y2T[s, :, cs].rearrange("(t p) l -> p t l", p=128)),
                      reads=[("y2T", s)], writes=[ygk, YT], dkey=ygk)
                P.op("act", lambda e, yg=yg: e.copy(out=y2bf[:, 0:2, :], in_=yg[:, 0:2, :]), reads=[ygk, YT], writes=[K("y2bf")])
                P.op("dve", lambda e, yg=yg: e.tensor_copy(out=y2bf[:, 2:4, :], in_=yg[:, 2:4, :]), reads=[ygk, YT],
                     writes=[K("y2bf")], waw=False)
                yield
                for oc in range(4):
                    ps, pk = self.psum()
                    for kc in range(4):
                        P.op("pe", lambda e, ps=ps, kc=kc, oc=oc: e.matmul(
                            ps[:], lhsT=WG[:, kc, oc * 128:(oc + 1) * 128], rhs=y2bf[:, kc, :], start=(kc == 0), stop=(kc == 3)),
                            reads=[K("WG"), K("y2bf"), YT], writes=[pk], waw=(kc == 0))
                    sg, sgk = self.nxt(self.wk32)
                    P.op("act", lambda e, ps=ps, sg=sg, oc=oc: e.activation(out=sg[:], in_=ps[:], func=AF.Sigmoid,
                                                                            bias=s5sm[:, 4 + oc:5 + oc]),
                         reads=[pk, "s5sm"], writes=[sgk])
                    gzt, gzk = self.nxt(self.ev32)
                    P.dma(lambda e, gzt=gzt, oc=oc, cs=cs, s=s: e.dma_start(out=gzt[:], in_=self.gzT[s, 1024 + oc * 128:1024 + (oc + 1) * 128, cs]),
                          reads=[("gzT", s)], writes=[gzk], dkey=(gzk, "ld"))
                    P.op("dve", lambda e, sg=sg, yg=yg, oc=oc: e.tensor_tensor(out=sg[:], in0=sg[:], in1=yg[:, oc, :], op=ALU.mult),
                         reads=[sgk, ygk, YT], writes=[sgk])
                    ob, obk = self.nxt(self.evbf)
                    P.op("dve", lambda e, sg=sg, gzt=gzt, ob=ob: e.tensor_tensor(out=ob[:], in0=sg[:], in1=gzt[:], op=ALU.mult),
                         reads=[sgk, gzk], writes=[obk])
                    self.store(ob[:], obk, self.mixT[s, 1024 + oc * 128:1024 + (oc + 1) * 128, cs], ("mixT", s))
                    yield

    def odd_layer(self, i, x_in, kin, x_out, kout):
        P = self.P
        L, NSEQ = self.L, self.NSEQ
        KB = 1024
        P.barrier()
        full_Wb = self.Wb
        self.Wb = self.arena[:, 0:8 * 3088].rearrange("p (k f) -> p k f", k=8)
        self.load_weights(self.odd_w_in[i], 8, 3088, self.Wb, "Wb", gain_dram=self.odd_norm_g[i])
        self.load_weights(self.odd_w_out[i], 8, 1024, self.Wo, "Wo")
        self.s5_consts()
        sm = self.s5sm
        wg16 = self.ov(50 * KB, [128, 512], BF16)[0:16, :]
        st, sk = self.nxt(self.wstage)
        P.dma(lambda e: e.dma_start(out=st[0:16, 0:512], in_=self.gla_w_gate[i]), writes=[sk], dkey=sk)
        P.op("dve", lambda e: e.tensor_copy(out=wg16, in_=st[0:16, 0:512]), reads=[sk], writes=["wg16"])
        P.dma(lambda e: e.dma_start(out=sm[:, 0:4], in_=self.gla_b_gate[i]), writes=["s5sm"], dkey="s5sm")
        P.dma(lambda e: e.dma_start(out=sm[:, 8:10], in_=self.gla_o_g[i]), writes=["s5sm"], dkey="s5sm")
        P.op("dve", lambda e: e.tensor_scalar(out=sm[:, 0:4], in0=sm[:, 0:4], scalar1=-1.0, scalar2=None, op0=ALU.mult),
             reads=["s5sm"], writes=["s5sm"])
        groups = [(s, tg) for s in range(NSEQ) for tg in range(self.NG)]
        for k, (s, tg) in enumerate(groups):
            if k == 0:
                self.prefetch_x(x_in, kin, s, tg)
            if k + 1 < len(groups):
                self.prefetch_x(x_in, kin, *groups[k + 1])
            self.odd_inproj_group(s, tg, x_in, kin, wg16)
        self.gla_phase()
        for s in range(NSEQ):
            self.out_proj(s, x_in, kin, x_out, kout, 8, self.mixT, "mixT")
        self.Wb = full_Wb

    def odd_inproj_group(self, s, tg, x_in, kin, wg16):
        P = self.P
        hT, hk, xt, xk = self.load_norm_group(x_in, kin, s, tg)
        cols = slice(tg * 512, (tg + 1) * 512)
        sm = self.s5sm
        for which, dst, dkey in ((0, self.q2T, "q2T"), (1, self.k2T, "k2T")):
            for c in range(4):
                ps, pk = self.proj_fm(hT, hk, which * 512 + c * 128)
                ob, obk = self.nxt(self.ev32)
                P.op("act" if c % 2 else "dve",
                     (lambda e, ps=ps, ob=ob: e.copy(out=ob[:], in_=ps[:])) if c % 2 else
                     (lambda e, ps=ps, ob=ob: e.tensor_copy(out=ob[:], in_=ps[:])), reads=[pk], writes=[obk])
                self.store(ob[:], obk, dst[s, c * 128:(c + 1) * 128, cols], (dkey, s))
        for j in range(4):
            for half in range(2):
                ps, pk = self.proj_tm(hT, hk, j, 1024 + half * 512)
                ob, obk = self.nxt(self.evbf)
                P.op("act" if half else "dve",
                     (lambda e, ps=ps, ob=ob: e.copy(out=ob[:], in_=ps[:])) if half else
                     (lambda e, ps=ps, ob=ob: e.tensor_copy(out=ob[:], in_=ps[:])), reads=[pk], writes=[obk])
                r0 = tg * 512 + j * 128
                self.store(ob[:], obk, self.vtm[s, r0:r0 + 128, half * 512:(half + 1) * 512], ("vtm", s))
        for c in range(8):
            ps, pk = self.proj_fm(hT, hk, 2048 + c * 128)
            ob, obk = self.nxt(self.ev32)
            P.op("act", lambda e, ps=ps, ob=ob: e.activation(out=ob[:], in_=ps[:], func=AF.Silu), reads=[pk], writes=[obk])
            self.store(ob[:], obk, self.gzT[s, c * 128:(c + 1) * 128, cols], ("gzT", s))
        ps, pk = self.proj_fm(hT, hk, 3072, M=16)
        rT, rk = self.nxt(self.wkbf)
        P.op("dve", lambda e, ps=ps, rT=rT: e.tensor_copy(out=rT[0:16, :], in_=ps[0:16, :]), reads=[pk], writes=[rk])
        for c in range(4):
            ps2, pk2 = self.psum()
            P.op("pe", lambda e, ps2=ps2, c=c, rT=rT: e.matmul(ps2[:], lhsT=wg16[:, c * 128:(c + 1) * 128], rhs=rT[0:16, :],
                                                               start=True, stop=True), reads=[rk, "wg16"], writes=[pk2])
            ex, exk = self.nxt(self.wk32)
            P.op("act", lambda e, ps2=ps2, ex=ex, c=c: e.activation(out=ex[:], in_=ps2[:], func=AF.Exp, scale=-1.0,
                                                                    bias=sm[:, c:c + 1]), reads=[pk2, "s5sm"], writes=[exk])
            ob, obk = self.nxt(self.ev32)
            P.op("act", lambda e, ex=ex, ob=ob: e.activation(out=ob[:], in_=ex[:], func=AF.Ln, bias=1.0), reads=[exk], writes=[obk])
            self.store(ob[:], obk, self.lgT[s, c * 128:(c + 1) * 128, cols], ("lgT", s))

    def gla_phase(self):
        P = self.P
        L, NSEQ = self.L, self.NSEQ
        KB = 1024
        ov = self.ov
        P.barrier()
        NCHN = 2
        CH = 26 * KB
        rmask = ov(NCHN * CH, [128, 512], F32)
        P.op("pool", lambda e: e.memset(rmask, 1.0), writes=["rmask"])
        for c in range(4):
            P.op("pool", lambda e, c=c: e.memset(rmask[:, c * 128:c * 128 + 1], 0.0), writes=["rmask"], waw=False)
        assert NCHN * CH + 2 * KB <= 80 * KB
        sm = self.s5sm
        psrot = [0]

        def lpsum():
            k = 4 + psrot[0] % 3
            psrot[0] += 1
            return self.psM[k], ("psM", k)

        def chain(s, h, ci):
            b = ci * CH
            T = lambda n: ("gla", n, ci)
            q32 = ov(b, [128, 512], F32)
            k32 = ov(b + 2 * KB, [128, 512], F32)
            Lg = ov(b + 4 * KB, [128, 512], F32)
            G = ov(b + 6 * KB, [128, 512], F32)
            E1 = ov(b + 8 * KB, [128, 512], F32)
            E2 = ov(b + 10 * KB, [128, 512], F32)
            gz = ov(b + 12 * KB, [128, 2, 512], F32)
            vt = ov(b + 16 * KB, [128, 4, 256], BF16)
            qd = ov(b + 18 * KB, [128, 512], BF16)
            ki = ov(b + 19 * KB, [128, 512], BF16)
            kdT = ov(b + 20 * KB, [128, 512], BF16)
            kdec = ov(b + 21 * KB, [128, 4, 128], BF16)
            S32 = ov(b + 22 * KB, [128, 256], F32)
            Sbf = ov(b + 23 * KB, [128, 256], BF16)
            scT = [ov(b + 23 * KB + 512 + k * 256, [128, 128], BF16) for k in range(2)]
            rs = ov(b + 24 * KB, [128, 512], F32)
            psO = [self.psM[2 * ci], self.psM[2 * ci + 1]]
            psOk = [("psM", 2 * ci), ("psM", 2 * ci + 1)]
            psT = self.psT[0]
            P.op("pool", lambda e: e.memset(S32, 0.0), writes=[T("S32")])
            P.op("pool", lambda e: e.memset(Sbf, 0.0), writes=[T("Sbf")])
            for tg in range(self.NG):
                cs = slice(tg * 512, (tg + 1) * 512)
                P.dma(lambda e, cs=cs: e.dma_start(out=q32, in_=self.q2T[s, h * 128:(h + 1) * 128, cs]),
                      reads=[("q2T", s)], writes=[T("q32")], dkey=T("q32"))
                P.dma(lambda e, cs=cs: e.dma_start(out=k32, in_=self.k2T[s, h * 128:(h + 1) * 128, cs]),
                      reads=[("k2T", s)], writes=[T("k32")], dkey=T("k32"))
                P.dma(lambda e, cs=cs: e.dma_start(out=Lg, in_=self.lgT[s, h * 128:(h + 1) * 128, cs]),
                      reads=[("lgT", s)], writes=[T("Lg")], dkey=T("Lg"))
                P.dma(lambda e, cs=cs: e.dma_start(out=vt, in_=self.vtm[s, cs, h * 256:(h + 1) * 256].rearrange(
                    "(j p) d -> p j d", p=128)), reads=[("vtm", s)], writes=[T("vt")], dkey=T("vt"))
                P.dma(lambda e, cs=cs: e.dma_start(out=gz, in_=self.gzT[s, h * 256:(h + 1) * 256, cs].rearrange(
                    "(v p) t -> p v t", p=128)), reads=[("gzT", s)], writes=[T("gz")], dkey=T("gz"))
                P.op("dve", lambda e: e.tensor_tensor_scan(out=G, data0=rmask, data1=Lg, initial=0.0, op0=ALU.mult,
                                                           op1=ALU.add), reads=[T("Lg"), "rmask"], writes=[T("G")])
                P.op("act", lambda e: e.activation(out=E1, in_=G, func=AF.Exp, scale=-1.0 / 16), reads=[T("G")], writes=[T("E1")])
                P.op("act", lambda e: e.activation(out=E2, in_=G, func=AF.Exp, scale=1.0 / 16), reads=[T("G")], writes=[T("E2")])
                P.op("dve", lambda e: e.scalar_tensor_tensor(out=qd, in0=q32, scalar=128 ** -0.5, in1=E1, op0=ALU.mult,
                                                             op1=ALU.mult), reads=[T("q32"), T("E1")], writes=[T("qd")])
                P.op("pool", lambda e: e.tensor_tensor(out=ki, in0=k32, in1=E2, op=ALU.mult), reads=[T("k32"), T("E2")],
                     writes=[T("ki")])
                for c in range(4):
                    cc = slice(c * 128, (c + 1) * 128)
                    P.op("dve", lambda e, c=c, cc=cc: e.scalar_tensor_tensor(
                        out=kdT[:, cc], in0=k32[:, cc], scalar=E1[:, c * 128 + 127:c * 128 + 128], in1=E2[:, cc],
                        op0=ALU.mult, op1=ALU.mult), reads=[T("k32"), T("E1"), T("E2")], writes=[T("kdT")], waw=(c == 0))
                yield
                for c in range(4):
                    cc = slice(c * 128, (c + 1) * 128)
                    P.op("pe", lambda e, cc=cc: e.transpose(out=psT[:, cc], in_=kdT[:, cc], identity=self.ident[:]),
                         reads=[T("kdT"), "ident"], writes=["psT"], waw=(c == 0))
                P.op("act", lambda e: e.copy(out=kdec, in_=psT[:, 0:512].rearrange("p (c k) -> p c k", c=4)),
                     reads=["psT"], writes=[T("kdec")])
                yield
                for c in range(4):
                    cc = slice(c * 128, (c + 1) * 128)
                    pss, pssk = lpsum()
                    P.op("pe", lambda e, cc=cc, pss=pss: e.matmul(pss[:, 0:128], lhsT=ki[:, cc], rhs=qd[:, cc], start=True,
                                                                  stop=True), reads=[T("ki"), T("qd")], writes=[pssk])
                    sc = scT[c % 2]
                    sck = T(("scT", c % 2))
                    P.op("dve", lambda e, pss=pss, sc=sc: e.tensor_tensor(out=sc, in0=pss[:, 0:128], in1=self.maskLE[:],
                                                                          op=ALU.mult), reads=[pssk, "maskLE"], writes=[sck])
                    for vc in range(2):
                        P.op("pe", lambda e, vc=vc, c=c, cc=cc, sc=sc: e.matmul(
                            psO[vc][:, cc], lhsT=vt[:, c, vc * 128:(vc + 1) * 128], rhs=sc, start=True, stop=False,
                            skip_group_check=True), reads=[T("vt"), sck], writes=[psOk[vc]], waw=(c == 0))
                        P.op("pe", lambda e, vc=vc, cc=cc: e.matmul(
                            psO[vc][:, cc], lhsT=Sbf[:, vc * 128:(vc + 1) * 128], rhs=qd[:, cc], start=False, stop=True,
                            skip_group_check=True), reads=[T("Sbf"), T("qd")], writes=[psOk[vc]], waw=False)
                    psS, psSk = lpsum()
                    P.op("pe", lambda e, c=c, psS=psS: e.matmul(psS[:, 0:256], lhsT=kdec[:, c, :], rhs=vt[:, c, :], start=True,
                                                                stop=True), reads=[T("kdec"), T("vt")], writes=[psSk])
                    P.op("dve", lambda e, c=c, psS=psS: e.scalar_tensor_tensor(
                        out=S32, in0=S32, scalar=E1[:, c * 128 + 127:c * 128 + 128], in1=psS[:, 0:256], op0=ALU.mult,
                        op1=ALU.add), reads=[T("S32"), T("E1"), psSk], writes=[T("S32")])
                    P.op("act", lambda e: e.copy(out=Sbf, in_=S32), reads=[T("S32")], writes=[T("Sbf")])
                    yield
                sqs = []
                for vc in range(2):
                    sq, sqk = self.nxt(self.wkbf)
                    P.op("act", lambda e, vc=vc, sq=sq: e.activation(out=sq[:], in_=psO[vc][:], func=AF.Square),
                         reads=[psOk[vc]], writes=[sqk])
                    sqs.append((sq, sqk))
                pn, pnk = lpsum()
                for vc in range(2):
                    P.op("pe", lambda e, vc=vc, pn=pn, sq=sqs[vc][0]: e.matmul(pn[:], lhsT=self.ones[:], rhs=sq[:], start=(vc == 0),
                                                                stop=(vc == 1)), reads=[sqs[vc][1], "ones"], writes=[pnk],
                         waw=(vc == 0))
                P.op("act", lambda e, pn=pn: e.activation(out=rs, in_=pn[:], func=AF.Ln, scale=1.0 / 256, bias=EPS),
                     reads=[pnk], writes=[T("rs")])
                P.op("act", lambda e: e.activation(out=rs, in_=rs, func=AF.Exp, scale=-0.5), reads=[T("rs")], writes=[T("rs")])
                for vc in range(2):
                    tmp, tk = self.nxt(self.wk32)
                    P.op("dve", lambda e, vc=vc, tmp=tmp: e.scalar_tensor_tensor(
                        out=tmp[:], in0=psO[vc][:], scalar=sm[:, 8 + vc:9 + vc], in1=rs, op0=ALU.mult, op1=ALU.mult),
                        reads=[psOk[vc], "s5sm", T("rs")], writes=[tk])
                    ob, obk = self.nxt(self.evbf)
                    if os.environ.get("GLA_DBG") == "1":
                        P.op("dve", lambda e, vc=vc, ob=ob: e.tensor_copy(out=ob[:], in_=psO[vc][:]), reads=[psOk[vc]], writes=[obk])
                    elif os.environ.get("GLA_DBG") == "2":
                        P.op("dve", lambda e, vc=vc, ob=ob, tmp=tmp: e.tensor_copy(out=ob[:], in_=tmp[:]), reads=[tk], writes=[obk])
                    else:
                        P.op("pool", lambda e, vc=vc, tmp=tmp, ob=ob: e.tensor_tensor(out=ob[:], in0=tmp[:], in1=gz[:, vc, :],
                                                                                      op=ALU.mult), reads=[tk, T("gz")], writes=[obk])
                    self.store(ob[:], obk, self.mixT[s, h * 256 + vc * 128:h * 256 + (vc + 1) * 128, cs], ("mixT", s))
                yield

        jobs = [(s, h) for s in range(NSEQ) for h in range(4)]
        for r0 in range(0, len(jobs), NCHN):
            run_interleaved([chain(s, h, ci) for ci, (s, h) in enumerate(jobs[r0:r0 + NCHN])])


def host_layout(inputs, nseq_total=16):
    f = lambda a: np.ascontiguousarray(np.asarray(a, dtype=np.float32))
    d = {}
    d["even_norm_g"] = f(inputs["even_norm_g"].reshape(2, 8, 128).transpose(0, 2, 1))
    d["odd_norm_g"] = f(inputs["odd_norm_g"].reshape(2, 8, 128).transpose(0, 2, 1))
    d["even_w_in"] = f(inputs["even_w_in"])
    d["even_w_out"] = f(inputs["even_w_out"])
    d["odd_w_in"] = f(inputs["odd_w_in"])
    d["odd_w_out"] = f(inputs["odd_w_out"])
    d["sb_q_norm_g"] = f(inputs["sb_q_norm_g"].reshape(2, 128, 1))
    d["sb_k_norm_g"] = f(inputs["sb_k_norm_g"].reshape(2, 128, 1))
    d["gla_w_gate"] = f(inputs["gla_w_gate"])
    d["gla_b_gate"] = f(inputs["gla_b_gate"].reshape(2, 4, 128).transpose(0, 2, 1))
    d["gla_o_norm_g"] = f(inputs["gla_o_norm_g"].reshape(2, 2, 128).transpose(0, 2, 1))
    lam = np.stack([inputs["s5_lambda_re"], inputs["s5_lambda_im"]], axis=1)
    lam = lam.transpose(0, 3, 1, 2)
    d["s5_lam"] = f(np.concatenate([lam, lam], axis=1))
    d["s5_log_dt"] = f(np.broadcast_to(inputs["s5_log_dt"][:, None, :], (2, 128, 32)))
    b = np.concatenate([inputs["s5_b_re"], inputs["s5_b_im"]], axis=2)
    d["s5_b0"] = f(b.transpose(0, 2, 1, 3))
    c = np.concatenate([inputs["s5_c_re"], inputs["s5_c_im"]], axis=3)
    d["s5_c0"] = f(c.transpose(0, 3, 1, 2))
    d["s5_d"] = f(inputs["s5_d"].reshape(2, 4, 128).transpose(0, 2, 1))
    d["s5_w_glu"] = f(inputs["s5_w_glu"])
    d["s5_b_glu"] = f(inputs["s5_b_glu"].reshape(2, 4, 128).transpose(0, 2, 1))
    return d


def kernel(**inputs):
    x = np.asarray(inputs["x"], dtype=np.float32)
    bsz, L, _ = x.shape
    nseq = bsz // N_CORES
    b = Builder(L=L, NSEQ=nseq, depth=4)
    nc = b.build()
    params = host_layout({k: np.asarray(v) for k, v in inputs.items() if k != "x"})
    in_maps = []
    for c in range(N_CORES):
        m = dict(params)
        m["x"] = np.ascontiguousarray(x[c * nseq:(c + 1) * nseq])
        in_maps.append(m)
    res = run_bass_kernel_spmd(nc, in_maps, core_ids=list(range(N_CORES)))
    return np.concatenate([r["y"] for r in res.results], axis=0).astype(np.float32)
```

```python
import math
from contextlib import ExitStack

import numpy as np
import concourse.bass as bass
import concourse.mybir as mybir
from concourse.bass_utils import run_bass_kernel_spmd

F32 = mybir.dt.float32
BF16 = mybir.dt.bfloat16
AF = mybir.ActivationFunctionType
ALU = mybir.AluOpType
AX = mybir.AxisListType

D_MODEL = 1024
EPS = 1e-6
N_CORES = 8
import os
OVERLAP = False
N_FILL = int(os.environ.get('N_FILL', '0'))
SB_W = int(os.environ.get('SB_W', '2'))
S5_W = int(os.environ.get('S5_W', '3'))
S5_STOP = int(os.environ.get('S5_STOP', '0'))


class _Op:
    __slots__ = ("eng", "fn", "deps", "val", "needed", "dkey", "dval", "same_ok")


class Prog:
    ENGS = ("pe", "act", "dve", "pool", "sp")

    def __init__(self, nc, es):
        self.nc = nc
        self.es = es
        self.q = {e: [] for e in self.ENGS}
        self.W = {}
        self.R = {}
        self.dcount = {}
        self.n_ops = 0
        self.last_op = {}
        self.last_dma = {}
        self.pending = {}

    @staticmethod
    def _is_psum(b):
        return (isinstance(b, tuple) and b[0] == "psM") or b == "psT"

    @staticmethod
    def _chan(op):
        return ("d", op.dkey) if op.dkey is not None else op.eng

    def op(self, eng, fn, reads=(), writes=(), dkey=None, waw=True, same_ok=False):
        if eng == "pe":
            same_ok = True
        o = _Op()
        o.eng, o.fn, o.val, o.needed, o.dkey, o.same_ok = eng, fn, 0, False, dkey, same_ok
        deps = {}

        def add(d):
            deps[id(d)] = d

        for b in reads:
            for d in self.W.get(b, {}).values():
                add(d)
            if self._is_psum(b):
                for chn, d in self.R.get(b, {}).items():
                    if chn != eng:
                        add(d)
        for b in writes:
            for d in self.R.get(b, {}).values():
                add(d)
            if waw:
                for d in self.W.get(b, {}).values():
                    add(d)
        if eng in self.pending:
            for d in self.pending.pop(eng):
                add(d)
        o.deps = list(deps.values())
        if dkey is None:
            self.last_op[eng] = o
        else:
            self.last_dma[dkey] = o
        if dkey is not None:
            self.dcount[dkey] = self.dcount.get(dkey, 0) + 16
            o.dval = self.dcount[dkey]
        ch = self._chan(o)
        for b in reads:
            self.R.setdefault(b, {})[ch] = o
        for b in writes:
            if waw:
                self.W[b] = {ch: o}
                self.R[b] = {}
            else:
                self.W.setdefault(b, {})[ch] = o
        self.q[eng].append(o)
        self.n_ops += 1
        return o

    def barrier(self):
        deps = list(self.last_op.values()) + list(self.last_dma.values())
        for e in self.ENGS:
            self.pending[e] = list(deps)

    def dma(self, fn, reads=(), writes=(), dkey=None, eng="sp", waw=False):
        assert dkey is not None
        return self.op(eng, fn, reads, writes, dkey=dkey, waw=waw)

    def finalize(self, final_keys=()):
        nc, es = self.nc, self.es
        for e in self.ENGS:
            for o in self.q[e]:
                for d in o.deps:
                    if d.dkey is None:
                        if d.eng == o.eng and o.same_ok:
                            continue
                        d.needed = True
        for e in self.ENGS:
            c = 0
            for o in self.q[e]:
                if o.dkey is None and o.needed:
                    c += 1
                    o.val = c
        esem = {e: es.enter_context(nc.semaphore("S_" + e)) for e in self.ENGS}
        dsem = {}
        for k in self.dcount:
            dsem[k] = es.enter_context(nc.semaphore("D%d" % len(dsem)))
        self.n_sems = len(esem) + len(dsem)
        handles = {"pe": "tensor", "act": "scalar", "dve": "vector", "pool": "gpsimd", "sp": "sync"}
        block = es.enter_context(nc.Block())
        final_waits = [(dsem[k], self.dcount[k]) for k in final_keys]
        for e in self.ENGS:
            ops = self.q[e]

            def body(eng, ops=ops, e=e):
                seen = {}
                for o in ops:
                    need = {}
                    for d in o.deps:
                        if d.dkey is not None:
                            key, v = ("d", d.dkey), d.dval
                        else:
                            if d.eng == e and o.same_ok:
                                continue
                            key, v = d.eng, d.val
                        if v > seen.get(key, 0) and v > need.get(key, 0):
                            need[key] = v
                    for key, v in need.items():
                        seen[key] = v
                        sem = dsem[key[1]] if isinstance(key, tuple) else esem[key]
                        eng.wait_ge(sem, v)
                    ins = o.fn(eng)
                    if o.dkey is not None:
                        ins.then_inc(dsem[o.dkey], 16)
                    elif o.needed:
                        ins.then_inc(esem[e], 1)
                if e == "sp":
                    for sem, v in final_waits:
                        eng.wait_ge(sem, v)

            getattr(block, handles[e])(body)


class Ring:
    def __init__(self, name, n):
        self.name, self.n, self.i = name, n, -1

    def next(self):
        self.i += 1
        return self.i % self.n

    def key(self, slot):
        return (self.name, slot)


def run_interleaved(gens):
    gens = list(gens)
    while gens:
        for g in list(gens):
            try:
                next(g)
            except StopIteration:
                gens.remove(g)


def interleave(gens):
    gens = list(gens)
    while gens:
        for g in list(gens):
            try:
                next(g)
            except StopIteration:
                gens.remove(g)
        yield


def interleave1(gens):
    gens = list(gens)
    while gens:
        for g in list(gens):
            try:
                next(g)
            except StopIteration:
                gens.remove(g)
                continue
            yield


def run_weighted(pairs):
    pairs = list(pairs)
    while pairs:
        for p in list(pairs):
            g, w = p
            for _ in range(w):
                try:
                    next(g)
                except StopIteration:
                    pairs.remove(p)
                    break


class Builder:
    def __init__(self, L=4096, NSEQ=2, depth=4, dump=()):
        self.L, self.NSEQ, self.depth, self.dump = L, NSEQ, depth, tuple(dump)
        self.NG = L // 512
        self.nc = bass.Bass("TRN2", target_bir_lowering=False)
        self.es = ExitStack()
        self.rings = {}

    def sb(self, name, shape, dt):
        return self.es.enter_context(self.nc.sbuf_tensor(name, list(shape), dt))

    def ps(self, name, shape, dt):
        return self.es.enter_context(self.nc.psum_tensor(name, list(shape), dt))

    def dram(self, name, shape, dt, kind=None):
        if name in self.dump:
            kind = "ExternalOutput"
        if kind is None:
            return self.nc.dram_tensor(name, list(shape), dt).ap()
        return self.nc.dram_tensor(name, list(shape), dt, kind=kind).ap()

    def ring(self, name, n, shape, dt):
        tiles = [self.sb("%s%d" % (name, i), shape, dt) for i in range(n)]
        r = Ring(name, n)
        r.tiles = tiles
        self.rings[name] = r
        return r

    def nxt(self, r):
        s = r.next()
        return r.tiles[s], r.key(s)

    def build(self):
        nc, es = self.nc, self.es
        L, NSEQ = self.L, self.NSEQ
        with es:
            self.P = P = Prog(nc, es)
            self.declare_io()
            self.setup_consts()
            x_in = self.x_in
            bufs = [self.xbufA, self.xbufB]
            for layer in range(self.depth):
                last = layer == self.depth - 1
                x_out = self.y_out if last else bufs[layer % 2]
                kin = "x_in" if layer == 0 else "xbuf%d" % ((layer - 1) % 2)
                kout = "y_out" if last else "xbuf%d" % (layer % 2)
                if layer % 2 == 0 and os.environ.get('ONLY_ODD') != '1':
                    self.even_layer(layer // 2, x_in, kin, x_out, kout)
                else:
                    self.odd_layer(layer // 2, x_in, kin, x_out, kout)
                x_in = x_out
            P.finalize(final_keys=self.final_keys)
        return nc

    def declare_io(self):
        L, NSEQ = self.L, self.NSEQ
        d = self.dram
        self.x_in = d("x", [NSEQ, L, 1024], F32, "ExternalInput")
        self.y_out = d("y", [NSEQ, L, 1024], F32, "ExternalOutput")
        self.xbufA = d("xbufA", [NSEQ, L, 1024], F32)
        self.xbufB = d("xbufB", [NSEQ, L, 1024], F32)
        I = lambda n, s: d(n, s, F32, "ExternalInput")
        self.even_norm_g = I("even_norm_g", [2, 128, 8])
        self.even_w_in = I("even_w_in", [2, 1024, 5120])
        self.sb_q_g = I("sb_q_norm_g", [2, 128, 1])
        self.sb_k_g = I("sb_k_norm_g", [2, 128, 1])
        self.even_w_out = I("even_w_out", [2, 1536, 1024])
        self.odd_norm_g = I("odd_norm_g", [2, 128, 8])
        self.odd_w_in = I("odd_w_in", [2, 1024, 3088])
        self.gla_w_gate = I("gla_w_gate", [2, 16, 512])
        self.gla_b_gate = I("gla_b_gate", [2, 128, 4])
        self.gla_o_g = I("gla_o_norm_g", [2, 128, 2])
        self.odd_w_out = I("odd_w_out", [2, 1024, 1024])
        self.s5_lam = I("s5_lam", [2, 128, 2, 32])
        self.s5_dt = I("s5_log_dt", [2, 128, 32])
        self.s5_b0 = I("s5_b0", [2, 128, 32, 16])
        self.s5_c0 = I("s5_c0", [2, 128, 32, 16])
        self.s5_d = I("s5_d", [2, 128, 4])
        self.s5_w_glu = I("s5_w_glu", [2, 512, 512])
        self.s5_b_glu = I("s5_b_glu", [2, 128, 4])
        B = lambda n, s: d(n, s, BF16)
        Fd = lambda n, s: d(n, s, F32)
        self.qT = B("qT", [NSEQ, 8, 128, L])
        self.kT = B("kT", [NSEQ, 8, 128, L])
        self.vtm = B("vtm", [NSEQ, L, 1024])
        self.gzT = Fd("gzT", [NSEQ, 1536, L])
        self.uT = B("uT", [NSEQ, 512, L])
        self.mixT = B("mixT", [NSEQ, 1536, L])
        self.q2T = Fd("q2T", [NSEQ, 512, L])
        self.k2T = Fd("k2T", [NSEQ, 512, L])
        self.lgT = Fd("lgT", [NSEQ, 512, L])
        self.final_keys = []

    def setup_consts(self):
        P, nc = self.P, self.nc
        sb = self.sb
        self.identf = sb("identf", [128, 128], F32)
        self.ident = sb("ident", [128, 128], BF16)
        self.ones = sb("ones", [128, 128], BF16)
        self.negones = sb("negones", [128, 128], BF16)
        self.maskLT = sb("maskLT", [128, 128], BF16)
        self.maskLE = sb("maskLE", [128, 128], F32)
        self.negtri = sb("negtri", [128, 128], BF16)
        tmpf = sb("ctmpf", [128, 128], F32)
        identf, ident = self.identf, self.ident
        P.op("pool", lambda e: e.memset(identf[:], 0.0), writes=["identf"])
        P.op("pool", lambda e: e.affine_select(out=identf[:], in_=identf[:], pattern=[[-1, 128]],
                                               compare_op=ALU.not_equal, fill=1.0, base=0,
                                               channel_multiplier=1), reads=["identf"], writes=["identf"])
        P.op("dve", lambda e: e.tensor_copy(out=ident[:], in_=identf[:]), reads=["identf"], writes=["ident"])
        P.op("pool", lambda e: e.memset(self.ones[:], 1.0), writes=["ones"])
        P.op("pool", lambda e: e.memset(self.negones[:], -1.0), writes=["negones"])
        P.op("pool", lambda e: e.memset(tmpf[:], 1.0), writes=["ctmpf"])
        P.op("pool", lambda e: e.affine_select(out=tmpf[:], in_=tmpf[:], pattern=[[1, 128]],
                                               compare_op=ALU.is_gt, fill=0.0, base=0,
                                               channel_multiplier=-1), reads=["ctmpf"], writes=["ctmpf"])
        P.op("dve", lambda e: e.tensor_copy(out=self.maskLT[:], in_=tmpf[:]), reads=["ctmpf"], writes=["maskLT"])
        P.op("pool", lambda e: e.memset(self.maskLE[:], 1.0), writes=["maskLE"])
        P.op("pool", lambda e: e.affine_select(out=self.maskLE[:], in_=self.maskLE[:], pattern=[[1, 128]],
                                               compare_op=ALU.is_ge, fill=0.0, base=0,
                                               channel_multiplier=-1), reads=["maskLE"], writes=["maskLE"])
        P.op("pool", lambda e: e.memset(tmpf[:], -1.0), reads=["maskLT"], writes=["ctmpf"])
        P.op("pool", lambda e: e.affine_select(out=tmpf[:], in_=tmpf[:], pattern=[[-1, 128]],
                                               compare_op=ALU.is_ge, fill=0.0, base=0,
                                               channel_multiplier=1), reads=["ctmpf"], writes=["ctmpf"])
        P.op("dve", lambda e: e.tensor_copy(out=self.negtri[:], in_=tmpf[:]), reads=["ctmpf"], writes=["negtri"])
        self.negbig = sb("negbig", [128, 128], BF16)
        P.op("pool", lambda e: e.memset(tmpf[:], -30000.0), reads=["negtri"], writes=["ctmpf"])
        P.op("pool", lambda e: e.affine_select(out=tmpf[:], in_=tmpf[:], pattern=[[-1, 128]],
                                               compare_op=ALU.is_ge, fill=0.0, base=0,
                                               channel_multiplier=1), reads=["ctmpf"], writes=["ctmpf"])
        P.op("dve", lambda e: e.tensor_copy(out=self.negbig[:], in_=tmpf[:]), reads=["ctmpf"], writes=["negbig"])
        self.CONST = ["ident", "identf", "ones", "negones", "maskLT", "maskLE", "negtri"]

        self.arena = sb("arena", [128, 40960], BF16)
        self.Wb = self.arena[:, :].rearrange("p (k f) -> p k f", k=8)
        self.Wo = sb("Wo", [128, 12, 1024], BF16)
        self.wstage = self.ring("wst", 2, [128, 1024], F32)
        self.gcol = sb("gcol", [128, 8], F32)
        self.xt = self.ring("xt", 2, [128, 4, 1024], F32)
        self.junk = sb("junk", [128, 1024], BF16)
        self.ss4 = sb("ss4", [128, 4], F32)
        self.rstd4 = sb("rstd4", [128, 4], F32)
        self.hb = self.ring("hb", 2, [128, 1024], BF16)
        self.hT = self.ring("hT", 2, [128, 8, 512], BF16)
        self.psT = [self.ps("psT%d" % i, [128, 1024], BF16) for i in range(1)]
        self.psM = [self.ps("psM%d" % i, [128, 512], F32) for i in range(7)]
        self.psM_i = 0
        self.ev32 = self.ring("ev32", 3, [128, 512], F32)
        self.evbf = self.ring("evbf", 4, [128, 512], BF16)
        self.wk32 = self.ring("wk32", 3, [128, 512], F32)
        self.wkbf = self.ring("wkbf", 3, [128, 512], BF16)

    def ov(self, off, shape, dt):
        n = int(np.prod(shape[1:]))
        esz = 4 if dt == F32 else 2
        a = self.arena[:, off // 2: off // 2 + n * esz // 2]
        if dt == F32:
            a = a.bitcast(F32)
        if len(shape) == 3:
            a = a.rearrange("p (a b) -> p a b", a=shape[1])
        return a

    def psum(self):
        if getattr(self, "ps_restrict", False) and os.environ.get("PSR", "1") == "1":
            self.psR_i = getattr(self, "psR_i", 0) + 1
            if self.psR_i % 2:
                return self.psM[6], ("psM", 6)
            return self.psT[0][:, :].bitcast(F32), "psT"
        i = self.psM_i % len(self.psM)
        self.psM_i += 1
        return self.psM[i], ("psM", i)

    def load_weights(self, w_dram, n_k, n_f, dst, dst_key, gain_dram=None):
        P = self.P
        gcol = self.gcol
        if gain_dram is not None:
            P.dma(lambda e: e.dma_start(out=gcol[:], in_=gain_dram), writes=["gcol"], dkey="gcol")
        CH = 1024
        for f0 in range(0, n_f, CH):
            fw = min(CH, n_f - f0)
            ck = (dst_key, f0 // CH) if dst_key == "Wb" else dst_key
            for kc in range(n_k):
                st, sk = self.nxt(self.wstage)
                P.dma(lambda e, st=st, kc=kc, f0=f0, fw=fw: e.dma_start(
                    out=st[:, :fw], in_=w_dram[kc * 128:(kc + 1) * 128, f0:f0 + fw]), writes=[sk], dkey=sk)
                eng = "act" if (kc % 2 == 0) else "dve"
                first = (kc == 0) if dst_key == "Wb" else (kc == 0 and f0 == 0)
                if gain_dram is not None:
                    if eng == "act":
                        P.op(eng, lambda e, st=st, kc=kc, f0=f0, fw=fw: e.activation(
                            out=dst[:, kc, f0:f0 + fw], in_=st[:, :fw], func=AF.Copy, scale=gcol[:, kc:kc + 1]),
                            reads=[sk, "gcol"], writes=[ck], waw=first)
                    else:
                        P.op(eng, lambda e, st=st, kc=kc, f0=f0, fw=fw: e.tensor_scalar(
                            out=dst[:, kc, f0:f0 + fw], in0=st[:, :fw], scalar1=gcol[:, kc:kc + 1], scalar2=None,
                            op0=ALU.mult), reads=[sk, "gcol"], writes=[ck], waw=first)
                else:
                    if eng == "act":
                        P.op(eng, lambda e, st=st, kc=kc, f0=f0, fw=fw: e.copy(
                            out=dst[:, kc, f0:f0 + fw], in_=st[:, :fw]), reads=[sk], writes=[ck], waw=first)
                    else:
                        P.op(eng, lambda e, st=st, kc=kc, f0=f0, fw=fw: e.tensor_copy(
                            out=dst[:, kc, f0:f0 + fw], in_=st[:, :fw]), reads=[sk], writes=[ck], waw=first)

    def prefetch_x(self, x_dram, xkey, s, tg):
        P = self.P
        xt, xk = self.nxt(self.xt)
        P.dma(lambda e: e.dma_start(out=xt[:], in_=x_dram[s, tg * 512:(tg + 1) * 512, :].rearrange(
            "(j p) d -> p j d", p=128)), reads=[xkey], writes=[xk], dkey=xk)
        if not hasattr(self, "xpref"):
            self.xpref = {}
        self.xpref[(xkey, s, tg)] = (xt, xk)

    def load_norm_group(self, x_dram, xkey, s, tg):
        P = self.P
        if (xkey, s, tg) not in getattr(self, "xpref", {}):
            self.prefetch_x(x_dram, xkey, s, tg)
        xt, xk = self.xpref.pop((xkey, s, tg))
        ss4, rstd4, junk = self.ss4, self.rstd4, self.junk
        for j in range(4):
            P.op("act", lambda e, j=j: e.activation(out=junk[:], in_=xt[:, j, :], func=AF.Square,
                                                    accum_out=ss4[:, j:j + 1]),
                 reads=[xk], writes=["junk", "ss4"])
        P.op("act", lambda e: e.activation(out=rstd4[:], in_=ss4[:], func=AF.Ln, scale=1.0 / 1024, bias=EPS),
             reads=["ss4"], writes=["rstd4"])
        P.op("act", lambda e: e.activation(out=rstd4[:], in_=rstd4[:], func=AF.Exp, scale=-0.5), reads=["rstd4"], writes=["rstd4"])
        hT, hk = self.nxt(self.hT)
        psT = self.psT[0]
        for j in range(4):
            hb, hbk = self.nxt(self.hb)
            P.op("dve", lambda e, j=j, hb=hb: e.tensor_scalar(out=hb[:], in0=xt[:, j, :], scalar1=rstd4[:, j:j + 1],
                                                              scalar2=None, op0=ALU.mult),
                 reads=[xk, "rstd4"], writes=[hbk])
            for kc in range(8):
                P.op("pe", lambda e, kc=kc, hb=hb: e.transpose(out=psT[:, kc * 128:(kc + 1) * 128],
                                                               in_=hb[:, kc * 128:(kc + 1) * 128],
                                                               identity=self.ident[:]),
                     reads=[hbk, "ident"], writes=["psT"], same_ok=True, waw=(kc == 0))
            P.op("act", lambda e, j=j: e.copy(out=hT[:, :, j * 128:(j + 1) * 128],
                                              in_=psT[:].rearrange("p (k t) -> p k t", k=8)),
                 reads=["psT"], writes=[hk], waw=(j == 0))
        return hT, hk, xt, xk

    def proj_fm(self, hT, hk, f0, M=128):
        P = self.P
        Wb = self.Wb
        ps, pk = self.psum()
        for kc in range(8):
            P.op("pe", lambda e, kc=kc: e.matmul(ps[:M, :], lhsT=Wb[:, kc, f0:f0 + M], rhs=hT[:, kc, :],
                                                 start=(kc == 0), stop=(kc == 7)),
                 reads=[hk, ("Wb", f0 // 1024)], writes=[pk], same_ok=True, waw=(kc == 0))
        return ps, pk

    def proj_tm(self, hT, hk, j, f0):
        P = self.P
        Wb = self.Wb
        ps, pk = self.psum()
        for kc in range(8):
            P.op("pe", lambda e, kc=kc: e.matmul(ps[:, :], lhsT=hT[:, kc, j * 128:(j + 1) * 128],
                                                 rhs=Wb[:, kc, f0:f0 + 512], start=(kc == 0), stop=(kc == 7)),
                 reads=[hk, ("Wb", f0 // 1024)], writes=[pk], same_ok=True, waw=(kc == 0))
        return ps, pk

    def store(self, tile_ap, tkey, dram_ap, dkey_dram):
        self.P.dma(lambda e: e.dma_start(out=dram_ap, in_=tile_ap), reads=[tkey], writes=[dkey_dram], dkey=tkey)

    def even_layer(self, i, x_in, kin, x_out, kout):
        P = self.P
        L, NSEQ = self.L, self.NSEQ
        P.barrier()
        self.load_weights(self.even_w_in[i], 8, 5120, self.Wb, "Wb", gain_dram=self.even_norm_g[i])
        self.load_weights(self.even_w_out[i], 12, 1024, self.Wo, "Wo")
        if not hasattr(self, "qkg"):
            self.qkg = self.sb("qkg", [128, 2], F32)
        qkg = self.qkg
        P.dma(lambda e: e.dma_start(out=qkg[:, 0:1], in_=self.sb_q_g[i]), writes=["qkg"], dkey="qkg")
        P.dma(lambda e: e.dma_start(out=qkg[:, 1:2], in_=self.sb_k_g[i]), writes=["qkg"], dkey="qkg")
        P.op("dve", lambda e: e.tensor_scalar(out=qkg[:, 0:1], in0=qkg[:, 0:1], scalar1=128 ** -0.5, scalar2=None,
                                              op0=ALU.mult), reads=["qkg"], writes=["qkg"])
        self.s5_prep(i)
        groups = [(s, tg) for s in range(NSEQ) for tg in range(self.NG)]
        for k, (s, tg) in enumerate(groups):
            if k == 0:
                self.prefetch_x(x_in, kin, s, tg)
            if k + 1 < len(groups):
                self.prefetch_x(x_in, kin, *groups[k + 1])
            self.even_inproj_group(i, x_in, kin, s, tg)
        for s in range(NSEQ):
            for _ in self.sb_attention(s):
                pass
        for _ in self.s5_main(i):
            pass
        for s in range(NSEQ):
            self.out_proj(s, x_in, kin, x_out, kout, 12, self.mixT, "mixT")

    def even_inproj_group(self, i, x_in, kin, s, tg):
        P = self.P
        hT, hk, xt, xk = self.load_norm_group(x_in, kin, s, tg)
        cols = slice(tg * 512, (tg + 1) * 512)
        for which in range(2):
            dst = self.qT if which == 0 else self.kT
            dkey = "qT" if which == 0 else "kT"
            for h in range(8):
                ps, pk = self.proj_fm(hT, hk, which * 1024 + h * 128)
                sq, sqk = self.nxt(self.wkbf)
                P.op("act", lambda e, ps=ps, sq=sq: e.activation(out=sq[:], in_=ps[:], func=AF.Square),
                     reads=[pk], writes=[sqk])
                ps2, pk2 = self.psum()
                P.op("pe", lambda e, ps2=ps2, sq=sq: e.matmul(ps2[:], lhsT=self.ones[:], rhs=sq[:], start=True, stop=True),
                     reads=[sqk, "ones"], writes=[pk2])
                rs, rsk = self.nxt(self.wk32)
                P.op("act", lambda e, ps2=ps2, rs=rs: e.activation(out=rs[:], in_=ps2[:], func=AF.Ln,
                                                                   scale=1.0 / 128, bias=EPS),
                     reads=[pk2], writes=[rsk])
                P.op("act", lambda e, rs=rs: e.activation(out=rs[:], in_=rs[:], func=AF.Exp, scale=-0.5), reads=[rsk], writes=[rsk])
                ob, obk = self.nxt(self.evbf)
                P.op("dve", lambda e, ps=ps, rs=rs, ob=ob, which=which: e.scalar_tensor_tensor(
                    out=ob[:], in0=ps[:], scalar=self.qkg[:, which:which + 1], in1=rs[:], op0=ALU.mult, op1=ALU.mult),
                    reads=[pk, rsk, "qkg"], writes=[obk])
                self.store(ob[:], obk, dst[s, h, :, cols], (dkey, s))
        for j in range(4):
            for half in range(2):
                ps, pk = self.proj_tm(hT, hk, j, 2048 + half * 512)
                ob, obk = self.nxt(self.evbf)
                P.op("act" if half else "dve",
                     (lambda e, ps=ps, ob=ob: e.copy(out=ob[:], in_=ps[:])) if half else
                     (lambda e, ps=ps, ob=ob: e.tensor_copy(out=ob[:], in_=ps[:])),
                     reads=[pk], writes=[obk])
                r0 = tg * 512 + j * 128
                self.store(ob[:], obk, self.vtm[s, r0:r0 + 128, half * 512:(half + 1) * 512], ("vtm", s))
        for c in range(12):
            f0 = 3072 + c * 128 if c < 8 else 4608 + (c - 8) * 128
            ps, pk = self.proj_fm(hT, hk, f0)
            ob, obk = self.nxt(self.ev32)
            P.op("act", lambda e, ps=ps, ob=ob: e.activation(out=ob[:], in_=ps[:], func=AF.Silu),
                 reads=[pk], writes=[obk])
            self.store(ob[:], obk, self.gzT[s, c * 128:(c + 1) * 128, cols], ("gzT", s))
        for c in range(4):
            ps, pk = self.proj_fm(hT, hk, 4096 + c * 128)
            ob, obk = self.nxt(self.evbf)
            P.op("dve", lambda e, ps=ps, ob=ob: e.tensor_copy(out=ob[:], in_=ps[:]), reads=[pk], writes=[obk])
            self.store(ob[:], obk, self.uT[s, c * 128:(c + 1) * 128, cols], ("uT", s))

    def out_proj(self, s, x_in, kin, x_out, kout, n_k, mix_dram, mixkey):
        P = self.P
        if not hasattr(self, "mxin"):
            self.mxin = Ring("mxin", 2)
            self.mxin.tiles = [self.ov(k * 12288, [128, 12, 512], BF16) for k in range(2)]
            self.xres = self.xt
        def issue(tg):
            mx, mk = self.nxt(self.mxin)
            P.dma(lambda e, mx=mx, tg=tg: e.dma_start(
                out=mx[:, :n_k, :], in_=mix_dram[s, :n_k * 128, tg * 512:(tg + 1) * 512].rearrange(
                    "(kc p) t -> p kc t", p=128)), reads=[(mixkey, s)], writes=[mk], dkey=mk)
            xr, xrk = self.nxt(self.xres)
            P.dma(lambda e, xr=xr, tg=tg: e.dma_start(
                out=xr[:], in_=x_in[s, tg * 512:(tg + 1) * 512, :].rearrange("(j p) d -> p j d", p=128)),
                reads=[kin], writes=[xrk], dkey=xrk)
            return mx, mk, xr, xrk

        pend = issue(0)
        for tg in range(self.NG):
            mx, mk, xr, xrk = pend
            if tg + 1 < self.NG:
                pend = issue(tg + 1)
            for j in range(4):
                for half in range(2):
                    ps, pk = self.psum()
                    for kc in range(n_k):
                        P.op("pe", lambda e, ps=ps, kc=kc, j=j, half=half, mx=mx: e.matmul(
                            ps[:], lhsT=mx[:, kc, j * 128:(j + 1) * 128],
                            rhs=self.Wo[:, kc, half * 512:(half + 1) * 512], start=(kc == 0), stop=(kc == n_k - 1)),
                            reads=[mk, "Wo"], writes=[pk], same_ok=True, waw=(kc == 0))
                    P.op("dve", lambda e, ps=ps, j=j, half=half, xr=xr: e.tensor_tensor(
                        out=xr[:, j, half * 512:(half + 1) * 512], in0=ps[:], in1=xr[:, j, half * 512:(half + 1) * 512],
                        op=ALU.add), reads=[pk, xrk], writes=[xrk])
            P.dma(lambda e, xr=xr, tg=tg: e.dma_start(
                out=x_out[s, tg * 512:(tg + 1) * 512, :].rearrange("(j p) d -> p j d", p=128), in_=xr[:]),
                reads=[xrk], writes=[kout], dkey=(xrk, "st"))
            if kout == "y_out":
                if (xrk, "st") not in self.final_keys:
                    self.final_keys.append((xrk, "st"))

    def sb_attention(self, s):
        P = self.P
        L = self.L
        NB = L // 128
        if not OVERLAP:
            P.barrier()
        KB = 1024
        qkv = []
        NQ = 1 if OVERLAP else 2
        for r in range(NQ):
            base = r * 3 * (L * 2)
            qkv.append((self.ov(base, [128, L], BF16), self.ov(base + 2 * L, [128, L], BF16),
                        self.ov(base + 4 * L, [128, NB, 128], BF16)))
        cbase = NQ * 3 * L * 2
        NCH = 3
        chains = []
        for c in range(NCH):
            b = cbase + c * 7 * KB
            chains.append(dict(e32=self.ov(b, [128, 512], F32), Lb=self.ov(b + 2 * KB, [128, 512], BF16),
                               wb=self.ov(b + 3 * KB, [128, 512], BF16), R32=self.ov(b + 4 * KB, [128, 512], F32),
                               Rbf=self.ov(b + 6 * KB, [128, 512], BF16), id=c,
                               psZ=self.psM[2 * c], psZk=("psM", 2 * c), psO=self.psM[2 * c + 1], psOk=("psM", 2 * c + 1)))
        gbase = cbase + NCH * 7 * KB
        gz = [self.ov(gbase + k * 2 * KB, [128, 512], F32) for k in range(NCH)]
        assert gbase + NCH * 2 * KB <= (51 * KB if OVERLAP else 81920)
        for h in range(8):
            qh, kh, vh = qkv[h % NQ]
            kq, kk, kv = ("sbq", h % NQ), ("sbk", h % NQ), ("sbv", h % NQ)
            P.dma(lambda e, qh=qh, h=h: e.dma_start(out=qh, in_=self.qT[s, h]), reads=[("qT", s)], writes=[kq], dkey=kq)
            P.dma(lambda e, kh=kh, h=h: e.dma_start(out=kh, in_=self.kT[s, h]), reads=[("kT", s)], writes=[kk], dkey=kk)
            P.dma(lambda e, vh=vh, h=h: e.dma_start(
                out=vh, in_=self.vtm[s, :, h * 128:(h + 1) * 128].rearrange("(b p) d -> p b d", p=128)),
                reads=[("vtm", s)], writes=[kv], dkey=kv)

            def chain(qg, ch, h=h, qh=qh, kh=kh, vh=vh, kq=kq, kk=kk, kv=kv):
                c = ch["id"]
                tag = lambda n: ("sbc", n, c)
                e32, Lb, wb, R32, Rbf = ch["e32"], ch["Lb"], ch["wb"], ch["R32"], ch["Rbf"]
                psZ, psZk, psO, psOk = ch["psZ"], ch["psZk"], ch["psO"], ch["psOk"]
                g = gz[c]
                P.dma(lambda e: e.dma_start(out=g, in_=self.gzT[s, h * 128:(h + 1) * 128, qg * 512:(qg + 1) * 512]),
                      reads=[("gzT", s)], writes=[tag("gz")], dkey=tag("gz"))
                P.op("pool", lambda e: e.memset(R32, 0.0), writes=[tag("R32")])
                kbs = list(reversed(range(4 * qg + 4)))

                def zmm(kb):
                    c0 = max(0, kb - 4 * qg) * 128
                    q0 = qg * 512 + c0
                    P.op("pe", lambda e: e.matmul(
                        psZ[:, c0:512], lhsT=kh[:, kb * 128:(kb + 1) * 128], rhs=qh[:, q0:qg * 512 + 512],
                        start=True, stop=False, skip_group_check=True), reads=[kq, kk], writes=[psZk])
                    if kb >= 4 * qg:
                        P.op("pe", lambda e: e.matmul(
                            psZ[:, c0:c0 + 128], lhsT=self.ident[:], rhs=self.negbig[:], start=False, stop=False,
                            skip_group_check=True), reads=["ident", "negbig"], writes=[psZk], waw=False)

                zmm(kbs[0])
                for idx, kb in enumerate(kbs):
                    c0 = max(0, kb - 4 * qg) * 128
                    N = 512 - c0
                    diag = kb >= 4 * qg
                    P.op("act", lambda e, c0=c0: e.activation(out=e32[:, c0:512], in_=psZ[:, c0:512], func=AF.Exp),
                         reads=[psZk], writes=[tag("e32")])
                    yield
                    P.op("act", lambda e, c0=c0: e.activation(out=Lb[:, c0:512], in_=e32[:, c0:512], func=AF.Ln, bias=1.0),
                         reads=[tag("e32")], writes=[tag("Lb")])
                    yield
                    P.op("pe", lambda e, c0=c0, idx=idx: e.matmul(
                        psZ[:, c0:512], lhsT=self.negtri[:], rhs=Lb[:, c0:512], start=False, stop=(idx == 0),
                        skip_group_check=True), reads=[tag("Lb"), "negtri"], writes=[psZk], same_ok=True, waw=False)
                    if idx > 0:
                        P.op("pe", lambda e, c0=c0: e.matmul(
                            psZ[:, c0:512], lhsT=self.negones[:], rhs=Rbf[:, c0:512], start=False, stop=True,
                            skip_group_check=True), reads=[tag("Rbf"), "negones"], writes=[psZk], same_ok=True, waw=False)
                    P.op("act", lambda e, c0=c0: e.activation(out=wb[:, c0:512], in_=psZ[:, c0:512], func=AF.Exp),
                         reads=[psZk], writes=[tag("wb")])
                    yield
                    if idx + 1 < len(kbs):
                        zmm(kbs[idx + 1])
                    P.op("pe", lambda e, kb=kb, c0=c0, idx=idx: e.matmul(
                        psO[:, c0:512], lhsT=vh[:, kb, :], rhs=wb[:, c0:512], start=(idx == 0), stop=(idx == len(kbs) - 1),
                        skip_group_check=True), reads=[tag("wb"), kv], writes=[psOk], same_ok=True, waw=(idx == 0))
                    for _f in range(N_FILL):
                        P.op("pe", lambda e: e.matmul(self.psM[6][:, :], lhsT=self.ones[:], rhs=qh[:, 0:512], start=True,
                                                      stop=True, skip_group_check=True), reads=[], writes=[("psM", 6)])
                    if idx < len(kbs) - 1:
                        c1 = max(0, kbs[idx + 1] - 4 * qg) * 128
                        P.op("pool", lambda e, c0=c0: e.tensor_tensor(out=R32[:, c0:512], in0=R32[:, c0:512],
                                                                      in1=Lb[:, c0:512], op=ALU.add),
                             reads=[tag("Lb"), tag("R32")], writes=[tag("R32")])
                        P.op("dve", lambda e, c1=c1: e.tensor_copy(out=Rbf[:, c1:512], in_=R32[:, c1:512]),
                             reads=[tag("R32")], writes=[tag("Rbf")])
                    yield
                ob, obk = self.nxt(self.evbf)
                P.op("dve", lambda e, ob=ob: e.tensor_tensor(out=ob[:], in0=psO[:], in1=g, op=ALU.mult),
                     reads=[psOk, tag("gz")], writes=[obk])
                self.store(ob[:], obk, self.mixT[s, h * 128:(h + 1) * 128, qg * 512:(qg + 1) * 512], ("mixT", s))
                yield

            def head_gen(c):
                for qg in range(self.NG - 1 - c, -1, -NCH):
                    yield from chain(qg, chains[c])
            yield from interleave([head_gen(c) for c in range(NCH)])

    def s5_prep(self, i):
        pass

    def s5_consts(self):
        if hasattr(self, "Jm"):
            return
        P, sb = self.P, self.sb
        self.Jm = sb("Jm", [128, 128], F32)
        self.nJm = sb("nJm", [128, 128], F32)
        self.bd = sb("bdmask", [128, 128], F32)
        self.Em = sb("Emat", [8, 128], F32)
        self.rowm = sb("rowm", [128, 2], F32)
        self.sgn = sb("sgn", [128, 1], F32)
        self.s5sm = sb("s5sm", [128, 12], F32)
        Jm, nJm, bd, Em, rowm, sgn = self.Jm, self.nJm, self.bd, self.Em, self.rowm, self.sgn
        P.op("pool", lambda e: e.memset(Jm[:], 0.0), writes=["Jm"])
        P.op("pool", lambda e: e.affine_select(out=Jm[:], in_=Jm[:], pattern=[[-1, 128]], compare_op=ALU.not_equal,
                                               fill=-1.0, base=64, channel_multiplier=1), reads=["Jm"], writes=["Jm"])
        P.op("pool", lambda e: e.affine_select(out=Jm[:], in_=Jm[:], pattern=[[-1, 128]], compare_op=ALU.not_equal,
                                               fill=1.0, base=-64, channel_multiplier=1), reads=["Jm"], writes=["Jm"])
        P.op("dve", lambda e: e.tensor_scalar(out=nJm[:], in0=Jm[:], scalar1=-1.0, scalar2=None, op0=ALU.mult),
             reads=["Jm"], writes=["nJm"])
        P.op("pool", lambda e: e.memset(Em[:], 1.0), writes=["Em"])
        P.op("pool", lambda e: e.affine_select(out=Em[:], in_=Em[:], pattern=[[1, 128]], compare_op=ALU.is_ge,
                                               fill=0.0, base=0, channel_multiplier=-16), reads=["Em"], writes=["Em"])
        P.op("pool", lambda e: e.affine_select(out=Em[:], in_=Em[:], pattern=[[-1, 128]], compare_op=ALU.is_ge,
                                               fill=0.0, base=15, channel_multiplier=16), reads=["Em"], writes=["Em"])
        ps, pk = self.psum()
        P.op("pe", lambda e: e.matmul(ps[:, 0:128], lhsT=Em[:], rhs=Em[:], start=True, stop=True), reads=["Em"], writes=[pk])
        P.op("dve", lambda e: e.tensor_copy(out=bd[:], in_=ps[:, 0:128]), reads=[pk], writes=["bd"])
        P.op("dve", lambda e: e.tensor_reduce(out=rowm[:], in_=bd[:].rearrange("p (q m w) -> p m q w", q=4, m=2, w=16),
                                              axis=AX.XY, op=ALU.add), reads=["bd"], writes=["rowm"])
        P.op("dve", lambda e: e.tensor_scalar(out=rowm[:], in0=rowm[:], scalar1=1.0 / 16, scalar2=None, op0=ALU.mult),
             reads=["rowm"], writes=["rowm"])
        self.Ecol = sb("Ecol", [128, 8], F32)
        ps2, pk2 = self.psum()
        P.op("pe", lambda e: e.transpose(out=ps2[:, 0:8], in_=Em[:], identity=self.identf[0:8, 0:8]), reads=["Em", "identf"], writes=[pk2])
        P.op("dve", lambda e: e.tensor_copy(out=self.Ecol[:], in_=ps2[:, 0:8]), reads=[pk2], writes=["Ecol"])
        self.gml = Ring("gml", 2)
        self.gml.tiles = [t[:, :].rearrange("p (m k) -> p m k", m=8) for t in self.hb.tiles]
        P.op("pool", lambda e: e.memset(sgn[0:64, :], 1.0), writes=["sgn"])
        P.op("pool", lambda e: e.memset(sgn[64:128, :], -1.0), writes=["sgn"], waw=False)

    def s5_main(self, i):
        P = self.P
        L, NSEQ = self.L, self.NSEQ
        NCH = L // 8
        self.s5_consts()
        P.barrier()
        KB = 1024
        ov = self.ov
        SM = ov(0, [128, 16, 32], F32)
        LAM = ov(2 * KB, [128, 2, 32], F32)
        MTr = [ov(2 * KB + 512 + k * 512, [128, 128], F32) for k in range(3)]
        AT = ov(4 * KB, [128, 32, 128], F32)
        Am = ov(20 * KB, [128, 32, 128], F32)
        Wsb = ov(36 * KB, [128, 8, 512], F32)
        Vsb = ov(52 * KB, [128, 9, 512], F32)
        PWre = ov(70 * KB, [128, 17, 32], F32)
        PWim = ov(70 * KB + 2176, [128, 17, 32], F32)
        B0 = self.ev32.tiles[0]
        C0 = self.ev32.tiles[1]
        Vpad = self.xt.tiles[0][:, :, :].rearrange("p a b -> p (a b)").bitcast(BF16).rearrange(
            "p (m g c) -> p m g c", m=8, g=32)
        WTf = self.xt.tiles[1][:, :, :].rearrange("p a b -> p (a b)").bitcast(BF16)[:, 0:4096].rearrange(
            "p (t j k) -> p t j k", t=4, j=8)
        Kblk = self.hT.tiles[0][:, :, :].rearrange("p a b -> p (a b)").rearrange("p (t j k) -> p t j k", t=4, j=8)
        WG = self.hT.tiles[1][:, 0:4, :]
        s5sm = self.s5sm
        identf, Jm, nJm = self.identf, self.Jm, self.nJm
        sm = lambda k: SM[:, k, :]
        K = lambda n: ("s5", n)

        P.dma(lambda e: e.dma_start(out=LAM, in_=self.s5_lam[i]), writes=[K("LAM")], dkey=K("LAM"))
        P.dma(lambda e: e.dma_start(out=sm(0), in_=self.s5_dt[i]), writes=[K("sm0")], dkey=K("sm0"))
        P.dma(lambda e: e.dma_start(out=B0[:].rearrange("p (g c) -> p g c", g=32), in_=self.s5_b0[i]),
              writes=[("ev32", 0)], dkey=("ev32", 0))
        P.dma(lambda e: e.dma_start(out=C0[:].rearrange("p (g c) -> p g c", g=32), in_=self.s5_c0[i]),
              writes=[("ev32", 1)], dkey=("ev32", 1))
        P.dma(lambda e: e.dma_start(out=s5sm[:, 0:4], in_=self.s5_d[i]), writes=["s5sm"], dkey="s5sm")
        P.dma(lambda e: e.dma_start(out=s5sm[:, 4:8], in_=self.s5_b_glu[i]), writes=["s5sm"], dkey="s5sm")
        for kc in range(4):
            st, sk = self.nxt(self.wstage)
            P.dma(lambda e, st=st, kc=kc: e.dma_start(out=st[:, :512], in_=self.s5_w_glu[i, kc * 128:(kc + 1) * 128, :]),
                  writes=[sk], dkey=sk)
            P.op("dve", lambda e, st=st, kc=kc: e.tensor_copy(out=WG[:, kc, :], in_=st[:, :512]), reads=[sk],
                 writes=[K("WG")], waw=(kc == 0))
        lre, lim = LAM[:, 0, :], LAM[:, 1, :]
        kl = K("LAM")

        def tt(eng, out, a, b, op, rk, wk):
            P.op(eng, lambda e: e.tensor_tensor(out=out, in0=a, in1=b, op=op), reads=rk, writes=wk)

        def ts(eng, out, a, s1, s2, op0, op1, rk, wk):
            if op1 is None:
                P.op(eng, lambda e: e.tensor_scalar(out=out, in0=a, scalar1=s1, scalar2=None, op0=op0), reads=rk, writes=wk)
            else:
                P.op(eng, lambda e: e.tensor_scalar(out=out, in0=a, scalar1=s1, scalar2=s2, op0=op0, op1=op1),
                     reads=rk, writes=wk)

        S = K("SM")
        P.op("act", lambda e: e.activation(out=sm(0), in_=sm(0), func=AF.Exp), reads=[K("sm0")], writes=[S])
        tt("dve", sm(1), lre, sm(0), ALU.mult, [kl, S], [S])
        P.op("act", lambda e: e.activation(out=sm(1), in_=sm(1), func=AF.Exp), reads=[S], writes=[S])
        tt("dve", sm(2), lim, sm(0), ALU.mult, [kl, S], [S])
        ts("dve", sm(3), sm(2), math.pi / 2, None, ALU.add, None, [S], [S])
        for src in (2, 3):
            ts("dve", sm(6), sm(src), 0.0, None, ALU.mult, None, [S], [S])
            for j in range(6):
                ts("dve", sm(5), sm(src), (2 * j + 1) * math.pi, -2 * math.pi, ALU.is_ge, ALU.mult, [S], [S])
                tt("dve", sm(6), sm(6), sm(5), ALU.add, [S], [S])
            tt("dve", sm(src), sm(src), sm(6), ALU.add, [S], [S])
        P.op("act", lambda e: e.activation(out=sm(7), in_=sm(2), func=AF.Sin), reads=[S], writes=[S])
        P.op("act", lambda e: e.activation(out=sm(8), in_=sm(3), func=AF.Sin), reads=[S], writes=[S])
        PK = K("PW")
        tt("dve", PWre[:, 1, :], sm(1), sm(8), ALU.mult, [S], [PK])
        tt("dve", PWim[:, 1, :], sm(1), sm(7), ALU.mult, [S], [PK])
        are, aim = PWre[:, 1, :], PWim[:, 1, :]
        tt("dve", sm(9), lre, lre, ALU.mult, [kl], [S])
        tt("dve", sm(10), lim, lim, ALU.mult, [kl], [S])
        tt("dve", sm(9), sm(9), sm(10), ALU.add, [S], [S])
        P.op("dve", lambda e: e.reciprocal(out=sm(10), in_=sm(9)), reads=[S], writes=[S])
        ts("dve", sm(11), are, -1.0, None, ALU.add, None, [PK], [S])
        tt("dve", sm(12), sm(11), lre, ALU.mult, [S, kl], [S])
        tt("dve", sm(13), aim, lim, ALU.mult, [PK, kl], [S])
        tt("dve", sm(12), sm(12), sm(13), ALU.add, [S], [S])
        tt("dve", PWre[:, 0, :], sm(12), sm(10), ALU.mult, [S], [PK])
        tt("dve", sm(12), aim, lre, ALU.mult, [PK, kl], [S])
        tt("dve", sm(13), sm(11), lim, ALU.mult, [S, kl], [S])
        tt("dve", sm(12), sm(12), sm(13), ALU.subtract, [S], [S])
        tt("dve", PWim[:, 0, :], sm(12), sm(10), ALU.mult, [S], [PK])

        def cmul(dst, a, b):
            tt("dve", sm(12), PWre[:, a, :], PWre[:, b, :], ALU.mult, [PK], [S])
            tt("dve", sm(13), PWim[:, a, :], PWim[:, b, :], ALU.mult, [PK], [S])
            tt("dve", sm(14), PWre[:, a, :], PWim[:, b, :], ALU.mult, [PK], [S])
            tt("dve", sm(15), PWim[:, a, :], PWre[:, b, :], ALU.mult, [PK], [S])
            tt("dve", PWre[:, dst, :], sm(12), sm(13), ALU.subtract, [S], [PK])
            tt("dve", PWim[:, dst, :], sm(14), sm(15), ALU.add, [S], [PK])

        for k in range(2, 9):
            cmul(k, k - 1, 1)
        for k in range(9, 17):
            cmul(k, k - 1, k - 1)

        if S5_STOP == 1:
            return
        def build_M(out, pidx, g, transpose, eng2="dve", rk=(), wk=()):
            P.op("act", lambda e: e.activation(out=out, in_=identf[:], func=AF.Copy, scale=PWre[:, pidx, g:g + 1]),
                 reads=[PK, "identf"] + list(rk), writes=list(wk))
            Jx = nJm if transpose else Jm
            P.op("dve", lambda e: e.scalar_tensor_tensor(out=out, in0=Jx[:], scalar=PWim[:, pidx, g:g + 1], in1=out,
                                                         op0=ALU.mult, op1=ALU.add),
                 reads=[PK, "Jm", "nJm"] + list(wk), writes=list(wk))

        for g in range(32):
            build_M(AT[:, g, :], 1, g, True, wk=[K("AT")])
            build_M(Am[:, g, :], 1, g, False, wk=[K("A")])
        if S5_STOP == 2:
            return
        psW, pkW = self.psum()
        mi = 0
        for g in range(32):
            mt = MTr[mi % 3]
            mk = K(("MT", mi % 3))
            mi += 1
            build_M(mt, 0, g, True, wk=[mk])
            P.op("pe", lambda e, g=g, mt=mt: e.matmul(psW[:, g * 16:(g + 1) * 16], lhsT=mt, rhs=B0[:, g * 16:(g + 1) * 16],
                                                      start=True, stop=True, skip_group_check=True),
                 reads=[mk, ("ev32", 0)], writes=[pkW], waw=(g == 0))
        P.op("act", lambda e: e.copy(out=Wsb[:, 0, :], in_=psW[:]), reads=[pkW], writes=[K("W0")])
        for j in range(1, 8):
            psW, pkW = self.psum()
            for g in range(32):
                P.op("pe", lambda e, g=g, j=j, psW=psW: e.matmul(
                    psW[:, g * 16:(g + 1) * 16], lhsT=AT[:, g, :], rhs=Wsb[:, j - 1, g * 16:(g + 1) * 16],
                    start=True, stop=True, skip_group_check=True), reads=[K("AT"), K("W%d" % (j - 1))], writes=[pkW],
                    waw=(g == 0))
            P.op("act", lambda e, j=j, psW=psW: e.copy(out=Wsb[:, j, :], in_=psW[:]), reads=[pkW], writes=[K("W%d" % j)])
        if S5_STOP == 3:
            return
        P.op("dve", lambda e: e.tensor_scalar(out=Vsb[:, 0, :], in0=C0[:], scalar1=self.sgn[:, 0:1], scalar2=None,
                                              op0=ALU.mult), reads=[("ev32", 1), "sgn"], writes=[K("V0")])
        for j in range(1, 9):
            psV, pkV = self.psum()
            for g in range(32):
                P.op("pe", lambda e, g=g, j=j, psV=psV: e.matmul(
                    psV[:, g * 16:(g + 1) * 16], lhsT=Am[:, g, :], rhs=Vsb[:, j - 1, g * 16:(g + 1) * 16],
                    start=True, stop=True, skip_group_check=True), reads=[K("A"), K("V%d" % (j - 1))], writes=[pkV],
                    waw=(g == 0))
            P.op("act", lambda e, j=j, psV=psV: e.copy(out=Vsb[:, j, :], in_=psV[:]), reads=[pkV], writes=[K("V%d" % j)])
        if S5_STOP == 4:
            return
        for T in range(4):
            for tau in range(8):
                ps, pk = self.psum()
                P.op("pe", lambda e, T=T, tau=tau, ps=ps: e.matmul(
                    ps[:, 0:128], lhsT=Wsb[:, tau, T * 128:(T + 1) * 128], rhs=Vsb[:, 0, T * 128:(T + 1) * 128],
                    start=True, stop=True), reads=[K("W%d" % tau), K("V0")], writes=[pk])
                if tau == 0:
                    tmp, tk = self.nxt(self.wk32)
                    P.op("dve", lambda e, ps=ps, tmp=tmp: e.tensor_tensor(out=tmp[:, 0:128], in0=ps[:, 0:128], in1=self.bd[:],
                                                                          op=ALU.mult), reads=[pk, "bd"], writes=[tk])
                    P.op("dve", lambda e, T=T, tmp=tmp: e.scalar_tensor_tensor(
                        out=Kblk[:, T, 0, :], in0=identf[:], scalar=s5sm[:, T:T + 1], in1=tmp[:, 0:128], op0=ALU.mult,
                        op1=ALU.add), reads=[tk, "s5sm", "identf"], writes=[K("Kblk")], waw=False)
                else:
                    P.op("dve", lambda e, T=T, tau=tau, ps=ps: e.tensor_tensor(
                        out=Kblk[:, T, tau, :], in0=ps[:, 0:128], in1=self.bd[:], op=ALU.mult), reads=[pk, "bd"],
                        writes=[K("Kblk")], waw=False)
        if S5_STOP == 5:
            return
        for T in range(4):
            for j in range(8):
                ps, pk = self.psum()
                P.op("pe", lambda e, T=T, j=j, ps=ps: e.transpose(out=ps[:, 0:128], in_=Wsb[:, j, T * 128:(T + 1) * 128],
                                                                  identity=identf[:]),
                     reads=[K("W%d" % j), "identf"], writes=[pk])
                P.op("act", lambda e, T=T, j=j, ps=ps: e.copy(out=WTf[:, T, j, :], in_=ps[:, 0:128]),
                     reads=[pk], writes=[K("WTm")], waw=False)
        if S5_STOP == 6:
            return
        P.op("pool", lambda e: e.memset(Vpad.rearrange("p m g c -> p (m g c)"), 0.0), writes=[K("Vpad")])
        for m in range(8):
            for mem in range(2):
                P.op("dve" if mem else "pool", lambda e, m=m, mem=mem: e.tensor_copy(
                    out=Vpad[:, m, mem::2, 16 * mem:16 * mem + 16],
                    in_=Vsb[:, m + 1, :].rearrange("p (g c) -> p g c", g=32)[:, mem::2, :]),
                    reads=[K("V%d" % (m + 1)), K("Vpad")], writes=[K("Vpad")], waw=False)

        if S5_STOP == 7:
            return
        if os.environ.get("DBG_S5") == "1":
            dbg = {"dPWre": (PWre, [128, 17, 32]), "dPWim": (PWim, [128, 17, 32]), "dWsb": (Wsb, [128, 8, 512]),
                   "dVsb": (Vsb, [128, 9, 512])}
            P.barrier()
            for nm, (ap_, shp) in dbg.items():
                dt_ = self.nc.dram_tensor(nm, shp, F32, kind="ExternalOutput").ap()
                P.dma(lambda e, ap_=ap_, dt_=dt_: e.dma_start(out=dt_, in_=ap_), writes=[("dbg", nm)], dkey=("dbg", nm))
            dk = self.nc.dram_tensor("dKblk", [128, 4, 8, 128], BF16, kind="ExternalOutput").ap()
            P.dma(lambda e: e.dma_start(out=dk, in_=Kblk), writes=[("dbg", "k")], dkey=("dbg", "k"))
            P.barrier()
        yield
        Wof = self.Wo[:, :, :].rearrange("p a b -> p (a b)")

        def wov(off, shape, dt):
            n = int(np.prod(shape[1:]))
            esz = 4 if dt == F32 else 2
            a = Wof[:, off // 2: off // 2 + n * esz // 2]
            if dt == F32:
                a = a.bitcast(F32)
            if len(shape) == 3:
                a = a.rearrange("p (a b) -> p a b", a=shape[1])
            return a

        P.barrier()
        U = ov(0, [128, L], BF16)
        Hprev = ov(2 * L, [128, 8, NCH], BF16)
        b1 = 2 * L + 16 * NCH
        ytile = ov(b1, [128, L], F32)
        cH32 = [ov(b1 + 4 * L + k * 2 * KB, [128, 512], F32) for k in range(8)]
        cHb = [ov(b1 + 4 * L + 16 * KB + k * KB, [128, 512], BF16) for k in range(8)]
        assert b1 + 4 * L + 24 * KB <= 70 * KB
        y2gs = [ov(b1, [128, 4, 512], F32), ov(b1 + 4 * L + 24 * KB, [128, 4, 512], F32)]
        y2bfs = [ov(b1 + 8 * KB, [128, 4, 512], BF16), ov(b1 + 4 * L + 32 * KB, [128, 4, 512], BF16)]
        assert b1 + 4 * L + 36 * KB <= 70 * KB
        if not hasattr(self, "y2T"):
            self.y2T = self.dram("y2T", [NSEQ, 512, L], F32)
        nsteps = int(math.log2(NCH))
        xt1b = self.xt.tiles[1][:, :, :].rearrange("p a b -> p (a b)").bitcast(BF16)[:, 4096:8192].rearrange(
            "p (c k) -> p c k", c=32)
        cgm = [[xt1b[:, ci * 4 + r, :] for r in range(4)] for ci in range(8)]
        hT1b = self.hT.tiles[1][:, 4:8, :].rearrange("p a b -> p (a b)").rearrange("p (c k) -> p c k", c=16)
        cMT = [[hT1b[:, ci * 2 + r, :] for r in range(2)] for ci in range(8)]
        YT = K("ytile")
        for s in range(NSEQ):
            for T in range(4):
                P.dma(lambda e, s=s, T=T: e.dma_start(out=U, in_=self.uT[s, T * 128:(T + 1) * 128, :]),
                      reads=[("uT", s)], writes=[K("U")], dkey=K("U"))

                def gchain(g8, ci, T=T, s=s):
                    g = 8 * T + g8
                    H32, hk32 = cH32[ci], K(("cH32", ci))
                    Hb, hkb = cHb[ci], K(("cHb", ci))
                    psG, pkG = self.psum()
                    for m in range(8):
                        gmt = cgm[ci][m % 4]
                        gmk = K(("cgm", ci, m % 4))
                        if m % 2:
                            P.op("act", lambda e, m=m, gmt=gmt: e.activation(
                                out=gmt, in_=WTf[:, T, 7 - m, :], func=AF.Copy, scale=self.Ecol[:, g8:g8 + 1]),
                                reads=[K("WTm"), "Ecol"], writes=[gmk])
                        else:
                            P.op("dve", lambda e, m=m, gmt=gmt: e.tensor_scalar(
                                out=gmt, in0=WTf[:, T, 7 - m, :], scalar1=self.Ecol[:, g8:g8 + 1], scalar2=None,
                                op0=ALU.mult), reads=[K("WTm"), "Ecol"], writes=[gmk])
                        P.op("pe", lambda e, m=m, gmt=gmt: e.matmul(
                            psG[:, :NCH], lhsT=gmt, rhs=U[:, m::8], start=(m == 0), stop=(m == 7)),
                            reads=[gmk, K("U")], writes=[pkG], waw=(m == 0))
                    P.op("act", lambda e: e.copy(out=H32[:, :NCH], in_=psG[:, :NCH]), reads=[pkG], writes=[hk32])
                    P.op("dve", lambda e: e.tensor_copy(out=Hb[:, :NCH], in_=psG[:, :NCH]), reads=[pkG], writes=[hkb])
                    yield
                    for j in range(nsteps):
                        sft = 1 << j
                        mt = cMT[ci][j % 2]
                        mk = K(("cMT", ci, j % 2))
                        build_M(mt, 8 + j, g, True, wk=[mk])
                        yield
                        psS, pkS = self.psum()
                        P.op("pe", lambda e, mt=mt, sft=sft, psS=psS: e.matmul(
                            psS[:, :NCH - sft], lhsT=mt, rhs=Hb[:, 0:NCH - sft], start=True, stop=True),
                            reads=[mk, hkb], writes=[pkS])
                        P.op("dve", lambda e, psS=psS, sft=sft: e.tensor_tensor(
                            out=H32[:, sft:NCH], in0=H32[:, sft:NCH], in1=psS[:, :NCH - sft], op=ALU.add),
                            reads=[hk32, pkS], writes=[hk32])
                        yield
                        if j < nsteps - 1:
                            P.op("act", lambda e, sft=sft: e.copy(out=Hb[:, sft:NCH], in_=H32[:, sft:NCH]),
                                 reads=[hk32], writes=[hkb])
                            yield
                    P.op("act", lambda e: e.copy(out=Hprev[:, g8, :], in_=H32[:, 0:NCH]),
                         reads=[hk32], writes=[K(("Hp", g8))])
                    yield

                yield from interleave([gchain(ci, ci) for ci in range(8)])
                for m in range(8):
                    psY, pkY = self.psum()
                    for tau in range(m + 1):
                        P.op("pe", lambda e, T=T, tau=tau, m=m, psY=psY: e.matmul(
                            psY[:, :NCH], lhsT=Kblk[:, T, tau, :], rhs=U[:, (m - tau)::8], start=(tau == 0), stop=False,
                            skip_group_check=True), reads=[K("Kblk"), K("U")], writes=[pkY], waw=(tau == 0))
                    for g8 in range(8):
                        q = g8 // 2
                        P.op("pe", lambda e, T=T, g8=g8, q=q, m=m, psY=psY: e.matmul(
                            psY[32 * q:32 * q + 32, 1:NCH], lhsT=Vpad[:, m, 8 * T + g8, :], rhs=Hprev[:, g8, 0:NCH - 1],
                            start=False, stop=(g8 == 7), tile_position=(0, 32 * q), skip_group_check=True),
                            reads=[K("Vpad"), K(("Hp", g8))], writes=[pkY], waw=False)
                    P.op("act", lambda e, m=m, psY=psY: e.copy(out=ytile[:, m::8], in_=psY[:, :NCH]), reads=[pkY],
                         writes=[YT], waw=(m == 0))
                    yield
                for c in range(L // 512):
                    cs = slice(c * 512, (c + 1) * 512)
                    t1, t1k = self.nxt(self.wk32)
                    P.op("dve", lambda e, t1=t1, cs=cs: e.tensor_tensor(out=t1[:], in0=ytile[:, cs], in1=ytile[:, cs], op=ALU.mult),
                         reads=[YT], writes=[t1k])
                    P.op("pool", lambda e, t1=t1: e.tensor_scalar(out=t1[:], in0=t1[:], scalar1=0.044715, scalar2=1.0,
                                                                  op0=ALU.mult, op1=ALU.add), reads=[t1k], writes=[t1k])
                    yield
                    P.op("dve", lambda e, t1=t1, cs=cs: e.tensor_tensor(out=t1[:], in0=t1[:], in1=ytile[:, cs], op=ALU.mult),
                         reads=[t1k, YT], writes=[t1k])
                    P.op("act", lambda e, t1=t1: e.activation(out=t1[:], in_=t1[:], func=AF.Sigmoid, scale=1.5957691216),
                         reads=[t1k], writes=[t1k])
                    yield
                    ob, obk = self.nxt(self.ev32)
                    P.op("dve", lambda e, t1=t1, ob=ob, cs=cs: e.tensor_tensor(out=ob[:], in0=t1[:], in1=ytile[:, cs], op=ALU.mult),
                         reads=[t1k, YT], writes=[obk])
                    self.store(ob[:], obk, self.y2T[s, T * 128:(T + 1) * 128, cs], ("y2T", s))
                    yield
            for tg in range(self.NG):
                cs = slice(tg * 512, (tg + 1) * 512)
                yg = y2gs[tg % 2]
                y2bf = y2bfs[tg % 2]
                ygk = K(("y2g", tg % 2))
                ybk = K(("y2bf", tg % 2))
                P.dma(lambda e, yg=yg, cs=cs, s=s: e.dma_start(out=yg, in_=self.y2T[s, :, cs].rearrange("(t p) l -> p t l", p=128)),
                      reads=[("y2T", s)], writes=[ygk, YT], dkey=ygk)
                P.op("act", lambda e, yg=yg, y2bf=y2bf: e.copy(out=y2bf[:, 0:2, :], in_=yg[:, 0:2, :]), reads=[ygk, YT], writes=[ybk])
                P.op("dve", lambda e, yg=yg, y2bf=y2bf: e.tensor_copy(out=y2bf[:, 2:4, :], in_=yg[:, 2:4, :]), reads=[ygk, YT],
                     writes=[ybk], waw=False)
                yield
                for oc in range(4):
                    ps, pk = self.psum()
                    for kc in range(4):
                        P.op("pe", lambda e, ps=ps, kc=kc, oc=oc, y2bf=y2bf: e.matmul(
                            ps[:], lhsT=WG[:, kc, oc * 128:(oc + 1) * 128], rhs=y2bf[:, kc, :], start=(kc == 0), stop=(kc == 3)),
                            reads=[K("WG"), ybk, YT], writes=[pk], waw=(kc == 0))
                    sg, sgk = self.nxt(self.wk32)
                    P.op("act", lambda e, ps=ps, sg=sg, oc=oc: e.activation(out=sg[:], in_=ps[:], func=AF.Sigmoid,
                                                                            bias=s5sm[:, 4 + oc:5 + oc]),
                         reads=[pk, "s5sm"], writes=[sgk])
                    gzt, gzk = self.nxt(self.ev32)
                    P.dma(lambda e, gzt=gzt, oc=oc, cs=cs, s=s: e.dma_start(out=gzt[:], in_=self.gzT[s, 1024 + oc * 128:1024 + (oc + 1) * 128, cs]),
                          reads=[("gzT", s)], writes=[gzk], dkey=(gzk, "ld"))
                    P.op("dve", lambda e, sg=sg, yg=yg, oc=oc: e.tensor_tensor(out=sg[:], in0=sg[:], in1=yg[:, oc, :], op=ALU.mult),
                         reads=[sgk, ygk, YT], writes=[sgk])
                    ob, obk = self.nxt(self.evbf)
                    P.op("dve", lambda e, sg=sg, gzt=gzt, ob=ob: e.tensor_tensor(out=ob[:], in0=sg[:], in1=gzt[:], op=ALU.mult),
                         reads=[sgk, gzk], writes=[obk])
                    self.store(ob[:], obk, self.mixT[s, 1024 + oc * 128:1024 + (oc + 1) * 128, cs], ("mixT", s))
                    yield

    def odd_layer(self, i, x_in, kin, x_out, kout):
        P = self.P
        L, NSEQ = self.L, self.NSEQ
        KB = 1024
        P.barrier()
        full_Wb = self.Wb
        self.Wb = self.arena[:, 0:8 * 3088].rearrange("p (k f) -> p k f", k=8)
        self.load_weights(self.odd_w_in[i], 8, 3088, self.Wb, "Wb", gain_dram=self.odd_norm_g[i])
        self.load_weights(self.odd_w_out[i], 8, 1024, self.Wo, "Wo")
        self.s5_consts()
        sm = self.s5sm
        wg16 = self.ov(50 * KB, [128, 512], BF16)[0:16, :]
        st, sk = self.nxt(self.wstage)
        P.dma(lambda e: e.dma_start(out=st[0:16, 0:512], in_=self.gla_w_gate[i]), writes=[sk], dkey=sk)
        P.op("dve", lambda e: e.tensor_copy(out=wg16, in_=st[0:16, 0:512]), reads=[sk], writes=["wg16"])
        P.dma(lambda e: e.dma_start(out=sm[:, 0:4], in_=self.gla_b_gate[i]), writes=["s5sm"], dkey="s5sm")
        P.dma(lambda e: e.dma_start(out=sm[:, 8:10], in_=self.gla_o_g[i]), writes=["s5sm"], dkey="s5sm")
        P.op("dve", lambda e: e.tensor_scalar(out=sm[:, 0:4], in0=sm[:, 0:4], scalar1=-1.0, scalar2=None, op0=ALU.mult),
             reads=["s5sm"], writes=["s5sm"])
        groups = [(s, tg) for s in range(NSEQ) for tg in range(self.NG)]
        for k, (s, tg) in enumerate(groups):
            if k == 0:
                self.prefetch_x(x_in, kin, s, tg)
            if k + 1 < len(groups):
                self.prefetch_x(x_in, kin, *groups[k + 1])
            self.odd_inproj_group(s, tg, x_in, kin, wg16)
        self.gla_phase()
        for s in range(NSEQ):
            self.out_proj(s, x_in, kin, x_out, kout, 8, self.mixT, "mixT")
        self.Wb = full_Wb

    def odd_inproj_group(self, s, tg, x_in, kin, wg16):
        P = self.P
        hT, hk, xt, xk = self.load_norm_group(x_in, kin, s, tg)
        cols = slice(tg * 512, (tg + 1) * 512)
        sm = self.s5sm
        for which, dst, dkey in ((0, self.q2T, "q2T"), (1, self.k2T, "k2T")):
            for c in range(4):
                ps, pk = self.proj_fm(hT, hk, which * 512 + c * 128)
                ob, obk = self.nxt(self.ev32)
                P.op("act" if c % 2 else "dve",
                     (lambda e, ps=ps, ob=ob: e.copy(out=ob[:], in_=ps[:])) if c % 2 else
                     (lambda e, ps=ps, ob=ob: e.tensor_copy(out=ob[:], in_=ps[:])), reads=[pk], writes=[obk])
                self.store(ob[:], obk, dst[s, c * 128:(c + 1) * 128, cols], (dkey, s))
        for j in range(4):
            for half in range(2):
                ps, pk = self.proj_tm(hT, hk, j, 1024 + half * 512)
                ob, obk = self.nxt(self.evbf)
                P.op("act" if half else "dve",
                     (lambda e, ps=ps, ob=ob: e.copy(out=ob[:], in_=ps[:])) if half else
                     (lambda e, ps=ps, ob=ob: e.tensor_copy(out=ob[:], in_=ps[:])), reads=[pk], writes=[obk])
                r0 = tg * 512 + j * 128
                self.store(ob[:], obk, self.vtm[s, r0:r0 + 128, half * 512:(half + 1) * 512], ("vtm", s))
        for c in range(8):
            ps, pk = self.proj_fm(hT, hk, 2048 + c * 128)
            ob, obk = self.nxt(self.ev32)
            P.op("act", lambda e, ps=ps, ob=ob: e.activation(out=ob[:], in_=ps[:], func=AF.Silu), reads=[pk], writes=[obk])
            self.store(ob[:], obk, self.gzT[s, c * 128:(c + 1) * 128, cols], ("gzT", s))
        ps, pk = self.proj_fm(hT, hk, 3072, M=16)
        rT, rk = self.nxt(self.wkbf)
        P.op("dve", lambda e, ps=ps, rT=rT: e.tensor_copy(out=rT[0:16, :], in_=ps[0:16, :]), reads=[pk], writes=[rk])
        for c in range(4):
            ps2, pk2 = self.psum()
            P.op("pe", lambda e, ps2=ps2, c=c, rT=rT: e.matmul(ps2[:], lhsT=wg16[:, c * 128:(c + 1) * 128], rhs=rT[0:16, :],
                                                               start=True, stop=True), reads=[rk, "wg16"], writes=[pk2])
            ex, exk = self.nxt(self.wk32)
            P.op("act", lambda e, ps2=ps2, ex=ex, c=c: e.activation(out=ex[:], in_=ps2[:], func=AF.Exp, scale=-1.0,
                                                                    bias=sm[:, c:c + 1]), reads=[pk2, "s5sm"], writes=[exk])
            ob, obk = self.nxt(self.ev32)
            P.op("act", lambda e, ex=ex, ob=ob: e.activation(out=ob[:], in_=ex[:], func=AF.Ln, bias=1.0), reads=[exk], writes=[obk])
            self.store(ob[:], obk, self.lgT[s, c * 128:(c + 1) * 128, cols], ("lgT", s))

    def gla_phase(self):
        P = self.P
        L, NSEQ = self.L, self.NSEQ
        KB = 1024
        ov = self.ov
        P.barrier()
        NCHN = 2
        CH = 26 * KB
        rmask = ov(NCHN * CH, [128, 512], F32)
        P.op("pool", lambda e: e.memset(rmask, 1.0), writes=["rmask"])
        for c in range(4):
            P.op("pool", lambda e, c=c: e.memset(rmask[:, c * 128:c * 128 + 1], 0.0), writes=["rmask"], waw=False)
        assert NCHN * CH + 2 * KB <= 80 * KB
        sm = self.s5sm
        psrot = [0]

        def lpsum():
            k = 4 + psrot[0] % 3
            psrot[0] += 1
            return self.psM[k], ("psM", k)

        def chain(s, h, ci):
            b = ci * CH
            T = lambda n: ("gla", n, ci)
            q32 = ov(b, [128, 512], F32)
            k32 = ov(b + 2 * KB, [128, 512], F32)
            Lg = ov(b + 4 * KB, [128, 512], F32)
            G = ov(b + 6 * KB, [128, 512], F32)
            E1 = ov(b + 8 * KB, [128, 512], F32)
            E2 = ov(b + 10 * KB, [128, 512], F32)
            gz = ov(b + 12 * KB, [128, 2, 512], F32)
            vt = ov(b + 16 * KB, [128, 4, 256], BF16)
            qd = ov(b + 18 * KB, [128, 512], BF16)
            ki = ov(b + 19 * KB, [128, 512], BF16)
            kdT = ov(b + 20 * KB, [128, 512], BF16)
            kdec = ov(b + 21 * KB, [128, 4, 128], BF16)
            S32 = ov(b + 22 * KB, [128, 256], F32)
            Sbf = ov(b + 23 * KB, [128, 256], BF16)
            scT = [ov(b + 23 * KB + 512 + k * 256, [128, 128], BF16) for k in range(2)]
            rs = ov(b + 24 * KB, [128, 512], F32)
            psO = [self.psM[2 * ci], self.psM[2 * ci + 1]]
            psOk = [("psM", 2 * ci), ("psM", 2 * ci + 1)]
            psT = self.psT[0]
            P.op("pool", lambda e: e.memset(S32, 0.0), writes=[T("S32")])
            P.op("pool", lambda e: e.memset(Sbf, 0.0), writes=[T("Sbf")])
            for tg in range(self.NG):
                cs = slice(tg * 512, (tg + 1) * 512)
                P.dma(lambda e, cs=cs: e.dma_start(out=q32, in_=self.q2T[s, h * 128:(h + 1) * 128, cs]),
                      reads=[("q2T", s)], writes=[T("q32")], dkey=T("q32"))
                P.dma(lambda e, cs=cs: e.dma_start(out=k32, in_=self.k2T[s, h * 128:(h + 1) * 128, cs]),
                      reads=[("k2T", s)], writes=[T("k32")], dkey=T("k32"))
                P.dma(lambda e, cs=cs: e.dma_start(out=Lg, in_=self.lgT[s, h * 128:(h + 1) * 128, cs]),
                      reads=[("lgT", s)], writes=[T("Lg")], dkey=T("Lg"))
                P.dma(lambda e, cs=cs: e.dma_start(out=vt, in_=self.vtm[s, cs, h * 256:(h + 1) * 256].rearrange(
                    "(j p) d -> p j d", p=128)), reads=[("vtm", s)], writes=[T("vt")], dkey=T("vt"))
                P.dma(lambda e, cs=cs: e.dma_start(out=gz, in_=self.gzT[s, h * 256:(h + 1) * 256, cs].rearrange(
                    "(v p) t -> p v t", p=128)), reads=[("gzT", s)], writes=[T("gz")], dkey=T("gz"))
                P.op("dve", lambda e: e.tensor_tensor_scan(out=G, data0=rmask, data1=Lg, initial=0.0, op0=ALU.mult,
                                                           op1=ALU.add), reads=[T("Lg"), "rmask"], writes=[T("G")])
                P.op("act", lambda e: e.activation(out=E1, in_=G, func=AF.Exp, scale=-1.0 / 16), reads=[T("G")], writes=[T("E1")])
                P.op("act", lambda e: e.activation(out=E2, in_=G, func=AF.Exp, scale=1.0 / 16), reads=[T("G")], writes=[T("E2")])
                P.op("dve", lambda e: e.scalar_tensor_tensor(out=qd, in0=q32, scalar=128 ** -0.5, in1=E1, op0=ALU.mult,
                                                             op1=ALU.mult), reads=[T("q32"), T("E1")], writes=[T("qd")])
                P.op("pool", lambda e: e.tensor_tensor(out=ki, in0=k32, in1=E2, op=ALU.mult), reads=[T("k32"), T("E2")],
                     writes=[T("ki")])
                for c in range(4):
                    cc = slice(c * 128, (c + 1) * 128)
                    P.op("dve", lambda e, c=c, cc=cc: e.scalar_tensor_tensor(
                        out=kdT[:, cc], in0=k32[:, cc], scalar=E1[:, c * 128 + 127:c * 128 + 128], in1=E2[:, cc],
                        op0=ALU.mult, op1=ALU.mult), reads=[T("k32"), T("E1"), T("E2")], writes=[T("kdT")], waw=(c == 0))
                yield
                for c in range(4):
                    cc = slice(c * 128, (c + 1) * 128)
                    P.op("pe", lambda e, cc=cc: e.transpose(out=psT[:, cc], in_=kdT[:, cc], identity=self.ident[:]),
                         reads=[T("kdT"), "ident"], writes=["psT"], waw=(c == 0))
                P.op("act", lambda e: e.copy(out=kdec, in_=psT[:, 0:512].rearrange("p (c k) -> p c k", c=4)),
                     reads=["psT"], writes=[T("kdec")])
                yield
                for c in range(4):
                    cc = slice(c * 128, (c + 1) * 128)
                    pss, pssk = lpsum()
                    P.op("pe", lambda e, cc=cc, pss=pss: e.matmul(pss[:, 0:128], lhsT=ki[:, cc], rhs=qd[:, cc], start=True,
                                                                  stop=True), reads=[T("ki"), T("qd")], writes=[pssk])
                    sc = scT[c % 2]
                    sck = T(("scT", c % 2))
                    P.op("dve", lambda e, pss=pss, sc=sc: e.tensor_tensor(out=sc, in0=pss[:, 0:128], in1=self.maskLE[:],
                                                                          op=ALU.mult), reads=[pssk, "maskLE"], writes=[sck])
                    for vc in range(2):
                        P.op("pe", lambda e, vc=vc, c=c, cc=cc, sc=sc: e.matmul(
                            psO[vc][:, cc], lhsT=vt[:, c, vc * 128:(vc + 1) * 128], rhs=sc, start=True, stop=False,
                            skip_group_check=True), reads=[T("vt"), sck], writes=[psOk[vc]], waw=(c == 0))
                        P.op("pe", lambda e, vc=vc, cc=cc: e.matmul(
                            psO[vc][:, cc], lhsT=Sbf[:, vc * 128:(vc + 1) * 128], rhs=qd[:, cc], start=False, stop=True,
                            skip_group_check=True), reads=[T("Sbf"), T("qd")], writes=[psOk[vc]], waw=False)
                    psS, psSk = lpsum()
                    P.op("pe", lambda e, c=c, psS=psS: e.matmul(psS[:, 0:256], lhsT=kdec[:, c, :], rhs=vt[:, c, :], start=True,
                                                                stop=True), reads=[T("kdec"), T("vt")], writes=[psSk])
                    P.op("dve", lambda e, c=c, psS=psS: e.scalar_tensor_tensor(
                        out=S32, in0=S32, scalar=E1[:, c * 128 + 127:c * 128 + 128], in1=psS[:, 0:256], op0=ALU.mult,
                        op1=ALU.add), reads=[T("S32"), T("E1"), psSk], writes=[T("S32")])
                    P.op("act", lambda e: e.copy(out=Sbf, in_=S32), reads=[T("S32")], writes=[T("Sbf")])
                    yield
                sqs = []
                for vc in range(2):
                    sq, sqk = self.nxt(self.wkbf)
                    P.op("act", lambda e, vc=vc, sq=sq: e.activation(out=sq[:], in_=psO[vc][:], func=AF.Square),
                         reads=[psOk[vc]], writes=[sqk])
                    sqs.append((sq, sqk))
                pn, pnk = lpsum()
                for vc in range(2):
                    P.op("pe", lambda e, vc=vc, pn=pn, sq=sqs[vc][0]: e.matmul(pn[:], lhsT=self.ones[:], rhs=sq[:], start=(vc == 0),
                                                                stop=(vc == 1)), reads=[sqs[vc][1], "ones"], writes=[pnk],
                         waw=(vc == 0))
                P.op("act", lambda e, pn=pn: e.activation(out=rs, in_=pn[:], func=AF.Ln, scale=1.0 / 256, bias=EPS),
                     reads=[pnk], writes=[T("rs")])
                P.op("act", lambda e: e.activation(out=rs, in_=rs, func=AF.Exp, scale=-0.5), reads=[T("rs")], writes=[T("rs")])
                for vc in range(2):
                    tmp, tk = self.nxt(self.wk32)
                    P.op("dve", lambda e, vc=vc, tmp=tmp: e.scalar_tensor_tensor(
                        out=tmp[:], in0=psO[vc][:], scalar=sm[:, 8 + vc:9 + vc], in1=rs, op0=ALU.mult, op1=ALU.mult),
                        reads=[psOk[vc], "s5sm", T("rs")], writes=[tk])
                    ob, obk = self.nxt(self.evbf)
                    if os.environ.get("GLA_DBG") == "1":
                        P.op("dve", lambda e, vc=vc, ob=ob: e.tensor_copy(out=ob[:], in_=psO[vc][:]), reads=[psOk[vc]], writes=[obk])
                    elif os.environ.get("GLA_DBG") == "2":
                        P.op("dve", lambda e, vc=vc, ob=ob, tmp=tmp: e.tensor_copy(out=ob[:], in_=tmp[:]), reads=[tk], writes=[obk])
                    else:
                        P.op("pool", lambda e, vc=vc, tmp=tmp, ob=ob: e.tensor_tensor(out=ob[:], in0=tmp[:], in1=gz[:, vc, :],
                                                                                      op=ALU.mult), reads=[tk, T("gz")], writes=[obk])
                    self.store(ob[:], obk, self.mixT[s, h * 256 + vc * 128:h * 256 + (vc + 1) * 128, cs], ("mixT", s))
                yield

        jobs = [(s, h) for s in range(NSEQ) for h in range(4)]
        for r0 in range(0, len(jobs), NCHN):
            run_interleaved([chain(s, h, ci) for ci, (s, h) in enumerate(jobs[r0:r0 + NCHN])])


def host_layout(inputs, nseq_total=16):
    f = lambda a: np.ascontiguousarray(np.asarray(a, dtype=np.float32))
    d = {}
    d["even_norm_g"] = f(inputs["even_norm_g"].reshape(2, 8, 128).transpose(0, 2, 1))
    d["odd_norm_g"] = f(inputs["odd_norm_g"].reshape(2, 8, 128).transpose(0, 2, 1))
    d["even_w_in"] = f(inputs["even_w_in"])
    d["even_w_out"] = f(inputs["even_w_out"])
    d["odd_w_in"] = f(inputs["odd_w_in"])
    d["odd_w_out"] = f(inputs["odd_w_out"])
    d["sb_q_norm_g"] = f(inputs["sb_q_norm_g"].reshape(2, 128, 1))
    d["sb_k_norm_g"] = f(inputs["sb_k_norm_g"].reshape(2, 128, 1))
    d["gla_w_gate"] = f(inputs["gla_w_gate"])
    d["gla_b_gate"] = f(inputs["gla_b_gate"].reshape(2, 4, 128).transpose(0, 2, 1))
    d["gla_o_norm_g"] = f(inputs["gla_o_norm_g"].reshape(2, 2, 128).transpose(0, 2, 1))
    lam = np.stack([inputs["s5_lambda_re"], inputs["s5_lambda_im"]], axis=1)
    lam = lam.transpose(0, 3, 1, 2)
    d["s5_lam"] = f(np.concatenate([lam, lam], axis=1))
    d["s5_log_dt"] = f(np.broadcast_to(inputs["s5_log_dt"][:, None, :], (2, 128, 32)))
    b = np.concatenate([inputs["s5_b_re"], inputs["s5_b_im"]], axis=2)
    d["s5_b0"] = f(b.transpose(0, 2, 1, 3))
    c = np.concatenate([inputs["s5_c_re"], inputs["s5_c_im"]], axis=3)
    d["s5_c0"] = f(c.transpose(0, 3, 1, 2))
    d["s5_d"] = f(inputs["s5_d"].reshape(2, 4, 128).transpose(0, 2, 1))
    d["s5_w_glu"] = f(inputs["s5_w_glu"])
    d["s5_b_glu"] = f(inputs["s5_b_glu"].reshape(2, 4, 128).transpose(0, 2, 1))
    return d


def kernel(**inputs):
    x = np.asarray(inputs["x"], dtype=np.float32)
    bsz, L, _ = x.shape
    nseq = bsz // N_CORES
    b = Builder(L=L, NSEQ=nseq, depth=4)
    nc = b.build()
    params = host_layout({k: np.asarray(v) for k, v in inputs.items() if k != "x"})
    in_maps = []
    for c in range(N_CORES):
        m = dict(params)
        m["x"] = np.ascontiguousarray(x[c * nseq:(c + 1) * nseq])
        in_maps.append(m)
    res = run_bass_kernel_spmd(nc, in_maps, core_ids=list(range(N_CORES)))
    return np.concatenate([r["y"] for r in res.results], axis=0).astype(np.float32)
```

```python
import math
from contextlib import ExitStack

import numpy as np
import concourse.bass as bass
import concourse.mybir as mybir
from concourse.bass_utils import run_bass_kernel_spmd

F32 = mybir.dt.float32
BF16 = mybir.dt.bfloat16
AF = mybir.ActivationFunctionType
ALU = mybir.AluOpType
AX = mybir.AxisListType

D_MODEL = 1024
EPS = 1e-6
N_CORES = 8
import os
OVERLAP = False
N_FILL = int(os.environ.get('N_FILL', '0'))
SB_W = int(os.environ.get('SB_W', '2'))
S5_W = int(os.environ.get('S5_W', '3'))
S5_STOP = int(os.environ.get('S5_STOP', '0'))


class _Op:
    __slots__ = ("eng", "fn", "deps", "val", "needed", "dkey", "dval", "same_ok")


class Prog:
    ENGS = ("pe", "act", "dve", "pool", "sp")

    def __init__(self, nc, es):
        self.nc = nc
        self.es = es
        self.q = {e: [] for e in self.ENGS}
        self.W = {}
        self.R = {}
        self.dcount = {}
        self.n_ops = 0
        self.last_op = {}
        self.last_dma = {}
        self.pending = {}

    @staticmethod
    def _is_psum(b):
        return (isinstance(b, tuple) and b[0] == "psM") or b == "psT"

    @staticmethod
    def _chan(op):
        return ("d", op.dkey) if op.dkey is not None else op.eng

    def op(self, eng, fn, reads=(), writes=(), dkey=None, waw=True, same_ok=False):
        if eng == "pe":
            same_ok = True
        o = _Op()
        o.eng, o.fn, o.val, o.needed, o.dkey, o.same_ok = eng, fn, 0, False, dkey, same_ok
        deps = {}

        def add(d):
            deps[id(d)] = d

        for b in reads:
            for d in self.W.get(b, {}).values():
                add(d)
            if self._is_psum(b):
                for chn, d in self.R.get(b, {}).items():
                    if chn != eng:
                        add(d)
        for b in writes:
            for d in self.R.get(b, {}).values():
                add(d)
            if waw:
                for d in self.W.get(b, {}).values():
                    add(d)
        if eng in self.pending:
            for d in self.pending.pop(eng):
                add(d)
        o.deps = list(deps.values())
        if dkey is None:
            self.last_op[eng] = o
        else:
            self.last_dma[dkey] = o
        if dkey is not None:
            self.dcount[dkey] = self.dcount.get(dkey, 0) + 16
            o.dval = self.dcount[dkey]
        ch = self._chan(o)
        for b in reads:
            self.R.setdefault(b, {})[ch] = o
        for b in writes:
            if waw:
                self.W[b] = {ch: o}
                self.R[b] = {}
            else:
                self.W.setdefault(b, {})[ch] = o
        self.q[eng].append(o)
        self.n_ops += 1
        return o

    def barrier(self):
        deps = list(self.last_op.values()) + list(self.last_dma.values())
        for e in self.ENGS:
            self.pending[e] = list(deps)

    def dma(self, fn, reads=(), writes=(), dkey=None, eng="sp", waw=False):
        assert dkey is not None
        return self.op(eng, fn, reads, writes, dkey=dkey, waw=waw)

    def finalize(self, final_keys=()):
        nc, es = self.nc, self.es
        for e in self.ENGS:
            for o in self.q[e]:
                for d in o.deps:
                    if d.dkey is None:
                        if d.eng == o.eng and o.same_ok:
                            continue
                        d.needed = True
        for e in self.ENGS:
            c = 0
            for o in self.q[e]:
                if o.dkey is None and o.needed:
                    c += 1
                    o.val = c
        esem = {e: es.enter_context(nc.semaphore("S_" + e)) for e in self.ENGS}
        dsem = {}
        for k in self.dcount:
            dsem[k] = es.enter_context(nc.semaphore("D%d" % len(dsem)))
        self.n_sems = len(esem) + len(dsem)
        handles = {"pe": "tensor", "act": "scalar", "dve": "vector", "pool": "gpsimd", "sp": "sync"}
        block = es.enter_context(nc.Block())
        final_waits = [(dsem[k], self.dcount[k]) for k in final_keys]
        for e in self.ENGS:
            ops = self.q[e]

            def body(eng, ops=ops, e=e):
                seen = {}
                for o in ops:
                    need = {}
                    for d in o.deps:
                        if d.dkey is not None:
                            key, v = ("d", d.dkey), d.dval
                        else:
                            if d.eng == e and o.same_ok:
                                continue
                            key, v = d.eng, d.val
                        if v > seen.get(key, 0) and v > need.get(key, 0):
                            need[key] = v
                    for key, v in need.items():
                        seen[key] = v
                        sem = dsem[key[1]] if isinstance(key, tuple) else esem[key]
                        eng.wait_ge(sem, v)
                    ins = o.fn(eng)
                    if o.dkey is not None:
                        ins.then_inc(dsem[o.dkey], 16)
                    elif o.needed:
                        ins.then_inc(esem[e], 1)
                if e == "sp":
                    for sem, v in final_waits:
                        eng.wait_ge(sem, v)

            getattr(block, handles[e])(body)


class Ring:
    def __init__(self, name, n):
        self.name, self.n, self.i = name, n, -1

    def next(self):
        self.i += 1
        return self.i % self.n

    def key(self, slot):
        return (self.name, slot)


def run_interleaved(gens):
    gens = list(gens)
    while gens:
        for g in list(gens):
            try:
                next(g)
            except StopIteration:
                gens.remove(g)


def interleave(gens):
    gens = list(gens)
    while gens:
        for g in list(gens):
            try:
                next(g)
            except StopIteration:
                gens.remove(g)
        yield


def interleave1(gens):
    gens = list(gens)
    while gens:
        for g in list(gens):
            try:
                next(g)
            except StopIteration:
                gens.remove(g)
                continue
            yield


def run_weighted(pairs):
    pairs = list(pairs)
    while pairs:
        for p in list(pairs):
            g, w = p
            for _ in range(w):
                try:
                    next(g)
                except StopIteration:
                    pairs.remove(p)
                    break


class Builder:
    def __init__(self, L=4096, NSEQ=2, depth=4, dump=()):
        self.L, self.NSEQ, self.depth, self.dump = L, NSEQ, depth, tuple(dump)
        self.NG = L // 512
        self.nc = bass.Bass("TRN2", target_bir_lowering=False)
        self.es = ExitStack()
        self.rings = {}

    def sb(self, name, shape, dt):
        return self.es.enter_context(self.nc.sbuf_tensor(name, list(shape), dt))

    def ps(self, name, shape, dt):
        return self.es.enter_context(self.nc.psum_tensor(name, list(shape), dt))

    def dram(self, name, shape, dt, kind=None):
        if name in self.dump:
            kind = "ExternalOutput"
        if kind is None:
            return self.nc.dram_tensor(name, list(shape), dt).ap()
        return self.nc.dram_tensor(name, list(shape), dt, kind=kind).ap()

    def ring(self, name, n, shape, dt):
        tiles = [self.sb("%s%d" % (name, i), shape, dt) for i in range(n)]
        r = Ring(name, n)
        r.tiles = tiles
        self.rings[name] = r
        return r

    def nxt(self, r):
        s = r.next()
        return r.tiles[s], r.key(s)

    def build(self):
        nc, es = self.nc, self.es
        L, NSEQ = self.L, self.NSEQ
        with es:
            self.P = P = Prog(nc, es)
            self.declare_io()
            self.setup_consts()
            x_in = self.x_in
            bufs = [self.xbufA, self.xbufB]
            for layer in range(self.depth):
                last = layer == self.depth - 1
                x_out = self.y_out if last else bufs[layer % 2]
                kin = "x_in" if layer == 0 else "xbuf%d" % ((layer - 1) % 2)
                kout = "y_out" if last else "xbuf%d" % (layer % 2)
                if layer % 2 == 0 and os.environ.get('ONLY_ODD') != '1':
                    self.even_layer(layer // 2, x_in, kin, x_out, kout)
                else:
                    self.odd_layer(layer // 2, x_in, kin, x_out, kout)
                x_in = x_out
            P.finalize(final_keys=self.final_keys)
        return nc

    def declare_io(self):
        L, NSEQ = self.L, self.NSEQ
        d = self.dram
        self.x_in = d("x", [NSEQ, L, 1024], F32, "ExternalInput")
        self.y_out = d("y", [NSEQ, L, 1024], F32, "ExternalOutput")
        self.xbufA = d("xbufA", [NSEQ, L, 1024], F32)
        self.xbufB = d("xbufB", [NSEQ, L, 1024], F32)
        I = lambda n, s: d(n, s, F32, "ExternalInput")
        self.even_norm_g = I("even_norm_g", [2, 128, 8])
        self.even_w_in = I("even_w_in", [2, 1024, 5120])
        self.sb_q_g = I("sb_q_norm_g", [2, 128, 1])
        self.sb_k_g = I("sb_k_norm_g", [2, 128, 1])
        self.even_w_out = I("even_w_out", [2, 1536, 1024])
        self.odd_norm_g = I("odd_norm_g", [2, 128, 8])
        self.odd_w_in = I("odd_w_in", [2, 1024, 3088])
        self.gla_w_gate = I("gla_w_gate", [2, 16, 512])
        self.gla_b_gate = I("gla_b_gate", [2, 128, 4])
        self.gla_o_g = I("gla_o_norm_g", [2, 128, 2])
        self.odd_w_out = I("odd_w_out", [2, 1024, 1024])
        self.s5_lam = I("s5_lam", [2, 128, 2, 32])
        self.s5_dt = I("s5_log_dt", [2, 128, 32])
        self.s5_b0 = I("s5_b0", [2, 128, 32, 16])
        self.s5_c0 = I("s5_c0", [2, 128, 32, 16])
        self.s5_d = I("s5_d", [2, 128, 4])
        self.s5_w_glu = I("s5_w_glu", [2, 512, 512])
        self.s5_b_glu = I("s5_b_glu", [2, 128, 4])
        B = lambda n, s: d(n, s, BF16)
        Fd = lambda n, s: d(n, s, F32)
        self.qT = B("qT", [NSEQ, 8, 128, L])
        self.kT = B("kT", [NSEQ, 8, 128, L])
        self.vtm = B("vtm", [NSEQ, L, 1024])
        self.gzT = Fd("gzT", [NSEQ, 1536, L])
        self.uT = B("uT", [NSEQ, 512, L])
        self.mixT = B("mixT", [NSEQ, 1536, L])
        self.q2T = Fd("q2T", [NSEQ, 512, L])
        self.k2T = Fd("k2T", [NSEQ, 512, L])
        self.lgT = Fd("lgT", [NSEQ, 512, L])
        self.final_keys = []

    def setup_consts(self):
        P, nc = self.P, self.nc
        sb = self.sb
        self.identf = sb("identf", [128, 128], F32)
        self.ident = sb("ident", [128, 128], BF16)
        self.ones = sb("ones", [128, 128], BF16)
        self.negones = sb("negones", [128, 128], BF16)
        self.maskLT = sb("maskLT", [128, 128], BF16)
        self.maskLE = sb("maskLE", [128, 128], F32)
        self.negtri = sb("negtri", [128, 128], BF16)
        tmpf = sb("ctmpf", [128, 128], F32)
        identf, ident = self.identf, self.ident
        P.op("pool", lambda e: e.memset(identf[:], 0.0), writes=["identf"])
        P.op("pool", lambda e: e.affine_select(out=identf[:], in_=identf[:], pattern=[[-1, 128]],
                                               compare_op=ALU.not_equal, fill=1.0, base=0,
                                               channel_multiplier=1), reads=["identf"], writes=["identf"])
        P.op("dve", lambda e: e.tensor_copy(out=ident[:], in_=identf[:]), reads=["identf"], writes=["ident"])
        P.op("pool", lambda e: e.memset(self.ones[:], 1.0), writes=["ones"])
        P.op("pool", lambda e: e.memset(self.negones[:], -1.0), writes=["negones"])
        P.op("pool", lambda e: e.memset(tmpf[:], 1.0), writes=["ctmpf"])
        P.op("pool", lambda e: e.affine_select(out=tmpf[:], in_=tmpf[:], pattern=[[1, 128]],
                                               compare_op=ALU.is_gt, fill=0.0, base=0,
                                               channel_multiplier=-1), reads=["ctmpf"], writes=["ctmpf"])
        P.op("dve", lambda e: e.tensor_copy(out=self.maskLT[:], in_=tmpf[:]), reads=["ctmpf"], writes=["maskLT"])
        P.op("pool", lambda e: e.memset(self.maskLE[:], 1.0), writes=["maskLE"])
        P.op("pool", lambda e: e.affine_select(out=self.maskLE[:], in_=self.maskLE[:], pattern=[[1, 128]],
                                               compare_op=ALU.is_ge, fill=0.0, base=0,
                                               channel_multiplier=-1), reads=["maskLE"], writes=["maskLE"])
        P.op("pool", lambda e: e.memset(tmpf[:], -1.0), reads=["maskLT"], writes=["ctmpf"])
        P.op("pool", lambda e: e.affine_select(out=tmpf[:], in_=tmpf[:], pattern=[[-1, 128]],
                                               compare_op=ALU.is_ge, fill=0.0, base=0,
                                               channel_multiplier=1), reads=["ctmpf"], writes=["ctmpf"])
        P.op("dve", lambda e: e.tensor_copy(out=self.negtri[:], in_=tmpf[:]), reads=["ctmpf"], writes=["negtri"])
        self.negbig = sb("negbig", [128, 128], BF16)
        P.op("pool", lambda e: e.memset(tmpf[:], -30000.0), reads=["negtri"], writes=["ctmpf"])
        P.op("pool", lambda e: e.affine_select(out=tmpf[:], in_=tmpf[:], pattern=[[-1, 128]],
                                               compare_op=ALU.is_ge, fill=0.0, base=0,
                                               channel_multiplier=1), reads=["ctmpf"], writes=["ctmpf"])
        P.op("dve", lambda e: e.tensor_copy(out=self.negbig[:], in_=tmpf[:]), reads=["ctmpf"], writes=["negbig"])
        self.CONST = ["ident", "identf", "ones", "negones", "maskLT", "maskLE", "negtri"]

        self.arena = sb("arena", [128, 40960], BF16)
        self.Wb = self.arena[:, :].rearrange("p (k f) -> p k f", k=8)
        self.Wo = sb("Wo", [128, 12, 1024], BF16)
        self.wstage = self.ring("wst", 2, [128, 1024], F32)
        self.gcol = sb("gcol", [128, 8], F32)
        self.xt = self.ring("xt", 2, [128, 4, 1024], F32)
        self.junk = sb("junk", [128, 1024], BF16)
        self.ss4 = sb("ss4", [128, 4], F32)
        self.rstd4 = sb("rstd4", [128, 4], F32)
        self.hb = self.ring("hb", 2, [128, 1024], BF16)
        self.hT = self.ring("hT", 2, [128, 8, 512], BF16)
        self.psT = [self.ps("psT%d" % i, [128, 1024], BF16) for i in range(1)]
        self.psM = [self.ps("psM%d" % i, [128, 512], F32) for i in range(7)]
        self.psM_i = 0
        self.ev32 = self.ring("ev32", 3, [128, 512], F32)
        self.evbf = self.ring("evbf", 4, [128, 512], BF16)
        self.wk32 = self.ring("wk32", 3, [128, 512], F32)
        self.wkbf = self.ring("wkbf", 3, [128, 512], BF16)

    def ov(self, off, shape, dt):
        n = int(np.prod(shape[1:]))
        esz = 4 if dt == F32 else 2
        a = self.arena[:, off // 2: off // 2 + n * esz // 2]
        if dt == F32:
            a = a.bitcast(F32)
        if len(shape) == 3:
            a = a.rearrange("p (a b) -> p a b", a=shape[1])
        return a

    def psum(self):
        if getattr(self, "ps_restrict", False) and os.environ.get("PSR", "1") == "1":
            self.psR_i = getattr(self, "psR_i", 0) + 1
            if self.psR_i % 2:
                return self.psM[6], ("psM", 6)
            return self.psT[0][:, :].bitcast(F32), "psT"
        i = self.psM_i % len(self.psM)
        self.psM_i += 1
        return self.psM[i], ("psM", i)

    def load_weights(self, w_dram, n_k, n_f, dst, dst_key, gain_dram=None):
        P = self.P
        gcol = self.gcol
        if gain_dram is not None:
            P.dma(lambda e: e.dma_start(out=gcol[:], in_=gain_dram), writes=["gcol"], dkey="gcol")
        CH = 1024
        for f0 in range(0, n_f, CH):
            fw = min(CH, n_f - f0)
            ck = (dst_key, f0 // CH) if dst_key == "Wb" else dst_key
            for kc in range(n_k):
                st, sk = self.nxt(self.wstage)
                P.dma(lambda e, st=st, kc=kc, f0=f0, fw=fw: e.dma_start(
                    out=st[:, :fw], in_=w_dram[kc * 128:(kc + 1) * 128, f0:f0 + fw]), writes=[sk], dkey=sk)
                eng = "act" if (kc % 2 == 0) else "dve"
                first = (kc == 0) if dst_key == "Wb" else (kc == 0 and f0 == 0)
                if gain_dram is not None:
                    if eng == "act":
                        P.op(eng, lambda e, st=st, kc=kc, f0=f0, fw=fw: e.activation(
                            out=dst[:, kc, f0:f0 + fw], in_=st[:, :fw], func=AF.Copy, scale=gcol[:, kc:kc + 1]),
                            reads=[sk, "gcol"], writes=[ck], waw=first)
                    else:
                        P.op(eng, lambda e, st=st, kc=kc, f0=f0, fw=fw: e.tensor_scalar(
                            out=dst[:, kc, f0:f0 + fw], in0=st[:, :fw], scalar1=gcol[:, kc:kc + 1], scalar2=None,
                            op0=ALU.mult), reads=[sk, "gcol"], writes=[ck], waw=first)
                else:
                    if eng == "act":
                        P.op(eng, lambda e, st=st, kc=kc, f0=f0, fw=fw: e.copy(
                            out=dst[:, kc, f0:f0 + fw], in_=st[:, :fw]), reads=[sk], writes=[ck], waw=first)
                    else:
                        P.op(eng, lambda e, st=st, kc=kc, f0=f0, fw=fw: e.tensor_copy(
                            out=dst[:, kc, f0:f0 + fw], in_=st[:, :fw]), reads=[sk], writes=[ck], waw=first)

    def prefetch_x(self, x_dram, xkey, s, tg):
        P = self.P
        xt, xk = self.nxt(self.xt)
        P.dma(lambda e: e.dma_start(out=xt[:], in_=x_dram[s, tg * 512:(tg + 1) * 512, :].rearrange(
            "(j p) d -> p j d", p=128)), reads=[xkey], writes=[xk], dkey=xk)
        if not hasattr(self, "xpref"):
            self.xpref = {}
        self.xpref[(xkey, s, tg)] = (xt, xk)

    def load_norm_group(self, x_dram, xkey, s, tg):
        P = self.P
        if (xkey, s, tg) not in getattr(self, "xpref", {}):
            self.prefetch_x(x_dram, xkey, s, tg)
        xt, xk = self.xpref.pop((xkey, s, tg))
        ss4, rstd4, junk = self.ss4, self.rstd4, self.junk
        for j in range(4):
            P.op("act", lambda e, j=j: e.activation(out=junk[:], in_=xt[:, j, :], func=AF.Square,
                                                    accum_out=ss4[:, j:j + 1]),
                 reads=[xk], writes=["junk", "ss4"])
        P.op("act", lambda e: e.activation(out=rstd4[:], in_=ss4[:], func=AF.Ln, scale=1.0 / 1024, bias=EPS),
             reads=["ss4"], writes=["rstd4"])
        P.op("act", lambda e: e.activation(out=rstd4[:], in_=rstd4[:], func=AF.Exp, scale=-0.5), reads=["rstd4"], writes=["rstd4"])
        hT, hk = self.nxt(self.hT)
        psT = self.psT[0]
        for j in range(4):
            hb, hbk = self.nxt(self.hb)
            P.op("dve", lambda e, j=j, hb=hb: e.tensor_scalar(out=hb[:], in0=xt[:, j, :], scalar1=rstd4[:, j:j + 1],
                                                              scalar2=None, op0=ALU.mult),
                 reads=[xk, "rstd4"], writes=[hbk])
            for kc in range(8):
                P.op("pe", lambda e, kc=kc, hb=hb: e.transpose(out=psT[:, kc * 128:(kc + 1) * 128],
                                                               in_=hb[:, kc * 128:(kc + 1) * 128],
                                                               identity=self.ident[:]),
                     reads=[hbk, "ident"], writes=["psT"], same_ok=True, waw=(kc == 0))
            P.op("act", lambda e, j=j: e.copy(out=hT[:, :, j * 128:(j + 1) * 128],
                                              in_=psT[:].rearrange("p (k t) -> p k t", k=8)),
                 reads=["psT"], writes=[hk], waw=(j == 0))
        return hT, hk, xt, xk

    def proj_fm(self, hT, hk, f0, M=128):
        P = self.P
        Wb = self.Wb
        ps, pk = self.psum()
        for kc in range(8):
            P.op("pe", lambda e, kc=kc: e.matmul(ps[:M, :], lhsT=Wb[:, kc, f0:f0 + M], rhs=hT[:, kc, :],
                                                 start=(kc == 0), stop=(kc == 7)),
                 reads=[hk, ("Wb", f0 // 1024)], writes=[pk], same_ok=True, waw=(kc == 0))
        return ps, pk

    def proj_tm(self, hT, hk, j, f0):
        P = self.P
        Wb = self.Wb
        ps, pk = self.psum()
        for kc in range(8):
            P.op("pe", lambda e, kc=kc: e.matmul(ps[:, :], lhsT=hT[:, kc, j * 128:(j + 1) * 128],
                                                 rhs=Wb[:, kc, f0:f0 + 512], start=(kc == 0), stop=(kc == 7)),
                 reads=[hk, ("Wb", f0 // 1024)], writes=[pk], same_ok=True, waw=(kc == 0))
        return ps, pk

    def store(self, tile_ap, tkey, dram_ap, dkey_dram):
        self.P.dma(lambda e: e.dma_start(out=dram_ap, in_=tile_ap), reads=[tkey], writes=[dkey_dram], dkey=tkey)

    def even_layer(self, i, x_in, kin, x_out, kout):
        P = self.P
        L, NSEQ = self.L, self.NSEQ
        P.barrier()
        self.load_weights(self.even_w_in[i], 8, 5120, self.Wb, "Wb", gain_dram=self.even_norm_g[i])
        self.load_weights(self.even_w_out[i], 12, 1024, self.Wo, "Wo")
        if not hasattr(self, "qkg"):
            self.qkg = self.sb("qkg", [128, 2], F32)
        qkg = self.qkg
        P.dma(lambda e: e.dma_start(out=qkg[:, 0:1], in_=self.sb_q_g[i]), writes=["qkg"], dkey="qkg")
        P.dma(lambda e: e.dma_start(out=qkg[:, 1:2], in_=self.sb_k_g[i]), writes=["qkg"], dkey="qkg")
        P.op("dve", lambda e: e.tensor_scalar(out=qkg[:, 0:1], in0=qkg[:, 0:1], scalar1=128 ** -0.5, scalar2=None,
                                              op0=ALU.mult), reads=["qkg"], writes=["qkg"])
        self.s5_prep(i)
        groups = [(s, tg) for s in range(NSEQ) for tg in range(self.NG)]
        for k, (s, tg) in enumerate(groups):
            if k == 0:
                self.prefetch_x(x_in, kin, s, tg)
            if k + 1 < len(groups):
                self.prefetch_x(x_in, kin, *groups[k + 1])
            self.even_inproj_group(i, x_in, kin, s, tg)
        for s in range(NSEQ):
            for _ in self.sb_attention(s):
                pass
        for _ in self.s5_main(i):
            pass
        for s in range(NSEQ):
            self.out_proj(s, x_in, kin, x_out, kout, 12, self.mixT, "mixT")

    def even_inproj_group(self, i, x_in, kin, s, tg):
        P = self.P
        hT, hk, xt, xk = self.load_norm_group(x_in, kin, s, tg)
        cols = slice(tg * 512, (tg + 1) * 512)
        for which in range(2):
            dst = self.qT if which == 0 else self.kT
            dkey = "qT" if which == 0 else "kT"
            for h in range(8):
                ps, pk = self.proj_fm(hT, hk, which * 1024 + h * 128)
                sq, sqk = self.nxt(self.wkbf)
                P.op("act", lambda e, ps=ps, sq=sq: e.activation(out=sq[:], in_=ps[:], func=AF.Square),
                     reads=[pk], writes=[sqk])
                ps2, pk2 = self.psum()
                P.op("pe", lambda e, ps2=ps2, sq=sq: e.matmul(ps2[:], lhsT=self.ones[:], rhs=sq[:], start=True, stop=True),
                     reads=[sqk, "ones"], writes=[pk2])
                rs, rsk = self.nxt(self.wk32)
                P.op("act", lambda e, ps2=ps2, rs=rs: e.activation(out=rs[:], in_=ps2[:], func=AF.Ln,
                                                                   scale=1.0 / 128, bias=EPS),
                     reads=[pk2], writes=[rsk])
                P.op("act", lambda e, rs=rs: e.activation(out=rs[:], in_=rs[:], func=AF.Exp, scale=-0.5), reads=[rsk], writes=[rsk])
                ob, obk = self.nxt(self.evbf)
                P.op("dve", lambda e, ps=ps, rs=rs, ob=ob, which=which: e.scalar_tensor_tensor(
                    out=ob[:], in0=ps[:], scalar=self.qkg[:, which:which + 1], in1=rs[:], op0=ALU.mult, op1=ALU.mult),
                    reads=[pk, rsk, "qkg"], writes=[obk])
                self.store(ob[:], obk, dst[s, h, :, cols], (dkey, s))
        for j in range(4):
            for half in range(2):
                ps, pk = self.proj_tm(hT, hk, j, 2048 + half * 512)
                ob, obk = self.nxt(self.evbf)
                P.op("act" if half else "dve",
                     (lambda e, ps=ps, ob=ob: e.copy(out=ob[:], in_=ps[:])) if half else
                     (lambda e, ps=ps, ob=ob: e.tensor_copy(out=ob[:], in_=ps[:])),
                     reads=[pk], writes=[obk])
                r0 = tg * 512 + j * 128
                self.store(ob[:], obk, self.vtm[s, r0:r0 + 128, half * 512:(half + 1) * 512], ("vtm", s))
        for c in range(12):
            f0 = 3072 + c * 128 if c < 8 else 4608 + (c - 8) * 128
            ps, pk = self.proj_fm(hT, hk, f0)
            ob, obk = self.nxt(self.ev32)
            P.op("act", lambda e, ps=ps, ob=ob: e.activation(out=ob[:], in_=ps[:], func=AF.Silu),
                 reads=[pk], writes=[obk])
            self.store(ob[:], obk, self.gzT[s, c * 128:(c + 1) * 128, cols], ("gzT", s))
        for c in range(4):
            ps, pk = self.proj_fm(hT, hk, 4096 + c * 128)
            ob, obk = self.nxt(self.evbf)
            P.op("dve", lambda e, ps=ps, ob=ob: e.tensor_copy(out=ob[:], in_=ps[:]), reads=[pk], writes=[obk])
            self.store(ob[:], obk, self.uT[s, c * 128:(c + 1) * 128, cols], ("uT", s))

    def out_proj(self, s, x_in, kin, x_out, kout, n_k, mix_dram, mixkey):
        P = self.P
        if not hasattr(self, "mxin"):
            self.mxin = Ring("mxin", 2)
            self.mxin.tiles = [self.ov(k * 12288, [128, 12, 512], BF16) for k in range(2)]
            self.xres = self.xt
        def issue(tg):
            mx, mk = self.nxt(self.mxin)
            P.dma(lambda e, mx=mx, tg=tg: e.dma_start(
                out=mx[:, :n_k, :], in_=mix_dram[s, :n_k * 128, tg * 512:(tg + 1) * 512].rearrange(
                    "(kc p) t -> p kc t", p=128)), reads=[(mixkey, s)], writes=[mk], dkey=mk)
            xr, xrk = self.nxt(self.xres)
            P.dma(lambda e, xr=xr, tg=tg: e.dma_start(
                out=xr[:], in_=x_in[s, tg * 512:(tg + 1) * 512, :].rearrange("(j p) d -> p j d", p=128)),
                reads=[kin], writes=[xrk], dkey=xrk)
            return mx, mk, xr, xrk

        pend = issue(0)
        for tg in range(self.NG):
            mx, mk, xr, xrk = pend
            if tg + 1 < self.NG:
                pend = issue(tg + 1)
            for j in range(4):
                for half in range(2):
                    ps, pk = self.psum()
                    for kc in range(n_k):
                        P.op("pe", lambda e, ps=ps, kc=kc, j=j, half=half, mx=mx: e.matmul(
                            ps[:], lhsT=mx[:, kc, j * 128:(j + 1) * 128],
                            rhs=self.Wo[:, kc, half * 512:(half + 1) * 512], start=(kc == 0), stop=(kc == n_k - 1)),
                            reads=[mk, "Wo"], writes=[pk], same_ok=True, waw=(kc == 0))
                    P.op("dve", lambda e, ps=ps, j=j, half=half, xr=xr: e.tensor_tensor(
                        out=xr[:, j, half * 512:(half + 1) * 512], in0=ps[:], in1=xr[:, j, half * 512:(half + 1) * 512],
                        op=ALU.add), reads=[pk, xrk], writes=[xrk])
            P.dma(lambda e, xr=xr, tg=tg: e.dma_start(
                out=x_out[s, tg * 512:(tg + 1) * 512, :].rearrange("(j p) d -> p j d", p=128), in_=xr[:]),
                reads=[xrk], writes=[kout], dkey=(xrk, "st"))
            if kout == "y_out":
                if (xrk, "st") not in self.final_keys:
                    self.final_keys.append((xrk, "st"))

    def sb_attention(self, s):
        P = self.P
        L = self.L
        NB = L // 128
        if not OVERLAP:
            P.barrier()
        KB = 1024
        qkv = []
        NQ = 1 if OVERLAP else 2
        for r in range(NQ):
            base = r * 3 * (L * 2)
            qkv.append((self.ov(base, [128, L], BF16), self.ov(base + 2 * L, [128, L], BF16),
                        self.ov(base + 4 * L, [128, NB, 128], BF16)))
        cbase = NQ * 3 * L * 2
        NCH = 3
        chains = []
        for c in range(NCH):
            b = cbase + c * 7 * KB
            chains.append(dict(e32=self.ov(b, [128, 512], F32), Lb=self.ov(b + 2 * KB, [128, 512], BF16),
                               wb=self.ov(b + 3 * KB, [128, 512], BF16), R32=self.ov(b + 4 * KB, [128, 512], F32),
                               Rbf=self.ov(b + 6 * KB, [128, 512], BF16), id=c,
                               psZ=self.psM[2 * c], psZk=("psM", 2 * c), psO=self.psM[2 * c + 1], psOk=("psM", 2 * c + 1)))
        gbase = cbase + NCH * 7 * KB
        gz = [self.ov(gbase + k * 2 * KB, [128, 512], F32) for k in range(NCH)]
        assert gbase + NCH * 2 * KB <= (51 * KB if OVERLAP else 81920)
        def load_head(h):
            qh, kh, vh = qkv[h % NQ]
            kq, kk, kv = ("sbq", h % NQ), ("sbk", h % NQ), ("sbv", h % NQ)
            P.dma(lambda e, qh=qh, h=h: e.dma_start(out=qh, in_=self.qT[s, h]), reads=[("qT", s)], writes=[kq], dkey=kq)
            P.dma(lambda e, kh=kh, h=h: e.dma_start(out=kh, in_=self.kT[s, h]), reads=[("kT", s)], writes=[kk], dkey=kk)
            P.dma(lambda e, vh=vh, h=h: e.dma_start(
                out=vh, in_=self.vtm[s, :, h * 128:(h + 1) * 128].rearrange("(b p) d -> p b d", p=128)),
                reads=[("vtm", s)], writes=[kv], dkey=kv)

        load_head(0)
        for h in range(8):
            qh, kh, vh = qkv[h % NQ]
            kq, kk, kv = ("sbq", h % NQ), ("sbk", h % NQ), ("sbv", h % NQ)
            if h + 1 < 8 and NQ == 2:
                load_head(h + 1)
            elif h > 0 and NQ == 1:
                load_head(h)

            def chain(qg, ch, h=h, qh=qh, kh=kh, vh=vh, kq=kq, kk=kk, kv=kv):
                c = ch["id"]
                tag = lambda n: ("sbc", n, c)
                e32, Lb, wb, R32, Rbf = ch["e32"], ch["Lb"], ch["wb"], ch["R32"], ch["Rbf"]
                psZ, psZk, psO, psOk = ch["psZ"], ch["psZk"], ch["psO"], ch["psOk"]
                g = gz[c]
                P.dma(lambda e: e.dma_start(out=g, in_=self.gzT[s, h * 128:(h + 1) * 128, qg * 512:(qg + 1) * 512]),
                      reads=[("gzT", s)], writes=[tag("gz")], dkey=tag("gz"))
                P.op("pool", lambda e: e.memset(R32, 0.0), writes=[tag("R32")])
                kbs = list(reversed(range(4 * qg + 4)))

                def zmm(kb):
                    c0 = max(0, kb - 4 * qg) * 128
                    q0 = qg * 512 + c0
                    P.op("pe", lambda e: e.matmul(
                        psZ[:, c0:512], lhsT=kh[:, kb * 128:(kb + 1) * 128], rhs=qh[:, q0:qg * 512 + 512],
                        start=True, stop=False, skip_group_check=True), reads=[kq, kk], writes=[psZk])
                    if kb >= 4 * qg:
                        P.op("pe", lambda e: e.matmul(
                            psZ[:, c0:c0 + 128], lhsT=self.ident[:], rhs=self.negbig[:], start=False, stop=False,
                            skip_group_check=True), reads=["ident", "negbig"], writes=[psZk], waw=False)

                zmm(kbs[0])
                for idx, kb in enumerate(kbs):
                    c0 = max(0, kb - 4 * qg) * 128
                    N = 512 - c0
                    diag = kb >= 4 * qg
                    P.op("act", lambda e, c0=c0: e.activation(out=e32[:, c0:512], in_=psZ[:, c0:512], func=AF.Exp),
                         reads=[psZk], writes=[tag("e32")])
                    yield
                    P.op("act", lambda e, c0=c0: e.activation(out=Lb[:, c0:512], in_=e32[:, c0:512], func=AF.Ln, bias=1.0),
                         reads=[tag("e32")], writes=[tag("Lb")])
                    yield
                    P.op("pe", lambda e, c0=c0, idx=idx: e.matmul(
                        psZ[:, c0:512], lhsT=self.negtri[:], rhs=Lb[:, c0:512], start=False, stop=(idx == 0),
                        skip_group_check=True), reads=[tag("Lb"), "negtri"], writes=[psZk], same_ok=True, waw=False)
                    if idx > 0:
                        P.op("pe", lambda e, c0=c0: e.matmul(
                            psZ[:, c0:512], lhsT=self.negones[:], rhs=Rbf[:, c0:512], start=False, stop=True,
                            skip_group_check=True), reads=[tag("Rbf"), "negones"], writes=[psZk], same_ok=True, waw=False)
                    P.op("act", lambda e, c0=c0: e.activation(out=wb[:, c0:512], in_=psZ[:, c0:512], func=AF.Exp),
                         reads=[psZk], writes=[tag("wb")])
                    yield
                    if idx + 1 < len(kbs):
                        zmm(kbs[idx + 1])
                    P.op("pe", lambda e, kb=kb, c0=c0, idx=idx: e.matmul(
                        psO[:, c0:512], lhsT=vh[:, kb, :], rhs=wb[:, c0:512], start=(idx == 0), stop=(idx == len(kbs) - 1),
                        skip_group_check=True), reads=[tag("wb"), kv], writes=[psOk], same_ok=True, waw=(idx == 0))
                    for _f in range(N_FILL):
                        P.op("pe", lambda e: e.matmul(self.psM[6][:, :], lhsT=self.ones[:], rhs=qh[:, 0:512], start=True,
                                                      stop=True, skip_group_check=True), reads=[], writes=[("psM", 6)])
                    if idx < len(kbs) - 1:
                        c1 = max(0, kbs[idx + 1] - 4 * qg) * 128
                        P.op("pool", lambda e, c0=c0: e.tensor_tensor(out=R32[:, c0:512], in0=R32[:, c0:512],
                                                                      in1=Lb[:, c0:512], op=ALU.add),
                             reads=[tag("Lb"), tag("R32")], writes=[tag("R32")])
                        P.op("dve", lambda e, c1=c1: e.tensor_copy(out=Rbf[:, c1:512], in_=R32[:, c1:512]),
                             reads=[tag("R32")], writes=[tag("Rbf")])
                    yield
                ob, obk = self.nxt(self.evbf)
                P.op("dve", lambda e, ob=ob: e.tensor_tensor(out=ob[:], in0=psO[:], in1=g, op=ALU.mult),
                     reads=[psOk, tag("gz")], writes=[obk])
                self.store(ob[:], obk, self.mixT[s, h * 128:(h + 1) * 128, qg * 512:(qg + 1) * 512], ("mixT", s))
                yield

            def head_gen(c):
                for qg in range(self.NG - 1 - c, -1, -NCH):
                    yield from chain(qg, chains[c])
            yield from interleave([head_gen(c) for c in range(NCH)])

    def s5_prep(self, i):
        pass

    def s5_consts(self):
        if hasattr(self, "Jm"):
            return
        P, sb = self.P, self.sb
        self.Jm = sb("Jm", [128, 128], F32)
        self.nJm = sb("nJm", [128, 128], F32)
        self.bd = sb("bdmask", [128, 128], F32)
        self.Em = sb("Emat", [8, 128], F32)
        self.rowm = sb("rowm", [128, 2], F32)
        self.sgn = sb("sgn", [128, 1], F32)
        self.s5sm = sb("s5sm", [128, 12], F32)
        Jm, nJm, bd, Em, rowm, sgn = self.Jm, self.nJm, self.bd, self.Em, self.rowm, self.sgn
        P.op("pool", lambda e: e.memset(Jm[:], 0.0), writes=["Jm"])
        P.op("pool", lambda e: e.affine_select(out=Jm[:], in_=Jm[:], pattern=[[-1, 128]], compare_op=ALU.not_equal,
                                               fill=-1.0, base=64, channel_multiplier=1), reads=["Jm"], writes=["Jm"])
        P.op("pool", lambda e: e.affine_select(out=Jm[:], in_=Jm[:], pattern=[[-1, 128]], compare_op=ALU.not_equal,
                                               fill=1.0, base=-64, channel_multiplier=1), reads=["Jm"], writes=["Jm"])
        P.op("dve", lambda e: e.tensor_scalar(out=nJm[:], in0=Jm[:], scalar1=-1.0, scalar2=None, op0=ALU.mult),
             reads=["Jm"], writes=["nJm"])
        P.op("pool", lambda e: e.memset(Em[:], 1.0), writes=["Em"])
        P.op("pool", lambda e: e.affine_select(out=Em[:], in_=Em[:], pattern=[[1, 128]], compare_op=ALU.is_ge,
                                               fill=0.0, base=0, channel_multiplier=-16), reads=["Em"], writes=["Em"])
        P.op("pool", lambda e: e.affine_select(out=Em[:], in_=Em[:], pattern=[[-1, 128]], compare_op=ALU.is_ge,
                                               fill=0.0, base=15, channel_multiplier=16), reads=["Em"], writes=["Em"])
        ps, pk = self.psum()
        P.op("pe", lambda e: e.matmul(ps[:, 0:128], lhsT=Em[:], rhs=Em[:], start=True, stop=True), reads=["Em"], writes=[pk])
        P.op("dve", lambda e: e.tensor_copy(out=bd[:], in_=ps[:, 0:128]), reads=[pk], writes=["bd"])
        P.op("dve", lambda e: e.tensor_reduce(out=rowm[:], in_=bd[:].rearrange("p (q m w) -> p m q w", q=4, m=2, w=16),
                                              axis=AX.XY, op=ALU.add), reads=["bd"], writes=["rowm"])
        P.op("dve", lambda e: e.tensor_scalar(out=rowm[:], in0=rowm[:], scalar1=1.0 / 16, scalar2=None, op0=ALU.mult),
             reads=["rowm"], writes=["rowm"])
        self.Ecol = sb("Ecol", [128, 8], F32)
        ps2, pk2 = self.psum()
        P.op("pe", lambda e: e.transpose(out=ps2[:, 0:8], in_=Em[:], identity=self.identf[0:8, 0:8]), reads=["Em", "identf"], writes=[pk2])
        P.op("dve", lambda e: e.tensor_copy(out=self.Ecol[:], in_=ps2[:, 0:8]), reads=[pk2], writes=["Ecol"])
        self.gml = Ring("gml", 2)
        self.gml.tiles = [t[:, :].rearrange("p (m k) -> p m k", m=8) for t in self.hb.tiles]
        P.op("pool", lambda e: e.memset(sgn[0:64, :], 1.0), writes=["sgn"])
        P.op("pool", lambda e: e.memset(sgn[64:128, :], -1.0), writes=["sgn"], waw=False)

    def s5_main(self, i):
        P = self.P
        L, NSEQ = self.L, self.NSEQ
        NCH = L // 8
        self.s5_consts()
        P.barrier()
        KB = 1024
        ov = self.ov
        SM = ov(0, [128, 16, 32], F32)
        LAM = ov(2 * KB, [128, 2, 32], F32)
        MTr = [ov(2 * KB + 512 + k * 512, [128, 128], F32) for k in range(3)]
        AT = ov(4 * KB, [128, 32, 128], F32)
        Am = ov(20 * KB, [128, 32, 128], F32)
        Wsb = ov(36 * KB, [128, 8, 512], F32)
        Vsb = ov(52 * KB, [128, 9, 512], F32)
        PWre = ov(70 * KB, [128, 17, 32], F32)
        PWim = ov(70 * KB + 2176, [128, 17, 32], F32)
        B0 = self.ev32.tiles[0]
        C0 = self.ev32.tiles[1]
        Vpad = self.xt.tiles[0][:, :, :].rearrange("p a b -> p (a b)").bitcast(BF16).rearrange(
            "p (m g c) -> p m g c", m=8, g=32)
        WTf = self.xt.tiles[1][:, :, :].rearrange("p a b -> p (a b)").bitcast(BF16)[:, 0:4096].rearrange(
            "p (t j k) -> p t j k", t=4, j=8)
        Kblk = self.hT.tiles[0][:, :, :].rearrange("p a b -> p (a b)").rearrange("p (t j k) -> p t j k", t=4, j=8)
        WG = self.hT.tiles[1][:, 0:4, :]
        s5sm = self.s5sm
        identf, Jm, nJm = self.identf, self.Jm, self.nJm
        sm = lambda k: SM[:, k, :]
        K = lambda n: ("s5", n)

        P.dma(lambda e: e.dma_start(out=LAM, in_=self.s5_lam[i]), writes=[K("LAM")], dkey=K("LAM"))
        P.dma(lambda e: e.dma_start(out=sm(0), in_=self.s5_dt[i]), writes=[K("sm0")], dkey=K("sm0"))
        P.dma(lambda e: e.dma_start(out=B0[:].rearrange("p (g c) -> p g c", g=32), in_=self.s5_b0[i]),
              writes=[("ev32", 0)], dkey=("ev32", 0))
        P.dma(lambda e: e.dma_start(out=C0[:].rearrange("p (g c) -> p g c", g=32), in_=self.s5_c0[i]),
              writes=[("ev32", 1)], dkey=("ev32", 1))
        P.dma(lambda e: e.dma_start(out=s5sm[:, 0:4], in_=self.s5_d[i]), writes=["s5sm"], dkey="s5sm")
        P.dma(lambda e: e.dma_start(out=s5sm[:, 4:8], in_=self.s5_b_glu[i]), writes=["s5sm"], dkey="s5sm")
        for kc in range(4):
            st, sk = self.nxt(self.wstage)
            P.dma(lambda e, st=st, kc=kc: e.dma_start(out=st[:, :512], in_=self.s5_w_glu[i, kc * 128:(kc + 1) * 128, :]),
                  writes=[sk], dkey=sk)
            P.op("dve", lambda e, st=st, kc=kc: e.tensor_copy(out=WG[:, kc, :], in_=st[:, :512]), reads=[sk],
                 writes=[K("WG")], waw=(kc == 0))
        lre, lim = LAM[:, 0, :], LAM[:, 1, :]
        kl = K("LAM")

        def tt(eng, out, a, b, op, rk, wk):
            P.op(eng, lambda e: e.tensor_tensor(out=out, in0=a, in1=b, op=op), reads=rk, writes=wk)

        def ts(eng, out, a, s1, s2, op0, op1, rk, wk):
            if op1 is None:
                P.op(eng, lambda e: e.tensor_scalar(out=out, in0=a, scalar1=s1, scalar2=None, op0=op0), reads=rk, writes=wk)
            else:
                P.op(eng, lambda e: e.tensor_scalar(out=out, in0=a, scalar1=s1, scalar2=s2, op0=op0, op1=op1),
                     reads=rk, writes=wk)

        S = K("SM")
        P.op("act", lambda e: e.activation(out=sm(0), in_=sm(0), func=AF.Exp), reads=[K("sm0")], writes=[S])
        tt("dve", sm(1), lre, sm(0), ALU.mult, [kl, S], [S])
        P.op("act", lambda e: e.activation(out=sm(1), in_=sm(1), func=AF.Exp), reads=[S], writes=[S])
        tt("dve", sm(2), lim, sm(0), ALU.mult, [kl, S], [S])
        ts("dve", sm(3), sm(2), math.pi / 2, None, ALU.add, None, [S], [S])
        for src in (2, 3):
            ts("dve", sm(6), sm(src), 0.0, None, ALU.mult, None, [S], [S])
            for j in range(6):
                ts("dve", sm(5), sm(src), (2 * j + 1) * math.pi, -2 * math.pi, ALU.is_ge, ALU.mult, [S], [S])
                tt("dve", sm(6), sm(6), sm(5), ALU.add, [S], [S])
            tt("dve", sm(src), sm(src), sm(6), ALU.add, [S], [S])
        P.op("act", lambda e: e.activation(out=sm(7), in_=sm(2), func=AF.Sin), reads=[S], writes=[S])
        P.op("act", lambda e: e.activation(out=sm(8), in_=sm(3), func=AF.Sin), reads=[S], writes=[S])
        PK = K("PW")
        tt("dve", PWre[:, 1, :], sm(1), sm(8), ALU.mult, [S], [PK])
        tt("dve", PWim[:, 1, :], sm(1), sm(7), ALU.mult, [S], [PK])
        are, aim = PWre[:, 1, :], PWim[:, 1, :]
        tt("dve", sm(9), lre, lre, ALU.mult, [kl], [S])
        tt("dve", sm(10), lim, lim, ALU.mult, [kl], [S])
        tt("dve", sm(9), sm(9), sm(10), ALU.add, [S], [S])
        P.op("dve", lambda e: e.reciprocal(out=sm(10), in_=sm(9)), reads=[S], writes=[S])
        ts("dve", sm(11), are, -1.0, None, ALU.add, None, [PK], [S])
        tt("dve", sm(12), sm(11), lre, ALU.mult, [S, kl], [S])
        tt("dve", sm(13), aim, lim, ALU.mult, [PK, kl], [S])
        tt("dve", sm(12), sm(12), sm(13), ALU.add, [S], [S])
        tt("dve", PWre[:, 0, :], sm(12), sm(10), ALU.mult, [S], [PK])
        tt("dve", sm(12), aim, lre, ALU.mult, [PK, kl], [S])
        tt("dve", sm(13), sm(11), lim, ALU.mult, [S, kl], [S])
        tt("dve", sm(12), sm(12), sm(13), ALU.subtract, [S], [S])
        tt("dve", PWim[:, 0, :], sm(12), sm(10), ALU.mult, [S], [PK])

        def cmul(dst, a, b):
            tt("dve", sm(12), PWre[:, a, :], PWre[:, b, :], ALU.mult, [PK], [S])
            tt("dve", sm(13), PWim[:, a, :], PWim[:, b, :], ALU.mult, [PK], [S])
            tt("dve", sm(14), PWre[:, a, :], PWim[:, b, :], ALU.mult, [PK], [S])
            tt("dve", sm(15), PWim[:, a, :], PWre[:, b, :], ALU.mult, [PK], [S])
            tt("dve", PWre[:, dst, :], sm(12), sm(13), ALU.subtract, [S], [PK])
            tt("dve", PWim[:, dst, :], sm(14), sm(15), ALU.add, [S], [PK])

        for k in range(2, 9):
            cmul(k, k - 1, 1)
        for k in range(9, 17):
            cmul(k, k - 1, k - 1)

        if S5_STOP == 1:
            return
        def build_M(out, pidx, g, transpose, eng2="dve", rk=(), wk=()):
            P.op("act", lambda e: e.activation(out=out, in_=identf[:], func=AF.Copy, scale=PWre[:, pidx, g:g + 1]),
                 reads=[PK, "identf"] + list(rk), writes=list(wk))
            Jx = nJm if transpose else Jm
            P.op("dve", lambda e: e.scalar_tensor_tensor(out=out, in0=Jx[:], scalar=PWim[:, pidx, g:g + 1], in1=out,
                                                         op0=ALU.mult, op1=ALU.add),
                 reads=[PK, "Jm", "nJm"] + list(wk), writes=list(wk))

        for g in range(32):
            build_M(AT[:, g, :], 1, g, True, wk=[K("AT")])
            build_M(Am[:, g, :], 1, g, False, wk=[K("A")])
        if S5_STOP == 2:
            return
        psW, pkW = self.psum()
        mi = 0
        for g in range(32):
            mt = MTr[mi % 3]
            mk = K(("MT", mi % 3))
            mi += 1
            build_M(mt, 0, g, True, wk=[mk])
            P.op("pe", lambda e, g=g, mt=mt: e.matmul(psW[:, g * 16:(g + 1) * 16], lhsT=mt, rhs=B0[:, g * 16:(g + 1) * 16],
                                                      start=True, stop=True, skip_group_check=True),
                 reads=[mk, ("ev32", 0)], writes=[pkW], waw=(g == 0))
        P.op("act", lambda e: e.copy(out=Wsb[:, 0, :], in_=psW[:]), reads=[pkW], writes=[K("W0")])
        for j in range(1, 8):
            psW, pkW = self.psum()
            for g in range(32):
                P.op("pe", lambda e, g=g, j=j, psW=psW: e.matmul(
                    psW[:, g * 16:(g + 1) * 16], lhsT=AT[:, g, :], rhs=Wsb[:, j - 1, g * 16:(g + 1) * 16],
                    start=True, stop=True, skip_group_check=True), reads=[K("AT"), K("W%d" % (j - 1))], writes=[pkW],
                    waw=(g == 0))
            P.op("act", lambda e, j=j, psW=psW: e.copy(out=Wsb[:, j, :], in_=psW[:]), reads=[pkW], writes=[K("W%d" % j)])
        if S5_STOP == 3:
            return
        P.op("dve", lambda e: e.tensor_scalar(out=Vsb[:, 0, :], in0=C0[:], scalar1=self.sgn[:, 0:1], scalar2=None,
                                              op0=ALU.mult), reads=[("ev32", 1), "sgn"], writes=[K("V0")])
        for j in range(1, 9):
            psV, pkV = self.psum()
            for g in range(32):
                P.op("pe", lambda e, g=g, j=j, psV=psV: e.matmul(
                    psV[:, g * 16:(g + 1) * 16], lhsT=Am[:, g, :], rhs=Vsb[:, j - 1, g * 16:(g + 1) * 16],
                    start=True, stop=True, skip_group_check=True), reads=[K("A"), K("V%d" % (j - 1))], writes=[pkV],
                    waw=(g == 0))
            P.op("act", lambda e, j=j, psV=psV: e.copy(out=Vsb[:, j, :], in_=psV[:]), reads=[pkV], writes=[K("V%d" % j)])
        if S5_STOP == 4:
            return
        for T in range(4):
            for tau in range(8):
                ps, pk = self.psum()
                P.op("pe", lambda e, T=T, tau=tau, ps=ps: e.matmul(
                    ps[:, 0:128], lhsT=Wsb[:, tau, T * 128:(T + 1) * 128], rhs=Vsb[:, 0, T * 128:(T + 1) * 128],
                    start=True, stop=True), reads=[K("W%d" % tau), K("V0")], writes=[pk])
                if tau == 0:
                    tmp, tk = self.nxt(self.wk32)
                    P.op("dve", lambda e, ps=ps, tmp=tmp: e.tensor_tensor(out=tmp[:, 0:128], in0=ps[:, 0:128], in1=self.bd[:],
                                                                          op=ALU.mult), reads=[pk, "bd"], writes=[tk])
                    P.op("dve", lambda e, T=T, tmp=tmp: e.scalar_tensor_tensor(
                        out=Kblk[:, T, 0, :], in0=identf[:], scalar=s5sm[:, T:T + 1], in1=tmp[:, 0:128], op0=ALU.mult,
                        op1=ALU.add), reads=[tk, "s5sm", "identf"], writes=[K("Kblk")], waw=False)
                else:
                    P.op("dve", lambda e, T=T, tau=tau, ps=ps: e.tensor_tensor(
                        out=Kblk[:, T, tau, :], in0=ps[:, 0:128], in1=self.bd[:], op=ALU.mult), reads=[pk, "bd"],
                        writes=[K("Kblk")], waw=False)
        if S5_STOP == 5:
            return
        for T in range(4):
            for j in range(8):
                ps, pk = self.psum()
                P.op("pe", lambda e, T=T, j=j, ps=ps: e.transpose(out=ps[:, 0:128], in_=Wsb[:, j, T * 128:(T + 1) * 128],
                                                                  identity=identf[:]),
                     reads=[K("W%d" % j), "identf"], writes=[pk])
                P.op("act", lambda e, T=T, j=j, ps=ps: e.copy(out=WTf[:, T, j, :], in_=ps[:, 0:128]),
                     reads=[pk], writes=[K("WTm")], waw=False)
        if S5_STOP == 6:
            return
        P.op("pool", lambda e: e.memset(Vpad.rearrange("p m g c -> p (m g c)"), 0.0), writes=[K("Vpad")])
        for m in range(8):
            for mem in range(2):
                P.op("dve" if mem else "pool", lambda e, m=m, mem=mem: e.tensor_copy(
                    out=Vpad[:, m, mem::2, 16 * mem:16 * mem + 16],
                    in_=Vsb[:, m + 1, :].rearrange("p (g c) -> p g c", g=32)[:, mem::2, :]),
                    reads=[K("V%d" % (m + 1)), K("Vpad")], writes=[K("Vpad")], waw=False)

        if S5_STOP == 7:
            return
        if os.environ.get("DBG_S5") == "1":
            dbg = {"dPWre": (PWre, [128, 17, 32]), "dPWim": (PWim, [128, 17, 32]), "dWsb": (Wsb, [128, 8, 512]),
                   "dVsb": (Vsb, [128, 9, 512])}
            P.barrier()
            for nm, (ap_, shp) in dbg.items():
                dt_ = self.nc.dram_tensor(nm, shp, F32, kind="ExternalOutput").ap()
                P.dma(lambda e, ap_=ap_, dt_=dt_: e.dma_start(out=dt_, in_=ap_), writes=[("dbg", nm)], dkey=("dbg", nm))
            dk = self.nc.dram_tensor("dKblk", [128, 4, 8, 128], BF16, kind="ExternalOutput").ap()
            P.dma(lambda e: e.dma_start(out=dk, in_=Kblk), writes=[("dbg", "k")], dkey=("dbg", "k"))
            P.barrier()
        yield
        Wof = self.Wo[:, :, :].rearrange("p a b -> p (a b)")

        def wov(off, shape, dt):
            n = int(np.prod(shape[1:]))
            esz = 4 if dt == F32 else 2
            a = Wof[:, off // 2: off // 2 + n * esz // 2]
            if dt == F32:
                a = a.bitcast(F32)
            if len(shape) == 3:
                a = a.rearrange("p (a b) -> p a b", a=shape[1])
            return a

        P.barrier()
        U = ov(0, [128, L], BF16)
        Hprev = ov(2 * L, [128, 8, NCH], BF16)
        b1 = 2 * L + 16 * NCH
        ytile = ov(b1, [128, L], F32)
        cH32 = [ov(b1 + 4 * L + k * 2 * KB, [128, 512], F32) for k in range(8)]
        cHb = [ov(b1 + 4 * L + 16 * KB + k * KB, [128, 512], BF16) for k in range(8)]
        assert b1 + 4 * L + 24 * KB <= 70 * KB
        y2gs = [ov(b1, [128, 4, 512], F32), ov(b1 + 4 * L + 24 * KB, [128, 4, 512], F32)]
        y2bfs = [ov(b1 + 8 * KB, [128, 4, 512], BF16), ov(b1 + 4 * L + 32 * KB, [128, 4, 512], BF16)]
        assert b1 + 4 * L + 36 * KB <= 70 * KB
        if not hasattr(self, "y2T"):
            self.y2T = self.dram("y2T", [NSEQ, 512, L], F32)
        nsteps = int(math.log2(NCH))
        xt1b = self.xt.tiles[1][:, :, :].rearrange("p a b -> p (a b)").bitcast(BF16)[:, 4096:8192].rearrange(
            "p (c k) -> p c k", c=32)
        cgm = [[xt1b[:, ci * 4 + r, :] for r in range(4)] for ci in range(8)]
        hT1b = self.hT.tiles[1][:, 4:8, :].rearrange("p a b -> p (a b)").rearrange("p (c k) -> p c k", c=16)
        cMT = [[hT1b[:, ci * 2 + r, :] for r in range(2)] for ci in range(8)]
        YT = K("ytile")
        for s in range(NSEQ):
            for T in range(4):
                P.dma(lambda e, s=s, T=T: e.dma_start(out=U, in_=self.uT[s, T * 128:(T + 1) * 128, :]),
                      reads=[("uT", s)], writes=[K("U")], dkey=K("U"))

                def gchain(g8, ci, T=T, s=s):
                    g = 8 * T + g8
                    H32, hk32 = cH32[ci], K(("cH32", ci))
                    Hb, hkb = cHb[ci], K(("cHb", ci))
                    psG, pkG = self.psum()
                    for m in range(8):
                        gmt = cgm[ci][m % 4]
                        gmk = K(("cgm", ci, m % 4))
                        if m % 2:
                            P.op("act", lambda e, m=m, gmt=gmt: e.activation(
                                out=gmt, in_=WTf[:, T, 7 - m, :], func=AF.Copy, scale=self.Ecol[:, g8:g8 + 1]),
                                reads=[K("WTm"), "Ecol"], writes=[gmk])
                        else:
                            P.op("dve", lambda e, m=m, gmt=gmt: e.tensor_scalar(
                                out=gmt, in0=WTf[:, T, 7 - m, :], scalar1=self.Ecol[:, g8:g8 + 1], scalar2=None,
                                op0=ALU.mult), reads=[K("WTm"), "Ecol"], writes=[gmk])
                        P.op("pe", lambda e, m=m, gmt=gmt: e.matmul(
                            psG[:, :NCH], lhsT=gmt, rhs=U[:, m::8], start=(m == 0), stop=(m == 7)),
                            reads=[gmk, K("U")], writes=[pkG], waw=(m == 0))
                    P.op("act", lambda e: e.copy(out=H32[:, :NCH], in_=psG[:, :NCH]), reads=[pkG], writes=[hk32])
                    P.op("dve", lambda e: e.tensor_copy(out=Hb[:, :NCH], in_=psG[:, :NCH]), reads=[pkG], writes=[hkb])
                    yield
                    for j in range(nsteps):
                        sft = 1 << j
                        mt = cMT[ci][j % 2]
                        mk = K(("cMT", ci, j % 2))
                        build_M(mt, 8 + j, g, True, wk=[mk])
                        yield
                        psS, pkS = self.psum()
                        P.op("pe", lambda e, mt=mt, sft=sft, psS=psS: e.matmul(
                            psS[:, :NCH - sft], lhsT=mt, rhs=Hb[:, 0:NCH - sft], start=True, stop=True),
                            reads=[mk, hkb], writes=[pkS])
                        P.op("dve", lambda e, psS=psS, sft=sft: e.tensor_tensor(
                            out=H32[:, sft:NCH], in0=H32[:, sft:NCH], in1=psS[:, :NCH - sft], op=ALU.add),
                            reads=[hk32, pkS], writes=[hk32])
                        yield
                        if j < nsteps - 1:
                            P.op("act", lambda e, sft=sft: e.copy(out=Hb[:, sft:NCH], in_=H32[:, sft:NCH]),
                                 reads=[hk32], writes=[hkb])
                            yield
                    P.op("act", lambda e: e.copy(out=Hprev[:, g8, :], in_=H32[:, 0:NCH]),
                         reads=[hk32], writes=[K(("Hp", g8))])
                    yield

                yield from interleave([gchain(ci, ci) for ci in range(8)])
                for m in range(8):
                    psY, pkY = self.psum()
                    for tau in range(m + 1):
                        P.op("pe", lambda e, T=T, tau=tau, m=m, psY=psY: e.matmul(
                            psY[:, :NCH], lhsT=Kblk[:, T, tau, :], rhs=U[:, (m - tau)::8], start=(tau == 0), stop=False,
                            skip_group_check=True), reads=[K("Kblk"), K("U")], writes=[pkY], waw=(tau == 0))
                    for g8 in range(8):
                        q = g8 // 2
                        P.op("pe", lambda e, T=T, g8=g8, q=q, m=m, psY=psY: e.matmul(
                            psY[32 * q:32 * q + 32, 1:NCH], lhsT=Vpad[:, m, 8 * T + g8, :], rhs=Hprev[:, g8, 0:NCH - 1],
                            start=False, stop=(g8 == 7), tile_position=(0, 32 * q), skip_group_check=True),
                            reads=[K("Vpad"), K(("Hp", g8))], writes=[pkY], waw=False)
                    P.op("act", lambda e, m=m, psY=psY: e.copy(out=ytile[:, m::8], in_=psY[:, :NCH]), reads=[pkY],
                         writes=[YT], waw=(m == 0))
                    yield
                for c in range(L // 512):
                    cs = slice(c * 512, (c + 1) * 512)
                    t1, t1k = self.nxt(self.wk32)
                    P.op("dve", lambda e, t1=t1, cs=cs: e.tensor_tensor(out=t1[:], in0=ytile[:, cs], in1=ytile[:, cs], op=ALU.mult),
                         reads=[YT], writes=[t1k])
                    P.op("pool", lambda e, t1=t1: e.tensor_scalar(out=t1[:], in0=t1[:], scalar1=0.044715, scalar2=1.0,
                                                                  op0=ALU.mult, op1=ALU.add), reads=[t1k], writes=[t1k])
                    yield
                    P.op("dve", lambda e, t1=t1, cs=cs: e.tensor_tensor(out=t1[:], in0=t1[:], in1=ytile[:, cs], op=ALU.mult),
                         reads=[t1k, YT], writes=[t1k])
                    P.op("act", lambda e, t1=t1: e.activation(out=t1[:], in_=t1[:], func=AF.Sigmoid, scale=1.5957691216),
                         reads=[t1k], writes=[t1k])
                    yield
                    ob, obk = self.nxt(self.ev32)
                    P.op("dve", lambda e, t1=t1, ob=ob, cs=cs: e.tensor_tensor(out=ob[:], in0=t1[:], in1=ytile[:, cs], op=ALU.mult),
                         reads=[t1k, YT], writes=[obk])
                    self.store(ob[:], obk, self.y2T[s, T * 128:(T + 1) * 128, cs], ("y2T", s))
                    yield
            for tg in range(self.NG):
                cs = slice(tg * 512, (tg + 1) * 512)
                yg = y2gs[tg % 2]
                y2bf = y2bfs[tg % 2]
                ygk = K(("y2g", tg % 2))
                ybk = K(("y2bf", tg % 2))
                P.dma(lambda e, yg=yg, cs=cs, s=s: e.dma_start(out=yg, in_=self.y2T[s, :, cs].rearrange("(t p) l -> p t l", p=128)),
                      reads=[("y2T", s)], writes=[ygk, YT], dkey=ygk)
                P.op("act", lambda e, yg=yg, y2bf=y2bf: e.copy(out=y2bf[:, 0:2, :], in_=yg[:, 0:2, :]), reads=[ygk, YT], writes=[ybk])
                P.op("dve", lambda e, yg=yg, y2bf=y2bf: e.tensor_copy(out=y2bf[:, 2:4, :], in_=yg[:, 2:4, :]), reads=[ygk, YT],
                     writes=[ybk], waw=False)
                yield
                for oc in range(4):
                    ps, pk = self.psum()
                    for kc in range(4):
                        P.op("pe", lambda e, ps=ps, kc=kc, oc=oc, y2bf=y2bf: e.matmul(
                            ps[:], lhsT=WG[:, kc, oc * 128:(oc + 1) * 128], rhs=y2bf[:, kc, :], start=(kc == 0), stop=(kc == 3)),
                            reads=[K("WG"), ybk, YT], writes=[pk], waw=(kc == 0))
                    sg, sgk = self.nxt(self.wk32)
                    P.op("act", lambda e, ps=ps, sg=sg, oc=oc: e.activation(out=sg[:], in_=ps[:], func=AF.Sigmoid,
                                                                            bias=s5sm[:, 4 + oc:5 + oc]),
                         reads=[pk, "s5sm"], writes=[sgk])
                    gzt, gzk = self.nxt(self.ev32)
                    P.dma(lambda e, gzt=gzt, oc=oc, cs=cs, s=s: e.dma_start(out=gzt[:], in_=self.gzT[s, 1024 + oc * 128:1024 + (oc + 1) * 128, cs]),
                          reads=[("gzT", s)], writes=[gzk], dkey=(gzk, "ld"))
                    P.op("dve", lambda e, sg=sg, yg=yg, oc=oc: e.tensor_tensor(out=sg[:], in0=sg[:], in1=yg[:, oc, :], op=ALU.mult),
                         reads=[sgk, ygk, YT], writes=[sgk])
                    ob, obk = self.nxt(self.evbf)
                    P.op("dve", lambda e, sg=sg, gzt=gzt, ob=ob: e.tensor_tensor(out=ob[:], in0=sg[:], in1=gzt[:], op=ALU.mult),
                         reads=[sgk, gzk], writes=[obk])
                    self.store(ob[:], obk, self.mixT[s, 1024 + oc * 128:1024 + (oc + 1) * 128, cs], ("mixT", s))
                    yield

    def odd_layer(self, i, x_in, kin, x_out, kout):
        P = self.P
        L, NSEQ = self.L, self.NSEQ
        KB = 1024
        P.barrier()
        full_Wb = self.Wb
        self.Wb = self.arena[:, 0:8 * 3088].rearrange("p (k f) -> p k f", k=8)
        self.load_weights(self.odd_w_in[i], 8, 3088, self.Wb, "Wb", gain_dram=self.odd_norm_g[i])
        self.load_weights(self.odd_w_out[i], 8, 1024, self.Wo, "Wo")
        self.s5_consts()
        sm = self.s5sm
        wg16 = self.ov(50 * KB, [128, 512], BF16)[0:16, :]
        st, sk = self.nxt(self.wstage)
        P.dma(lambda e: e.dma_start(out=st[0:16, 0:512], in_=self.gla_w_gate[i]), writes=[sk], dkey=sk)
        P.op("dve", lambda e: e.tensor_copy(out=wg16, in_=st[0:16, 0:512]), reads=[sk], writes=["wg16"])
        P.dma(lambda e: e.dma_start(out=sm[:, 0:4], in_=self.gla_b_gate[i]), writes=["s5sm"], dkey="s5sm")
        P.dma(lambda e: e.dma_start(out=sm[:, 8:10], in_=self.gla_o_g[i]), writes=["s5sm"], dkey="s5sm")
        P.op("dve", lambda e: e.tensor_scalar(out=sm[:, 0:4], in0=sm[:, 0:4], scalar1=-1.0, scalar2=None, op0=ALU.mult),
             reads=["s5sm"], writes=["s5sm"])
        groups = [(s, tg) for s in range(NSEQ) for tg in range(self.NG)]
        for k, (s, tg) in enumerate(groups):
            if k == 0:
                self.prefetch_x(x_in, kin, s, tg)
            if k + 1 < len(groups):
                self.prefetch_x(x_in, kin, *groups[k + 1])
            self.odd_inproj_group(s, tg, x_in, kin, wg16)
        self.gla_phase()
        for s in range(NSEQ):
            self.out_proj(s, x_in, kin, x_out, kout, 8, self.mixT, "mixT")
        self.Wb = full_Wb

    def odd_inproj_group(self, s, tg, x_in, kin, wg16):
        P = self.P
        hT, hk, xt, xk = self.load_norm_group(x_in, kin, s, tg)
        cols = slice(tg * 512, (tg + 1) * 512)
        sm = self.s5sm
        for which, dst, dkey in ((0, self.q2T, "q2T"), (1, self.k2T, "k2T")):
            for c in range(4):
                ps, pk = self.proj_fm(hT, hk, which * 512 + c * 128)
                ob, obk = self.nxt(self.ev32)
                P.op("act" if c % 2 else "dve",
                     (lambda e, ps=ps, ob=ob: e.copy(out=ob[:], in_=ps[:])) if c % 2 else
                     (lambda e, ps=ps, ob=ob: e.tensor_copy(out=ob[:], in_=ps[:])), reads=[pk], writes=[obk])
                self.store(ob[:], obk, dst[s, c * 128:(c + 1) * 128, cols], (dkey, s))
        for j in range(4):
            for half in range(2):
                ps, pk = self.proj_tm(hT, hk, j, 1024 + half * 512)
                ob, obk = self.nxt(self.evbf)
                P.op("act" if half else "dve",
                     (lambda e, ps=ps, ob=ob: e.copy(out=ob[:], in_=ps[:])) if half else
                     (lambda e, ps=ps, ob=ob: e.tensor_copy(out=ob[:], in_=ps[:])), reads=[pk], writes=[obk])
                r0 = tg * 512 + j * 128
                self.store(ob[:], obk, self.vtm[s, r0:r0 + 128, half * 512:(half + 1) * 512], ("vtm", s))
        for c in range(8):
            ps, pk = self.proj_fm(hT, hk, 2048 + c * 128)
            ob, obk = self.nxt(self.ev32)
            P.op("act", lambda e, ps=ps, ob=ob: e.activation(out=ob[:], in_=ps[:], func=AF.Silu), reads=[pk], writes=[obk])
            self.store(ob[:], obk, self.gzT[s, c * 128:(c + 1) * 128, cols], ("gzT", s))
        ps, pk = self.proj_fm(hT, hk, 3072, M=16)
        rT, rk = self.nxt(self.wkbf)
        P.op("dve", lambda e, ps=ps, rT=rT: e.tensor_copy(out=rT[0:16, :], in_=ps[0:16, :]), reads=[pk], writes=[rk])
        for c in range(4):
            ps2, pk2 = self.psum()
            P.op("pe", lambda e, ps2=ps2, c=c, rT=rT: e.matmul(ps2[:], lhsT=wg16[:, c * 128:(c + 1) * 128], rhs=rT[0:16, :],
                                                               start=True, stop=True), reads=[rk, "wg16"], writes=[pk2])
            ex, exk = self.nxt(self.wk32)
            P.op("act", lambda e, ps2=ps2, ex=ex, c=c: e.activation(out=ex[:], in_=ps2[:], func=AF.Exp, scale=-1.0,
                                                                    bias=sm[:, c:c + 1]), reads=[pk2, "s5sm"], writes=[exk])
            ob, obk = self.nxt(self.ev32)
            P.op("act", lambda e, ex=ex, ob=ob: e.activation(out=ob[:], in_=ex[:], func=AF.Ln, bias=1.0), reads=[exk], writes=[obk])
            self.store(ob[:], obk, self.lgT[s, c * 128:(c + 1) * 128, cols], ("lgT", s))

    def gla_phase(self):
        P = self.P
        L, NSEQ = self.L, self.NSEQ
        KB = 1024
        ov = self.ov
        P.barrier()
        NCHN = 2
        CH = 38 * KB
        rmask = ov(NCHN * CH, [128, 512], F32)
        P.op("pool", lambda e: e.memset(rmask, 1.0), writes=["rmask"])
        for c in range(4):
            P.op("pool", lambda e, c=c: e.memset(rmask[:, c * 128:c * 128 + 1], 0.0), writes=["rmask"], waw=False)
        assert NCHN * CH + 2 * KB <= 80 * KB
        sm = self.s5sm
        psrot = [0]

        def lpsum():
            k = 4 + psrot[0] % 3
            psrot[0] += 1
            return self.psM[k], ("psM", k)

        def chain(s, h, ci):
            b = ci * CH
            T = lambda n: ("gla", n, ci)
            q32 = ov(b, [128, 512], F32)
            k32 = ov(b + 2 * KB, [128, 512], F32)
            Lg = ov(b + 4 * KB, [128, 512], F32)
            G = ov(b + 6 * KB, [128, 512], F32)
            E1 = ov(b + 8 * KB, [128, 512], F32)
            E2 = ov(b + 10 * KB, [128, 512], F32)
            gz = ov(b + 12 * KB, [128, 2, 512], F32)
            vt = ov(b + 16 * KB, [128, 4, 256], BF16)
            qd = ov(b + 18 * KB, [128, 512], BF16)
            ki = ov(b + 19 * KB, [128, 512], BF16)
            kdT = ov(b + 20 * KB, [128, 512], BF16)
            kdec = ov(b + 21 * KB, [128, 4, 128], BF16)
            S32 = ov(b + 22 * KB, [128, 256], F32)
            Sbf = ov(b + 23 * KB, [128, 256], BF16)
            scT = [ov(b + 23 * KB + 512 + k * 256, [128, 128], BF16) for k in range(2)]
            rs = ov(b + 24 * KB, [128, 512], F32)
            psO = [self.psM[2 * ci], self.psM[2 * ci + 1]]
            psOk = [("psM", 2 * ci), ("psM", 2 * ci + 1)]
            psT = self.psT[0]
            P.op("pool", lambda e: e.memset(S32, 0.0), writes=[T("S32")])
            P.op("pool", lambda e: e.memset(Sbf, 0.0), writes=[T("Sbf")])
            LD = [dict(q32=q32, k32=k32, Lg=Lg, vt=vt, gz=gz),
                  dict(q32=ov(b + 26 * KB, [128, 512], F32), k32=ov(b + 28 * KB, [128, 512], F32),
                       Lg=ov(b + 30 * KB, [128, 512], F32), vt=ov(b + 32 * KB, [128, 4, 256], BF16),
                       gz=ov(b + 34 * KB, [128, 2, 512], F32))]

            def issue(tg):
                d = LD[tg % 2]
                r = tg % 2
                cs = slice(tg * 512, (tg + 1) * 512)
                P.dma(lambda e: e.dma_start(out=d["q32"], in_=self.q2T[s, h * 128:(h + 1) * 128, cs]),
                      reads=[("q2T", s)], writes=[T(("q32", r))], dkey=T(("q32", r)))
                P.dma(lambda e: e.dma_start(out=d["k32"], in_=self.k2T[s, h * 128:(h + 1) * 128, cs]),
                      reads=[("k2T", s)], writes=[T(("k32", r))], dkey=T(("k32", r)))
                P.dma(lambda e: e.dma_start(out=d["Lg"], in_=self.lgT[s, h * 128:(h + 1) * 128, cs]),
                      reads=[("lgT", s)], writes=[T(("Lg", r))], dkey=T(("Lg", r)))
                P.dma(lambda e: e.dma_start(out=d["vt"], in_=self.vtm[s, cs, h * 256:(h + 1) * 256].rearrange(
                    "(j p) d -> p j d", p=128)), reads=[("vtm", s)], writes=[T(("vt", r))], dkey=T(("vt", r)))
                P.dma(lambda e: e.dma_start(out=d["gz"], in_=self.gzT[s, h * 256:(h + 1) * 256, cs].rearrange(
                    "(v p) t -> p v t", p=128)), reads=[("gzT", s)], writes=[T(("gz", r))], dkey=T(("gz", r)))

            issue(0)
            for tg in range(self.NG):
                cs = slice(tg * 512, (tg + 1) * 512)
                if tg + 1 < self.NG:
                    issue(tg + 1)
                r = tg % 2
                q32, k32, Lg, vt, gz = (LD[r][n] for n in ("q32", "k32", "Lg", "vt", "gz"))
                Tq, Tk, TL, Tv, Tg = (T((n, r)) for n in ("q32", "k32", "Lg", "vt", "gz"))
                P.op("dve", lambda e, Lg=Lg: e.tensor_tensor_scan(out=G, data0=rmask, data1=Lg, initial=0.0, op0=ALU.mult,
                                                           op1=ALU.add), reads=[TL, "rmask"], writes=[T("G")])
                P.op("act", lambda e: e.activation(out=E1, in_=G, func=AF.Exp, scale=-1.0 / 16), reads=[T("G")], writes=[T("E1")])
                P.op("act", lambda e: e.activation(out=E2, in_=G, func=AF.Exp, scale=1.0 / 16), reads=[T("G")], writes=[T("E2")])
                P.op("dve", lambda e, q32=q32: e.scalar_tensor_tensor(out=qd, in0=q32, scalar=128 ** -0.5, in1=E1, op0=ALU.mult,
                                                             op1=ALU.mult), reads=[Tq, T("E1")], writes=[T("qd")])
                P.op("pool", lambda e, k32=k32: e.tensor_tensor(out=ki, in0=k32, in1=E2, op=ALU.mult), reads=[Tk, T("E2")],
                     writes=[T("ki")])
                for c in range(4):
                    cc = slice(c * 128, (c + 1) * 128)
                    P.op("dve", lambda e, c=c, cc=cc, k32=k32: e.scalar_tensor_tensor(
                        out=kdT[:, cc], in0=k32[:, cc], scalar=E1[:, c * 128 + 127:c * 128 + 128], in1=E2[:, cc],
                        op0=ALU.mult, op1=ALU.mult), reads=[Tk, T("E1"), T("E2")], writes=[T("kdT")], waw=(c == 0))
                yield
                for c in range(4):
                    cc = slice(c * 128, (c + 1) * 128)
                    P.op("pe", lambda e, cc=cc: e.transpose(out=psT[:, cc], in_=kdT[:, cc], identity=self.ident[:]),
                         reads=[T("kdT"), "ident"], writes=["psT"], waw=(c == 0))
                P.op("act", lambda e: e.copy(out=kdec, in_=psT[:, 0:512].rearrange("p (c k) -> p c k", c=4)),
                     reads=["psT"], writes=[T("kdec")])
                yield
                for c in range(4):
                    cc = slice(c * 128, (c + 1) * 128)
                    pss, pssk = lpsum()
                    P.op("pe", lambda e, cc=cc, pss=pss: e.matmul(pss[:, 0:128], lhsT=ki[:, cc], rhs=qd[:, cc], start=True,
                                                                  stop=True), reads=[T("ki"), T("qd")], writes=[pssk])
                    sc = scT[c % 2]
                    sck = T(("scT", c % 2))
                    P.op("dve", lambda e, pss=pss, sc=sc: e.tensor_tensor(out=sc, in0=pss[:, 0:128], in1=self.maskLE[:],
                                                                          op=ALU.mult), reads=[pssk, "maskLE"], writes=[sck])
                    for vc in range(2):
                        P.op("pe", lambda e, vc=vc, c=c, cc=cc, sc=sc, vt=vt: e.matmul(
                            psO[vc][:, cc], lhsT=vt[:, c, vc * 128:(vc + 1) * 128], rhs=sc, start=True, stop=False,
                            skip_group_check=True), reads=[Tv, sck], writes=[psOk[vc]], waw=(c == 0))
                        P.op("pe", lambda e, vc=vc, cc=cc: e.matmul(
                            psO[vc][:, cc], lhsT=Sbf[:, vc * 128:(vc + 1) * 128], rhs=qd[:, cc], start=False, stop=True,
                            skip_group_check=True), reads=[T("Sbf"), T("qd")], writes=[psOk[vc]], waw=False)
                    psS, psSk = lpsum()
                    P.op("pe", lambda e, c=c, psS=psS, vt=vt: e.matmul(psS[:, 0:256], lhsT=kdec[:, c, :], rhs=vt[:, c, :], start=True,
                                                                stop=True), reads=[T("kdec"), Tv], writes=[psSk])
                    P.op("dve", lambda e, c=c, psS=psS: e.scalar_tensor_tensor(
                        out=S32, in0=S32, scalar=E1[:, c * 128 + 127:c * 128 + 128], in1=psS[:, 0:256], op0=ALU.mult,
                        op1=ALU.add), reads=[T("S32"), T("E1"), psSk], writes=[T("S32")])
                    P.op("act", lambda e: e.copy(out=Sbf, in_=S32), reads=[T("S32")], writes=[T("Sbf")])
                    yield
                sqs = []
                for vc in range(2):
                    sq, sqk = self.nxt(self.wkbf)
                    P.op("act", lambda e, vc=vc, sq=sq: e.activation(out=sq[:], in_=psO[vc][:], func=AF.Square),
                         reads=[psOk[vc]], writes=[sqk])
                    sqs.append((sq, sqk))
                pn, pnk = lpsum()
                for vc in range(2):
                    P.op("pe", lambda e, vc=vc, pn=pn, sq=sqs[vc][0]: e.matmul(pn[:], lhsT=self.ones[:], rhs=sq[:], start=(vc == 0),
                                                                stop=(vc == 1)), reads=[sqs[vc][1], "ones"], writes=[pnk],
                         waw=(vc == 0))
                P.op("act", lambda e, pn=pn: e.activation(out=rs, in_=pn[:], func=AF.Ln, scale=1.0 / 256, bias=EPS),
                     reads=[pnk], writes=[T("rs")])
                P.op("act", lambda e: e.activation(out=rs, in_=rs, func=AF.Exp, scale=-0.5), reads=[T("rs")], writes=[T("rs")])
                for vc in range(2):
                    tmp, tk = self.nxt(self.wk32)
                    P.op("dve", lambda e, vc=vc, tmp=tmp: e.scalar_tensor_tensor(
                        out=tmp[:], in0=psO[vc][:], scalar=sm[:, 8 + vc:9 + vc], in1=rs, op0=ALU.mult, op1=ALU.mult),
                        reads=[psOk[vc], "s5sm", T("rs")], writes=[tk])
                    ob, obk = self.nxt(self.evbf)
                    if os.environ.get("GLA_DBG") == "1":
                        P.op("dve", lambda e, vc=vc, ob=ob: e.tensor_copy(out=ob[:], in_=psO[vc][:]), reads=[psOk[vc]], writes=[obk])
                    elif os.environ.get("GLA_DBG") == "2":
                        P.op("dve", lambda e, vc=vc, ob=ob, tmp=tmp: e.tensor_copy(out=ob[:], in_=tmp[:]), reads=[tk], writes=[obk])
                    else:
                        P.op("pool", lambda e, vc=vc, tmp=tmp, ob=ob, gz=gz: e.tensor_tensor(out=ob[:], in0=tmp[:], in1=gz[:, vc, :],
                                                                                      op=ALU.mult), reads=[tk, Tg], writes=[obk])
                    self.store(ob[:], obk, self.mixT[s, h * 256 + vc * 128:h * 256 + (vc + 1) * 128, cs], ("mixT", s))
                yield

        jobs = [(s, h) for s in range(NSEQ) for h in range(4)]
        for r0 in range(0, len(jobs), NCHN):
            run_interleaved([chain(s, h, ci) for ci, (s, h) in enumerate(jobs[r0:r0 + NCHN])])


def host_layout(inputs, nseq_total=16):
    f = lambda a: np.ascontiguousarray(np.asarray(a, dtype=np.float32))
    d = {}
    d["even_norm_g"] = f(inputs["even_norm_g"].reshape(2, 8, 128).transpose(0, 2, 1))
    d["odd_norm_g"] = f(inputs["odd_norm_g"].reshape(2, 8, 128).transpose(0, 2, 1))
    d["even_w_in"] = f(inputs["even_w_in"])
    d["even_w_out"] = f(inputs["even_w_out"])
    d["odd_w_in"] = f(inputs["odd_w_in"])
    d["odd_w_out"] = f(inputs["odd_w_out"])
    d["sb_q_norm_g"] = f(inputs["sb_q_norm_g"].reshape(2, 128, 1))
    d["sb_k_norm_g"] = f(inputs["sb_k_norm_g"].reshape(2, 128, 1))
    d["gla_w_gate"] = f(inputs["gla_w_gate"])
    d["gla_b_gate"] = f(inputs["gla_b_gate"].reshape(2, 4, 128).transpose(0, 2, 1))
    d["gla_o_norm_g"] = f(inputs["gla_o_norm_g"].reshape(2, 2, 128).transpose(0, 2, 1))
    lam = np.stack([inputs["s5_lambda_re"], inputs["s5_lambda_im"]], axis=1)
    lam = lam.transpose(0, 3, 1, 2)
    d["s5_lam"] = f(np.concatenate([lam, lam], axis=1))
    d["s5_log_dt"] = f(np.broadcast_to(inputs["s5_log_dt"][:, None, :], (2, 128, 32)))
    b = np.concatenate([inputs["s5_b_re"], inputs["s5_b_im"]], axis=2)
    d["s5_b0"] = f(b.transpose(0, 2, 1, 3))
    c = np.concatenate([inputs["s5_c_re"], inputs["s5_c_im"]], axis=3)
    d["s5_c0"] = f(c.transpose(0, 3, 1, 2))
    d["s5_d"] = f(inputs["s5_d"].reshape(2, 4, 128).transpose(0, 2, 1))
    d["s5_w_glu"] = f(inputs["s5_w_glu"])
    d["s5_b_glu"] = f(inputs["s5_b_glu"].reshape(2, 4, 128).transpose(0, 2, 1))
    return d


def kernel(**inputs):
    x = np.asarray(inputs["x"], dtype=np.float32)
    bsz, L, _ = x.shape
    nseq = bsz // N_CORES
    b = Builder(L=L, NSEQ=nseq, depth=4)
    nc = b.build()
    params = host_layout({k: np.asarray(v) for k, v in inputs.items() if k != "x"})
    in_maps = []
    for c in range(N_CORES):
        m = dict(params)
        m["x"] = np.ascontiguousarray(x[c * nseq:(c + 1) * nseq])
        in_maps.append(m)
    res = run_bass_kernel_spmd(nc, in_maps, core_ids=list(range(N_CORES)))
    return np.concatenate([r["y"] for r in res.results], axis=0).astype(np.float32)
```

```python
import math
from contextlib import ExitStack

import numpy as np
import concourse.bass as bass
import concourse.mybir as mybir
from concourse.bass_utils import run_bass_kernel_spmd

F32 = mybir.dt.float32
BF16 = mybir.dt.bfloat16
AF = mybir.ActivationFunctionType
ALU = mybir.AluOpType
AX = mybir.AxisListType

D_MODEL = 1024
EPS = 1e-6
N_CORES = 8
import os
OVERLAP = False
N_FILL = int(os.environ.get('N_FILL', '0'))
SB_W = int(os.environ.get('SB_W', '2'))
S5_W = int(os.environ.get('S5_W', '3'))
S5_STOP = int(os.environ.get('S5_STOP', '0'))


class _Op:
    __slots__ = ("eng", "fn", "deps", "val", "needed", "dkey", "dval", "same_ok")


class Prog:
    ENGS = ("pe", "act", "dve", "pool", "sp")

    def __init__(self, nc, es):
        self.nc = nc
        self.es = es
        self.q = {e: [] for e in self.ENGS}
        self.W = {}
        self.R = {}
        self.dcount = {}
        self.n_ops = 0
        self.last_op = {}
        self.last_dma = {}
        self.pending = {}

    @staticmethod
    def _is_psum(b):
        return (isinstance(b, tuple) and b[0] == "psM") or b == "psT"

    @staticmethod
    def _chan(op):
        return ("d", op.dkey) if op.dkey is not None else op.eng

    def op(self, eng, fn, reads=(), writes=(), dkey=None, waw=True, same_ok=False):
        if eng == "pe":
            same_ok = True
        o = _Op()
        o.eng, o.fn, o.val, o.needed, o.dkey, o.same_ok = eng, fn, 0, False, dkey, same_ok
        deps = {}

        def add(d):
            deps[id(d)] = d

        for b in reads:
            for d in self.W.get(b, {}).values():
                add(d)
            if self._is_psum(b):
                for chn, d in self.R.get(b, {}).items():
                    if chn != eng:
                        add(d)
        for b in writes:
            for d in self.R.get(b, {}).values():
                add(d)
            if waw:
                for d in self.W.get(b, {}).values():
                    add(d)
        if eng in self.pending:
            for d in self.pending.pop(eng):
                add(d)
        o.deps = list(deps.values())
        if dkey is None:
            self.last_op[eng] = o
        else:
            self.last_dma[dkey] = o
        if dkey is not None:
            self.dcount[dkey] = self.dcount.get(dkey, 0) + 16
            o.dval = self.dcount[dkey]
        ch = self._chan(o)
        for b in reads:
            self.R.setdefault(b, {})[ch] = o
        for b in writes:
            if waw:
                self.W[b] = {ch: o}
                self.R[b] = {}
            else:
                self.W.setdefault(b, {})[ch] = o
        self.q[eng].append(o)
        self.n_ops += 1
        return o

    def barrier(self):
        deps = list(self.last_op.values()) + list(self.last_dma.values())
        for e in self.ENGS:
            self.pending[e] = list(deps)

    def dma(self, fn, reads=(), writes=(), dkey=None, eng="sp", waw=False):
        assert dkey is not None
        return self.op(eng, fn, reads, writes, dkey=dkey, waw=waw)

    def finalize(self, final_keys=()):
        nc, es = self.nc, self.es
        for e in self.ENGS:
            for o in self.q[e]:
                for d in o.deps:
                    if d.dkey is None:
                        if d.eng == o.eng and o.same_ok:
                            continue
                        d.needed = True
        for e in self.ENGS:
            c = 0
            for o in self.q[e]:
                if o.dkey is None and o.needed:
                    c += 1
                    o.val = c
        esem = {e: es.enter_context(nc.semaphore("S_" + e)) for e in self.ENGS}
        dsem = {}
        for k in self.dcount:
            dsem[k] = es.enter_context(nc.semaphore("D%d" % len(dsem)))
        self.n_sems = len(esem) + len(dsem)
        handles = {"pe": "tensor", "act": "scalar", "dve": "vector", "pool": "gpsimd", "sp": "sync"}
        block = es.enter_context(nc.Block())
        final_waits = [(dsem[k], self.dcount[k]) for k in final_keys]
        for e in self.ENGS:
            ops = self.q[e]

            def body(eng, ops=ops, e=e):
                seen = {}
                for o in ops:
                    need = {}
                    for d in o.deps:
                        if d.dkey is not None:
                            key, v = ("d", d.dkey), d.dval
                        else:
                            if d.eng == e and o.same_ok:
                                continue
                            key, v = d.eng, d.val
                        if v > seen.get(key, 0) and v > need.get(key, 0):
                            need[key] = v
                    for key, v in need.items():
                        seen[key] = v
                        sem = dsem[key[1]] if isinstance(key, tuple) else esem[key]
                        eng.wait_ge(sem, v)
                    ins = o.fn(eng)
                    if o.dkey is not None:
                        ins.then_inc(dsem[o.dkey], 16)
                    elif o.needed:
                        ins.then_inc(esem[e], 1)
                if e == "sp":
                    for sem, v in final_waits:
                        eng.wait_ge(sem, v)

            getattr(block, handles[e])(body)


class Ring:
    def __init__(self, name, n):
        self.name, self.n, self.i = name, n, -1

    def next(self):
        self.i += 1
        return self.i % self.n

    def key(self, slot):
        return (self.name, slot)


def run_interleaved(gens):
    gens = list(gens)
    while gens:
        for g in list(gens):
            try:
                next(g)
            except StopIteration:
                gens.remove(g)


def interleave(gens):
    gens = list(gens)
    while gens:
        for g in list(gens):
            try:
                next(g)
            except StopIteration:
                gens.remove(g)
        yield


def interleave1(gens):
    gens = list(gens)
    while gens:
        for g in list(gens):
            try:
                next(g)
            except StopIteration:
                gens.remove(g)
                continue
            yield


def run_weighted(pairs):
    pairs = list(pairs)
    while pairs:
        for p in list(pairs):
            g, w = p
            for _ in range(w):
                try:
                    next(g)
                except StopIteration:
                    pairs.remove(p)
                    break


class Builder:
    def __init__(self, L=4096, NSEQ=2, depth=4, dump=()):
        self.L, self.NSEQ, self.depth, self.dump = L, NSEQ, depth, tuple(dump)
        self.NG = L // 512
        self.nc = bass.Bass("TRN2", target_bir_lowering=False)
        self.es = ExitStack()
        self.rings = {}

    def sb(self, name, shape, dt):
        return self.es.enter_context(self.nc.sbuf_tensor(name, list(shape), dt))

    def ps(self, name, shape, dt):
        return self.es.enter_context(self.nc.psum_tensor(name, list(shape), dt))

    def dram(self, name, shape, dt, kind=None):
        if name in self.dump:
            kind = "ExternalOutput"
        if kind is None:
            return self.nc.dram_tensor(name, list(shape), dt).ap()
        return self.nc.dram_tensor(name, list(shape), dt, kind=kind).ap()

    def ring(self, name, n, shape, dt):
        tiles = [self.sb("%s%d" % (name, i), shape, dt) for i in range(n)]
        r = Ring(name, n)
        r.tiles = tiles
        self.rings[name] = r
        return r

    def nxt(self, r):
        s = r.next()
        return r.tiles[s], r.key(s)

    def build(self):
        nc, es = self.nc, self.es
        L, NSEQ = self.L, self.NSEQ
        with es:
            self.P = P = Prog(nc, es)
            self.declare_io()
            self.setup_consts()
            x_in = self.x_in
            bufs = [self.xbufA, self.xbufB]
            for layer in range(self.depth):
                last = layer == self.depth - 1
                x_out = self.y_out if last else bufs[layer % 2]
                kin = "x_in" if layer == 0 else "xbuf%d" % ((layer - 1) % 2)
                kout = "y_out" if last else "xbuf%d" % (layer % 2)
                if layer % 2 == 0 and os.environ.get('ONLY_ODD') != '1':
                    self.even_layer(layer // 2, x_in, kin, x_out, kout)
                else:
                    self.odd_layer(layer // 2, x_in, kin, x_out, kout)
                x_in = x_out
            P.finalize(final_keys=self.final_keys)
        return nc

    def declare_io(self):
        L, NSEQ = self.L, self.NSEQ
        d = self.dram
        self.x_in = d("x", [NSEQ, L, 1024], F32, "ExternalInput")
        self.y_out = d("y", [NSEQ, L, 1024], F32, "ExternalOutput")
        self.xbufA = d("xbufA", [NSEQ, L, 1024], F32)
        self.xbufB = d("xbufB", [NSEQ, L, 1024], F32)
        I = lambda n, s: d(n, s, F32, "ExternalInput")
        self.even_norm_g = I("even_norm_g", [2, 128, 8])
        self.even_w_in = I("even_w_in", [2, 1024, 5120])
        self.sb_q_g = I("sb_q_norm_g", [2, 128, 1])
        self.sb_k_g = I("sb_k_norm_g", [2, 128, 1])
        self.even_w_out = I("even_w_out", [2, 1536, 1024])
        self.odd_norm_g = I("odd_norm_g", [2, 128, 8])
        self.odd_w_in = I("odd_w_in", [2, 1024, 3088])
        self.gla_w_gate = I("gla_w_gate", [2, 16, 512])
        self.gla_b_gate = I("gla_b_gate", [2, 128, 4])
        self.gla_o_g = I("gla_o_norm_g", [2, 128, 2])
        self.odd_w_out = I("odd_w_out", [2, 1024, 1024])
        self.s5_lam = I("s5_lam", [2, 128, 2, 32])
        self.s5_dt = I("s5_log_dt", [2, 128, 32])
        self.s5_b0 = I("s5_b0", [2, 128, 32, 16])
        self.s5_c0 = I("s5_c0", [2, 128, 32, 16])
        self.s5_d = I("s5_d", [2, 128, 4])
        self.s5_w_glu = I("s5_w_glu", [2, 512, 512])
        self.s5_b_glu = I("s5_b_glu", [2, 128, 4])
        B = lambda n, s: d(n, s, BF16)
        Fd = lambda n, s: d(n, s, F32)
        self.qT = B("qT", [NSEQ, 8, 128, L])
        self.kT = B("kT", [NSEQ, 8, 128, L])
        self.vtm = B("vtm", [NSEQ, L, 1024])
        self.gzT = Fd("gzT", [NSEQ, 1536, L])
        self.uT = B("uT", [NSEQ, 512, L])
        self.mixT = B("mixT", [NSEQ, 1536, L])
        self.q2T = Fd("q2T", [NSEQ, 512, L])
        self.k2T = Fd("k2T", [NSEQ, 512, L])
        self.lgT = Fd("lgT", [NSEQ, 512, L])
        self.final_keys = []

    def setup_consts(self):
        P, nc = self.P, self.nc
        sb = self.sb
        self.identf = sb("identf", [128, 128], F32)
        self.ident = sb("ident", [128, 128], BF16)
        self.ones = sb("ones", [128, 128], BF16)
        self.negones = sb("negones", [128, 128], BF16)
        self.maskLT = sb("maskLT", [128, 128], BF16)
        self.maskLE = sb("maskLE", [128, 128], F32)
        self.negtri = sb("negtri", [128, 128], BF16)
        tmpf = sb("ctmpf", [128, 128], F32)
        identf, ident = self.identf, self.ident
        P.op("pool", lambda e: e.memset(identf[:], 0.0), writes=["identf"])
        P.op("pool", lambda e: e.affine_select(out=identf[:], in_=identf[:], pattern=[[-1, 128]],
                                               compare_op=ALU.not_equal, fill=1.0, base=0,
                                               channel_multiplier=1), reads=["identf"], writes=["identf"])
        P.op("dve", lambda e: e.tensor_copy(out=ident[:], in_=identf[:]), reads=["identf"], writes=["ident"])
        P.op("pool", lambda e: e.memset(self.ones[:], 1.0), writes=["ones"])
        P.op("pool", lambda e: e.memset(self.negones[:], -1.0), writes=["negones"])
        P.op("pool", lambda e: e.memset(tmpf[:], 1.0), writes=["ctmpf"])
        P.op("pool", lambda e: e.affine_select(out=tmpf[:], in_=tmpf[:], pattern=[[1, 128]],
                                               compare_op=ALU.is_gt, fill=0.0, base=0,
                                               channel_multiplier=-1), reads=["ctmpf"], writes=["ctmpf"])
        P.op("dve", lambda e: e.tensor_copy(out=self.maskLT[:], in_=tmpf[:]), reads=["ctmpf"], writes=["maskLT"])
        P.op("pool", lambda e: e.memset(self.maskLE[:], 1.0), writes=["maskLE"])
        P.op("pool", lambda e: e.affine_select(out=self.maskLE[:], in_=self.maskLE[:], pattern=[[1, 128]],
                                               compare_op=ALU.is_ge, fill=0.0, base=0,
                                               channel_multiplier=-1), reads=["maskLE"], writes=["maskLE"])
        P.op("pool", lambda e: e.memset(tmpf[:], -1.0), reads=["maskLT"], writes=["ctmpf"])
        P.op("pool", lambda e: e.affine_select(out=tmpf[:], in_=tmpf[:], pattern=[[-1, 128]],
                                               compare_op=ALU.is_ge, fill=0.0, base=0,
                                               channel_multiplier=1), reads=["ctmpf"], writes=["ctmpf"])
        P.op("dve", lambda e: e.tensor_copy(out=self.negtri[:], in_=tmpf[:]), reads=["ctmpf"], writes=["negtri"])
        self.negbig = sb("negbig", [128, 128], BF16)
        P.op("pool", lambda e: e.memset(tmpf[:], -30000.0), reads=["negtri"], writes=["ctmpf"])
        P.op("pool", lambda e: e.affine_select(out=tmpf[:], in_=tmpf[:], pattern=[[-1, 128]],
                                               compare_op=ALU.is_ge, fill=0.0, base=0,
                                               channel_multiplier=1), reads=["ctmpf"], writes=["ctmpf"])
        P.op("dve", lambda e: e.tensor_copy(out=self.negbig[:], in_=tmpf[:]), reads=["ctmpf"], writes=["negbig"])
        self.CONST = ["ident", "identf", "ones", "negones", "maskLT", "maskLE", "negtri"]

        self.arena = sb("arena", [128, 40960], BF16)
        self.Wb = self.arena[:, :].rearrange("p (k f) -> p k f", k=8)
        self.Wo = sb("Wo", [128, 12, 1024], BF16)
        self.wstage = self.ring("wst", 2, [128, 1024], F32)
        self.gcol = sb("gcol", [128, 8], F32)
        self.xt = self.ring("xt", 2, [128, 4, 1024], F32)
        self.junk = sb("junk", [128, 1024], BF16)
        self.ss4 = sb("ss4", [128, 4], F32)
        self.rstd4 = sb("rstd4", [128, 4], F32)
        self.hb = self.ring("hb", 2, [128, 1024], BF16)
        self.hT = self.ring("hT", 2, [128, 8, 512], BF16)
        self.psT = [self.ps("psT%d" % i, [128, 1024], BF16) for i in range(1)]
        self.psM = [self.ps("psM%d" % i, [128, 512], F32) for i in range(7)]
        self.psM_i = 0
        self.ev32 = self.ring("ev32", 3, [128, 512], F32)
        self.evbf = self.ring("evbf", 4, [128, 512], BF16)
        self.wk32 = self.ring("wk32", 3, [128, 512], F32)
        self.wkbf = self.ring("wkbf", 3, [128, 512], BF16)

    def ov(self, off, shape, dt):
        n = int(np.prod(shape[1:]))
        esz = 4 if dt == F32 else 2
        a = self.arena[:, off // 2: off // 2 + n * esz // 2]
        if dt == F32:
            a = a.bitcast(F32)
        if len(shape) == 3:
            a = a.rearrange("p (a b) -> p a b", a=shape[1])
        return a

    def psum(self):
        if getattr(self, "ps_restrict", False) and os.environ.get("PSR", "1") == "1":
            self.psR_i = getattr(self, "psR_i", 0) + 1
            if self.psR_i % 2:
                return self.psM[6], ("psM", 6)
            return self.psT[0][:, :].bitcast(F32), "psT"
        i = self.psM_i % len(self.psM)
        self.psM_i += 1
        return self.psM[i], ("psM", i)

    def load_weights(self, w_dram, n_k, n_f, dst, dst_key, gain_dram=None):
        P = self.P
        gcol = self.gcol
        if gain_dram is not None:
            P.dma(lambda e: e.dma_start(out=gcol[:], in_=gain_dram), writes=["gcol"], dkey="gcol")
        CH = 1024
        for f0 in range(0, n_f, CH):
            fw = min(CH, n_f - f0)
            ck = (dst_key, f0 // CH) if dst_key == "Wb" else dst_key
            for kc in range(n_k):
                st, sk = self.nxt(self.wstage)
                P.dma(lambda e, st=st, kc=kc, f0=f0, fw=fw: e.dma_start(
                    out=st[:, :fw], in_=w_dram[kc * 128:(kc + 1) * 128, f0:f0 + fw]), writes=[sk], dkey=sk)
                eng = "act" if (kc % 2 == 0) else "dve"
                first = (kc == 0) if dst_key == "Wb" else (kc == 0 and f0 == 0)
                if gain_dram is not None:
                    if eng == "act":
                        P.op(eng, lambda e, st=st, kc=kc, f0=f0, fw=fw: e.activation(
                            out=dst[:, kc, f0:f0 + fw], in_=st[:, :fw], func=AF.Copy, scale=gcol[:, kc:kc + 1]),
                            reads=[sk, "gcol"], writes=[ck], waw=first)
                    else:
                        P.op(eng, lambda e, st=st, kc=kc, f0=f0, fw=fw: e.tensor_scalar(
                            out=dst[:, kc, f0:f0 + fw], in0=st[:, :fw], scalar1=gcol[:, kc:kc + 1], scalar2=None,
                            op0=ALU.mult), reads=[sk, "gcol"], writes=[ck], waw=first)
                else:
                    if eng == "act":
                        P.op(eng, lambda e, st=st, kc=kc, f0=f0, fw=fw: e.copy(
                            out=dst[:, kc, f0:f0 + fw], in_=st[:, :fw]), reads=[sk], writes=[ck], waw=first)
                    else:
                        P.op(eng, lambda e, st=st, kc=kc, f0=f0, fw=fw: e.tensor_copy(
                            out=dst[:, kc, f0:f0 + fw], in_=st[:, :fw]), reads=[sk], writes=[ck], waw=first)

    def prefetch_x(self, x_dram, xkey, s, tg):
        P = self.P
        xt, xk = self.nxt(self.xt)
        P.dma(lambda e: e.dma_start(out=xt[:], in_=x_dram[s, tg * 512:(tg + 1) * 512, :].rearrange(
            "(j p) d -> p j d", p=128)), reads=[xkey], writes=[xk], dkey=xk)
        if not hasattr(self, "xpref"):
            self.xpref = {}
        self.xpref[(xkey, s, tg)] = (xt, xk)

    def load_norm_group(self, x_dram, xkey, s, tg):
        P = self.P
        if (xkey, s, tg) not in getattr(self, "xpref", {}):
            self.prefetch_x(x_dram, xkey, s, tg)
        xt, xk = self.xpref.pop((xkey, s, tg))
        ss4, rstd4, junk = self.ss4, self.rstd4, self.junk
        for j in range(4):
            P.op("act", lambda e, j=j: e.activation(out=junk[:], in_=xt[:, j, :], func=AF.Square,
                                                    accum_out=ss4[:, j:j + 1]),
                 reads=[xk], writes=["junk", "ss4"])
        P.op("act", lambda e: e.activation(out=rstd4[:], in_=ss4[:], func=AF.Ln, scale=1.0 / 1024, bias=EPS),
             reads=["ss4"], writes=["rstd4"])
        P.op("act", lambda e: e.activation(out=rstd4[:], in_=rstd4[:], func=AF.Exp, scale=-0.5), reads=["rstd4"], writes=["rstd4"])
        hT, hk = self.nxt(self.hT)
        psT = self.psT[0]
        for j in range(4):
            hb, hbk = self.nxt(self.hb)
            P.op("dve", lambda e, j=j, hb=hb: e.tensor_scalar(out=hb[:], in0=xt[:, j, :], scalar1=rstd4[:, j:j + 1],
                                                              scalar2=None, op0=ALU.mult),
                 reads=[xk, "rstd4"], writes=[hbk])
            for kc in range(8):
                P.op("pe", lambda e, kc=kc, hb=hb: e.transpose(out=psT[:, kc * 128:(kc + 1) * 128],
                                                               in_=hb[:, kc * 128:(kc + 1) * 128],
                                                               identity=self.ident[:]),
                     reads=[hbk, "ident"], writes=["psT"], same_ok=True, waw=(kc == 0))
            P.op("act", lambda e, j=j: e.copy(out=hT[:, :, j * 128:(j + 1) * 128],
                                              in_=psT[:].rearrange("p (k t) -> p k t", k=8)),
                 reads=["psT"], writes=[hk], waw=(j == 0))
        return hT, hk, xt, xk

    def proj_fm(self, hT, hk, f0, M=128):
        P = self.P
        Wb = self.Wb
        ps, pk = self.psum()
        for kc in range(8):
            P.op("pe", lambda e, kc=kc: e.matmul(ps[:M, :], lhsT=Wb[:, kc, f0:f0 + M], rhs=hT[:, kc, :],
                                                 start=(kc == 0), stop=(kc == 7)),
                 reads=[hk, ("Wb", f0 // 1024)], writes=[pk], same_ok=True, waw=(kc == 0))
        return ps, pk

    def proj_tm(self, hT, hk, j, f0):
        P = self.P
        Wb = self.Wb
        ps, pk = self.psum()
        for kc in range(8):
            P.op("pe", lambda e, kc=kc: e.matmul(ps[:, :], lhsT=hT[:, kc, j * 128:(j + 1) * 128],
                                                 rhs=Wb[:, kc, f0:f0 + 512], start=(kc == 0), stop=(kc == 7)),
                 reads=[hk, ("Wb", f0 // 1024)], writes=[pk], same_ok=True, waw=(kc == 0))
        return ps, pk

    def store(self, tile_ap, tkey, dram_ap, dkey_dram):
        self.P.dma(lambda e: e.dma_start(out=dram_ap, in_=tile_ap), reads=[tkey], writes=[dkey_dram], dkey=tkey)

    def even_layer(self, i, x_in, kin, x_out, kout):
        P = self.P
        L, NSEQ = self.L, self.NSEQ
        P.barrier()
        self.load_weights(self.even_w_in[i], 8, 5120, self.Wb, "Wb", gain_dram=self.even_norm_g[i])
        self.load_weights(self.even_w_out[i], 12, 1024, self.Wo, "Wo")
        if not hasattr(self, "qkg"):
            self.qkg = self.sb("qkg", [128, 2], F32)
        qkg = self.qkg
        P.dma(lambda e: e.dma_start(out=qkg[:, 0:1], in_=self.sb_q_g[i]), writes=["qkg"], dkey="qkg")
        P.dma(lambda e: e.dma_start(out=qkg[:, 1:2], in_=self.sb_k_g[i]), writes=["qkg"], dkey="qkg")
        P.op("dve", lambda e: e.tensor_scalar(out=qkg[:, 0:1], in0=qkg[:, 0:1], scalar1=128 ** -0.5, scalar2=None,
                                              op0=ALU.mult), reads=["qkg"], writes=["qkg"])
        self.s5_prep(i)
        groups = [(s, tg) for s in range(NSEQ) for tg in range(self.NG)]
        for k, (s, tg) in enumerate(groups):
            if k == 0:
                self.prefetch_x(x_in, kin, s, tg)
            if k + 1 < len(groups):
                self.prefetch_x(x_in, kin, *groups[k + 1])
            self.even_inproj_group(i, x_in, kin, s, tg)
        for s in range(NSEQ):
            for _ in self.sb_attention(s):
                pass
        for _ in self.s5_main(i):
            pass
        for s in range(NSEQ):
            self.out_proj(s, x_in, kin, x_out, kout, 12, self.mixT, "mixT")

    def even_inproj_group(self, i, x_in, kin, s, tg):
        P = self.P
        hT, hk, xt, xk = self.load_norm_group(x_in, kin, s, tg)
        cols = slice(tg * 512, (tg + 1) * 512)
        for which in range(2):
            dst = self.qT if which == 0 else self.kT
            dkey = "qT" if which == 0 else "kT"
            for h in range(8):
                ps, pk = self.proj_fm(hT, hk, which * 1024 + h * 128)
                sq, sqk = self.nxt(self.wkbf)
                P.op("act", lambda e, ps=ps, sq=sq: e.activation(out=sq[:], in_=ps[:], func=AF.Square),
                     reads=[pk], writes=[sqk])
                ps2, pk2 = self.psum()
                P.op("pe", lambda e, ps2=ps2, sq=sq: e.matmul(ps2[:], lhsT=self.ones[:], rhs=sq[:], start=True, stop=True),
                     reads=[sqk, "ones"], writes=[pk2])
                rs, rsk = self.nxt(self.wk32)
                P.op("act", lambda e, ps2=ps2, rs=rs: e.activation(out=rs[:], in_=ps2[:], func=AF.Ln,
                                                                   scale=1.0 / 128, bias=EPS),
                     reads=[pk2], writes=[rsk])
                P.op("act", lambda e, rs=rs: e.activation(out=rs[:], in_=rs[:], func=AF.Exp, scale=-0.5), reads=[rsk], writes=[rsk])
                ob, obk = self.nxt(self.evbf)
                P.op("dve", lambda e, ps=ps, rs=rs, ob=ob, which=which: e.scalar_tensor_tensor(
                    out=ob[:], in0=ps[:], scalar=self.qkg[:, which:which + 1], in1=rs[:], op0=ALU.mult, op1=ALU.mult),
                    reads=[pk, rsk, "qkg"], writes=[obk])
                self.store(ob[:], obk, dst[s, h, :, cols], (dkey, s))
        for j in range(4):
            for half in range(2):
                ps, pk = self.proj_tm(hT, hk, j, 2048 + half * 512)
                ob, obk = self.nxt(self.evbf)
                P.op("act" if half else "dve",
                     (lambda e, ps=ps, ob=ob: e.copy(out=ob[:], in_=ps[:])) if half else
                     (lambda e, ps=ps, ob=ob: e.tensor_copy(out=ob[:], in_=ps[:])),
                     reads=[pk], writes=[obk])
                r0 = tg * 512 + j * 128
                self.store(ob[:], obk, self.vtm[s, r0:r0 + 128, half * 512:(half + 1) * 512], ("vtm", s))
        for c in range(12):
            f0 = 3072 + c * 128 if c < 8 else 4608 + (c - 8) * 128
            ps, pk = self.proj_fm(hT, hk, f0)
            ob, obk = self.nxt(self.ev32)
            P.op("act", lambda e, ps=ps, ob=ob: e.activation(out=ob[:], in_=ps[:], func=AF.Silu),
                 reads=[pk], writes=[obk])
            self.store(ob[:], obk, self.gzT[s, c * 128:(c + 1) * 128, cols], ("gzT", s))
        for c in range(4):
            ps, pk = self.proj_fm(hT, hk, 4096 + c * 128)
            ob, obk = self.nxt(self.evbf)
            P.op("dve", lambda e, ps=ps, ob=ob: e.tensor_copy(out=ob[:], in_=ps[:]), reads=[pk], writes=[obk])
            self.store(ob[:], obk, self.uT[s, c * 128:(c + 1) * 128, cols], ("uT", s))

    def out_proj(self, s, x_in, kin, x_out, kout, n_k, mix_dram, mixkey):
        P = self.P
        if not hasattr(self, "mxin"):
            self.mxin = Ring("mxin", 2)
            self.mxin.tiles = [self.ov(k * 12288, [128, 12, 512], BF16) for k in range(2)]
            self.xres = self.xt
        def issue(tg):
            mx, mk = self.nxt(self.mxin)
            P.dma(lambda e, mx=mx, tg=tg: e.dma_start(
                out=mx[:, :n_k, :], in_=mix_dram[s, :n_k * 128, tg * 512:(tg + 1) * 512].rearrange(
                    "(kc p) t -> p kc t", p=128)), reads=[(mixkey, s)], writes=[mk], dkey=mk)
            xr, xrk = self.nxt(self.xres)
            P.dma(lambda e, xr=xr, tg=tg: e.dma_start(
                out=xr[:], in_=x_in[s, tg * 512:(tg + 1) * 512, :].rearrange("(j p) d -> p j d", p=128)),
                reads=[kin], writes=[xrk], dkey=xrk)
            return mx, mk, xr, xrk

        pend = issue(0)
        for tg in range(self.NG):
            mx, mk, xr, xrk = pend
            if tg + 1 < self.NG:
                pend = issue(tg + 1)
            for j in range(4):
                for half in range(2):
                    ps, pk = self.psum()
                    for kc in range(n_k):
                        P.op("pe", lambda e, ps=ps, kc=kc, j=j, half=half, mx=mx: e.matmul(
                            ps[:], lhsT=mx[:, kc, j * 128:(j + 1) * 128],
                            rhs=self.Wo[:, kc, half * 512:(half + 1) * 512], start=(kc == 0), stop=(kc == n_k - 1)),
                            reads=[mk, "Wo"], writes=[pk], same_ok=True, waw=(kc == 0))
                    P.op("dve", lambda e, ps=ps, j=j, half=half, xr=xr: e.tensor_tensor(
                        out=xr[:, j, half * 512:(half + 1) * 512], in0=ps[:], in1=xr[:, j, half * 512:(half + 1) * 512],
                        op=ALU.add), reads=[pk, xrk], writes=[xrk])
            P.dma(lambda e, xr=xr, tg=tg: e.dma_start(
                out=x_out[s, tg * 512:(tg + 1) * 512, :].rearrange("(j p) d -> p j d", p=128), in_=xr[:]),
                reads=[xrk], writes=[kout], dkey=(xrk, "st"))
            if kout == "y_out":
                if (xrk, "st") not in self.final_keys:
                    self.final_keys.append((xrk, "st"))

    def sb_attention(self, s):
        P = self.P
        L = self.L
        NB = L // 128
        if not OVERLAP:
            P.barrier()
        KB = 1024
        qkv = []
        NQ = 1 if OVERLAP else 2
        for r in range(NQ):
            base = r * 3 * (L * 2)
            qkv.append((self.ov(base, [128, L], BF16), self.ov(base + 2 * L, [128, L], BF16),
                        self.ov(base + 4 * L, [128, NB, 128], BF16)))
        cbase = NQ * 3 * L * 2
        NCH = 3
        chains = []
        for c in range(NCH):
            b = cbase + c * 7 * KB
            chains.append(dict(e32=self.ov(b, [128, 512], F32), Lb=self.ov(b + 2 * KB, [128, 512], BF16),
                               wb=self.ov(b + 3 * KB, [128, 512], BF16), R32=self.ov(b + 4 * KB, [128, 512], F32),
                               Rbf=self.ov(b + 6 * KB, [128, 512], BF16), id=c,
                               psZ=self.psM[2 * c], psZk=("psM", 2 * c), psO=self.psM[2 * c + 1], psOk=("psM", 2 * c + 1)))
        gbase = cbase + NCH * 7 * KB
        gz = [self.ov(gbase + k * 2 * KB, [128, 512], F32) for k in range(NCH)]
        assert gbase + NCH * 2 * KB <= (51 * KB if OVERLAP else 81920)
        def load_head(h):
            qh, kh, vh = qkv[h % NQ]
            kq, kk, kv = ("sbq", h % NQ), ("sbk", h % NQ), ("sbv", h % NQ)
            P.dma(lambda e, qh=qh, h=h: e.dma_start(out=qh, in_=self.qT[s, h]), reads=[("qT", s)], writes=[kq], dkey=kq)
            P.dma(lambda e, kh=kh, h=h: e.dma_start(out=kh, in_=self.kT[s, h]), reads=[("kT", s)], writes=[kk], dkey=kk)
            P.dma(lambda e, vh=vh, h=h: e.dma_start(
                out=vh, in_=self.vtm[s, :, h * 128:(h + 1) * 128].rearrange("(b p) d -> p b d", p=128)),
                reads=[("vtm", s)], writes=[kv], dkey=kv)

        load_head(0)
        for h in range(8):
            qh, kh, vh = qkv[h % NQ]
            kq, kk, kv = ("sbq", h % NQ), ("sbk", h % NQ), ("sbv", h % NQ)
            if h + 1 < 8 and NQ == 2:
                load_head(h + 1)
            elif h > 0 and NQ == 1:
                load_head(h)

            def chain(qg, ch, h=h, qh=qh, kh=kh, vh=vh, kq=kq, kk=kk, kv=kv):
                c = ch["id"]
                tag = lambda n: ("sbc", n, c)
                e32, Lb, wb, R32, Rbf = ch["e32"], ch["Lb"], ch["wb"], ch["R32"], ch["Rbf"]
                psZ, psZk, psO, psOk = ch["psZ"], ch["psZk"], ch["psO"], ch["psOk"]
                g = gz[c]
                P.dma(lambda e: e.dma_start(out=g, in_=self.gzT[s, h * 128:(h + 1) * 128, qg * 512:(qg + 1) * 512]),
                      reads=[("gzT", s)], writes=[tag("gz")], dkey=tag("gz"))
                P.op("pool", lambda e: e.memset(R32, 0.0), writes=[tag("R32")])
                kbs = list(reversed(range(4 * qg + 4)))

                def zmm(kb):
                    c0 = max(0, kb - 4 * qg) * 128
                    q0 = qg * 512 + c0
                    P.op("pe", lambda e: e.matmul(
                        psZ[:, c0:512], lhsT=kh[:, kb * 128:(kb + 1) * 128], rhs=qh[:, q0:qg * 512 + 512],
                        start=True, stop=False, skip_group_check=True), reads=[kq, kk], writes=[psZk])
                    if kb >= 4 * qg:
                        P.op("pe", lambda e: e.matmul(
                            psZ[:, c0:c0 + 128], lhsT=self.ident[:], rhs=self.negbig[:], start=False, stop=False,
                            skip_group_check=True), reads=["ident", "negbig"], writes=[psZk], waw=False)

                zmm(kbs[0])
                for idx, kb in enumerate(kbs):
                    c0 = max(0, kb - 4 * qg) * 128
                    N = 512 - c0
                    diag = kb >= 4 * qg
                    P.op("act", lambda e, c0=c0: e.activation(out=e32[:, c0:512], in_=psZ[:, c0:512], func=AF.Exp),
                         reads=[psZk], writes=[tag("e32")])
                    yield
                    P.op("act", lambda e, c0=c0: e.activation(out=Lb[:, c0:512], in_=e32[:, c0:512], func=AF.Ln, bias=1.0),
                         reads=[tag("e32")], writes=[tag("Lb")])
                    yield
                    P.op("pe", lambda e, c0=c0, idx=idx: e.matmul(
                        psZ[:, c0:512], lhsT=self.negtri[:], rhs=Lb[:, c0:512], start=False, stop=(idx == 0),
                        skip_group_check=True), reads=[tag("Lb"), "negtri"], writes=[psZk], same_ok=True, waw=False)
                    if idx > 0:
                        P.op("pe", lambda e, c0=c0: e.matmul(
                            psZ[:, c0:512], lhsT=self.negones[:], rhs=Rbf[:, c0:512], start=False, stop=True,
                            skip_group_check=True), reads=[tag("Rbf"), "negones"], writes=[psZk], same_ok=True, waw=False)
                    P.op("act", lambda e, c0=c0: e.activation(out=wb[:, c0:512], in_=psZ[:, c0:512], func=AF.Exp),
                         reads=[psZk], writes=[tag("wb")])
                    yield
                    if idx + 1 < len(kbs):
                        zmm(kbs[idx + 1])
                    P.op("pe", lambda e, kb=kb, c0=c0, idx=idx: e.matmul(
                        psO[:, c0:512], lhsT=vh[:, kb, :], rhs=wb[:, c0:512], start=(idx == 0), stop=(idx == len(kbs) - 1),
                        skip_group_check=True), reads=[tag("wb"), kv], writes=[psOk], same_ok=True, waw=(idx == 0))
                    for _f in range(N_FILL):
                        P.op("pe", lambda e: e.matmul(self.psM[6][:, :], lhsT=self.ones[:], rhs=qh[:, 0:512], start=True,
                                                      stop=True, skip_group_check=True), reads=[], writes=[("psM", 6)])
                    if idx < len(kbs) - 1:
                        c1 = max(0, kbs[idx + 1] - 4 * qg) * 128
                        P.op("pool", lambda e, c0=c0: e.tensor_tensor(out=R32[:, c0:512], in0=R32[:, c0:512],
                                                                      in1=Lb[:, c0:512], op=ALU.add),
                             reads=[tag("Lb"), tag("R32")], writes=[tag("R32")])
                        P.op("dve", lambda e, c1=c1: e.tensor_copy(out=Rbf[:, c1:512], in_=R32[:, c1:512]),
                             reads=[tag("R32")], writes=[tag("Rbf")])
                    yield
                ob, obk = self.nxt(self.evbf)
                P.op("dve", lambda e, ob=ob: e.tensor_tensor(out=ob[:], in0=psO[:], in1=g, op=ALU.mult),
                     reads=[psOk, tag("gz")], writes=[obk])
                self.store(ob[:], obk, self.mixT[s, h * 128:(h + 1) * 128, qg * 512:(qg + 1) * 512], ("mixT", s))
                yield

            def head_gen(c):
                if self.NG == 8 and NCH == 3:
                    qgs = [[7, 3], [6, 4], [5, 2, 1, 0]][c]
                else:
                    qgs = list(range(self.NG - 1 - c, -1, -NCH))
                for qg in qgs:
                    yield from chain(qg, chains[c])
            yield from interleave([head_gen(c) for c in range(NCH)])

    def s5_prep(self, i):
        pass

    def s5_consts(self):
        if hasattr(self, "Jm"):
            return
        P, sb = self.P, self.sb
        self.Jm = sb("Jm", [128, 128], F32)
        self.nJm = sb("nJm", [128, 128], F32)
        self.bd = sb("bdmask", [128, 128], F32)
        self.Em = sb("Emat", [8, 128], F32)
        self.rowm = sb("rowm", [128, 2], F32)
        self.sgn = sb("sgn", [128, 1], F32)
        self.s5sm = sb("s5sm", [128, 12], F32)
        Jm, nJm, bd, Em, rowm, sgn = self.Jm, self.nJm, self.bd, self.Em, self.rowm, self.sgn
        P.op("pool", lambda e: e.memset(Jm[:], 0.0), writes=["Jm"])
        P.op("pool", lambda e: e.affine_select(out=Jm[:], in_=Jm[:], pattern=[[-1, 128]], compare_op=ALU.not_equal,
                                               fill=-1.0, base=64, channel_multiplier=1), reads=["Jm"], writes=["Jm"])
        P.op("pool", lambda e: e.affine_select(out=Jm[:], in_=Jm[:], pattern=[[-1, 128]], compare_op=ALU.not_equal,
                                               fill=1.0, base=-64, channel_multiplier=1), reads=["Jm"], writes=["Jm"])
        P.op("dve", lambda e: e.tensor_scalar(out=nJm[:], in0=Jm[:], scalar1=-1.0, scalar2=None, op0=ALU.mult),
             reads=["Jm"], writes=["nJm"])
        P.op("pool", lambda e: e.memset(Em[:], 1.0), writes=["Em"])
        P.op("pool", lambda e: e.affine_select(out=Em[:], in_=Em[:], pattern=[[1, 128]], compare_op=ALU.is_ge,
                                               fill=0.0, base=0, channel_multiplier=-16), reads=["Em"], writes=["Em"])
        P.op("pool", lambda e: e.affine_select(out=Em[:], in_=Em[:], pattern=[[-1, 128]], compare_op=ALU.is_ge,
                                               fill=0.0, base=15, channel_multiplier=16), reads=["Em"], writes=["Em"])
        ps, pk = self.psum()
        P.op("pe", lambda e: e.matmul(ps[:, 0:128], lhsT=Em[:], rhs=Em[:], start=True, stop=True), reads=["Em"], writes=[pk])
        P.op("dve", lambda e: e.tensor_copy(out=bd[:], in_=ps[:, 0:128]), reads=[pk], writes=["bd"])
        P.op("dve", lambda e: e.tensor_reduce(out=rowm[:], in_=bd[:].rearrange("p (q m w) -> p m q w", q=4, m=2, w=16),
                                              axis=AX.XY, op=ALU.add), reads=["bd"], writes=["rowm"])
        P.op("dve", lambda e: e.tensor_scalar(out=rowm[:], in0=rowm[:], scalar1=1.0 / 16, scalar2=None, op0=ALU.mult),
             reads=["rowm"], writes=["rowm"])
        self.Ecol = sb("Ecol", [128, 8], F32)
        ps2, pk2 = self.psum()
        P.op("pe", lambda e: e.transpose(out=ps2[:, 0:8], in_=Em[:], identity=self.identf[0:8, 0:8]), reads=["Em", "identf"], writes=[pk2])
        P.op("dve", lambda e: e.tensor_copy(out=self.Ecol[:], in_=ps2[:, 0:8]), reads=[pk2], writes=["Ecol"])
        self.gml = Ring("gml", 2)
        self.gml.tiles = [t[:, :].rearrange("p (m k) -> p m k", m=8) for t in self.hb.tiles]
        P.op("pool", lambda e: e.memset(sgn[0:64, :], 1.0), writes=["sgn"])
        P.op("pool", lambda e: e.memset(sgn[64:128, :], -1.0), writes=["sgn"], waw=False)

    def s5_main(self, i):
        P = self.P
        L, NSEQ = self.L, self.NSEQ
        NCH = L // 8
        self.s5_consts()
        P.barrier()
        KB = 1024
        ov = self.ov
        SM = ov(0, [128, 16, 32], F32)
        LAM = ov(2 * KB, [128, 2, 32], F32)
        MTr = [ov(2 * KB + 512 + k * 512, [128, 128], F32) for k in range(3)]
        AT = ov(4 * KB, [128, 32, 128], F32)
        Am = ov(20 * KB, [128, 32, 128], F32)
        Wsb = ov(36 * KB, [128, 8, 512], F32)
        Vsb = ov(52 * KB, [128, 9, 512], F32)
        PWre = ov(70 * KB, [128, 17, 32], F32)
        PWim = ov(70 * KB + 2176, [128, 17, 32], F32)
        B0 = self.ev32.tiles[0]
        C0 = self.ev32.tiles[1]
        Vpad = self.xt.tiles[0][:, :, :].rearrange("p a b -> p (a b)").bitcast(BF16).rearrange(
            "p (m g c) -> p m g c", m=8, g=32)
        WTf = self.xt.tiles[1][:, :, :].rearrange("p a b -> p (a b)").bitcast(BF16)[:, 0:4096].rearrange(
            "p (t j k) -> p t j k", t=4, j=8)
        Kblk = self.hT.tiles[0][:, :, :].rearrange("p a b -> p (a b)").rearrange("p (t j k) -> p t j k", t=4, j=8)
        WG = self.hT.tiles[1][:, 0:4, :]
        s5sm = self.s5sm
        identf, Jm, nJm = self.identf, self.Jm, self.nJm
        sm = lambda k: SM[:, k, :]
        K = lambda n: ("s5", n)

        P.dma(lambda e: e.dma_start(out=LAM, in_=self.s5_lam[i]), writes=[K("LAM")], dkey=K("LAM"))
        P.dma(lambda e: e.dma_start(out=sm(0), in_=self.s5_dt[i]), writes=[K("sm0")], dkey=K("sm0"))
        P.dma(lambda e: e.dma_start(out=B0[:].rearrange("p (g c) -> p g c", g=32), in_=self.s5_b0[i]),
              writes=[("ev32", 0)], dkey=("ev32", 0))
        P.dma(lambda e: e.dma_start(out=C0[:].rearrange("p (g c) -> p g c", g=32), in_=self.s5_c0[i]),
              writes=[("ev32", 1)], dkey=("ev32", 1))
        P.dma(lambda e: e.dma_start(out=s5sm[:, 0:4], in_=self.s5_d[i]), writes=["s5sm"], dkey="s5sm")
        P.dma(lambda e: e.dma_start(out=s5sm[:, 4:8], in_=self.s5_b_glu[i]), writes=["s5sm"], dkey="s5sm")
        for kc in range(4):
            st, sk = self.nxt(self.wstage)
            P.dma(lambda e, st=st, kc=kc: e.dma_start(out=st[:, :512], in_=self.s5_w_glu[i, kc * 128:(kc + 1) * 128, :]),
                  writes=[sk], dkey=sk)
            P.op("dve", lambda e, st=st, kc=kc: e.tensor_copy(out=WG[:, kc, :], in_=st[:, :512]), reads=[sk],
                 writes=[K("WG")], waw=(kc == 0))
        lre, lim = LAM[:, 0, :], LAM[:, 1, :]
        kl = K("LAM")

        def tt(eng, out, a, b, op, rk, wk):
            P.op(eng, lambda e: e.tensor_tensor(out=out, in0=a, in1=b, op=op), reads=rk, writes=wk)

        def ts(eng, out, a, s1, s2, op0, op1, rk, wk):
            if op1 is None:
                P.op(eng, lambda e: e.tensor_scalar(out=out, in0=a, scalar1=s1, scalar2=None, op0=op0), reads=rk, writes=wk)
            else:
                P.op(eng, lambda e: e.tensor_scalar(out=out, in0=a, scalar1=s1, scalar2=s2, op0=op0, op1=op1),
                     reads=rk, writes=wk)

        S = K("SM")
        P.op("act", lambda e: e.activation(out=sm(0), in_=sm(0), func=AF.Exp), reads=[K("sm0")], writes=[S])
        tt("dve", sm(1), lre, sm(0), ALU.mult, [kl, S], [S])
        P.op("act", lambda e: e.activation(out=sm(1), in_=sm(1), func=AF.Exp), reads=[S], writes=[S])
        tt("dve", sm(2), lim, sm(0), ALU.mult, [kl, S], [S])
        ts("dve", sm(3), sm(2), math.pi / 2, None, ALU.add, None, [S], [S])
        for src in (2, 3):
            ts("dve", sm(6), sm(src), 0.0, None, ALU.mult, None, [S], [S])
            for j in range(6):
                ts("dve", sm(5), sm(src), (2 * j + 1) * math.pi, -2 * math.pi, ALU.is_ge, ALU.mult, [S], [S])
                tt("dve", sm(6), sm(6), sm(5), ALU.add, [S], [S])
            tt("dve", sm(src), sm(src), sm(6), ALU.add, [S], [S])
        P.op("act", lambda e: e.activation(out=sm(7), in_=sm(2), func=AF.Sin), reads=[S], writes=[S])
        P.op("act", lambda e: e.activation(out=sm(8), in_=sm(3), func=AF.Sin), reads=[S], writes=[S])
        PK = K("PW")
        tt("dve", PWre[:, 1, :], sm(1), sm(8), ALU.mult, [S], [PK])
        tt("dve", PWim[:, 1, :], sm(1), sm(7), ALU.mult, [S], [PK])
        are, aim = PWre[:, 1, :], PWim[:, 1, :]
        tt("dve", sm(9), lre, lre, ALU.mult, [kl], [S])
        tt("dve", sm(10), lim, lim, ALU.mult, [kl], [S])
        tt("dve", sm(9), sm(9), sm(10), ALU.add, [S], [S])
        P.op("dve", lambda e: e.reciprocal(out=sm(10), in_=sm(9)), reads=[S], writes=[S])
        ts("dve", sm(11), are, -1.0, None, ALU.add, None, [PK], [S])
        tt("dve", sm(12), sm(11), lre, ALU.mult, [S, kl], [S])
        tt("dve", sm(13), aim, lim, ALU.mult, [PK, kl], [S])
        tt("dve", sm(12), sm(12), sm(13), ALU.add, [S], [S])
        tt("dve", PWre[:, 0, :], sm(12), sm(10), ALU.mult, [S], [PK])
        tt("dve", sm(12), aim, lre, ALU.mult, [PK, kl], [S])
        tt("dve", sm(13), sm(11), lim, ALU.mult, [S, kl], [S])
        tt("dve", sm(12), sm(12), sm(13), ALU.subtract, [S], [S])
        tt("dve", PWim[:, 0, :], sm(12), sm(10), ALU.mult, [S], [PK])

        def cmul(dst, a, b):
            tt("dve", sm(12), PWre[:, a, :], PWre[:, b, :], ALU.mult, [PK], [S])
            tt("dve", sm(13), PWim[:, a, :], PWim[:, b, :], ALU.mult, [PK], [S])
            tt("dve", sm(14), PWre[:, a, :], PWim[:, b, :], ALU.mult, [PK], [S])
            tt("dve", sm(15), PWim[:, a, :], PWre[:, b, :], ALU.mult, [PK], [S])
            tt("dve", PWre[:, dst, :], sm(12), sm(13), ALU.subtract, [S], [PK])
            tt("dve", PWim[:, dst, :], sm(14), sm(15), ALU.add, [S], [PK])

        for k in range(2, 9):
            cmul(k, k - 1, 1)
        for k in range(9, 17):
            cmul(k, k - 1, k - 1)

        if S5_STOP == 1:
            return
        def build_M(out, pidx, g, transpose, eng2="dve", rk=(), wk=()):
            P.op("act", lambda e: e.activation(out=out, in_=identf[:], func=AF.Copy, scale=PWre[:, pidx, g:g + 1]),
                 reads=[PK, "identf"] + list(rk), writes=list(wk))
            Jx = nJm if transpose else Jm
            P.op("dve", lambda e: e.scalar_tensor_tensor(out=out, in0=Jx[:], scalar=PWim[:, pidx, g:g + 1], in1=out,
                                                         op0=ALU.mult, op1=ALU.add),
                 reads=[PK, "Jm", "nJm"] + list(wk), writes=list(wk))

        for g in range(32):
            build_M(AT[:, g, :], 1, g, True, wk=[K("AT")])
            build_M(Am[:, g, :], 1, g, False, wk=[K("A")])
        if S5_STOP == 2:
            return
        psW, pkW = self.psum()
        mi = 0
        for g in range(32):
            mt = MTr[mi % 3]
            mk = K(("MT", mi % 3))
            mi += 1
            build_M(mt, 0, g, True, wk=[mk])
            P.op("pe", lambda e, g=g, mt=mt: e.matmul(psW[:, g * 16:(g + 1) * 16], lhsT=mt, rhs=B0[:, g * 16:(g + 1) * 16],
                                                      start=True, stop=True, skip_group_check=True),
                 reads=[mk, ("ev32", 0)], writes=[pkW], waw=(g == 0))
        P.op("act", lambda e: e.copy(out=Wsb[:, 0, :], in_=psW[:]), reads=[pkW], writes=[K("W0")])
        for j in range(1, 8):
            psW, pkW = self.psum()
            for g in range(32):
                P.op("pe", lambda e, g=g, j=j, psW=psW: e.matmul(
                    psW[:, g * 16:(g + 1) * 16], lhsT=AT[:, g, :], rhs=Wsb[:, j - 1, g * 16:(g + 1) * 16],
                    start=True, stop=True, skip_group_check=True), reads=[K("AT"), K("W%d" % (j - 1))], writes=[pkW],
                    waw=(g == 0))
            P.op("act", lambda e, j=j, psW=psW: e.copy(out=Wsb[:, j, :], in_=psW[:]), reads=[pkW], writes=[K("W%d" % j)])
        if S5_STOP == 3:
            return
        P.op("dve", lambda e: e.tensor_scalar(out=Vsb[:, 0, :], in0=C0[:], scalar1=self.sgn[:, 0:1], scalar2=None,
                                              op0=ALU.mult), reads=[("ev32", 1), "sgn"], writes=[K("V0")])
        for j in range(1, 9):
            psV, pkV = self.psum()
            for g in range(32):
                P.op("pe", lambda e, g=g, j=j, psV=psV: e.matmul(
                    psV[:, g * 16:(g + 1) * 16], lhsT=Am[:, g, :], rhs=Vsb[:, j - 1, g * 16:(g + 1) * 16],
                    start=True, stop=True, skip_group_check=True), reads=[K("A"), K("V%d" % (j - 1))], writes=[pkV],
                    waw=(g == 0))
            P.op("act", lambda e, j=j, psV=psV: e.copy(out=Vsb[:, j, :], in_=psV[:]), reads=[pkV], writes=[K("V%d" % j)])
        if S5_STOP == 4:
            return
        for T in range(4):
            for tau in range(8):
                ps, pk = self.psum()
                P.op("pe", lambda e, T=T, tau=tau, ps=ps: e.matmul(
                    ps[:, 0:128], lhsT=Wsb[:, tau, T * 128:(T + 1) * 128], rhs=Vsb[:, 0, T * 128:(T + 1) * 128],
                    start=True, stop=True), reads=[K("W%d" % tau), K("V0")], writes=[pk])
                if tau == 0:
                    tmp, tk = self.nxt(self.wk32)
                    P.op("dve", lambda e, ps=ps, tmp=tmp: e.tensor_tensor(out=tmp[:, 0:128], in0=ps[:, 0:128], in1=self.bd[:],
                                                                          op=ALU.mult), reads=[pk, "bd"], writes=[tk])
                    P.op("dve", lambda e, T=T, tmp=tmp: e.scalar_tensor_tensor(
                        out=Kblk[:, T, 0, :], in0=identf[:], scalar=s5sm[:, T:T + 1], in1=tmp[:, 0:128], op0=ALU.mult,
                        op1=ALU.add), reads=[tk, "s5sm", "identf"], writes=[K("Kblk")], waw=False)
                else:
                    P.op("dve", lambda e, T=T, tau=tau, ps=ps: e.tensor_tensor(
                        out=Kblk[:, T, tau, :], in0=ps[:, 0:128], in1=self.bd[:], op=ALU.mult), reads=[pk, "bd"],
                        writes=[K("Kblk")], waw=False)
        if S5_STOP == 5:
            return
        for T in range(4):
            for j in range(8):
                ps, pk = self.psum()
                P.op("pe", lambda e, T=T, j=j, ps=ps: e.transpose(out=ps[:, 0:128], in_=Wsb[:, j, T * 128:(T + 1) * 128],
                                                                  identity=identf[:]),
                     reads=[K("W%d" % j), "identf"], writes=[pk])
                P.op("act", lambda e, T=T, j=j, ps=ps: e.copy(out=WTf[:, T, j, :], in_=ps[:, 0:128]),
                     reads=[pk], writes=[K("WTm")], waw=False)
        if S5_STOP == 6:
            return
        P.op("pool", lambda e: e.memset(Vpad.rearrange("p m g c -> p (m g c)"), 0.0), writes=[K("Vpad")])
        for m in range(8):
            for mem in range(2):
                P.op("dve" if mem else "pool", lambda e, m=m, mem=mem: e.tensor_copy(
                    out=Vpad[:, m, mem::2, 16 * mem:16 * mem + 16],
                    in_=Vsb[:, m + 1, :].rearrange("p (g c) -> p g c", g=32)[:, mem::2, :]),
                    reads=[K("V%d" % (m + 1)), K("Vpad")], writes=[K("Vpad")], waw=False)

        if S5_STOP == 7:
            return
        if os.environ.get("DBG_S5") == "1":
            dbg = {"dPWre": (PWre, [128, 17, 32]), "dPWim": (PWim, [128, 17, 32]), "dWsb": (Wsb, [128, 8, 512]),
                   "dVsb": (Vsb, [128, 9, 512])}
            P.barrier()
            for nm, (ap_, shp) in dbg.items():
                dt_ = self.nc.dram_tensor(nm, shp, F32, kind="ExternalOutput").ap()
                P.dma(lambda e, ap_=ap_, dt_=dt_: e.dma_start(out=dt_, in_=ap_), writes=[("dbg", nm)], dkey=("dbg", nm))
            dk = self.nc.dram_tensor("dKblk", [128, 4, 8, 128], BF16, kind="ExternalOutput").ap()
            P.dma(lambda e: e.dma_start(out=dk, in_=Kblk), writes=[("dbg", "k")], dkey=("dbg", "k"))
            P.barrier()
        yield
        Wof = self.Wo[:, :, :].rearrange("p a b -> p (a b)")

        def wov(off, shape, dt):
            n = int(np.prod(shape[1:]))
            esz = 4 if dt == F32 else 2
            a = Wof[:, off // 2: off // 2 + n * esz // 2]
            if dt == F32:
                a = a.bitcast(F32)
            if len(shape) == 3:
                a = a.rearrange("p (a b) -> p a b", a=shape[1])
            return a

        P.barrier()
        U = ov(0, [128, L], BF16)
        Hprev = ov(2 * L, [128, 8, NCH], BF16)
        b1 = 2 * L + 16 * NCH
        ytile = ov(b1, [128, L], F32)
        cH32 = [ov(b1 + 4 * L + k * 2 * KB, [128, 512], F32) for k in range(8)]
        cHb = [ov(b1 + 4 * L + 16 * KB + k * KB, [128, 512], BF16) for k in range(8)]
        assert b1 + 4 * L + 24 * KB <= 70 * KB
        y2gs = [ov(b1, [128, 4, 512], F32), ov(b1 + 4 * L + 24 * KB, [128, 4, 512], F32)]
        y2bfs = [ov(b1 + 8 * KB, [128, 4, 512], BF16), ov(b1 + 4 * L + 32 * KB, [128, 4, 512], BF16)]
        assert b1 + 4 * L + 36 * KB <= 70 * KB
        if not hasattr(self, "y2T"):
            self.y2T = self.dram("y2T", [NSEQ, 512, L], F32)
        nsteps = int(math.log2(NCH))
        xt1b = self.xt.tiles[1][:, :, :].rearrange("p a b -> p (a b)").bitcast(BF16)[:, 4096:8192].rearrange(
            "p (c k) -> p c k", c=32)
        cgm = [[xt1b[:, ci * 4 + r, :] for r in range(4)] for ci in range(8)]
        hT1b = self.hT.tiles[1][:, 4:8, :].rearrange("p a b -> p (a b)").rearrange("p (c k) -> p c k", c=16)
        cMT = [[hT1b[:, ci * 2 + r, :] for r in range(2)] for ci in range(8)]
        YT = K("ytile")
        for s in range(NSEQ):
            for T in range(4):
                P.dma(lambda e, s=s, T=T: e.dma_start(out=U, in_=self.uT[s, T * 128:(T + 1) * 128, :]),
                      reads=[("uT", s)], writes=[K("U")], dkey=K("U"))

                def gchain(g8, ci, T=T, s=s):
                    g = 8 * T + g8
                    H32, hk32 = cH32[ci], K(("cH32", ci))
                    Hb, hkb = cHb[ci], K(("cHb", ci))
                    psG, pkG = self.psum()
                    for m in range(8):
                        gmt = cgm[ci][m % 4]
                        gmk = K(("cgm", ci, m % 4))
                        if m % 2:
                            P.op("act", lambda e, m=m, gmt=gmt: e.activation(
                                out=gmt, in_=WTf[:, T, 7 - m, :], func=AF.Copy, scale=self.Ecol[:, g8:g8 + 1]),
                                reads=[K("WTm"), "Ecol"], writes=[gmk])
                        else:
                            P.op("dve", lambda e, m=m, gmt=gmt: e.tensor_scalar(
                                out=gmt, in0=WTf[:, T, 7 - m, :], scalar1=self.Ecol[:, g8:g8 + 1], scalar2=None,
                                op0=ALU.mult), reads=[K("WTm"), "Ecol"], writes=[gmk])
                        P.op("pe", lambda e, m=m, gmt=gmt: e.matmul(
                            psG[:, :NCH], lhsT=gmt, rhs=U[:, m::8], start=(m == 0), stop=(m == 7)),
                            reads=[gmk, K("U")], writes=[pkG], waw=(m == 0))
                    P.op("act", lambda e: e.copy(out=H32[:, :NCH], in_=psG[:, :NCH]), reads=[pkG], writes=[hk32])
                    P.op("dve", lambda e: e.tensor_copy(out=Hb[:, :NCH], in_=psG[:, :NCH]), reads=[pkG], writes=[hkb])
                    yield
                    for j in range(nsteps):
                        sft = 1 << j
                        mt = cMT[ci][j % 2]
                        mk = K(("cMT", ci, j % 2))
                        build_M(mt, 8 + j, g, True, wk=[mk])
                        yield
                        psS, pkS = self.psum()
                        P.op("pe", lambda e, mt=mt, sft=sft, psS=psS: e.matmul(
                            psS[:, :NCH - sft], lhsT=mt, rhs=Hb[:, 0:NCH - sft], start=True, stop=True),
                            reads=[mk, hkb], writes=[pkS])
                        P.op("dve", lambda e, psS=psS, sft=sft: e.tensor_tensor(
                            out=H32[:, sft:NCH], in0=H32[:, sft:NCH], in1=psS[:, :NCH - sft], op=ALU.add),
                            reads=[hk32, pkS], writes=[hk32])
                        yield
                        if j < nsteps - 1:
                            P.op("act", lambda e, sft=sft: e.copy(out=Hb[:, sft:NCH], in_=H32[:, sft:NCH]),
                                 reads=[hk32], writes=[hkb])
                            yield
                    P.op("act", lambda e: e.copy(out=Hprev[:, g8, :], in_=H32[:, 0:NCH]),
                         reads=[hk32], writes=[K(("Hp", g8))])
                    yield

                yield from interleave([gchain(ci, ci) for ci in range(8)])
                for m in range(8):
                    psY, pkY = self.psum()
                    for tau in range(m + 1):
                        P.op("pe", lambda e, T=T, tau=tau, m=m, psY=psY: e.matmul(
                            psY[:, :NCH], lhsT=Kblk[:, T, tau, :], rhs=U[:, (m - tau)::8], start=(tau == 0), stop=False,
                            skip_group_check=True), reads=[K("Kblk"), K("U")], writes=[pkY], waw=(tau == 0))
                    for g8 in range(8):
                        q = g8 // 2
                        P.op("pe", lambda e, T=T, g8=g8, q=q, m=m, psY=psY: e.matmul(
                            psY[32 * q:32 * q + 32, 1:NCH], lhsT=Vpad[:, m, 8 * T + g8, :], rhs=Hprev[:, g8, 0:NCH - 1],
                            start=False, stop=(g8 == 7), tile_position=(0, 32 * q), skip_group_check=True),
                            reads=[K("Vpad"), K(("Hp", g8))], writes=[pkY], waw=False)
                    P.op("act", lambda e, m=m, psY=psY: e.copy(out=ytile[:, m::8], in_=psY[:, :NCH]), reads=[pkY],
                         writes=[YT], waw=(m == 0))
                    yield
                for c in range(L // 512):
                    cs = slice(c * 512, (c + 1) * 512)
                    t1, t1k = self.nxt(self.wk32)
                    P.op("dve", lambda e, t1=t1, cs=cs: e.tensor_tensor(out=t1[:], in0=ytile[:, cs], in1=ytile[:, cs], op=ALU.mult),
                         reads=[YT], writes=[t1k])
                    P.op("pool", lambda e, t1=t1: e.tensor_scalar(out=t1[:], in0=t1[:], scalar1=0.044715, scalar2=1.0,
                                                                  op0=ALU.mult, op1=ALU.add), reads=[t1k], writes=[t1k])
                    yield
                    P.op("dve", lambda e, t1=t1, cs=cs: e.tensor_tensor(out=t1[:], in0=t1[:], in1=ytile[:, cs], op=ALU.mult),
                         reads=[t1k, YT], writes=[t1k])
                    P.op("act", lambda e, t1=t1: e.activation(out=t1[:], in_=t1[:], func=AF.Sigmoid, scale=1.5957691216),
                         reads=[t1k], writes=[t1k])
                    yield
                    ob, obk = self.nxt(self.ev32)
                    P.op("dve", lambda e, t1=t1, ob=ob, cs=cs: e.tensor_tensor(out=ob[:], in0=t1[:], in1=ytile[:, cs], op=ALU.mult),
                         reads=[t1k, YT], writes=[obk])
                    self.store(ob[:], obk, self.y2T[s, T * 128:(T + 1) * 128, cs], ("y2T", s))
                    yield
            for tg in range(self.NG):
                cs = slice(tg * 512, (tg + 1) * 512)
                yg = y2gs[tg % 2]
                y2bf = y2bfs[tg % 2]
                ygk = K(("y2g", tg % 2))
                ybk = K(("y2bf", tg % 2))
                P.dma(lambda e, yg=yg, cs=cs, s=s: e.dma_start(out=yg, in_=self.y2T[s, :, cs].rearrange("(t p) l -> p t l", p=128)),
                      reads=[("y2T", s)], writes=[ygk, YT], dkey=ygk)
                P.op("act", lambda e, yg=yg, y2bf=y2bf: e.copy(out=y2bf[:, 0:2, :], in_=yg[:, 0:2, :]), reads=[ygk, YT], writes=[ybk])
                P.op("dve", lambda e, yg=yg, y2bf=y2bf: e.tensor_copy(out=y2bf[:, 2:4, :], in_=yg[:, 2:4, :]), reads=[ygk, YT],
                     writes=[ybk], waw=False)
                yield
                for oc in range(4):
                    ps, pk = self.psum()
                    for kc in range(4):
                        P.op("pe", lambda e, ps=ps, kc=kc, oc=oc, y2bf=y2bf: e.matmul(
                            ps[:], lhsT=WG[:, kc, oc * 128:(oc + 1) * 128], rhs=y2bf[:, kc, :], start=(kc == 0), stop=(kc == 3)),
                            reads=[K("WG"), ybk, YT], writes=[pk], waw=(kc == 0))
                    sg, sgk = self.nxt(self.wk32)
                    P.op("act", lambda e, ps=ps, sg=sg, oc=oc: e.activation(out=sg[:], in_=ps[:], func=AF.Sigmoid,
                                                                            bias=s5sm[:, 4 + oc:5 + oc]),
                         reads=[pk, "s5sm"], writes=[sgk])
                    gzt, gzk = self.nxt(self.ev32)
                    P.dma(lambda e, gzt=gzt, oc=oc, cs=cs, s=s: e.dma_start(out=gzt[:], in_=self.gzT[s, 1024 + oc * 128:1024 + (oc + 1) * 128, cs]),
                          reads=[("gzT", s)], writes=[gzk], dkey=(gzk, "ld"))
                    P.op("dve", lambda e, sg=sg, yg=yg, oc=oc: e.tensor_tensor(out=sg[:], in0=sg[:], in1=yg[:, oc, :], op=ALU.mult),
                         reads=[sgk, ygk, YT], writes=[sgk])
                    ob, obk = self.nxt(self.evbf)
                    P.op("dve", lambda e, sg=sg, gzt=gzt, ob=ob: e.tensor_tensor(out=ob[:], in0=sg[:], in1=gzt[:], op=ALU.mult),
                         reads=[sgk, gzk], writes=[obk])
                    self.store(ob[:], obk, self.mixT[s, 1024 + oc * 128:1024 + (oc + 1) * 128, cs], ("mixT", s))
                    yield

    def odd_layer(self, i, x_in, kin, x_out, kout):
        P = self.P
        L, NSEQ = self.L, self.NSEQ
        KB = 1024
        P.barrier()
        full_Wb = self.Wb
        self.Wb = self.arena[:, 0:8 * 3088].rearrange("p (k f) -> p k f", k=8)
        self.load_weights(self.odd_w_in[i], 8, 3088, self.Wb, "Wb", gain_dram=self.odd_norm_g[i])
        self.load_weights(self.odd_w_out[i], 8, 1024, self.Wo, "Wo")
        self.s5_consts()
        sm = self.s5sm
        wg16 = self.ov(50 * KB, [128, 512], BF16)[0:16, :]
        st, sk = self.nxt(self.wstage)
        P.dma(lambda e: e.dma_start(out=st[0:16, 0:512], in_=self.gla_w_gate[i]), writes=[sk], dkey=sk)
        P.op("dve", lambda e: e.tensor_copy(out=wg16, in_=st[0:16, 0:512]), reads=[sk], writes=["wg16"])
        P.dma(lambda e: e.dma_start(out=sm[:, 0:4], in_=self.gla_b_gate[i]), writes=["s5sm"], dkey="s5sm")
        P.dma(lambda e: e.dma_start(out=sm[:, 8:10], in_=self.gla_o_g[i]), writes=["s5sm"], dkey="s5sm")
        P.op("dve", lambda e: e.tensor_scalar(out=sm[:, 0:4], in0=sm[:, 0:4], scalar1=-1.0, scalar2=None, op0=ALU.mult),
             reads=["s5sm"], writes=["s5sm"])
        groups = [(s, tg) for s in range(NSEQ) for tg in range(self.NG)]
        for k, (s, tg) in enumerate(groups):
            if k == 0:
                self.prefetch_x(x_in, kin, s, tg)
            if k + 1 < len(groups):
                self.prefetch_x(x_in, kin, *groups[k + 1])
            self.odd_inproj_group(s, tg, x_in, kin, wg16)
        self.gla_phase()
        for s in range(NSEQ):
            self.out_proj(s, x_in, kin, x_out, kout, 8, self.mixT, "mixT")
        self.Wb = full_Wb

    def odd_inproj_group(self, s, tg, x_in, kin, wg16):
        P = self.P
        hT, hk, xt, xk = self.load_norm_group(x_in, kin, s, tg)
        cols = slice(tg * 512, (tg + 1) * 512)
        sm = self.s5sm
        for which, dst, dkey in ((0, self.q2T, "q2T"), (1, self.k2T, "k2T")):
            for c in range(4):
                ps, pk = self.proj_fm(hT, hk, which * 512 + c * 128)
                ob, obk = self.nxt(self.ev32)
                P.op("act" if c % 2 else "dve",
                     (lambda e, ps=ps, ob=ob: e.copy(out=ob[:], in_=ps[:])) if c % 2 else
                     (lambda e, ps=ps, ob=ob: e.tensor_copy(out=ob[:], in_=ps[:])), reads=[pk], writes=[obk])
                self.store(ob[:], obk, dst[s, c * 128:(c + 1) * 128, cols], (dkey, s))
        for j in range(4):
            for half in range(2):
                ps, pk = self.proj_tm(hT, hk, j, 1024 + half * 512)
                ob, obk = self.nxt(self.evbf)
                P.op("act" if half else "dve",
                     (lambda e, ps=ps, ob=ob: e.copy(out=ob[:], in_=ps[:])) if half else
                     (lambda e, ps=ps, ob=ob: e.tensor_copy(out=ob[:], in_=ps[:])), reads=[pk], writes=[obk])
                r0 = tg * 512 + j * 128
                self.store(ob[:], obk, self.vtm[s, r0:r0 + 128, half * 512:(half + 1) * 512], ("vtm", s))
        for c in range(8):
            ps, pk = self.proj_fm(hT, hk, 2048 + c * 128)
            ob, obk = self.nxt(self.ev32)
            P.op("act", lambda e, ps=ps, ob=ob: e.activation(out=ob[:], in_=ps[:], func=AF.Silu), reads=[pk], writes=[obk])
            self.store(ob[:], obk, self.gzT[s, c * 128:(c + 1) * 128, cols], ("gzT", s))
        ps, pk = self.proj_fm(hT, hk, 3072, M=16)
        rT, rk = self.nxt(self.wkbf)
        P.op("dve", lambda e, ps=ps, rT=rT: e.tensor_copy(out=rT[0:16, :], in_=ps[0:16, :]), reads=[pk], writes=[rk])
        for c in range(4):
            ps2, pk2 = self.psum()
            P.op("pe", lambda e, ps2=ps2, c=c, rT=rT: e.matmul(ps2[:], lhsT=wg16[:, c * 128:(c + 1) * 128], rhs=rT[0:16, :],
                                                               start=True, stop=True), reads=[rk, "wg16"], writes=[pk2])
            ex, exk = self.nxt(self.wk32)
            P.op("act", lambda e, ps2=ps2, ex=ex, c=c: e.activation(out=ex[:], in_=ps2[:], func=AF.Exp, scale=-1.0,
                                                                    bias=sm[:, c:c + 1]), reads=[pk2, "s5sm"], writes=[exk])
            ob, obk = self.nxt(self.ev32)
            P.op("act", lambda e, ex=ex, ob=ob: e.activation(out=ob[:], in_=ex[:], func=AF.Ln, bias=1.0), reads=[exk], writes=[obk])
            self.store(ob[:], obk, self.lgT[s, c * 128:(c + 1) * 128, cols], ("lgT", s))

    def gla_phase(self):
        P = self.P
        L, NSEQ = self.L, self.NSEQ
        KB = 1024
        ov = self.ov
        P.barrier()
        NCHN = 2
        CH = 38 * KB
        rmask = ov(NCHN * CH, [128, 512], F32)
        P.op("pool", lambda e: e.memset(rmask, 1.0), writes=["rmask"])
        for c in range(4):
            P.op("pool", lambda e, c=c: e.memset(rmask[:, c * 128:c * 128 + 1], 0.0), writes=["rmask"], waw=False)
        assert NCHN * CH + 2 * KB <= 80 * KB
        sm = self.s5sm
        psrot = [0]

        def lpsum():
            k = 4 + psrot[0] % 3
            psrot[0] += 1
            return self.psM[k], ("psM", k)

        def chain(s, h, ci):
            b = ci * CH
            T = lambda n: ("gla", n, ci)
            q32 = ov(b, [128, 512], F32)
            k32 = ov(b + 2 * KB, [128, 512], F32)
            Lg = ov(b + 4 * KB, [128, 512], F32)
            G = ov(b + 6 * KB, [128, 512], F32)
            E1 = ov(b + 8 * KB, [128, 512], F32)
            E2 = ov(b + 10 * KB, [128, 512], F32)
            gz = ov(b + 12 * KB, [128, 2, 512], F32)
            vt = ov(b + 16 * KB, [128, 4, 256], BF16)
            qd = ov(b + 18 * KB, [128, 512], BF16)
            ki = ov(b + 19 * KB, [128, 512], BF16)
            kdT = ov(b + 20 * KB, [128, 512], BF16)
            kdec = ov(b + 21 * KB, [128, 4, 128], BF16)
            S32 = ov(b + 22 * KB, [128, 256], F32)
            Sbf = ov(b + 23 * KB, [128, 256], BF16)
            scT = [ov(b + 23 * KB + 512 + k * 256, [128, 128], BF16) for k in range(2)]
            rs = ov(b + 24 * KB, [128, 512], F32)
            psO = [self.psM[2 * ci], self.psM[2 * ci + 1]]
            psOk = [("psM", 2 * ci), ("psM", 2 * ci + 1)]
            psT = self.psT[0]
            P.op("pool", lambda e: e.memset(S32, 0.0), writes=[T("S32")])
            P.op("pool", lambda e: e.memset(Sbf, 0.0), writes=[T("Sbf")])
            LD = [dict(q32=q32, k32=k32, Lg=Lg, vt=vt, gz=gz),
                  dict(q32=ov(b + 26 * KB, [128, 512], F32), k32=ov(b + 28 * KB, [128, 512], F32),
                       Lg=ov(b + 30 * KB, [128, 512], F32), vt=ov(b + 32 * KB, [128, 4, 256], BF16),
                       gz=ov(b + 34 * KB, [128, 2, 512], F32))]

            def issue(tg):
                d = LD[tg % 2]
                r = tg % 2
                cs = slice(tg * 512, (tg + 1) * 512)
                P.dma(lambda e: e.dma_start(out=d["q32"], in_=self.q2T[s, h * 128:(h + 1) * 128, cs]),
                      reads=[("q2T", s)], writes=[T(("q32", r))], dkey=T(("q32", r)))
                P.dma(lambda e: e.dma_start(out=d["k32"], in_=self.k2T[s, h * 128:(h + 1) * 128, cs]),
                      reads=[("k2T", s)], writes=[T(("k32", r))], dkey=T(("k32", r)))
                P.dma(lambda e: e.dma_start(out=d["Lg"], in_=self.lgT[s, h * 128:(h + 1) * 128, cs]),
                      reads=[("lgT", s)], writes=[T(("Lg", r))], dkey=T(("Lg", r)))
                P.dma(lambda e: e.dma_start(out=d["vt"], in_=self.vtm[s, cs, h * 256:(h + 1) * 256].rearrange(
                    "(j p) d -> p j d", p=128)), reads=[("vtm", s)], writes=[T(("vt", r))], dkey=T(("vt", r)))
                P.dma(lambda e: e.dma_start(out=d["gz"], in_=self.gzT[s, h * 256:(h + 1) * 256, cs].rearrange(
                    "(v p) t -> p v t", p=128)), reads=[("gzT", s)], writes=[T(("gz", r))], dkey=T(("gz", r)))

            issue(0)
            for tg in range(self.NG):
                cs = slice(tg * 512, (tg + 1) * 512)
                if tg + 1 < self.NG:
                    issue(tg + 1)
                r = tg % 2
                q32, k32, Lg, vt, gz = (LD[r][n] for n in ("q32", "k32", "Lg", "vt", "gz"))
                Tq, Tk, TL, Tv, Tg = (T((n, r)) for n in ("q32", "k32", "Lg", "vt", "gz"))
                P.op("dve", lambda e, Lg=Lg: e.tensor_tensor_scan(out=G, data0=rmask, data1=Lg, initial=0.0, op0=ALU.mult,
                                                           op1=ALU.add), reads=[TL, "rmask"], writes=[T("G")])
                P.op("act", lambda e: e.activation(out=E1, in_=G, func=AF.Exp, scale=-1.0 / 16), reads=[T("G")], writes=[T("E1")])
                P.op("act", lambda e: e.activation(out=E2, in_=G, func=AF.Exp, scale=1.0 / 16), reads=[T("G")], writes=[T("E2")])
                P.op("dve", lambda e, q32=q32: e.scalar_tensor_tensor(out=qd, in0=q32, scalar=128 ** -0.5, in1=E1, op0=ALU.mult,
                                                             op1=ALU.mult), reads=[Tq, T("E1")], writes=[T("qd")])
                P.op("pool", lambda e, k32=k32: e.tensor_tensor(out=ki, in0=k32, in1=E2, op=ALU.mult), reads=[Tk, T("E2")],
                     writes=[T("ki")])
                for c in range(4):
                    cc = slice(c * 128, (c + 1) * 128)
                    P.op("dve", lambda e, c=c, cc=cc, k32=k32: e.scalar_tensor_tensor(
                        out=kdT[:, cc], in0=k32[:, cc], scalar=E1[:, c * 128 + 127:c * 128 + 128], in1=E2[:, cc],
                        op0=ALU.mult, op1=ALU.mult), reads=[Tk, T("E1"), T("E2")], writes=[T("kdT")], waw=(c == 0))
                yield
                for c in range(4):
                    cc = slice(c * 128, (c + 1) * 128)
                    P.op("pe", lambda e, cc=cc: e.transpose(out=psT[:, cc], in_=kdT[:, cc], identity=self.ident[:]),
                         reads=[T("kdT"), "ident"], writes=["psT"], waw=(c == 0))
                P.op("act", lambda e: e.copy(out=kdec, in_=psT[:, 0:512].rearrange("p (c k) -> p c k", c=4)),
                     reads=["psT"], writes=[T("kdec")])
                yield
                for c in range(4):
                    cc = slice(c * 128, (c + 1) * 128)
                    pss, pssk = lpsum()
                    P.op("pe", lambda e, cc=cc, pss=pss: e.matmul(pss[:, 0:128], lhsT=ki[:, cc], rhs=qd[:, cc], start=True,
                                                                  stop=True), reads=[T("ki"), T("qd")], writes=[pssk])
                    sc = scT[c % 2]
                    sck = T(("scT", c % 2))
                    P.op("dve", lambda e, pss=pss, sc=sc: e.tensor_tensor(out=sc, in0=pss[:, 0:128], in1=self.maskLE[:],
                                                                          op=ALU.mult), reads=[pssk, "maskLE"], writes=[sck])
                    for vc in range(2):
                        P.op("pe", lambda e, vc=vc, c=c, cc=cc, sc=sc, vt=vt: e.matmul(
                            psO[vc][:, cc], lhsT=vt[:, c, vc * 128:(vc + 1) * 128], rhs=sc, start=True, stop=False,
                            skip_group_check=True), reads=[Tv, sck], writes=[psOk[vc]], waw=(c == 0))
                        P.op("pe", lambda e, vc=vc, cc=cc: e.matmul(
                            psO[vc][:, cc], lhsT=Sbf[:, vc * 128:(vc + 1) * 128], rhs=qd[:, cc], start=False, stop=True,
                            skip_group_check=True), reads=[T("Sbf"), T("qd")], writes=[psOk[vc]], waw=False)
                    psS, psSk = lpsum()
                    P.op("pe", lambda e, c=c, psS=psS, vt=vt: e.matmul(psS[:, 0:256], lhsT=kdec[:, c, :], rhs=vt[:, c, :], start=True,
                                                                stop=True), reads=[T("kdec"), Tv], writes=[psSk])
                    P.op("dve", lambda e, c=c, psS=psS: e.scalar_tensor_tensor(
                        out=S32, in0=S32, scalar=E1[:, c * 128 + 127:c * 128 + 128], in1=psS[:, 0:256], op0=ALU.mult,
                        op1=ALU.add), reads=[T("S32"), T("E1"), psSk], writes=[T("S32")])
                    P.op("act", lambda e: e.copy(out=Sbf, in_=S32), reads=[T("S32")], writes=[T("Sbf")])
                    yield
                sqs = []
                for vc in range(2):
                    sq, sqk = self.nxt(self.wkbf)
                    P.op("act", lambda e, vc=vc, sq=sq: e.activation(out=sq[:], in_=psO[vc][:], func=AF.Square),
                         reads=[psOk[vc]], writes=[sqk])
                    sqs.append((sq, sqk))
                pn, pnk = lpsum()
                for vc in range(2):
                    P.op("pe", lambda e, vc=vc, pn=pn, sq=sqs[vc][0]: e.matmul(pn[:], lhsT=self.ones[:], rhs=sq[:], start=(vc == 0),
                                                                stop=(vc == 1)), reads=[sqs[vc][1], "ones"], writes=[pnk],
                         waw=(vc == 0))
                P.op("act", lambda e, pn=pn: e.activation(out=rs, in_=pn[:], func=AF.Ln, scale=1.0 / 256, bias=EPS),
                     reads=[pnk], writes=[T("rs")])
                P.op("act", lambda e: e.activation(out=rs, in_=rs, func=AF.Exp, scale=-0.5), reads=[T("rs")], writes=[T("rs")])
                for vc in range(2):
                    tmp, tk = self.nxt(self.wk32)
                    P.op("dve", lambda e, vc=vc, tmp=tmp: e.scalar_tensor_tensor(
                        out=tmp[:], in0=psO[vc][:], scalar=sm[:, 8 + vc:9 + vc], in1=rs, op0=ALU.mult, op1=ALU.mult),
                        reads=[psOk[vc], "s5sm", T("rs")], writes=[tk])
                    ob, obk = self.nxt(self.evbf)
                    if os.environ.get("GLA_DBG") == "1":
                        P.op("dve", lambda e, vc=vc, ob=ob: e.tensor_copy(out=ob[:], in_=psO[vc][:]), reads=[psOk[vc]], writes=[obk])
                    elif os.environ.get("GLA_DBG") == "2":
                        P.op("dve", lambda e, vc=vc, ob=ob, tmp=tmp: e.tensor_copy(out=ob[:], in_=tmp[:]), reads=[tk], writes=[obk])
                    else:
                        P.op("pool", lambda e, vc=vc, tmp=tmp, ob=ob, gz=gz: e.tensor_tensor(out=ob[:], in0=tmp[:], in1=gz[:, vc, :],
                                                                                      op=ALU.mult), reads=[tk, Tg], writes=[obk])
                    self.store(ob[:], obk, self.mixT[s, h * 256 + vc * 128:h * 256 + (vc + 1) * 128, cs], ("mixT", s))
                yield

        jobs = [(s, h) for s in range(NSEQ) for h in range(4)]
        for r0 in range(0, len(jobs), NCHN):
            run_interleaved([chain(s, h, ci) for ci, (s, h) in enumerate(jobs[r0:r0 + NCHN])])


def host_layout(inputs, nseq_total=16):
    f = lambda a: np.ascontiguousarray(np.asarray(a, dtype=np.float32))
    d = {}
    d["even_norm_g"] = f(inputs["even_norm_g"].reshape(2, 8, 128).transpose(0, 2, 1))
    d["odd_norm_g"] = f(inputs["odd_norm_g"].reshape(2, 8, 128).transpose(0, 2, 1))
    d["even_w_in"] = f(inputs["even_w_in"])
    d["even_w_out"] = f(inputs["even_w_out"])
    d["odd_w_in"] = f(inputs["odd_w_in"])
    d["odd_w_out"] = f(inputs["odd_w_out"])
    d["sb_q_norm_g"] = f(inputs["sb_q_norm_g"].reshape(2, 128, 1))
    d["sb_k_norm_g"] = f(inputs["sb_k_norm_g"].reshape(2, 128, 1))
    d["gla_w_gate"] = f(inputs["gla_w_gate"])
    d["gla_b_gate"] = f(inputs["gla_b_gate"].reshape(2, 4, 128).transpose(0, 2, 1))
    d["gla_o_norm_g"] = f(inputs["gla_o_norm_g"].reshape(2, 2, 128).transpose(0, 2, 1))
    lam = np.stack([inputs["s5_lambda_re"], inputs["s5_lambda_im"]], axis=1)
    lam = lam.transpose(0, 3, 1, 2)
    d["s5_lam"] = f(np.concatenate([lam, lam], axis=1))
    d["s5_log_dt"] = f(np.broadcast_to(inputs["s5_log_dt"][:, None, :], (2, 128, 32)))
    b = np.concatenate([inputs["s5_b_re"], inputs["s5_b_im"]], axis=2)
    d["s5_b0"] = f(b.transpose(0, 2, 1, 3))
    c = np.concatenate([inputs["s5_c_re"], inputs["s5_c_im"]], axis=3)
    d["s5_c0"] = f(c.transpose(0, 3, 1, 2))
    d["s5_d"] = f(inputs["s5_d"].reshape(2, 4, 128).transpose(0, 2, 1))
    d["s5_w_glu"] = f(inputs["s5_w_glu"])
    d["s5_b_glu"] = f(inputs["s5_b_glu"].reshape(2, 4, 128).transpose(0, 2, 1))
    return d


def kernel(**inputs):
    x = np.asarray(inputs["x"], dtype=np.float32)
    bsz, L, _ = x.shape
    nseq = bsz // N_CORES
    b = Builder(L=L, NSEQ=nseq, depth=4)
    nc = b.build()
    params = host_layout({k: np.asarray(v) for k, v in inputs.items() if k != "x"})
    in_maps = []
    for c in range(N_CORES):
        m = dict(params)
        m["x"] = np.ascontiguousarray(x[c * nseq:(c + 1) * nseq])
        in_maps.append(m)
    res = run_bass_kernel_spmd(nc, in_maps, core_ids=list(range(N_CORES)))
    return np.concatenate([r["y"] for r in res.results], axis=0).astype(np.float32)
```

```python
import math
from contextlib import ExitStack

import numpy as np
import concourse.bass as bass
import concourse.mybir as mybir
from concourse.bass_utils import run_bass_kernel_spmd

F32 = mybir.dt.float32
BF16 = mybir.dt.bfloat16
AF = mybir.ActivationFunctionType
ALU = mybir.AluOpType
AX = mybir.AxisListType

D_MODEL = 1024
EPS = 1e-6
N_CORES = 8
import os
OVERLAP = False
N_FILL = int(os.environ.get('N_FILL', '0'))
SB_W = int(os.environ.get('SB_W', '2'))
S5_W = int(os.environ.get('S5_W', '3'))
S5_STOP = int(os.environ.get('S5_STOP', '0'))


class _Op:
    __slots__ = ("eng", "fn", "deps", "val", "needed", "dkey", "dval", "same_ok")


class Prog:
    ENGS = ("pe", "act", "dve", "pool", "sp")

    def __init__(self, nc, es):
        self.nc = nc
        self.es = es
        self.q = {e: [] for e in self.ENGS}
        self.W = {}
        self.R = {}
        self.dcount = {}
        self.n_ops = 0
        self.last_op = {}
        self.last_dma = {}
        self.pending = {}

    @staticmethod
    def _is_psum(b):
        return (isinstance(b, tuple) and b[0] == "psM") or b == "psT"

    @staticmethod
    def _chan(op):
        return ("d", op.dkey) if op.dkey is not None else op.eng

    def op(self, eng, fn, reads=(), writes=(), dkey=None, waw=True, same_ok=False):
        if eng == "pe":
            same_ok = True
        o = _Op()
        o.eng, o.fn, o.val, o.needed, o.dkey, o.same_ok = eng, fn, 0, False, dkey, same_ok
        deps = {}

        def add(d):
            deps[id(d)] = d

        for b in reads:
            for d in self.W.get(b, {}).values():
                add(d)
            if self._is_psum(b):
                for chn, d in self.R.get(b, {}).items():
                    if chn != eng:
                        add(d)
        for b in writes:
            for d in self.R.get(b, {}).values():
                add(d)
            if waw:
                for d in self.W.get(b, {}).values():
                    add(d)
        if eng in self.pending:
            for d in self.pending.pop(eng):
                add(d)
        o.deps = list(deps.values())
        if dkey is None:
            self.last_op[eng] = o
        else:
            self.last_dma[dkey] = o
        if dkey is not None:
            self.dcount[dkey] = self.dcount.get(dkey, 0) + 16
            o.dval = self.dcount[dkey]
        ch = self._chan(o)
        for b in reads:
            self.R.setdefault(b, {})[ch] = o
        for b in writes:
            if waw:
                self.W[b] = {ch: o}
                self.R[b] = {}
            else:
                self.W.setdefault(b, {})[ch] = o
        self.q[eng].append(o)
        self.n_ops += 1
        return o

    def barrier(self):
        deps = list(self.last_op.values()) + list(self.last_dma.values())
        for e in self.ENGS:
            self.pending[e] = list(deps)

    def dma(self, fn, reads=(), writes=(), dkey=None, eng="sp", waw=False):
        assert dkey is not None
        return self.op(eng, fn, reads, writes, dkey=dkey, waw=waw)

    def finalize(self, final_keys=()):
        nc, es = self.nc, self.es
        for e in self.ENGS:
            for o in self.q[e]:
                for d in o.deps:
                    if d.dkey is None:
                        if d.eng == o.eng and o.same_ok:
                            continue
                        d.needed = True
        for e in self.ENGS:
            c = 0
            for o in self.q[e]:
                if o.dkey is None and o.needed:
                    c += 1
                    o.val = c
        esem = {e: es.enter_context(nc.semaphore("S_" + e)) for e in self.ENGS}
        dsem = {}
        for k in self.dcount:
            dsem[k] = es.enter_context(nc.semaphore("D%d" % len(dsem)))
        self.n_sems = len(esem) + len(dsem)
        handles = {"pe": "tensor", "act": "scalar", "dve": "vector", "pool": "gpsimd", "sp": "sync"}
        block = es.enter_context(nc.Block())
        final_waits = [(dsem[k], self.dcount[k]) for k in final_keys]
        for e in self.ENGS:
            ops = self.q[e]

            def body(eng, ops=ops, e=e):
                seen = {}
                for o in ops:
                    need = {}
                    for d in o.deps:
                        if d.dkey is not None:
                            key, v = ("d", d.dkey), d.dval
                        else:
                            if d.eng == e and o.same_ok:
                                continue
                            key, v = d.eng, d.val
                        if v > seen.get(key, 0) and v > need.get(key, 0):
                            need[key] = v
                    for key, v in need.items():
                        seen[key] = v
                        sem = dsem[key[1]] if isinstance(key, tuple) else esem[key]
                        eng.wait_ge(sem, v)
                    ins = o.fn(eng)
                    if o.dkey is not None:
                        ins.then_inc(dsem[o.dkey], 16)
                    elif o.needed:
                        ins.then_inc(esem[e], 1)
                if e == "sp":
                    for sem, v in final_waits:
                        eng.wait_ge(sem, v)

            getattr(block, handles[e])(body)


class Ring:
    def __init__(self, name, n):
        self.name, self.n, self.i = name, n, -1

    def next(self):
        self.i += 1
        return self.i % self.n

    def key(self, slot):
        return (self.name, slot)


def run_interleaved(gens):
    gens = list(gens)
    while gens:
        for g in list(gens):
            try:
                next(g)
            except StopIteration:
                gens.remove(g)


def interleave(gens):
    gens = list(gens)
    while gens:
        for g in list(gens):
            try:
                next(g)
            except StopIteration:
                gens.remove(g)
        yield


def interleave1(gens):
    gens = list(gens)
    while gens:
        for g in list(gens):
            try:
                next(g)
            except StopIteration:
                gens.remove(g)
                continue
            yield


def run_weighted(pairs):
    pairs = list(pairs)
    while pairs:
        for p in list(pairs):
            g, w = p
            for _ in range(w):
                try:
                    next(g)
                except StopIteration:
                    pairs.remove(p)
                    break


class Builder:
    def __init__(self, L=4096, NSEQ=2, depth=4, dump=()):
        self.L, self.NSEQ, self.depth, self.dump = L, NSEQ, depth, tuple(dump)
        self.NG = L // 512
        self.nc = bass.Bass("TRN2", target_bir_lowering=False)
        self.es = ExitStack()
        self.rings = {}

    def sb(self, name, shape, dt):
        return self.es.enter_context(self.nc.sbuf_tensor(name, list(shape), dt))

    def ps(self, name, shape, dt):
        return self.es.enter_context(self.nc.psum_tensor(name, list(shape), dt))

    def dram(self, name, shape, dt, kind=None):
        if name in self.dump:
            kind = "ExternalOutput"
        if kind is None:
            return self.nc.dram_tensor(name, list(shape), dt).ap()
        return self.nc.dram_tensor(name, list(shape), dt, kind=kind).ap()

    def ring(self, name, n, shape, dt):
        tiles = [self.sb("%s%d" % (name, i), shape, dt) for i in range(n)]
        r = Ring(name, n)
        r.tiles = tiles
        self.rings[name] = r
        return r

    def nxt(self, r):
        s = r.next()
        return r.tiles[s], r.key(s)

    def build(self):
        nc, es = self.nc, self.es
        L, NSEQ = self.L, self.NSEQ
        with es:
            self.P = P = Prog(nc, es)
            self.declare_io()
            self.setup_consts()
            x_in = self.x_in
            bufs = [self.xbufA, self.xbufB]
            for layer in range(self.depth):
                last = layer == self.depth - 1
                x_out = self.y_out if last else bufs[layer % 2]
                kin = "x_in" if layer == 0 else "xbuf%d" % ((layer - 1) % 2)
                kout = "y_out" if last else "xbuf%d" % (layer % 2)
                if layer % 2 == 0 and os.environ.get('ONLY_ODD') != '1':
                    self.even_layer(layer // 2, x_in, kin, x_out, kout)
                else:
                    self.odd_layer(layer // 2, x_in, kin, x_out, kout)
                x_in = x_out
            P.finalize(final_keys=self.final_keys)
        return nc

    def declare_io(self):
        L, NSEQ = self.L, self.NSEQ
        d = self.dram
        self.x_in = d("x", [NSEQ, L, 1024], F32, "ExternalInput")
        self.y_out = d("y", [NSEQ, L, 1024], F32, "ExternalOutput")
        self.xbufA = d("xbufA", [NSEQ, L, 1024], F32)
        self.xbufB = d("xbufB", [NSEQ, L, 1024], F32)
        I = lambda n, s: d(n, s, F32, "ExternalInput")
        self.even_norm_g = I("even_norm_g", [2, 128, 8])
        self.even_w_in = I("even_w_in", [2, 1024, 5120])
        self.sb_q_g = I("sb_q_norm_g", [2, 128, 1])
        self.sb_k_g = I("sb_k_norm_g", [2, 128, 1])
        self.even_w_out = I("even_w_out", [2, 1536, 1024])
        self.odd_norm_g = I("odd_norm_g", [2, 128, 8])
        self.odd_w_in = I("odd_w_in", [2, 1024, 3088])
        self.gla_w_gate = I("gla_w_gate", [2, 16, 512])
        self.gla_b_gate = I("gla_b_gate", [2, 128, 4])
        self.gla_o_g = I("gla_o_norm_g", [2, 128, 2])
        self.odd_w_out = I("odd_w_out", [2, 1024, 1024])
        self.s5_lam = I("s5_lam", [2, 128, 2, 32])
        self.s5_dt = I("s5_log_dt", [2, 128, 32])
        self.s5_b0 = I("s5_b0", [2, 128, 32, 16])
        self.s5_c0 = I("s5_c0", [2, 128, 32, 16])
        self.s5_d = I("s5_d", [2, 128, 4])
        self.s5_w_glu = I("s5_w_glu", [2, 512, 512])
        self.s5_b_glu = I("s5_b_glu", [2, 128, 4])
        B = lambda n, s: d(n, s, BF16)
        Fd = lambda n, s: d(n, s, F32)
        self.qT = B("qT", [NSEQ, 8, 128, L])
        self.kT = B("kT", [NSEQ, 8, 128, L])
        self.vtm = B("vtm", [NSEQ, L, 1024])
        self.gzT = Fd("gzT", [NSEQ, 1536, L])
        self.uT = B("uT", [NSEQ, 512, L])
        self.mixT = B("mixT", [NSEQ, 1536, L])
        self.q2T = Fd("q2T", [NSEQ, 512, L])
        self.k2T = Fd("k2T", [NSEQ, 512, L])
        self.lgT = Fd("lgT", [NSEQ, 512, L])
        self.final_keys = []

    def setup_consts(self):
        P, nc = self.P, self.nc
        sb = self.sb
        self.identf = sb("identf", [128, 128], F32)
        self.ident = sb("ident", [128, 128], BF16)
        self.ones = sb("ones", [128, 128], BF16)
        self.negones = sb("negones", [128, 128], BF16)
        self.maskLT = sb("maskLT", [128, 128], BF16)
        self.maskLE = sb("maskLE", [128, 128], F32)
        self.negtri = sb("negtri", [128, 128], BF16)
        tmpf = sb("ctmpf", [128, 128], F32)
        identf, ident = self.identf, self.ident
        P.op("pool", lambda e: e.memset(identf[:], 0.0), writes=["identf"])
        P.op("pool", lambda e: e.affine_select(out=identf[:], in_=identf[:], pattern=[[-1, 128]],
                                               compare_op=ALU.not_equal, fill=1.0, base=0,
                                               channel_multiplier=1), reads=["identf"], writes=["identf"])
        P.op("dve", lambda e: e.tensor_copy(out=ident[:], in_=identf[:]), reads=["identf"], writes=["ident"])
        P.op("pool", lambda e: e.memset(self.ones[:], 1.0), writes=["ones"])
        P.op("pool", lambda e: e.memset(self.negones[:], -1.0), writes=["negones"])
        P.op("pool", lambda e: e.memset(tmpf[:], 1.0), writes=["ctmpf"])
        P.op("pool", lambda e: e.affine_select(out=tmpf[:], in_=tmpf[:], pattern=[[1, 128]],
                                               compare_op=ALU.is_gt, fill=0.0, base=0,
                                               channel_multiplier=-1), reads=["ctmpf"], writes=["ctmpf"])
        P.op("dve", lambda e: e.tensor_copy(out=self.maskLT[:], in_=tmpf[:]), reads=["ctmpf"], writes=["maskLT"])
        P.op("pool", lambda e: e.memset(self.maskLE[:], 1.0), writes=["maskLE"])
        P.op("pool", lambda e: e.affine_select(out=self.maskLE[:], in_=self.maskLE[:], pattern=[[1, 128]],
                                               compare_op=ALU.is_ge, fill=0.0, base=0,
                                               channel_multiplier=-1), reads=["maskLE"], writes=["maskLE"])
        P.op("pool", lambda e: e.memset(tmpf[:], -1.0), reads=["maskLT"], writes=["ctmpf"])
        P.op("pool", lambda e: e.affine_select(out=tmpf[:], in_=tmpf[:], pattern=[[-1, 128]],
                                               compare_op=ALU.is_ge, fill=0.0, base=0,
                                               channel_multiplier=1), reads=["ctmpf"], writes=["ctmpf"])
        P.op("dve", lambda e: e.tensor_copy(out=self.negtri[:], in_=tmpf[:]), reads=["ctmpf"], writes=["negtri"])
        self.negbig = sb("negbig", [128, 128], BF16)
        P.op("pool", lambda e: e.memset(tmpf[:], -30000.0), reads=["negtri"], writes=["ctmpf"])
        P.op("pool", lambda e: e.affine_select(out=tmpf[:], in_=tmpf[:], pattern=[[-1, 128]],
                                               compare_op=ALU.is_ge, fill=0.0, base=0,
                                               channel_multiplier=1), reads=["ctmpf"], writes=["ctmpf"])
        P.op("dve", lambda e: e.tensor_copy(out=self.negbig[:], in_=tmpf[:]), reads=["ctmpf"], writes=["negbig"])
        self.CONST = ["ident", "identf", "ones", "negones", "maskLT", "maskLE", "negtri"]

        self.arena = sb("arena", [128, 40960], BF16)
        self.Wb = self.arena[:, :].rearrange("p (k f) -> p k f", k=8)
        self.Wo = sb("Wo", [128, 12, 1024], BF16)
        self.wstage = self.ring("wst", 2, [128, 1024], F32)
        self.gcol = sb("gcol", [128, 8], F32)
        self.xt = self.ring("xt", 2, [128, 4, 1024], F32)
        self.junk = sb("junk", [128, 1024], BF16)
        self.ss4 = sb("ss4", [128, 4], F32)
        self.rstd4 = sb("rstd4", [128, 4], F32)
        self.hb = self.ring("hb", 2, [128, 1024], BF16)
        self.hT = self.ring("hT", 2, [128, 8, 512], BF16)
        self.psT = [self.ps("psT%d" % i, [128, 1024], BF16) for i in range(1)]
        self.psM = [self.ps("psM%d" % i, [128, 512], F32) for i in range(7)]
        self.psM_i = 0
        self.ev32 = self.ring("ev32", 3, [128, 512], F32)
        self.evbf = self.ring("evbf", 4, [128, 512], BF16)
        self.wk32 = self.ring("wk32", 3, [128, 512], F32)
        self.wkbf = self.ring("wkbf", 3, [128, 512], BF16)

    def ov(self, off, shape, dt):
        n = int(np.prod(shape[1:]))
        esz = 4 if dt == F32 else 2
        a = self.arena[:, off // 2: off // 2 + n * esz // 2]
        if dt == F32:
            a = a.bitcast(F32)
        if len(shape) == 3:
            a = a.rearrange("p (a b) -> p a b", a=shape[1])
        return a

    def psum(self):
        if getattr(self, "ps_restrict", False) and os.environ.get("PSR", "1") == "1":
            self.psR_i = getattr(self, "psR_i", 0) + 1
            if self.psR_i % 2:
                return self.psM[6], ("psM", 6)
            return self.psT[0][:, :].bitcast(F32), "psT"
        i = self.psM_i % len(self.psM)
        self.psM_i += 1
        return self.psM[i], ("psM", i)

    def load_weights(self, w_dram, n_k, n_f, dst, dst_key, gain_dram=None):
        P = self.P
        gcol = self.gcol
        if gain_dram is not None:
            P.dma(lambda e: e.dma_start(out=gcol[:], in_=gain_dram), writes=["gcol"], dkey="gcol")
        CH = 1024
        for f0 in range(0, n_f, CH):
            fw = min(CH, n_f - f0)
            ck = (dst_key, f0 // CH) if dst_key == "Wb" else dst_key
            for kc in range(n_k):
                st, sk = self.nxt(self.wstage)
                P.dma(lambda e, st=st, kc=kc, f0=f0, fw=fw: e.dma_start(
                    out=st[:, :fw], in_=w_dram[kc * 128:(kc + 1) * 128, f0:f0 + fw]), writes=[sk], dkey=sk)
                eng = "act" if (kc % 2 == 0) else "dve"
                first = (kc == 0) if dst_key == "Wb" else (kc == 0 and f0 == 0)
                if gain_dram is not None:
                    if eng == "act":
                        P.op(eng, lambda e, st=st, kc=kc, f0=f0, fw=fw: e.activation(
                            out=dst[:, kc, f0:f0 + fw], in_=st[:, :fw], func=AF.Copy, scale=gcol[:, kc:kc + 1]),
                            reads=[sk, "gcol"], writes=[ck], waw=first)
                    else:
                        P.op(eng, lambda e, st=st, kc=kc, f0=f0, fw=fw: e.tensor_scalar(
                            out=dst[:, kc, f0:f0 + fw], in0=st[:, :fw], scalar1=gcol[:, kc:kc + 1], scalar2=None,
                            op0=ALU.mult), reads=[sk, "gcol"], writes=[ck], waw=first)
                else:
                    if eng == "act":
                        P.op(eng, lambda e, st=st, kc=kc, f0=f0, fw=fw: e.copy(
                            out=dst[:, kc, f0:f0 + fw], in_=st[:, :fw]), reads=[sk], writes=[ck], waw=first)
                    else:
                        P.op(eng, lambda e, st=st, kc=kc, f0=f0, fw=fw: e.tensor_copy(
                            out=dst[:, kc, f0:f0 + fw], in_=st[:, :fw]), reads=[sk], writes=[ck], waw=first)

    def prefetch_x(self, x_dram, xkey, s, tg):
        P = self.P
        xt, xk = self.nxt(self.xt)
        P.dma(lambda e: e.dma_start(out=xt[:], in_=x_dram[s, tg * 512:(tg + 1) * 512, :].rearrange(
            "(j p) d -> p j d", p=128)), reads=[xkey], writes=[xk], dkey=xk)
        if not hasattr(self, "xpref"):
            self.xpref = {}
        self.xpref[(xkey, s, tg)] = (xt, xk)

    def load_norm_group(self, x_dram, xkey, s, tg):
        P = self.P
        if (xkey, s, tg) not in getattr(self, "xpref", {}):
            self.prefetch_x(x_dram, xkey, s, tg)
        xt, xk = self.xpref.pop((xkey, s, tg))
        ss4, rstd4, junk = self.ss4, self.rstd4, self.junk
        for j in range(4):
            P.op("act", lambda e, j=j: e.activation(out=junk[:], in_=xt[:, j, :], func=AF.Square,
                                                    accum_out=ss4[:, j:j + 1]),
                 reads=[xk], writes=["junk", "ss4"])
        P.op("act", lambda e: e.activation(out=rstd4[:], in_=ss4[:], func=AF.Ln, scale=1.0 / 1024, bias=EPS),
             reads=["ss4"], writes=["rstd4"])
        P.op("act", lambda e: e.activation(out=rstd4[:], in_=rstd4[:], func=AF.Exp, scale=-0.5), reads=["rstd4"], writes=["rstd4"])
        hT, hk = self.nxt(self.hT)
        psT = self.psT[0]
        for j in range(4):
            hb, hbk = self.nxt(self.hb)
            P.op("dve", lambda e, j=j, hb=hb: e.tensor_scalar(out=hb[:], in0=xt[:, j, :], scalar1=rstd4[:, j:j + 1],
                                                              scalar2=None, op0=ALU.mult),
                 reads=[xk, "rstd4"], writes=[hbk])
            for kc in range(8):
                P.op("pe", lambda e, kc=kc, hb=hb: e.transpose(out=psT[:, kc * 128:(kc + 1) * 128],
                                                               in_=hb[:, kc * 128:(kc + 1) * 128],
                                                               identity=self.ident[:]),
                     reads=[hbk, "ident"], writes=["psT"], same_ok=True, waw=(kc == 0))
            P.op("act", lambda e, j=j: e.copy(out=hT[:, :, j * 128:(j + 1) * 128],
                                              in_=psT[:].rearrange("p (k t) -> p k t", k=8)),
                 reads=["psT"], writes=[hk], waw=(j == 0))
        return hT, hk, xt, xk

    def proj_fm(self, hT, hk, f0, M=128):
        P = self.P
        Wb = self.Wb
        ps, pk = self.psum()
        for kc in range(8):
            P.op("pe", lambda e, kc=kc: e.matmul(ps[:M, :], lhsT=Wb[:, kc, f0:f0 + M], rhs=hT[:, kc, :],
                                                 start=(kc == 0), stop=(kc == 7)),
                 reads=[hk, ("Wb", f0 // 1024)], writes=[pk], same_ok=True, waw=(kc == 0))
        return ps, pk

    def proj_tm(self, hT, hk, j, f0):
        P = self.P
        Wb = self.Wb
        ps, pk = self.psum()
        for kc in range(8):
            P.op("pe", lambda e, kc=kc: e.matmul(ps[:, :], lhsT=hT[:, kc, j * 128:(j + 1) * 128],
                                                 rhs=Wb[:, kc, f0:f0 + 512], start=(kc == 0), stop=(kc == 7)),
                 reads=[hk, ("Wb", f0 // 1024)], writes=[pk], same_ok=True, waw=(kc == 0))
        return ps, pk

    def store(self, tile_ap, tkey, dram_ap, dkey_dram):
        self.P.dma(lambda e: e.dma_start(out=dram_ap, in_=tile_ap), reads=[tkey], writes=[dkey_dram], dkey=tkey)

    def even_layer(self, i, x_in, kin, x_out, kout):
        P = self.P
        L, NSEQ = self.L, self.NSEQ
        P.barrier()
        self.load_weights(self.even_w_in[i], 8, 5120, self.Wb, "Wb", gain_dram=self.even_norm_g[i])
        self.load_weights(self.even_w_out[i], 12, 1024, self.Wo, "Wo")
        if not hasattr(self, "qkg"):
            self.qkg = self.sb("qkg", [128, 2], F32)
        qkg = self.qkg
        P.dma(lambda e: e.dma_start(out=qkg[:, 0:1], in_=self.sb_q_g[i]), writes=["qkg"], dkey="qkg")
        P.dma(lambda e: e.dma_start(out=qkg[:, 1:2], in_=self.sb_k_g[i]), writes=["qkg"], dkey="qkg")
        P.op("dve", lambda e: e.tensor_scalar(out=qkg[:, 0:1], in0=qkg[:, 0:1], scalar1=128 ** -0.5, scalar2=None,
                                              op0=ALU.mult), reads=["qkg"], writes=["qkg"])
        self.s5_prep(i)
        groups = [(s, tg) for s in range(NSEQ) for tg in range(self.NG)]
        for k, (s, tg) in enumerate(groups):
            if k == 0:
                self.prefetch_x(x_in, kin, s, tg)
            if k + 1 < len(groups):
                self.prefetch_x(x_in, kin, *groups[k + 1])
            self.even_inproj_group(i, x_in, kin, s, tg)
        for s in range(NSEQ):
            for _ in self.sb_attention(s):
                pass
        for _ in self.s5_main(i):
            pass
        for s in range(NSEQ):
            self.out_proj(s, x_in, kin, x_out, kout, 12, self.mixT, "mixT")

    def even_inproj_group(self, i, x_in, kin, s, tg):
        P = self.P
        hT, hk, xt, xk = self.load_norm_group(x_in, kin, s, tg)
        cols = slice(tg * 512, (tg + 1) * 512)
        for which in range(2):
            dst = self.qT if which == 0 else self.kT
            dkey = "qT" if which == 0 else "kT"
            for h in range(8):
                ps, pk = self.proj_fm(hT, hk, which * 1024 + h * 128)
                sq, sqk = self.nxt(self.wkbf)
                P.op("act", lambda e, ps=ps, sq=sq: e.activation(out=sq[:], in_=ps[:], func=AF.Square),
                     reads=[pk], writes=[sqk])
                ps2, pk2 = self.psum()
                P.op("pe", lambda e, ps2=ps2, sq=sq: e.matmul(ps2[:], lhsT=self.ones[:], rhs=sq[:], start=True, stop=True),
                     reads=[sqk, "ones"], writes=[pk2])
                rs, rsk = self.nxt(self.wk32)
                P.op("act", lambda e, ps2=ps2, rs=rs: e.activation(out=rs[:], in_=ps2[:], func=AF.Ln,
                                                                   scale=1.0 / 128, bias=EPS),
                     reads=[pk2], writes=[rsk])
                P.op("act", lambda e, rs=rs: e.activation(out=rs[:], in_=rs[:], func=AF.Exp, scale=-0.5), reads=[rsk], writes=[rsk])
                ob, obk = self.nxt(self.evbf)
                P.op("dve", lambda e, ps=ps, rs=rs, ob=ob, which=which: e.scalar_tensor_tensor(
                    out=ob[:], in0=ps[:], scalar=self.qkg[:, which:which + 1], in1=rs[:], op0=ALU.mult, op1=ALU.mult),
                    reads=[pk, rsk, "qkg"], writes=[obk])
                self.store(ob[:], obk, dst[s, h, :, cols], (dkey, s))
        for j in range(4):
            for half in range(2):
                ps, pk = self.proj_tm(hT, hk, j, 2048 + half * 512)
                ob, obk = self.nxt(self.evbf)
                P.op("act" if half else "dve",
                     (lambda e, ps=ps, ob=ob: e.copy(out=ob[:], in_=ps[:])) if half else
                     (lambda e, ps=ps, ob=ob: e.tensor_copy(out=ob[:], in_=ps[:])),
                     reads=[pk], writes=[obk])
                r0 = tg * 512 + j * 128
                self.store(ob[:], obk, self.vtm[s, r0:r0 + 128, half * 512:(half + 1) * 512], ("vtm", s))
        for c in range(12):
            f0 = 3072 + c * 128 if c < 8 else 4608 + (c - 8) * 128
            ps, pk = self.proj_fm(hT, hk, f0)
            ob, obk = self.nxt(self.ev32)
            P.op("act", lambda e, ps=ps, ob=ob: e.activation(out=ob[:], in_=ps[:], func=AF.Silu),
                 reads=[pk], writes=[obk])
            self.store(ob[:], obk, self.gzT[s, c * 128:(c + 1) * 128, cols], ("gzT", s))
        for c in range(4):
            ps, pk = self.proj_fm(hT, hk, 4096 + c * 128)
            ob, obk = self.nxt(self.evbf)
            P.op("dve", lambda e, ps=ps, ob=ob: e.tensor_copy(out=ob[:], in_=ps[:]), reads=[pk], writes=[obk])
            self.store(ob[:], obk, self.uT[s, c * 128:(c + 1) * 128, cols], ("uT", s))

    def out_proj(self, s, x_in, kin, x_out, kout, n_k, mix_dram, mixkey):
        P = self.P
        if not hasattr(self, "mxin"):
            self.mxin = Ring("mxin", 2)
            self.mxin.tiles = [self.ov(k * 12288, [128, 12, 512], BF16) for k in range(2)]
            self.xres = self.xt
        def issue(tg):
            mx, mk = self.nxt(self.mxin)
            P.dma(lambda e, mx=mx, tg=tg: e.dma_start(
                out=mx[:, :n_k, :], in_=mix_dram[s, :n_k * 128, tg * 512:(tg + 1) * 512].rearrange(
                    "(kc p) t -> p kc t", p=128)), reads=[(mixkey, s)], writes=[mk], dkey=mk)
            xr, xrk = self.nxt(self.xres)
            P.dma(lambda e, xr=xr, tg=tg: e.dma_start(
                out=xr[:], in_=x_in[s, tg * 512:(tg + 1) * 512, :].rearrange("(j p) d -> p j d", p=128)),
                reads=[kin], writes=[xrk], dkey=xrk)
            return mx, mk, xr, xrk

        pend = issue(0)
        for tg in range(self.NG):
            mx, mk, xr, xrk = pend
            if tg + 1 < self.NG:
                pend = issue(tg + 1)
            for j in range(4):
                for half in range(2):
                    ps, pk = self.psum()
                    for kc in range(n_k):
                        P.op("pe", lambda e, ps=ps, kc=kc, j=j, half=half, mx=mx: e.matmul(
                            ps[:], lhsT=mx[:, kc, j * 128:(j + 1) * 128],
                            rhs=self.Wo[:, kc, half * 512:(half + 1) * 512], start=(kc == 0), stop=(kc == n_k - 1)),
                            reads=[mk, "Wo"], writes=[pk], same_ok=True, waw=(kc == 0))
                    P.op("dve", lambda e, ps=ps, j=j, half=half, xr=xr: e.tensor_tensor(
                        out=xr[:, j, half * 512:(half + 1) * 512], in0=ps[:], in1=xr[:, j, half * 512:(half + 1) * 512],
                        op=ALU.add), reads=[pk, xrk], writes=[xrk])
            P.dma(lambda e, xr=xr, tg=tg: e.dma_start(
                out=x_out[s, tg * 512:(tg + 1) * 512, :].rearrange("(j p) d -> p j d", p=128), in_=xr[:]),
                reads=[xrk], writes=[kout], dkey=(xrk, "st"))
            if kout == "y_out":
                if (xrk, "st") not in self.final_keys:
                    self.final_keys.append((xrk, "st"))

    def sb_attention(self, s):
        P = self.P
        L = self.L
        NB = L // 128
        if not OVERLAP:
            P.barrier()
        KB = 1024
        qkv = []
        NQ = 1 if OVERLAP else 2
        for r in range(NQ):
            base = r * 3 * (L * 2)
            qkv.append((self.ov(base, [128, L], BF16), self.ov(base + 2 * L, [128, L], BF16),
                        self.ov(base + 4 * L, [128, NB, 128], BF16)))
        cbase = NQ * 3 * L * 2
        NCH = 3
        chains = []
        for c in range(NCH):
            b = cbase + c * 7 * KB
            chains.append(dict(e32=self.ov(b, [128, 512], F32), Lb=self.ov(b + 2 * KB, [128, 512], BF16),
                               wb=self.ov(b + 3 * KB, [128, 512], BF16), R32=self.ov(b + 4 * KB, [128, 512], F32),
                               Rbf=self.ov(b + 6 * KB, [128, 512], BF16), id=c,
                               psZ=self.psM[2 * c], psZk=("psM", 2 * c), psO=self.psM[2 * c + 1], psOk=("psM", 2 * c + 1)))
        gbase = cbase + NCH * 7 * KB
        gz = [self.ov(gbase + k * 2 * KB, [128, 512], F32) for k in range(NCH)]
        assert gbase + NCH * 2 * KB <= (51 * KB if OVERLAP else 81920)
        def load_head(h):
            qh, kh, vh = qkv[h % NQ]
            kq, kk, kv = ("sbq", h % NQ), ("sbk", h % NQ), ("sbv", h % NQ)
            P.dma(lambda e, qh=qh, h=h: e.dma_start(out=qh, in_=self.qT[s, h]), reads=[("qT", s)], writes=[kq], dkey=kq)
            P.dma(lambda e, kh=kh, h=h: e.dma_start(out=kh, in_=self.kT[s, h]), reads=[("kT", s)], writes=[kk], dkey=kk)
            P.dma(lambda e, vh=vh, h=h: e.dma_start(
                out=vh, in_=self.vtm[s, :, h * 128:(h + 1) * 128].rearrange("(b p) d -> p b d", p=128)),
                reads=[("vtm", s)], writes=[kv], dkey=kv)

        load_head(0)
        for h in range(8):
            qh, kh, vh = qkv[h % NQ]
            kq, kk, kv = ("sbq", h % NQ), ("sbk", h % NQ), ("sbv", h % NQ)
            if h + 1 < 8 and NQ == 2:
                load_head(h + 1)
            elif h > 0 and NQ == 1:
                load_head(h)

            def chain(qg, ch, h=h, qh=qh, kh=kh, vh=vh, kq=kq, kk=kk, kv=kv):
                c = ch["id"]
                tag = lambda n: ("sbc", n, c)
                e32, Lb, wb, R32, Rbf = ch["e32"], ch["Lb"], ch["wb"], ch["R32"], ch["Rbf"]
                psZ, psZk, psO, psOk = ch["psZ"], ch["psZk"], ch["psO"], ch["psOk"]
                g = gz[c]
                P.dma(lambda e: e.dma_start(out=g, in_=self.gzT[s, h * 128:(h + 1) * 128, qg * 512:(qg + 1) * 512]),
                      reads=[("gzT", s)], writes=[tag("gz")], dkey=tag("gz"))
                P.op("pool", lambda e: e.memset(R32, 0.0), writes=[tag("R32")])
                kbs = list(reversed(range(4 * qg + 4)))

                def zmm(kb):
                    c0 = max(0, kb - 4 * qg) * 128
                    q0 = qg * 512 + c0
                    P.op("pe", lambda e: e.matmul(
                        psZ[:, c0:512], lhsT=kh[:, kb * 128:(kb + 1) * 128], rhs=qh[:, q0:qg * 512 + 512],
                        start=True, stop=False, skip_group_check=True), reads=[kq, kk], writes=[psZk])
                    if kb >= 4 * qg:
                        P.op("pe", lambda e: e.matmul(
                            psZ[:, c0:c0 + 128], lhsT=self.ident[:], rhs=self.negbig[:], start=False, stop=False,
                            skip_group_check=True), reads=["ident", "negbig"], writes=[psZk], waw=False)

                zmm(kbs[0])
                for idx, kb in enumerate(kbs):
                    c0 = max(0, kb - 4 * qg) * 128
                    N = 512 - c0
                    diag = kb >= 4 * qg
                    P.op("act", lambda e, c0=c0: e.activation(out=e32[:, c0:512], in_=psZ[:, c0:512], func=AF.Exp),
                         reads=[psZk], writes=[tag("e32")])
                    yield
                    P.op("act", lambda e, c0=c0: e.activation(out=Lb[:, c0:512], in_=e32[:, c0:512], func=AF.Ln, bias=1.0),
                         reads=[tag("e32")], writes=[tag("Lb")])
                    yield
                    P.op("pe", lambda e, c0=c0, idx=idx: e.matmul(
                        psZ[:, c0:512], lhsT=self.negtri[:], rhs=Lb[:, c0:512], start=False, stop=(idx == 0),
                        skip_group_check=True), reads=[tag("Lb"), "negtri"], writes=[psZk], same_ok=True, waw=False)
                    if idx > 0:
                        P.op("pe", lambda e, c0=c0: e.matmul(
                            psZ[:, c0:512], lhsT=self.negones[:], rhs=Rbf[:, c0:512], start=False, stop=True,
                            skip_group_check=True), reads=[tag("Rbf"), "negones"], writes=[psZk], same_ok=True, waw=False)
                    P.op("act", lambda e, c0=c0: e.activation(out=wb[:, c0:512], in_=psZ[:, c0:512], func=AF.Exp),
                         reads=[psZk], writes=[tag("wb")])
                    yield
                    if idx + 1 < len(kbs):
                        zmm(kbs[idx + 1])
                    P.op("pe", lambda e, kb=kb, c0=c0, idx=idx: e.matmul(
                        psO[:, c0:512], lhsT=vh[:, kb, :], rhs=wb[:, c0:512], start=(idx == 0), stop=(idx == len(kbs) - 1),
                        skip_group_check=True), reads=[tag("wb"), kv], writes=[psOk], same_ok=True, waw=(idx == 0))
                    for _f in range(N_FILL):
                        P.op("pe", lambda e: e.matmul(self.psM[6][:, :], lhsT=self.ones[:], rhs=qh[:, 0:512], start=True,
                                                      stop=True, skip_group_check=True), reads=[], writes=[("psM", 6)])
                    if idx < len(kbs) - 1:
                        c1 = max(0, kbs[idx + 1] - 4 * qg) * 128
                        P.op("pool", lambda e, c0=c0: e.tensor_tensor(out=R32[:, c0:512], in0=R32[:, c0:512],
                                                                      in1=Lb[:, c0:512], op=ALU.add),
                             reads=[tag("Lb"), tag("R32")], writes=[tag("R32")])
                        P.op("dve", lambda e, c1=c1: e.tensor_copy(out=Rbf[:, c1:512], in_=R32[:, c1:512]),
                             reads=[tag("R32")], writes=[tag("Rbf")])
                    yield
                ob, obk = self.nxt(self.evbf)
                P.op("dve", lambda e, ob=ob: e.tensor_tensor(out=ob[:], in0=psO[:], in1=g, op=ALU.mult),
                     reads=[psOk, tag("gz")], writes=[obk])
                self.store(ob[:], obk, self.mixT[s, h * 128:(h + 1) * 128, qg * 512:(qg + 1) * 512], ("mixT", s))
                yield

            def head_gen(c):
                if self.NG == 8 and NCH == 3:
                    qgs = [[7, 3], [6, 4], [5, 2, 1, 0]][c]
                else:
                    qgs = list(range(self.NG - 1 - c, -1, -NCH))
                for qg in qgs:
                    yield from chain(qg, chains[c])
            yield from interleave([head_gen(c) for c in range(NCH)])

    def s5_prep(self, i):
        pass

    def s5_consts(self):
        if hasattr(self, "Jm"):
            return
        P, sb = self.P, self.sb
        self.Jm = sb("Jm", [128, 128], F32)
        self.nJm = sb("nJm", [128, 128], F32)
        self.bd = sb("bdmask", [128, 128], F32)
        self.Em = sb("Emat", [8, 128], F32)
        self.rowm = sb("rowm", [128, 2], F32)
        self.sgn = sb("sgn", [128, 1], F32)
        self.s5sm = sb("s5sm", [128, 12], F32)
        Jm, nJm, bd, Em, rowm, sgn = self.Jm, self.nJm, self.bd, self.Em, self.rowm, self.sgn
        P.op("pool", lambda e: e.memset(Jm[:], 0.0), writes=["Jm"])
        P.op("pool", lambda e: e.affine_select(out=Jm[:], in_=Jm[:], pattern=[[-1, 128]], compare_op=ALU.not_equal,
                                               fill=-1.0, base=64, channel_multiplier=1), reads=["Jm"], writes=["Jm"])
        P.op("pool", lambda e: e.affine_select(out=Jm[:], in_=Jm[:], pattern=[[-1, 128]], compare_op=ALU.not_equal,
                                               fill=1.0, base=-64, channel_multiplier=1), reads=["Jm"], writes=["Jm"])
        P.op("dve", lambda e: e.tensor_scalar(out=nJm[:], in0=Jm[:], scalar1=-1.0, scalar2=None, op0=ALU.mult),
             reads=["Jm"], writes=["nJm"])
        P.op("pool", lambda e: e.memset(Em[:], 1.0), writes=["Em"])
        P.op("pool", lambda e: e.affine_select(out=Em[:], in_=Em[:], pattern=[[1, 128]], compare_op=ALU.is_ge,
                                               fill=0.0, base=0, channel_multiplier=-16), reads=["Em"], writes=["Em"])
        P.op("pool", lambda e: e.affine_select(out=Em[:], in_=Em[:], pattern=[[-1, 128]], compare_op=ALU.is_ge,
                                               fill=0.0, base=15, channel_multiplier=16), reads=["Em"], writes=["Em"])
        ps, pk = self.psum()
        P.op("pe", lambda e: e.matmul(ps[:, 0:128], lhsT=Em[:], rhs=Em[:], start=True, stop=True), reads=["Em"], writes=[pk])
        P.op("dve", lambda e: e.tensor_copy(out=bd[:], in_=ps[:, 0:128]), reads=[pk], writes=["bd"])
        P.op("dve", lambda e: e.tensor_reduce(out=rowm[:], in_=bd[:].rearrange("p (q m w) -> p m q w", q=4, m=2, w=16),
                                              axis=AX.XY, op=ALU.add), reads=["bd"], writes=["rowm"])
        P.op("dve", lambda e: e.tensor_scalar(out=rowm[:], in0=rowm[:], scalar1=1.0 / 16, scalar2=None, op0=ALU.mult),
             reads=["rowm"], writes=["rowm"])
        self.Ecol = sb("Ecol", [128, 8], F32)
        ps2, pk2 = self.psum()
        P.op("pe", lambda e: e.transpose(out=ps2[:, 0:8], in_=Em[:], identity=self.identf[0:8, 0:8]), reads=["Em", "identf"], writes=[pk2])
        P.op("dve", lambda e: e.tensor_copy(out=self.Ecol[:], in_=ps2[:, 0:8]), reads=[pk2], writes=["Ecol"])
        self.gml = Ring("gml", 2)
        self.gml.tiles = [t[:, :].rearrange("p (m k) -> p m k", m=8) for t in self.hb.tiles]
        P.op("pool", lambda e: e.memset(sgn[0:64, :], 1.0), writes=["sgn"])
        P.op("pool", lambda e: e.memset(sgn[64:128, :], -1.0), writes=["sgn"], waw=False)

    def s5_main(self, i):
        P = self.P
        L, NSEQ = self.L, self.NSEQ
        NCH = L // 8
        self.s5_consts()
        P.barrier()
        KB = 1024
        ov = self.ov
        SM = ov(0, [128, 16, 32], F32)
        LAM = ov(2 * KB, [128, 2, 32], F32)
        MTr = [ov(2 * KB + 512 + k * 512, [128, 128], F32) for k in range(3)]
        AT = ov(4 * KB, [128, 32, 128], F32)
        Am = ov(20 * KB, [128, 32, 128], F32)
        Wsb = ov(36 * KB, [128, 8, 512], F32)
        Vsb = ov(52 * KB, [128, 9, 512], F32)
        PWre = ov(70 * KB, [128, 17, 32], F32)
        PWim = ov(70 * KB + 2176, [128, 17, 32], F32)
        B0 = self.ev32.tiles[0]
        C0 = self.ev32.tiles[1]
        Vpad = self.xt.tiles[0][:, :, :].rearrange("p a b -> p (a b)").bitcast(BF16).rearrange(
            "p (m g c) -> p m g c", m=8, g=32)
        WTf = self.xt.tiles[1][:, :, :].rearrange("p a b -> p (a b)").bitcast(BF16)[:, 0:4096].rearrange(
            "p (t j k) -> p t j k", t=4, j=8)
        Kblk = self.hT.tiles[0][:, :, :].rearrange("p a b -> p (a b)").rearrange("p (t j k) -> p t j k", t=4, j=8)
        WG = self.hT.tiles[1][:, 0:4, :]
        s5sm = self.s5sm
        identf, Jm, nJm = self.identf, self.Jm, self.nJm
        sm = lambda k: SM[:, k, :]
        K = lambda n: ("s5", n)

        P.dma(lambda e: e.dma_start(out=LAM, in_=self.s5_lam[i]), writes=[K("LAM")], dkey=K("LAM"))
        P.dma(lambda e: e.dma_start(out=sm(0), in_=self.s5_dt[i]), writes=[K("sm0")], dkey=K("sm0"))
        P.dma(lambda e: e.dma_start(out=B0[:].rearrange("p (g c) -> p g c", g=32), in_=self.s5_b0[i]),
              writes=[("ev32", 0)], dkey=("ev32", 0))
        P.dma(lambda e: e.dma_start(out=C0[:].rearrange("p (g c) -> p g c", g=32), in_=self.s5_c0[i]),
              writes=[("ev32", 1)], dkey=("ev32", 1))
        P.dma(lambda e: e.dma_start(out=s5sm[:, 0:4], in_=self.s5_d[i]), writes=["s5sm"], dkey="s5sm")
        P.dma(lambda e: e.dma_start(out=s5sm[:, 4:8], in_=self.s5_b_glu[i]), writes=["s5sm"], dkey="s5sm")
        for kc in range(4):
            st, sk = self.nxt(self.wstage)
            P.dma(lambda e, st=st, kc=kc: e.dma_start(out=st[:, :512], in_=self.s5_w_glu[i, kc * 128:(kc + 1) * 128, :]),
                  writes=[sk], dkey=sk)
            P.op("dve", lambda e, st=st, kc=kc: e.tensor_copy(out=WG[:, kc, :], in_=st[:, :512]), reads=[sk],
                 writes=[K("WG")], waw=(kc == 0))
        lre, lim = LAM[:, 0, :], LAM[:, 1, :]
        kl = K("LAM")

        def tt(eng, out, a, b, op, rk, wk):
            P.op(eng, lambda e: e.tensor_tensor(out=out, in0=a, in1=b, op=op), reads=rk, writes=wk)

        def ts(eng, out, a, s1, s2, op0, op1, rk, wk):
            if op1 is None:
                P.op(eng, lambda e: e.tensor_scalar(out=out, in0=a, scalar1=s1, scalar2=None, op0=op0), reads=rk, writes=wk)
            else:
                P.op(eng, lambda e: e.tensor_scalar(out=out, in0=a, scalar1=s1, scalar2=s2, op0=op0, op1=op1),
                     reads=rk, writes=wk)

        S = K("SM")
        P.op("act", lambda e: e.activation(out=sm(0), in_=sm(0), func=AF.Exp), reads=[K("sm0")], writes=[S])
        tt("dve", sm(1), lre, sm(0), ALU.mult, [kl, S], [S])
        P.op("act", lambda e: e.activation(out=sm(1), in_=sm(1), func=AF.Exp), reads=[S], writes=[S])
        tt("dve", sm(2), lim, sm(0), ALU.mult, [kl, S], [S])
        ts("dve", sm(3), sm(2), math.pi / 2, None, ALU.add, None, [S], [S])
        for src in (2, 3):
            ts("dve", sm(6), sm(src), 0.0, None, ALU.mult, None, [S], [S])
            for j in range(6):
                ts("dve", sm(5), sm(src), (2 * j + 1) * math.pi, -2 * math.pi, ALU.is_ge, ALU.mult, [S], [S])
                tt("dve", sm(6), sm(6), sm(5), ALU.add, [S], [S])
            tt("dve", sm(src), sm(src), sm(6), ALU.add, [S], [S])
        P.op("act", lambda e: e.activation(out=sm(7), in_=sm(2), func=AF.Sin), reads=[S], writes=[S])
        P.op("act", lambda e: e.activation(out=sm(8), in_=sm(3), func=AF.Sin), reads=[S], writes=[S])
        PK = K("PW")
        tt("dve", PWre[:, 1, :], sm(1), sm(8), ALU.mult, [S], [PK])
        tt("dve", PWim[:, 1, :], sm(1), sm(7), ALU.mult, [S], [PK])
        are, aim = PWre[:, 1, :], PWim[:, 1, :]
        tt("dve", sm(9), lre, lre, ALU.mult, [kl], [S])
        tt("dve", sm(10), lim, lim, ALU.mult, [kl], [S])
        tt("dve", sm(9), sm(9), sm(10), ALU.add, [S], [S])
        P.op("dve", lambda e: e.reciprocal(out=sm(10), in_=sm(9)), reads=[S], writes=[S])
        ts("dve", sm(11), are, -1.0, None, ALU.add, None, [PK], [S])
        tt("dve", sm(12), sm(11), lre, ALU.mult, [S, kl], [S])
        tt("dve", sm(13), aim, lim, ALU.mult, [PK, kl], [S])
        tt("dve", sm(12), sm(12), sm(13), ALU.add, [S], [S])
        tt("dve", PWre[:, 0, :], sm(12), sm(10), ALU.mult, [S], [PK])
        tt("dve", sm(12), aim, lre, ALU.mult, [PK, kl], [S])
        tt("dve", sm(13), sm(11), lim, ALU.mult, [S, kl], [S])
        tt("dve", sm(12), sm(12), sm(13), ALU.subtract, [S], [S])
        tt("dve", PWim[:, 0, :], sm(12), sm(10), ALU.mult, [S], [PK])

        def cmul(dst, a, b):
            tt("dve", sm(12), PWre[:, a, :], PWre[:, b, :], ALU.mult, [PK], [S])
            tt("dve", sm(13), PWim[:, a, :], PWim[:, b, :], ALU.mult, [PK], [S])
            tt("dve", sm(14), PWre[:, a, :], PWim[:, b, :], ALU.mult, [PK], [S])
            tt("dve", sm(15), PWim[:, a, :], PWre[:, b, :], ALU.mult, [PK], [S])
            tt("dve", PWre[:, dst, :], sm(12), sm(13), ALU.subtract, [S], [PK])
            tt("dve", PWim[:, dst, :], sm(14), sm(15), ALU.add, [S], [PK])

        for k in range(2, 9):
            cmul(k, k - 1, 1)
        for k in range(9, 17):
            cmul(k, k - 1, k - 1)

        if S5_STOP == 1:
            return
        def build_M(out, pidx, g, transpose, eng2="dve", rk=(), wk=()):
            P.op("act", lambda e: e.activation(out=out, in_=identf[:], func=AF.Copy, scale=PWre[:, pidx, g:g + 1]),
                 reads=[PK, "identf"] + list(rk), writes=list(wk))
            Jx = nJm if transpose else Jm
            P.op("dve", lambda e: e.scalar_tensor_tensor(out=out, in0=Jx[:], scalar=PWim[:, pidx, g:g + 1], in1=out,
                                                         op0=ALU.mult, op1=ALU.add),
                 reads=[PK, "Jm", "nJm"] + list(wk), writes=list(wk))

        for g in range(32):
            build_M(AT[:, g, :], 1, g, True, wk=[K("AT")])
            build_M(Am[:, g, :], 1, g, False, wk=[K("A")])
        if S5_STOP == 2:
            return
        psW, pkW = self.psum()
        mi = 0
        for g in range(32):
            mt = MTr[mi % 3]
            mk = K(("MT", mi % 3))
            mi += 1
            build_M(mt, 0, g, True, wk=[mk])
            P.op("pe", lambda e, g=g, mt=mt: e.matmul(psW[:, g * 16:(g + 1) * 16], lhsT=mt, rhs=B0[:, g * 16:(g + 1) * 16],
                                                      start=True, stop=True, skip_group_check=True),
                 reads=[mk, ("ev32", 0)], writes=[pkW], waw=(g == 0))
        P.op("act", lambda e: e.copy(out=Wsb[:, 0, :], in_=psW[:]), reads=[pkW], writes=[K("W0")])
        for j in range(1, 8):
            psW, pkW = self.psum()
            for g in range(32):
                P.op("pe", lambda e, g=g, j=j, psW=psW: e.matmul(
                    psW[:, g * 16:(g + 1) * 16], lhsT=AT[:, g, :], rhs=Wsb[:, j - 1, g * 16:(g + 1) * 16],
                    start=True, stop=True, skip_group_check=True), reads=[K("AT"), K("W%d" % (j - 1))], writes=[pkW],
                    waw=(g == 0))
            P.op("act", lambda e, j=j, psW=psW: e.copy(out=Wsb[:, j, :], in_=psW[:]), reads=[pkW], writes=[K("W%d" % j)])
        if S5_STOP == 3:
            return
        P.op("dve", lambda e: e.tensor_scalar(out=Vsb[:, 0, :], in0=C0[:], scalar1=self.sgn[:, 0:1], scalar2=None,
                                              op0=ALU.mult), reads=[("ev32", 1), "sgn"], writes=[K("V0")])
        for j in range(1, 9):
            psV, pkV = self.psum()
            for g in range(32):
                P.op("pe", lambda e, g=g, j=j, psV=psV: e.matmul(
                    psV[:, g * 16:(g + 1) * 16], lhsT=Am[:, g, :], rhs=Vsb[:, j - 1, g * 16:(g + 1) * 16],
                    start=True, stop=True, skip_group_check=True), reads=[K("A"), K("V%d" % (j - 1))], writes=[pkV],
                    waw=(g == 0))
            P.op("act", lambda e, j=j, psV=psV: e.copy(out=Vsb[:, j, :], in_=psV[:]), reads=[pkV], writes=[K("V%d" % j)])
        if S5_STOP == 4:
            return
        for T in range(4):
            for tau in range(8):
                ps, pk = self.psum()
                P.op("pe", lambda e, T=T, tau=tau, ps=ps: e.matmul(
                    ps[:, 0:128], lhsT=Wsb[:, tau, T * 128:(T + 1) * 128], rhs=Vsb[:, 0, T * 128:(T + 1) * 128],
                    start=True, stop=True), reads=[K("W%d" % tau), K("V0")], writes=[pk])
                if tau == 0:
                    tmp, tk = self.nxt(self.wk32)
                    P.op("dve", lambda e, ps=ps, tmp=tmp: e.tensor_tensor(out=tmp[:, 0:128], in0=ps[:, 0:128], in1=self.bd[:],
                                                                          op=ALU.mult), reads=[pk, "bd"], writes=[tk])
                    P.op("dve", lambda e, T=T, tmp=tmp: e.scalar_tensor_tensor(
                        out=Kblk[:, T, 0, :], in0=identf[:], scalar=s5sm[:, T:T + 1], in1=tmp[:, 0:128], op0=ALU.mult,
                        op1=ALU.add), reads=[tk, "s5sm", "identf"], writes=[K("Kblk")], waw=False)
                else:
                    P.op("dve", lambda e, T=T, tau=tau, ps=ps: e.tensor_tensor(
                        out=Kblk[:, T, tau, :], in0=ps[:, 0:128], in1=self.bd[:], op=ALU.mult), reads=[pk, "bd"],
                        writes=[K("Kblk")], waw=False)
        if S5_STOP == 5:
            return
        for T in range(4):
            for j in range(8):
                ps, pk = self.psum()
                P.op("pe", lambda e, T=T, j=j, ps=ps: e.transpose(out=ps[:, 0:128], in_=Wsb[:, j, T * 128:(T + 1) * 128],
                                                                  identity=identf[:]),
                     reads=[K("W%d" % j), "identf"], writes=[pk])
                P.op("act", lambda e, T=T, j=j, ps=ps: e.copy(out=WTf[:, T, j, :], in_=ps[:, 0:128]),
                     reads=[pk], writes=[K("WTm")], waw=False)
        if S5_STOP == 6:
            return
        P.op("pool", lambda e: e.memset(Vpad.rearrange("p m g c -> p (m g c)"), 0.0), writes=[K("Vpad")])
        for m in range(8):
            for mem in range(2):
                P.op("dve" if mem else "pool", lambda e, m=m, mem=mem: e.tensor_copy(
                    out=Vpad[:, m, mem::2, 16 * mem:16 * mem + 16],
                    in_=Vsb[:, m + 1, :].rearrange("p (g c) -> p g c", g=32)[:, mem::2, :]),
                    reads=[K("V%d" % (m + 1)), K("Vpad")], writes=[K("Vpad")], waw=False)

        if S5_STOP == 7:
            return
        if os.environ.get("DBG_S5") == "1":
            dbg = {"dPWre": (PWre, [128, 17, 32]), "dPWim": (PWim, [128, 17, 32]), "dWsb": (Wsb, [128, 8, 512]),
                   "dVsb": (Vsb, [128, 9, 512])}
            P.barrier()
            for nm, (ap_, shp) in dbg.items():
                dt_ = self.nc.dram_tensor(nm, shp, F32, kind="ExternalOutput").ap()
                P.dma(lambda e, ap_=ap_, dt_=dt_: e.dma_start(out=dt_, in_=ap_), writes=[("dbg", nm)], dkey=("dbg", nm))
            dk = self.nc.dram_tensor("dKblk", [128, 4, 8, 128], BF16, kind="ExternalOutput").ap()
            P.dma(lambda e: e.dma_start(out=dk, in_=Kblk), writes=[("dbg", "k")], dkey=("dbg", "k"))
            P.barrier()
        yield
        Wof = self.Wo[:, :, :].rearrange("p a b -> p (a b)")

        def wov(off, shape, dt):
            n = int(np.prod(shape[1:]))
            esz = 4 if dt == F32 else 2
            a = Wof[:, off // 2: off // 2 + n * esz // 2]
            if dt == F32:
                a = a.bitcast(F32)
            if len(shape) == 3:
                a = a.rearrange("p (a b) -> p a b", a=shape[1])
            return a

        P.barrier()
        U = ov(0, [128, L], BF16)
        Hprev = ov(2 * L, [128, 8, NCH], BF16)
        b1 = 2 * L + 16 * NCH
        ytile = ov(b1, [128, L], F32)
        cH32 = [ov(b1 + 4 * L + k * 2 * KB, [128, 512], F32) for k in range(8)]
        cHb = [ov(b1 + 4 * L + 16 * KB + k * KB, [128, 512], BF16) for k in range(8)]
        assert b1 + 4 * L + 24 * KB <= 70 * KB
        y2gs = [ov(b1, [128, 4, 512], F32), ov(b1 + 4 * L + 24 * KB, [128, 4, 512], F32)]
        y2bfs = [ov(b1 + 8 * KB, [128, 4, 512], BF16), ov(b1 + 4 * L + 32 * KB, [128, 4, 512], BF16)]
        assert b1 + 4 * L + 36 * KB <= 70 * KB
        if not hasattr(self, "y2T"):
            self.y2T = self.dram("y2T", [NSEQ, 512, L], F32)
        nsteps = int(math.log2(NCH))
        xt1b = self.xt.tiles[1][:, :, :].rearrange("p a b -> p (a b)").bitcast(BF16)[:, 4096:8192].rearrange(
            "p (c k) -> p c k", c=32)
        cgm = [[xt1b[:, ci * 4 + r, :] for r in range(4)] for ci in range(8)]
        hT1b = self.hT.tiles[1][:, 4:8, :].rearrange("p a b -> p (a b)").rearrange("p (c k) -> p c k", c=16)
        cMT = [[hT1b[:, ci * 2 + r, :] for r in range(2)] for ci in range(8)]
        YT = K("ytile")
        for s in range(NSEQ):
            for T in range(4):
                P.dma(lambda e, s=s, T=T: e.dma_start(out=U, in_=self.uT[s, T * 128:(T + 1) * 128, :]),
                      reads=[("uT", s)], writes=[K("U")], dkey=K("U"))

                def gchain(g8, ci, T=T, s=s):
                    g = 8 * T + g8
                    H32, hk32 = cH32[ci], K(("cH32", ci))
                    Hb, hkb = cHb[ci], K(("cHb", ci))
                    psG, pkG = self.psum()
                    for m in range(8):
                        gmt = cgm[ci][m % 4]
                        gmk = K(("cgm", ci, m % 4))
                        if m % 2:
                            P.op("act", lambda e, m=m, gmt=gmt: e.activation(
                                out=gmt, in_=WTf[:, T, 7 - m, :], func=AF.Copy, scale=self.Ecol[:, g8:g8 + 1]),
                                reads=[K("WTm"), "Ecol"], writes=[gmk])
                        else:
                            P.op("dve", lambda e, m=m, gmt=gmt: e.tensor_scalar(
                                out=gmt, in0=WTf[:, T, 7 - m, :], scalar1=self.Ecol[:, g8:g8 + 1], scalar2=None,
                                op0=ALU.mult), reads=[K("WTm"), "Ecol"], writes=[gmk])
                        P.op("pe", lambda e, m=m, gmt=gmt: e.matmul(
                            psG[:, :NCH], lhsT=gmt, rhs=U[:, m::8], start=(m == 0), stop=(m == 7)),
                            reads=[gmk, K("U")], writes=[pkG], waw=(m == 0))
                    P.op("act", lambda e: e.copy(out=H32[:, :NCH], in_=psG[:, :NCH]), reads=[pkG], writes=[hk32])
                    P.op("dve", lambda e: e.tensor_copy(out=Hb[:, :NCH], in_=psG[:, :NCH]), reads=[pkG], writes=[hkb])
                    yield
                    for j in range(nsteps):
                        sft = 1 << j
                        mt = cMT[ci][j % 2]
                        mk = K(("cMT", ci, j % 2))
                        build_M(mt, 8 + j, g, True, wk=[mk])
                        yield
                        psS, pkS = self.psum()
                        P.op("pe", lambda e, mt=mt, sft=sft, psS=psS: e.matmul(
                            psS[:, :NCH - sft], lhsT=mt, rhs=Hb[:, 0:NCH - sft], start=True, stop=True),
                            reads=[mk, hkb], writes=[pkS])
                        P.op("dve", lambda e, psS=psS, sft=sft: e.tensor_tensor(
                            out=H32[:, sft:NCH], in0=H32[:, sft:NCH], in1=psS[:, :NCH - sft], op=ALU.add),
                            reads=[hk32, pkS], writes=[hk32])
                        yield
                        if j < nsteps - 1:
                            P.op("act", lambda e, sft=sft: e.copy(out=Hb[:, sft:NCH], in_=H32[:, sft:NCH]),
                                 reads=[hk32], writes=[hkb])
                            yield
                    P.op("act", lambda e: e.copy(out=Hprev[:, g8, :], in_=H32[:, 0:NCH]),
                         reads=[hk32], writes=[K(("Hp", g8))])
                    yield

                yield from interleave([gchain(ci, ci) for ci in range(8)])
                for m in range(8):
                    psY, pkY = self.psum()
                    for tau in range(m + 1):
                        P.op("pe", lambda e, T=T, tau=tau, m=m, psY=psY: e.matmul(
                            psY[:, :NCH], lhsT=Kblk[:, T, tau, :], rhs=U[:, (m - tau)::8], start=(tau == 0), stop=False,
                            skip_group_check=True), reads=[K("Kblk"), K("U")], writes=[pkY], waw=(tau == 0))
                    for g8 in range(8):
                        q = g8 // 2
                        P.op("pe", lambda e, T=T, g8=g8, q=q, m=m, psY=psY: e.matmul(
                            psY[32 * q:32 * q + 32, 1:NCH], lhsT=Vpad[:, m, 8 * T + g8, :], rhs=Hprev[:, g8, 0:NCH - 1],
                            start=False, stop=(g8 == 7), tile_position=(0, 32 * q), skip_group_check=True),
                            reads=[K("Vpad"), K(("Hp", g8))], writes=[pkY], waw=False)
                    P.op("act", lambda e, m=m, psY=psY: e.copy(out=ytile[:, m::8], in_=psY[:, :NCH]), reads=[pkY],
                         writes=[YT], waw=(m == 0))
                    yield
                for c0_ in range(0, L // 512, 2):
                    pcs = []
                    for c in range(c0_, min(c0_ + 2, L // 512)):
                        t1, t1k = self.nxt(self.wk32)
                        pcs.append((slice(c * 512, (c + 1) * 512), t1, t1k))
                    for cs, t1, t1k in pcs:
                        P.op("dve", lambda e, t1=t1, cs=cs: e.tensor_tensor(out=t1[:], in0=ytile[:, cs], in1=ytile[:, cs], op=ALU.mult),
                             reads=[YT], writes=[t1k])
                    for cs, t1, t1k in pcs:
                        P.op("pool", lambda e, t1=t1: e.tensor_scalar(out=t1[:], in0=t1[:], scalar1=0.044715, scalar2=1.0,
                                                                      op0=ALU.mult, op1=ALU.add), reads=[t1k], writes=[t1k])
                    for cs, t1, t1k in pcs:
                        P.op("dve", lambda e, t1=t1, cs=cs: e.tensor_tensor(out=t1[:], in0=t1[:], in1=ytile[:, cs], op=ALU.mult),
                             reads=[t1k, YT], writes=[t1k])
                    for cs, t1, t1k in pcs:
                        P.op("act", lambda e, t1=t1: e.activation(out=t1[:], in_=t1[:], func=AF.Sigmoid, scale=1.5957691216),
                             reads=[t1k], writes=[t1k])
                    for cs, t1, t1k in pcs:
                        ob, obk = self.nxt(self.ev32)
                        P.op("dve", lambda e, t1=t1, ob=ob, cs=cs: e.tensor_tensor(out=ob[:], in0=t1[:], in1=ytile[:, cs], op=ALU.mult),
                             reads=[t1k, YT], writes=[obk])
                        self.store(ob[:], obk, self.y2T[s, T * 128:(T + 1) * 128, cs], ("y2T", s))
                    yield
            for tg in range(self.NG):
                cs = slice(tg * 512, (tg + 1) * 512)
                yg = y2gs[tg % 2]
                y2bf = y2bfs[tg % 2]
                ygk = K(("y2g", tg % 2))
                ybk = K(("y2bf", tg % 2))
                P.dma(lambda e, yg=yg, cs=cs, s=s: e.dma_start(out=yg, in_=self.y2T[s, :, cs].rearrange("(t p) l -> p t l", p=128)),
                      reads=[("y2T", s)], writes=[ygk, YT], dkey=ygk)
                P.op("act", lambda e, yg=yg, y2bf=y2bf: e.copy(out=y2bf[:, 0:2, :], in_=yg[:, 0:2, :]), reads=[ygk, YT], writes=[ybk])
                P.op("dve", lambda e, yg=yg, y2bf=y2bf: e.tensor_copy(out=y2bf[:, 2:4, :], in_=yg[:, 2:4, :]), reads=[ygk, YT],
                     writes=[ybk], waw=False)
                yield
                for oc in range(4):
                    ps, pk = self.psum()
                    for kc in range(4):
                        P.op("pe", lambda e, ps=ps, kc=kc, oc=oc, y2bf=y2bf: e.matmul(
                            ps[:], lhsT=WG[:, kc, oc * 128:(oc + 1) * 128], rhs=y2bf[:, kc, :], start=(kc == 0), stop=(kc == 3)),
                            reads=[K("WG"), ybk, YT], writes=[pk], waw=(kc == 0))
                    sg, sgk = self.nxt(self.wk32)
                    P.op("act", lambda e, ps=ps, sg=sg, oc=oc: e.activation(out=sg[:], in_=ps[:], func=AF.Sigmoid,
                                                                            bias=s5sm[:, 4 + oc:5 + oc]),
                         reads=[pk, "s5sm"], writes=[sgk])
                    gzt, gzk = self.nxt(self.ev32)
                    P.dma(lambda e, gzt=gzt, oc=oc, cs=cs, s=s: e.dma_start(out=gzt[:], in_=self.gzT[s, 1024 + oc * 128:1024 + (oc + 1) * 128, cs]),
                          reads=[("gzT", s)], writes=[gzk], dkey=(gzk, "ld"))
                    P.op("dve", lambda e, sg=sg, yg=yg, oc=oc: e.tensor_tensor(out=sg[:], in0=sg[:], in1=yg[:, oc, :], op=ALU.mult),
                         reads=[sgk, ygk, YT], writes=[sgk])
                    ob, obk = self.nxt(self.evbf)
                    P.op("dve", lambda e, sg=sg, gzt=gzt, ob=ob: e.tensor_tensor(out=ob[:], in0=sg[:], in1=gzt[:], op=ALU.mult),
                         reads=[sgk, gzk], writes=[obk])
                    self.store(ob[:], obk, self.mixT[s, 1024 + oc * 128:1024 + (oc + 1) * 128, cs], ("mixT", s))
                    yield

    def odd_layer(self, i, x_in, kin, x_out, kout):
        P = self.P
        L, NSEQ = self.L, self.NSEQ
        KB = 1024
        P.barrier()
        full_Wb = self.Wb
        self.Wb = self.arena[:, 0:8 * 3088].rearrange("p (k f) -> p k f", k=8)
        self.load_weights(self.odd_w_in[i], 8, 3088, self.Wb, "Wb", gain_dram=self.odd_norm_g[i])
        self.load_weights(self.odd_w_out[i], 8, 1024, self.Wo, "Wo")
        self.s5_consts()
        sm = self.s5sm
        wg16 = self.ov(50 * KB, [128, 512], BF16)[0:16, :]
        st, sk = self.nxt(self.wstage)
        P.dma(lambda e: e.dma_start(out=st[0:16, 0:512], in_=self.gla_w_gate[i]), writes=[sk], dkey=sk)
        P.op("dve", lambda e: e.tensor_copy(out=wg16, in_=st[0:16, 0:512]), reads=[sk], writes=["wg16"])
        P.dma(lambda e: e.dma_start(out=sm[:, 0:4], in_=self.gla_b_gate[i]), writes=["s5sm"], dkey="s5sm")
        P.dma(lambda e: e.dma_start(out=sm[:, 8:10], in_=self.gla_o_g[i]), writes=["s5sm"], dkey="s5sm")
        P.op("dve", lambda e: e.tensor_scalar(out=sm[:, 0:4], in0=sm[:, 0:4], scalar1=-1.0, scalar2=None, op0=ALU.mult),
             reads=["s5sm"], writes=["s5sm"])
        groups = [(s, tg) for s in range(NSEQ) for tg in range(self.NG)]
        for k, (s, tg) in enumerate(groups):
            if k == 0:
                self.prefetch_x(x_in, kin, s, tg)
            if k + 1 < len(groups):
                self.prefetch_x(x_in, kin, *groups[k + 1])
            self.odd_inproj_group(s, tg, x_in, kin, wg16)
        self.gla_phase()
        for s in range(NSEQ):
            self.out_proj(s, x_in, kin, x_out, kout, 8, self.mixT, "mixT")
        self.Wb = full_Wb

    def odd_inproj_group(self, s, tg, x_in, kin, wg16):
        P = self.P
        hT, hk, xt, xk = self.load_norm_group(x_in, kin, s, tg)
        cols = slice(tg * 512, (tg + 1) * 512)
        sm = self.s5sm
        for which, dst, dkey in ((0, self.q2T, "q2T"), (1, self.k2T, "k2T")):
            for c in range(4):
                ps, pk = self.proj_fm(hT, hk, which * 512 + c * 128)
                ob, obk = self.nxt(self.ev32)
                P.op("act" if c % 2 else "dve",
                     (lambda e, ps=ps, ob=ob: e.copy(out=ob[:], in_=ps[:])) if c % 2 else
                     (lambda e, ps=ps, ob=ob: e.tensor_copy(out=ob[:], in_=ps[:])), reads=[pk], writes=[obk])
                self.store(ob[:], obk, dst[s, c * 128:(c + 1) * 128, cols], (dkey, s))
        for j in range(4):
            for half in range(2):
                ps, pk = self.proj_tm(hT, hk, j, 1024 + half * 512)
                ob, obk = self.nxt(self.evbf)
                P.op("act" if half else "dve",
                     (lambda e, ps=ps, ob=ob: e.copy(out=ob[:], in_=ps[:])) if half else
                     (lambda e, ps=ps, ob=ob: e.tensor_copy(out=ob[:], in_=ps[:])), reads=[pk], writes=[obk])
                r0 = tg * 512 + j * 128
                self.store(ob[:], obk, self.vtm[s, r0:r0 + 128, half * 512:(half + 1) * 512], ("vtm", s))
        for c in range(8):
            ps, pk = self.proj_fm(hT, hk, 2048 + c * 128)
            ob, obk = self.nxt(self.ev32)
            P.op("act", lambda e, ps=ps, ob=ob: e.activation(out=ob[:], in_=ps[:], func=AF.Silu), reads=[pk], writes=[obk])
            self.store(ob[:], obk, self.gzT[s, c * 128:(c + 1) * 128, cols], ("gzT", s))
        ps, pk = self.proj_fm(hT, hk, 3072, M=16)
        rT, rk = self.nxt(self.wkbf)
        P.op("dve", lambda e, ps=ps, rT=rT: e.tensor_copy(out=rT[0:16, :], in_=ps[0:16, :]), reads=[pk], writes=[rk])
        for c in range(4):
            ps2, pk2 = self.psum()
            P.op("pe", lambda e, ps2=ps2, c=c, rT=rT: e.matmul(ps2[:], lhsT=wg16[:, c * 128:(c + 1) * 128], rhs=rT[0:16, :],
                                                               start=True, stop=True), reads=[rk, "wg16"], writes=[pk2])
            ex, exk = self.nxt(self.wk32)
            P.op("act", lambda e, ps2=ps2, ex=ex, c=c: e.activation(out=ex[:], in_=ps2[:], func=AF.Exp, scale=-1.0,
                                                                    bias=sm[:, c:c + 1]), reads=[pk2, "s5sm"], writes=[exk])
            ob, obk = self.nxt(self.ev32)
            P.op("act", lambda e, ex=ex, ob=ob: e.activation(out=ob[:], in_=ex[:], func=AF.Ln, bias=1.0), reads=[exk], writes=[obk])
            self.store(ob[:], obk, self.lgT[s, c * 128:(c + 1) * 128, cols], ("lgT", s))

    def gla_phase(self):
        P = self.P
        L, NSEQ = self.L, self.NSEQ
        KB = 1024
        ov = self.ov
        P.barrier()
        NCHN = 2
        CH = 38 * KB
        rmask = ov(NCHN * CH, [128, 512], F32)
        P.op("pool", lambda e: e.memset(rmask, 1.0), writes=["rmask"])
        for c in range(4):
            P.op("pool", lambda e, c=c: e.memset(rmask[:, c * 128:c * 128 + 1], 0.0), writes=["rmask"], waw=False)
        assert NCHN * CH + 2 * KB <= 80 * KB
        sm = self.s5sm
        psrot = [0]

        def lpsum():
            k = 4 + psrot[0] % 3
            psrot[0] += 1
            return self.psM[k], ("psM", k)

        def chain(s, h, ci):
            b = ci * CH
            T = lambda n: ("gla", n, ci)
            q32 = ov(b, [128, 512], F32)
            k32 = ov(b + 2 * KB, [128, 512], F32)
            Lg = ov(b + 4 * KB, [128, 512], F32)
            G = ov(b + 6 * KB, [128, 512], F32)
            E1 = ov(b + 8 * KB, [128, 512], F32)
            E2 = ov(b + 10 * KB, [128, 512], F32)
            gz = ov(b + 12 * KB, [128, 2, 512], F32)
            vt = ov(b + 16 * KB, [128, 4, 256], BF16)
            qd = ov(b + 18 * KB, [128, 512], BF16)
            ki = ov(b + 19 * KB, [128, 512], BF16)
            kdT = ov(b + 20 * KB, [128, 512], BF16)
            kdec = ov(b + 21 * KB, [128, 4, 128], BF16)
            S32 = ov(b + 22 * KB, [128, 256], F32)
            Sbf = ov(b + 23 * KB, [128, 256], BF16)
            scT = [ov(b + 23 * KB + 512 + k * 256, [128, 128], BF16) for k in range(2)]
            rs = ov(b + 24 * KB, [128, 512], F32)
            psO = [self.psM[2 * ci], self.psM[2 * ci + 1]]
            psOk = [("psM", 2 * ci), ("psM", 2 * ci + 1)]
            psT = self.psT[0]
            P.op("pool", lambda e: e.memset(S32, 0.0), writes=[T("S32")])
            P.op("pool", lambda e: e.memset(Sbf, 0.0), writes=[T("Sbf")])
            LD = [dict(q32=q32, k32=k32, Lg=Lg, vt=vt, gz=gz),
                  dict(q32=ov(b + 26 * KB, [128, 512], F32), k32=ov(b + 28 * KB, [128, 512], F32),
                       Lg=ov(b + 30 * KB, [128, 512], F32), vt=ov(b + 32 * KB, [128, 4, 256], BF16),
                       gz=ov(b + 34 * KB, [128, 2, 512], F32))]

            def issue(tg):
                d = LD[tg % 2]
                r = tg % 2
                cs = slice(tg * 512, (tg + 1) * 512)
                P.dma(lambda e: e.dma_start(out=d["q32"], in_=self.q2T[s, h * 128:(h + 1) * 128, cs]),
                      reads=[("q2T", s)], writes=[T(("q32", r))], dkey=T(("q32", r)))
                P.dma(lambda e: e.dma_start(out=d["k32"], in_=self.k2T[s, h * 128:(h + 1) * 128, cs]),
                      reads=[("k2T", s)], writes=[T(("k32", r))], dkey=T(("k32", r)))
                P.dma(lambda e: e.dma_start(out=d["Lg"], in_=self.lgT[s, h * 128:(h + 1) * 128, cs]),
                      reads=[("lgT", s)], writes=[T(("Lg", r))], dkey=T(("Lg", r)))
                P.dma(lambda e: e.dma_start(out=d["vt"], in_=self.vtm[s, cs, h * 256:(h + 1) * 256].rearrange(
                    "(j p) d -> p j d", p=128)), reads=[("vtm", s)], writes=[T(("vt", r))], dkey=T(("vt", r)))
                P.dma(lambda e: e.dma_start(out=d["gz"], in_=self.gzT[s, h * 256:(h + 1) * 256, cs].rearrange(
                    "(v p) t -> p v t", p=128)), reads=[("gzT", s)], writes=[T(("gz", r))], dkey=T(("gz", r)))

            issue(0)
            for tg in range(self.NG):
                cs = slice(tg * 512, (tg + 1) * 512)
                if tg + 1 < self.NG:
                    issue(tg + 1)
                r = tg % 2
                q32, k32, Lg, vt, gz = (LD[r][n] for n in ("q32", "k32", "Lg", "vt", "gz"))
                Tq, Tk, TL, Tv, Tg = (T((n, r)) for n in ("q32", "k32", "Lg", "vt", "gz"))
                P.op("dve", lambda e, Lg=Lg: e.tensor_tensor_scan(out=G, data0=rmask, data1=Lg, initial=0.0, op0=ALU.mult,
                                                           op1=ALU.add), reads=[TL, "rmask"], writes=[T("G")])
                yield
                P.op("act", lambda e: e.activation(out=E1, in_=G, func=AF.Exp, scale=-1.0 / 16), reads=[T("G")], writes=[T("E1")])
                P.op("act", lambda e: e.activation(out=E2, in_=G, func=AF.Exp, scale=1.0 / 16), reads=[T("G")], writes=[T("E2")])
                yield
                P.op("dve", lambda e, q32=q32: e.scalar_tensor_tensor(out=qd, in0=q32, scalar=128 ** -0.5, in1=E1, op0=ALU.mult,
                                                             op1=ALU.mult), reads=[Tq, T("E1")], writes=[T("qd")])
                P.op("pool", lambda e, k32=k32: e.tensor_tensor(out=ki, in0=k32, in1=E2, op=ALU.mult), reads=[Tk, T("E2")],
                     writes=[T("ki")])
                for c in range(4):
                    cc = slice(c * 128, (c + 1) * 128)
                    P.op("dve", lambda e, c=c, cc=cc, k32=k32: e.scalar_tensor_tensor(
                        out=kdT[:, cc], in0=k32[:, cc], scalar=E1[:, c * 128 + 127:c * 128 + 128], in1=E2[:, cc],
                        op0=ALU.mult, op1=ALU.mult), reads=[Tk, T("E1"), T("E2")], writes=[T("kdT")], waw=(c == 0))
                yield
                for c in range(4):
                    cc = slice(c * 128, (c + 1) * 128)
                    P.op("pe", lambda e, cc=cc: e.transpose(out=psT[:, cc], in_=kdT[:, cc], identity=self.ident[:]),
                         reads=[T("kdT"), "ident"], writes=["psT"], waw=(c == 0))
                P.op("act", lambda e: e.copy(out=kdec, in_=psT[:, 0:512].rearrange("p (c k) -> p c k", c=4)),
                     reads=["psT"], writes=[T("kdec")])
                yield
                for c in range(4):
                    cc = slice(c * 128, (c + 1) * 128)
                    pss, pssk = lpsum()
                    P.op("pe", lambda e, cc=cc, pss=pss: e.matmul(pss[:, 0:128], lhsT=ki[:, cc], rhs=qd[:, cc], start=True,
                                                                  stop=True), reads=[T("ki"), T("qd")], writes=[pssk])
                    sc = scT[c % 2]
                    sck = T(("scT", c % 2))
                    P.op("dve", lambda e, pss=pss, sc=sc: e.tensor_tensor(out=sc, in0=pss[:, 0:128], in1=self.maskLE[:],
                                                                          op=ALU.mult), reads=[pssk, "maskLE"], writes=[sck])
                    for vc in range(2):
                        P.op("pe", lambda e, vc=vc, c=c, cc=cc, sc=sc, vt=vt: e.matmul(
                            psO[vc][:, cc], lhsT=vt[:, c, vc * 128:(vc + 1) * 128], rhs=sc, start=True, stop=False,
                            skip_group_check=True), reads=[Tv, sck], writes=[psOk[vc]], waw=(c == 0))
                        P.op("pe", lambda e, vc=vc, cc=cc: e.matmul(
                            psO[vc][:, cc], lhsT=Sbf[:, vc * 128:(vc + 1) * 128], rhs=qd[:, cc], start=False, stop=True,
                            skip_group_check=True), reads=[T("Sbf"), T("qd")], writes=[psOk[vc]], waw=False)
                    psS, psSk = lpsum()
                    P.op("pe", lambda e, c=c, psS=psS, vt=vt: e.matmul(psS[:, 0:256], lhsT=kdec[:, c, :], rhs=vt[:, c, :], start=True,
                                                                stop=True), reads=[T("kdec"), Tv], writes=[psSk])
                    P.op("dve", lambda e, c=c, psS=psS: e.scalar_tensor_tensor(
                        out=S32, in0=S32, scalar=E1[:, c * 128 + 127:c * 128 + 128], in1=psS[:, 0:256], op0=ALU.mult,
                        op1=ALU.add), reads=[T("S32"), T("E1"), psSk], writes=[T("S32")])
                    P.op("act", lambda e: e.copy(out=Sbf, in_=S32), reads=[T("S32")], writes=[T("Sbf")])
                    yield
                sqs = []
                for vc in range(2):
                    sq, sqk = self.nxt(self.wkbf)
                    P.op("act", lambda e, vc=vc, sq=sq: e.activation(out=sq[:], in_=psO[vc][:], func=AF.Square),
                         reads=[psOk[vc]], writes=[sqk])
                    sqs.append((sq, sqk))
                pn, pnk = lpsum()
                for vc in range(2):
                    P.op("pe", lambda e, vc=vc, pn=pn, sq=sqs[vc][0]: e.matmul(pn[:], lhsT=self.ones[:], rhs=sq[:], start=(vc == 0),
                                                                stop=(vc == 1)), reads=[sqs[vc][1], "ones"], writes=[pnk],
                         waw=(vc == 0))
                P.op("act", lambda e, pn=pn: e.activation(out=rs, in_=pn[:], func=AF.Ln, scale=1.0 / 256, bias=EPS),
                     reads=[pnk], writes=[T("rs")])
                P.op("act", lambda e: e.activation(out=rs, in_=rs, func=AF.Exp, scale=-0.5), reads=[T("rs")], writes=[T("rs")])
                for vc in range(2):
                    tmp, tk = self.nxt(self.wk32)
                    P.op("dve", lambda e, vc=vc, tmp=tmp: e.scalar_tensor_tensor(
                        out=tmp[:], in0=psO[vc][:], scalar=sm[:, 8 + vc:9 + vc], in1=rs, op0=ALU.mult, op1=ALU.mult),
                        reads=[psOk[vc], "s5sm", T("rs")], writes=[tk])
                    ob, obk = self.nxt(self.evbf)
                    if os.environ.get("GLA_DBG") == "1":
                        P.op("dve", lambda e, vc=vc, ob=ob: e.tensor_copy(out=ob[:], in_=psO[vc][:]), reads=[psOk[vc]], writes=[obk])
                    elif os.environ.get("GLA_DBG") == "2":
                        P.op("dve", lambda e, vc=vc, ob=ob, tmp=tmp: e.tensor_copy(out=ob[:], in_=tmp[:]), reads=[tk], writes=[obk])
                    else:
                        P.op("pool", lambda e, vc=vc, tmp=tmp, ob=ob, gz=gz: e.tensor_tensor(out=ob[:], in0=tmp[:], in1=gz[:, vc, :],
                                                                                      op=ALU.mult), reads=[tk, Tg], writes=[obk])
                    self.store(ob[:], obk, self.mixT[s, h * 256 + vc * 128:h * 256 + (vc + 1) * 128, cs], ("mixT", s))
                yield

        jobs = [(s, h) for s in range(NSEQ) for h in range(4)]
        for r0 in range(0, len(jobs), NCHN):
            run_interleaved([chain(s, h, ci) for ci, (s, h) in enumerate(jobs[r0:r0 + NCHN])])


def host_layout(inputs, nseq_total=16):
    f = lambda a: np.ascontiguousarray(np.asarray(a, dtype=np.float32))
    d = {}
    d["even_norm_g"] = f(inputs["even_norm_g"].reshape(2, 8, 128).transpose(0, 2, 1))
    d["odd_norm_g"] = f(inputs["odd_norm_g"].reshape(2, 8, 128).transpose(0, 2, 1))
    d["even_w_in"] = f(inputs["even_w_in"])
    d["even_w_out"] = f(inputs["even_w_out"])
    d["odd_w_in"] = f(inputs["odd_w_in"])
    d["odd_w_out"] = f(inputs["odd_w_out"])
    d["sb_q_norm_g"] = f(inputs["sb_q_norm_g"].reshape(2, 128, 1))
    d["sb_k_norm_g"] = f(inputs["sb_k_norm_g"].reshape(2, 128, 1))
    d["gla_w_gate"] = f(inputs["gla_w_gate"])
    d["gla_b_gate"] = f(inputs["gla_b_gate"].reshape(2, 4, 128).transpose(0, 2, 1))
    d["gla_o_norm_g"] = f(inputs["gla_o_norm_g"].reshape(2, 2, 128).transpose(0, 2, 1))
    lam = np.stack([inputs["s5_lambda_re"], inputs["s5_lambda_im"]], axis=1)
    lam = lam.transpose(0, 3, 1, 2)
    d["s5_lam"] = f(np.concatenate([lam, lam], axis=1))
    d["s5_log_dt"] = f(np.broadcast_to(inputs["s5_log_dt"][:, None, :], (2, 128, 32)))
    b = np.concatenate([inputs["s5_b_re"], inputs["s5_b_im"]], axis=2)
    d["s5_b0"] = f(b.transpose(0, 2, 1, 3))
    c = np.concatenate([inputs["s5_c_re"], inputs["s5_c_im"]], axis=3)
    d["s5_c0"] = f(c.transpose(0, 3, 1, 2))
    d["s5_d"] = f(inputs["s5_d"].reshape(2, 4, 128).transpose(0, 2, 1))
    d["s5_w_glu"] = f(inputs["s5_w_glu"])
    d["s5_b_glu"] = f(inputs["s5_b_glu"].reshape(2, 4, 128).transpose(0, 2, 1))
    return d


def kernel(**inputs):
    x = np.asarray(inputs["x"], dtype=np.float32)
    bsz, L, _ = x.shape
    nseq = bsz // N_CORES
    b = Builder(L=L, NSEQ=nseq, depth=4)
    nc = b.build()
    params = host_layout({k: np.asarray(v) for k, v in inputs.items() if k != "x"})
    in_maps = []
    for c in range(N_CORES):
        m = dict(params)
        m["x"] = np.ascontiguousarray(x[c * nseq:(c + 1) * nseq])
        in_maps.append(m)
    res = run_bass_kernel_spmd(nc, in_maps, core_ids=list(range(N_CORES)))
    return np.concatenate([r["y"] for r in res.results], axis=0).astype(np.float32)
```
